# Optimizing a Trainium2 kernel written in Bass

```python
import jax, jax.numpy as jnp
from jax import lax
import numpy as np

D_MODEL = 1024
BATCH = 8
SEQ = 4096
DEPTH = 2

GDN_HEADS = D_MODEL // 256
GDN_DK = 128
GDN_DV = 128
GDN_CONV = 4
GDN_CHUNK = 64
MOBA_HEADS = D_MODEL // 128
MOBA_DH = 64
MOBA_BLOCK = 256
MOBA_TOPK = 3
MOBA_QCHUNK = 32
ROPE_DIMS = MOBA_DH // 4
ROPE_THETA = 500000.0
GDN_QK_W = GDN_HEADS * GDN_DK
GDN_V_W = GDN_HEADS * GDN_DV
MOBA_W = MOBA_HEADS * MOBA_DH
D_MIX = GDN_V_W + MOBA_W
IN_SPLITS = (GDN_QK_W, GDN_QK_W, GDN_V_W, GDN_HEADS, GDN_HEADS, GDN_V_W, MOBA_W, MOBA_W, MOBA_W)
N_IN = sum(IN_SPLITS)
D_FF = 256 * ((8 * D_MODEL // 3 + 255) // 256)
FFN_CONV = 3
DEEPNORM_ALPHA = (2 * DEPTH) ** 0.25
DEEPNORM_BETA = (8 * DEPTH) ** -0.25
LN_EPS = 1e-5
NORM_EPS = 1e-6

kernel_name = 'hybrid_gdn_moba_deepnorm_convffn'


def layer_norm(x, g, b):
    xf = x.astype(jnp.float32)
    mu = xf.mean(-1, keepdims=True)
    var = jnp.square(xf - mu).mean(-1, keepdims=True)
    return ((xf - mu) * lax.rsqrt(var + LN_EPS) * g + b).astype(x.dtype)


def causal_dwconv(x, w):
    k_w = w.shape[0]
    s = x.shape[1]
    xp = jnp.pad(x, ((0, 0), (k_w - 1, 0), (0, 0)))
    return sum(xp[:, j:j + s, :] * w[j] for j in range(k_w))


def l2norm(x):
    return x * lax.rsqrt(jnp.sum(x * x, -1, keepdims=True) + NORM_EPS)


def split_heads(t, n_heads):
    b, s, _ = t.shape
    return t.reshape(b, s, n_heads, -1).transpose(0, 2, 1, 3)


def partial_rotary(x, pos):
    half = ROPE_DIMS // 2
    inv = ROPE_THETA ** (-jnp.arange(half, dtype=jnp.float32) / half)
    ang = pos.astype(jnp.float32)[:, None] * inv[None, :]
    cos = jnp.cos(ang).astype(x.dtype)
    sin = jnp.sin(ang).astype(x.dtype)
    x1, x2, rest = x[..., :half], x[..., half:ROPE_DIMS], x[..., ROPE_DIMS:]
    return jnp.concatenate([x1 * cos - x2 * sin, x2 * cos + x1 * sin, rest], -1)


def gated_delta_rule(q, k, v, g, beta):
    b_, h, s, dk = q.shape
    dv = v.shape[-1]
    c = GDN_CHUNK
    n = s // c
    q = q * dk ** -0.5
    resh = lambda t: t.reshape(b_, h, n, c, *t.shape[3:])
    q, k, v, g, beta = map(resh, (q, k, v, g, beta))
    g = jnp.cumsum(g, axis=-1)
    idx = jnp.arange(c)
    lower_incl = idx[:, None] >= idx[None, :]
    decay = jnp.exp(jnp.where(lower_incl, g[..., :, None] - g[..., None, :], -jnp.inf))
    k_beta = k * beta[..., None]
    a_strict = jnp.where(idx[:, None] > idx[None, :],
                         jnp.einsum('bhncd,bhnmd->bhncm', k_beta, k) * decay, 0.0)
    t_mat = a_strict + jnp.eye(c, dtype=q.dtype)
    rhs = jnp.concatenate([v * beta[..., None], k_beta * jnp.exp(g)[..., None]], -1)
    sol = lax.linalg.triangular_solve(t_mat, rhs, left_side=True, lower=True, unit_diagonal=True)
    u, w = sol[..., :dv], sol[..., dv:]
    attn_intra = jnp.einsum('bhncd,bhnmd->bhncm', q, k) * decay
    q_dec = q * jnp.exp(g)[..., None]
    g_last = g[..., -1]
    k_dec = k * jnp.exp(g_last[..., None] - g)[..., None]

    def step(state, xs):
        q_c, k_c, u_c, w_c, a_c, gl = xs
        v_new = u_c - jnp.einsum('bhcd,bhde->bhce', w_c, state)
        o = jnp.einsum('bhcd,bhde->bhce', q_c, state) + jnp.einsum('bhcm,bhme->bhce', a_c, v_new)
        state = state * jnp.exp(gl)[..., None, None] + jnp.einsum('bhcd,bhce->bhde', k_c, v_new)
        return state, o

    xs = tuple(jnp.moveaxis(t, 2, 0) for t in (q_dec, k_dec, u, w, attn_intra, g_last))
    s0 = jnp.zeros((b_, h, dk, dv), q.dtype)
    _, o = lax.scan(step, s0, xs)
    return jnp.moveaxis(o, 0, 2).reshape(b_, h, s, dv)


def gdn_mixer(q, k, v, a_logit, b_logit, z, conv_w, a_log, dt_bias, norm_g):
    b_, s, _ = q.shape
    qkv = jax.nn.silu(causal_dwconv(jnp.concatenate([q, k, v], -1), conv_w))
    q, k, v = jnp.split(qkv, [GDN_QK_W, 2 * GDN_QK_W], -1)
    f32 = jnp.float32
    qh = l2norm(split_heads(q, GDN_HEADS).astype(f32))
    kh = l2norm(split_heads(k, GDN_HEADS).astype(f32))
    vh = split_heads(v, GDN_HEADS).astype(f32)
    beta = jax.nn.sigmoid(b_logit.astype(f32)).transpose(0, 2, 1)
    g = (-jnp.exp(a_log.astype(f32)) * jax.nn.softplus(a_logit.astype(f32) + dt_bias.astype(f32))).transpose(0, 2, 1)
    o = gated_delta_rule(qh, kh, vh, g, beta).transpose(0, 2, 1, 3)
    o = o * lax.rsqrt(jnp.mean(o * o, -1, keepdims=True) + NORM_EPS) * norm_g.astype(f32)
    o = o * jax.nn.silu(z.reshape(b_, s, GDN_HEADS, GDN_DV).astype(f32))
    return o.reshape(b_, s, GDN_V_W).astype(q.dtype)


def moba_mixer(q, k, v, pos):
    b_, s, _ = q.shape
    h, dh, blk, qc_len = MOBA_HEADS, MOBA_DH, MOBA_BLOCK, MOBA_QCHUNK
    qh = partial_rotary(split_heads(q, h), pos)
    kh = partial_rotary(split_heads(k, h), pos)
    vh = split_heads(v, h)
    nb = -(-s // blk)
    pad = nb * blk - s
    n_sel = min(MOBA_TOPK, nb)
    kb = jnp.pad(kh, ((0, 0), (0, 0), (0, pad), (0, 0))).reshape(b_, h, nb, blk, dh)
    vb = jnp.pad(vh, ((0, 0), (0, 0), (0, pad), (0, 0))).reshape(b_, h, nb, blk, dh)
    k_mean = kb.mean(axis=3)
    nq = s // qc_len
    q_chunks = jnp.moveaxis(qh.reshape(b_, h, nq, qc_len, dh), 2, 0)
    scale = dh ** -0.5
    bi = jnp.arange(b_)[:, None, None, None]
    hi = jnp.arange(h)[None, :, None, None]
    blk_ids = jnp.arange(nb)
    sel_rank = jnp.arange(n_sel)

    def attend(args):
        q_c, ci = args
        t = ci * qc_len + jnp.arange(qc_len)
        own = (ci * qc_len) // blk
        gate = jnp.einsum('bhqd,bhnd->bhqn', q_c, k_mean).astype(jnp.float32)
        gate = jnp.where(blk_ids < own, gate, -jnp.inf)
        _, sel = lax.top_k(gate, n_sel)
        sel_valid = sel_rank < own
        k_sel = kb[bi, hi, sel]
        v_sel = vb[bi, hi, sel]
        s_sel = jnp.einsum('bhqd,bhqjkd->bhqjk', q_c, k_sel).astype(jnp.float32) * scale
        s_sel = jnp.where(sel_valid[:, None], s_sel, -jnp.inf).reshape(b_, h, qc_len, n_sel * blk)
        k_own = lax.dynamic_index_in_dim(kb, own, axis=2, keepdims=False)
        v_own = lax.dynamic_index_in_dim(vb, own, axis=2, keepdims=False)
        s_own = jnp.einsum('bhqd,bhkd->bhqk', q_c, k_own).astype(jnp.float32) * scale
        key_pos = own * blk + jnp.arange(blk)
        s_own = jnp.where(key_pos[None, :] <= t[:, None], s_own, -jnp.inf)
        p = jax.nn.softmax(jnp.concatenate([s_own, s_sel], -1), axis=-1).astype(v_sel.dtype)
        p_own = p[..., :blk]
        p_sel = p[..., blk:].reshape(b_, h, qc_len, n_sel, blk)
        return (jnp.einsum('bhqk,bhkd->bhqd', p_own, v_own)
                + jnp.einsum('bhqjk,bhqjkd->bhqd', p_sel, v_sel))

    o = lax.map(attend, (q_chunks, jnp.arange(nq)))
    o = jnp.moveaxis(o, 0, 2).reshape(b_, h, s, dh)
    return o.transpose(0, 2, 1, 3).reshape(b_, s, MOBA_W)


def hybrid_layer(x, w_in, gdn_conv_w, gdn_a_log, gdn_dt_bias, gdn_norm_g, w_out,
                 ln1_g, ln1_b, w_up, ffn_conv_w, ffn_conv_b, w_down, ln2_g, ln2_b, pos):
    proj = x @ w_in
    offs = np.cumsum(IN_SPLITS)[:-1].tolist()
    q_a, k_a, v_a, a_logit, b_logit, z, q_b, k_b, v_b = jnp.split(proj, offs, -1)
    o_a = gdn_mixer(q_a, k_a, v_a, a_logit, b_logit, z, gdn_conv_w, gdn_a_log, gdn_dt_bias, gdn_norm_g)
    o_b = moba_mixer(q_b, k_b, v_b, pos)
    mix = jnp.concatenate([o_a, o_b], -1) @ w_out
    x = layer_norm(DEEPNORM_ALPHA * x + mix, ln1_g, ln1_b)
    hid = causal_dwconv(x @ w_up, ffn_conv_w) + ffn_conv_b
    gate, val = jnp.split(hid, 2, -1)
    ffn = (jax.nn.silu(gate) * val) @ w_down
    return layer_norm(DEEPNORM_ALPHA * x + ffn, ln2_g, ln2_b)


def setup_inputs(seed: int = 0) -> dict:
    key = jax.random.key(seed)
    ks = jax.random.split(key, 16)
    f = jnp.float32
    nrm = lambda k, shape, scale: jax.random.normal(k, shape, f) * scale
    x = jax.random.normal(ks[0], (BATCH, SEQ, D_MODEL), f)
    w_in = nrm(ks[1], (DEPTH, D_MODEL, N_IN), D_MODEL ** -0.5)
    gdn_conv_w = nrm(ks[2], (DEPTH, GDN_CONV, 2 * GDN_QK_W + GDN_V_W), GDN_CONV ** -0.5)
    gdn_a_log = jnp.log(jax.random.uniform(ks[3], (DEPTH, GDN_HEADS), f, 1.0, 16.0))
    dt = jnp.exp(jax.random.uniform(ks[4], (DEPTH, GDN_HEADS), f, float(np.log(1e-3)), float(np.log(1e-1))))
    gdn_dt_bias = dt + jnp.log(-jnp.expm1(-dt))
    gdn_norm_g = 1.0 + nrm(ks[5], (DEPTH, GDN_DV), 0.02)
    w_out = nrm(ks[6], (DEPTH, D_MIX, D_MODEL), DEEPNORM_BETA * D_MIX ** -0.5)
    ln1_g = 1.0 + nrm(ks[7], (DEPTH, D_MODEL), 0.02)
    ln1_b = nrm(ks[8], (DEPTH, D_MODEL), 0.02)
    w_up = nrm(ks[9], (DEPTH, D_MODEL, 2 * D_FF), D_MODEL ** -0.5)
    ffn_conv_w = nrm(ks[10], (DEPTH, FFN_CONV, 2 * D_FF), FFN_CONV ** -0.5)
    ffn_conv_b = nrm(ks[11], (DEPTH, 2 * D_FF), 0.01)
    w_down = nrm(ks[12], (DEPTH, D_FF, D_MODEL), DEEPNORM_BETA * D_FF ** -0.5)
    ln2_g = 1.0 + nrm(ks[13], (DEPTH, D_MODEL), 0.02)
    ln2_b = nrm(ks[14], (DEPTH, D_MODEL), 0.02)
    return {'x': x, 'w_in': w_in, 'gdn_conv_w': gdn_conv_w, 'gdn_a_log': gdn_a_log,
            'gdn_dt_bias': gdn_dt_bias, 'gdn_norm_g': gdn_norm_g, 'w_out': w_out,
            'ln1_g': ln1_g, 'ln1_b': ln1_b, 'w_up': w_up, 'ffn_conv_w': ffn_conv_w,
            'ffn_conv_b': ffn_conv_b, 'w_down': w_down, 'ln2_g': ln2_g, 'ln2_b': ln2_b}


def reference(x, w_in, gdn_conv_w, gdn_a_log, gdn_dt_bias, gdn_norm_g, w_out,
              ln1_g, ln1_b, w_up, ffn_conv_w, ffn_conv_b, w_down, ln2_g, ln2_b):
    pos = jnp.arange(x.shape[1], dtype=jnp.int32)
    for l in range(DEPTH):
        x = hybrid_layer(x, w_in[l], gdn_conv_w[l], gdn_a_log[l], gdn_dt_bias[l], gdn_norm_g[l],
                         w_out[l], ln1_g[l], ln1_b[l], w_up[l], ffn_conv_w[l], ffn_conv_b[l],
                         w_down[l], ln2_g[l], ln2_b[l], pos)
    return x
```

```python
import os
import numpy as np
import ml_dtypes
from contextlib import ExitStack
import concourse.bass as bass
import concourse.mybir as mybir
from concourse.bass_utils import run_bass_kernel_spmd

F32 = mybir.dt.float32
BF16 = mybir.dt.bfloat16
AF = mybir.ActivationFunctionType
ALU = mybir.AluOpType
AX = mybir.AxisListType

D = 1024
NIN = 3592
DFF = 2816
ALPHA = float((2 * 2) ** 0.25)
NEG = -30000.0


class Res:
    __slots__ = ("name", "writers", "readers")

    def __init__(self, name):
        self.name = name
        self.writers = []
        self.readers = {}


class Op:
    __slots__ = ("eng", "fn", "dma", "key", "value", "deps", "signal", "barrier")

    def __init__(self, eng, fn, dma=False, key=None):
        self.eng = eng
        self.fn = fn
        self.dma = dma
        self.key = key
        self.value = None
        self.deps = []
        self.signal = False
        self.barrier = False


ENGS = ("pe", "act", "dve", "pool", "sp")


class Sched:
    def __init__(self):
        self.ops = {e: [] for e in ENGS}
        self.keycount = {}
        self.keylast = {}
        self.allres = []
        self.last_pe_f32 = False
        self.ident_b = None
        self.safe = False

    def res(self, name):
        r = Res(name)
        self.allres.append(r)
        return r

    def _dep(self, op, prod, raw):
        if prod is op:
            return
        if (not prod.dma) and (not op.dma) and prod.eng == op.eng:
            if op.eng == "pe":
                return
        op.deps.append(prod)
        prod.signal = True

    def add(self, eng, fn, reads=(), writes=(), dma=False, key=None, partial=False, f32=False, out=None):
        if eng == "pe":
            if (not f32) and self.last_pe_f32 and out is not None:
                fn0 = fn
                dmy = out.bitcast(F32) if out.dtype != F32 else out
                idb = self.ident_b

                def fn(e, fn0=fn0, dmy=dmy, idb=idb):
                    e.matmul(dmy[0:64, 0:8], lhsT=idb[:, 0:64], rhs=idb[:, 0:8], start=True, stop=True)
                    return fn0(e)
            self.last_pe_f32 = f32
        op = Op(eng, fn, dma, key)
        for r in reads:
            for w in r.writers:
                self._dep(op, w, True)
        for r in writes:
            for w in r.writers:
                if not (partial and w.dma and op.dma):
                    self._dep(op, w, False)
            for rd in r.readers.values():
                if isinstance(rd, list):
                    for x in rd:
                        self._dep(op, x, False)
                else:
                    self._dep(op, rd, False)
        for r in reads:
            if dma:
                r.readers.setdefault("dma", []).append(op)
            else:
                r.readers[eng] = op
        for r in writes:
            if partial:
                r.writers = r.writers + [op]
            else:
                r.writers = [op]
            r.readers = {}
        if dma:
            assert key is not None
            self.keycount[key] = self.keycount.get(key, 0) + 16
            op.value = self.keycount[key]
            self.keylast[key] = op
        self.ops[eng].append(op)
        if self.safe and not dma:
            self.barrier()
        return op

    def pe32(self, fn, **kw):
        return self.add("pe", fn, f32=True, **kw)

    def pe16(self, out, fn, **kw):
        return self.add("pe", fn, out=out, **kw)

    def barrier(self):
        prods = []
        for e in ENGS:
            for o in reversed(self.ops[e]):
                if not o.dma and not o.barrier:
                    prods.append(o)
                    break
        prods += list(self.keylast.values())
        for e in ENGS:
            b = Op(e, None)
            b.barrier = True
            for p in prods:
                if p.dma or p.eng != e or e != "pe":
                    b.deps.append(p)
                    p.signal = True
            self.ops[e].append(b)
        for r in self.allres:
            r.writers = []
            r.readers = {}

    def emit(self, nc, es):
        esem = {e: es.enter_context(nc.semaphore("s_" + e)) for e in ENGS}
        ksem = {}
        for i, k in enumerate(self.keycount):
            ksem[k] = es.enter_context(nc.semaphore("k%d" % i))
        for e in ENGS:
            c = 0
            for o in self.ops[e]:
                if (not o.dma) and o.signal and not o.barrier:
                    c += 1
                    o.value = c
            if os.environ.get("SEMDBG"): print("SEM", e, "final", c, "nops", len(self.ops[e]))
        block = es.enter_context(nc.Block())
        hooks = {"pe": block.tensor, "act": block.scalar, "dve": block.vector,
                 "pool": block.gpsimd, "sp": block.sync}
        final_keys = dict(self.keycount)

        def mk(ename):
            def body(eng):
                waited = {}
                for o in self.ops[ename]:
                    need = {}
                    for p in o.deps:
                        s = ksem[p.key] if p.dma else esem[p.eng]
                        sid = id(s)
                        v = p.value
                        if waited.get(sid, 0) >= v:
                            continue
                        if sid not in need or need[sid][1] < v:
                            need[sid] = (s, v)
                    for sid, (s, v) in need.items():
                        eng.wait_ge(s, v)
                        waited[sid] = v
                    if o.fn is None:
                        continue
                    ins = o.fn(eng)
                    if o.dma:
                        ins.then_inc(ksem[o.key], 16)
                    elif o.signal:
                        ins.then_inc(esem[ename], 1)
                if ename == "sp":
                    for k, v in final_keys.items():
                        if waited.get(id(ksem[k]), 0) < v:
                            eng.wait_ge(ksem[k], v)
            return body

        for e in ENGS:
            hooks[e](mk(e))


def make_consts(S):
    i = np.arange(128)[:, None]
    j = np.arange(128)[None, :]
    same = (i // 64) == (j // 64)
    c = {}
    c["ident"] = np.eye(128, dtype=np.float32)
    c["caus01"] = (i <= j).astype(np.float32)
    c["mstrict"] = (same & (j < i)).astype(np.float32)
    c["negincl"] = (same & (j <= i)).astype(np.float32)
    LT = (same & (j <= i)).T.astype(np.float32)
    UT = (same & (j > i)).T.astype(np.float32)
    CS0 = np.zeros((128, 128), np.float32); CS0[:64, :] = 1.0
    CS1 = np.zeros((128, 128), np.float32); CS1[64:, :] = 1.0
    c["gl"] = np.concatenate([LT, UT, CS0, CS1], 1)
    offs = []
    for s in (1, 2, 4, 8, 16, 32):
        m = ((i // (2 * s)) == (j // (2 * s))) & ((i // s) != (j // s)) & (i > j)
        offs.append(m.T.astype(np.float32))
    c["boff"] = np.concatenate(offs, 1)
    c["aoff1"] = (((i // 2) == (j // 2)) & (i != j) & (i > j)).astype(np.float32)
    half = 8
    inv = 500000.0 ** (-np.arange(half, dtype=np.float32) / half)
    ang = np.arange(S, dtype=np.float32)[:, None] * inv[None, :]
    cos = np.cos(ang).astype(np.float32)
    sin = np.sin(ang).astype(np.float32)
    NT = S // 128
    cc = np.concatenate([cos, cos], 1).reshape(NT, 128, 16).transpose(1, 0, 2)
    ss = np.concatenate([sin, sin], 1).reshape(NT, 128, 16).transpose(1, 0, 2)
    c["rope"] = np.ascontiguousarray(np.concatenate([cc, ss], 2)).reshape(128, NT * 32)
    return c


def build(S=4096, L=2, dbg=False, stop_after=None):
    NT = S // 128
    NB = S // 256
    nc = bass.Bass("TRN2", target_bir_lowering=False)
    sc = Sched()
    es = ExitStack()

    def din(name, shape, dt=F32):
        return nc.dram_tensor(name, list(shape), dt, kind="ExternalInput").ap()

    def dscr(name, shape, dt=F32, out=False):
        kind = "ExternalOutput" if (out or dbg) else "Internal"
        return nc.dram_tensor(name, list(shape), dt, kind=kind).ap()

    x_d = din("x", [S, D])
    w_in_d = din("w_in", [L, D, NIN])
    gconv_d = din("gdn_conv_w", [L, 4, 1536])
    alog_d = din("gdn_a_log", [L, 4])
    dtb_d = din("gdn_dt_bias", [L, 4])
    gng_d = din("gdn_norm_g", [L, 128])
    w_out_d = din("w_out", [L, D, D])
    ln1g_d = din("ln1_g", [L, D])
    ln1b_d = din("ln1_b", [L, D])
    w_up_d = din("w_up", [L, D, 2 * DFF])
    fconvw_d = din("ffn_conv_w", [L, 3, 2 * DFF])
    fconvb_d = din("ffn_conv_b", [L, 2 * DFF])
    w_down_d = din("w_down", [L, DFF, D])
    ln2g_d = din("ln2_g", [L, D])
    ln2b_d = din("ln2_b", [L, D])
    c_ident_d = din("c_ident", [128, 128])
    c_caus_d = din("c_caus01", [128, 128])
    c_mstrict_d = din("c_mstrict", [128, 128])
    c_negincl_d = din("c_negincl", [128, 128])
    c_gl_d = din("c_gl", [128, 512])
    c_boff_d = din("c_boff", [128, 768])
    c_aoff1_d = din("c_aoff1", [128, 128])
    c_rope_d = din("c_rope", [128, NT * 32])

    y_d = dscr("y", [S, D], out=True)
    x1_d = dscr("x1res", [S, D])
    x2_d = dscr("x2res", [S, D]) if L > 1 else None
    oT_d = dscr("oT", [8, 128, S], BF16)

    def sb(name, shape, dt=F32):
        return es.enter_context(nc.sbuf_tensor(name, list(shape), dt))

    def ps(name, shape, dt=F32):
        return es.enter_context(nc.psum_tensor(name, list(shape), dt))

    xT = sb("xT", [128, 8, S], BF16)
    xT_r = [sc.res("xT%d" % t) for t in range(NT)]
    ident_f = sb("ident_f", [128, 128]); ident_b = sb("ident_b", [128, 128], BF16)
    caus_b = sb("caus_b", [128, 128], BF16)
    rope = sb("rope", [128, NT, 32])
    r_const = sc.res("consts")
    sc.ident_b = ident_b
    cbias = sb("cbias", [128, 4])
    sc.add("dve", lambda e: e.memset(cbias[:, 0:1], 1e-6), writes=[r_const], partial=True)
    sc.add("dve", lambda e: e.memset(cbias[:, 1:2], 1.0), writes=[r_const], partial=True)
    sc.add("dve", lambda e: e.memset(cbias[:, 2:3], 1e-5), writes=[r_const], partial=True)
    sc.add("dve", lambda e: e.memset(cbias[:, 3:4], 0.0), writes=[r_const], partial=True)

    bank = [ps("bank%d" % i, [128, 512]) for i in range(8)]
    bank_r = [sc.res("bank%d" % i) for i in range(8)]

    ARENA = 136 * 1024 // 4
    arena = sb("arena", [128, ARENA])

    class Carver:
        def __init__(self):
            self.off = 0

        def reset(self):
            self.off = 0

        def get(self, shape, dt=F32):
            n = int(np.prod(shape[1:]))
            nwords = n if dt == F32 else (n + 1) // 2
            a = arena[0:shape[0], self.off:self.off + nwords]
            self.off += (nwords + 15) // 16 * 16
            assert self.off <= ARENA, "arena overflow %d" % self.off
            if dt != F32:
                a = a.bitcast(dt)[:, 0:n]
            if len(shape) > 2:
                names = " ".join("d%d" % k for k in range(len(shape) - 1))
                kw = {"d%d" % k: shape[k + 1] for k in range(len(shape) - 2)}
                a = a.rearrange("p (%s) -> p %s" % (names, names), **kw)
            return a

    cv = Carver()

    sc.add("sp", lambda e: e.dma_start(out=ident_f[:], in_=c_ident_d[:, :]), writes=[r_const], dma=True, key="c0", partial=True)
    sc.add("pool", lambda e: e.dma_start(out=ident_b[:], in_=c_ident_d[:, :]), writes=[r_const], dma=True, key="c1", partial=True)
    sc.add("pool", lambda e: e.dma_start(out=caus_b[:], in_=c_caus_d[:, :]), writes=[r_const], dma=True, key="c1", partial=True)
    sc.add("sp", lambda e: e.dma_start(out=rope[:].rearrange("p t c -> p (t c)"), in_=c_rope_d[:, :]), writes=[r_const], dma=True, key="c0", partial=True)

    def phase0(src_d):
        cv.reset()
        xb = [cv.get([128, 1024], BF16) for _ in range(3)]
        xb_r = [sc.res("xb%d" % i) for i in range(3)]
        pst = [bank[0][:].bitcast(BF16), bank[1][:].bitcast(BF16)]
        for t in range(NT):
            s = t % 3
            sc.add("pool", lambda e, t=t, s=s: e.dma_start(out=xb[s], in_=src_d[t * 128:(t + 1) * 128, :]),
                   writes=[xb_r[s]], dma=True, key="xb%d" % s)
            p = t % 2
            pt = pst[p].rearrange("p (k c) -> p k c", k=8)
            for kc in range(8):
                sc.add("pe", lambda e, kc=kc, s=s, pt=pt: e.transpose(out=pt[:, kc, :], in_=xb[s][:, kc * 128:(kc + 1) * 128], identity=ident_b[:]),
                       reads=[xb_r[s], r_const], writes=[bank_r[p]])
            if t % 2 == 0:
                sc.add("act", lambda e, t=t, pt=pt: e.copy(out=xT[:, :, t * 128:(t + 1) * 128], in_=pt),
                       reads=[bank_r[p]], writes=[xT_r[t]])
            else:
                sc.add("dve", lambda e, t=t, pt=pt: e.tensor_copy(out=xT[:, :, t * 128:(t + 1) * 128], in_=pt),
                       reads=[bank_r[p]], writes=[xT_r[t]])

    def phaseA(l):
        cv.reset()
        wA = cv.get([128, 8, 1536], BF16)
        wA_r = [sc.res("wA%d" % k) for k in range(8)]
        KT = cv.get([128, 4, S], BF16)
        KT_r = [sc.res("KT%d" % t) for t in range(NT)]
        Vp = cv.get([128, NT, 8, 65], BF16)
        Vp_r = [sc.res("Vp%d" % t) for t in range(NT)]
        QT = [cv.get([128, 4, 256], BF16) for _ in range(2)]
        QT_r = [sc.res("QT%d" % i) for i in range(2)]
        kmT = cv.get([128, 4, 16], BF16)
        kmf = cv.get([128, 4])
        kmT_r = sc.res("kmT")
        qb = [cv.get([128, 512], BF16) for _ in range(2)]
        kb = [cv.get([128, 512], BF16) for _ in range(2)]
        qb_r = [sc.res("qb%d" % i) for i in range(2)]
        kb_r = [sc.res("kb%d" % i) for i in range(2)]
        t1 = cv.get([128, 8, 16]); t2 = cv.get([128, 8, 16])
        t1_r = sc.res("t1"); t2_r = sc.res("t2")
        gsb = cv.get([128, 16, 16]); m8 = cv.get([128, 16, 8]); sel = cv.get([128, 16, 16])
        gsb_r = sc.res("gsb"); sel_r = sc.res("sel")
        NPT = 4
        PT = [cv.get([128, 2, 256], BF16) for _ in range(NPT)]
        PT_r = [sc.res("PT%d" % i) for i in range(NPT)]
        acc = cv.get([128, 2, 8, 65])
        acc_r = [[sc.res("acc%d_%d" % (q, h)) for h in range(8)] for q in range(2)]
        rec = cv.get([128, 16])
        ob = cv.get([128, 2, 512], BF16)
        ob_r = sc.res("ob")
        obT = [cv.get([128, 4, 256], BF16) for _ in range(2)]
        obT_r = [sc.res("obT%d" % i) for i in range(2)]

        for kc in range(8):
            sc.add("pool", lambda e, kc=kc: e.dma_start(out=wA[:, kc, :], in_=w_in_d[l, kc * 128:(kc + 1) * 128, 2056:3592]),
                   writes=[wA_r[kc]], dma=True, key="wA%d" % kc)
        sc.add("pool", lambda e: e.memset(Vp[:, :, :, 64:65], 1.0), writes=Vp_r)
        sc.add("pool", lambda e: e.memset(gsb[:], -1e30), writes=[gsb_r])
        sc.add("pool", lambda e: e.memset(kmT[:], 0.0), writes=[kmT_r])

        pq, pk, pv = bank[0], bank[1], bank[2]
        ptr = bank[3][:].bitcast(BF16).rearrange("p (k c) -> p k c", k=8)
        pg0 = bank[4][:, 0:128].rearrange("p (a b) -> p a b", a=8)
        pg1 = bank[6][:, 0:128].rearrange("p (a b) -> p a b", a=8)
        SB = (0, 1, 2, 5)
        OB = (6, 7)
        cnt = {"s": 0, "o": 0, "pt": 0, "ev": 0}


        KSTOP = int(os.environ.get("KSTOP", "99"))
        for t in range(NT):
            b = t // 2
            qt_ = t % 2
            tsl = slice(t * 128, (t + 1) * 128)
            if KSTOP <= 0:
                break
            for g, pp in enumerate((pq, pk, pv)):
                for kc in range(8):
                    sc.add("pe", lambda e, g=g, kc=kc, pp=pp, tsl=tsl: e.matmul(pp[:], lhsT=xT[:, kc, tsl], rhs=wA[:, kc, g * 512:(g + 1) * 512], start=(kc == 0), stop=(kc == 7)),
                           reads=[xT_r[t], wA_r[kc]], writes=[bank_r[g]])
            if KSTOP <= 1:
                continue
            sc.add("act", lambda e, t=t: e.copy(out=Vp[:, t, :, 0:64], in_=pv[:].rearrange("p (h d) -> p h d", h=8)),
                   reads=[bank_r[2]], writes=[Vp_r[t]])
            s2 = t % 2
            for (pp, dst, dst_r, bi) in ((pq, qb[s2], qb_r[s2], 0), (pk, kb[s2], kb_r[s2], 1)):
                p3 = pp[:].rearrange("p (h d) -> p h d", h=8)
                d3 = dst.rearrange("p (h d) -> p h d", h=8)
                sc.add("act", lambda e, p3=p3, d3=d3: e.copy(out=d3[:, :, 16:64], in_=p3[:, :, 16:64]),
                       reads=[bank_r[bi]], writes=[dst_r])
                ccb = rope[:, t, 0:16].unsqueeze(1).to_broadcast([128, 8, 16])
                ssb = rope[:, t, 16:32].unsqueeze(1).to_broadcast([128, 8, 16])
                sc.add("dve", lambda e, p3=p3, ccb=ccb: e.tensor_tensor(out=t1, in0=p3[:, :, 0:16], in1=ccb, op=ALU.mult),
                       reads=[bank_r[bi], r_const], writes=[t1_r])
                sc.add("dve", lambda e, p3=p3, ssb=ssb: e.tensor_tensor(out=t2, in0=p3[:, :, 0:16], in1=ssb, op=ALU.mult),
                       reads=[bank_r[bi], r_const], writes=[t2_r])
                sc.add("dve", lambda e, d3=d3: e.tensor_tensor(out=d3[:, :, 0:8], in0=t1[:, :, 0:8], in1=t2[:, :, 8:16], op=ALU.subtract),
                       reads=[t1_r, t2_r], writes=[dst_r], partial=True)
                sc.add("dve", lambda e, d3=d3: e.tensor_tensor(out=d3[:, :, 8:16], in0=t1[:, :, 8:16], in1=t2[:, :, 0:8], op=ALU.add),
                       reads=[t1_r, t2_r], writes=[dst_r], partial=True)
            if KSTOP <= 2:
                continue
            for j in range(4):
                sc.add("pe", lambda e, j=j, s2=s2: e.transpose(out=ptr[:, j, :], in_=qb[s2][:, j * 128:(j + 1) * 128], identity=ident_b[:]),
                       reads=[qb_r[s2], r_const], writes=[bank_r[3]])
            for j in range(4):
                sc.add("pe", lambda e, j=j, s2=s2: e.transpose(out=ptr[:, 4 + j, :], in_=kb[s2][:, j * 128:(j + 1) * 128], identity=ident_b[:]),
                       reads=[kb_r[s2], r_const], writes=[bank_r[3]])
            qs = b % 2
            if KSTOP == 3 and os.environ.get("KSUB") == "a":
                continue
            sc.add("dve", lambda e, qs=qs, qt_=qt_: e.tensor_copy(out=QT[qs][:, :, qt_ * 128:(qt_ + 1) * 128], in_=ptr[:, 0:4, :]),
                   reads=[bank_r[3]], writes=[QT_r[qs]], partial=(qt_ == 1))
            if KSTOP == 3 and os.environ.get("KSUB") == "b":
                continue
            sc.add("dve", lambda e, tsl=tsl: e.tensor_copy(out=KT[:, :, tsl], in_=ptr[:, 4:8, :]),
                   reads=[bank_r[3]], writes=[KT_r[t]])
            if qt_ == 0 or KSTOP <= 3:
                continue
            if b + 1 < NB:
                sc.add("dve", lambda e, b=b: e.tensor_reduce(out=kmf, in_=KT[:, :, b * 256:(b + 1) * 256], axis=AX.X, op=ALU.add),
                       reads=[KT_r[t - 1], KT_r[t]], writes=[kmT_r])
                sc.add("dve", lambda e, b=b: e.tensor_scalar(out=kmT[:, :, b], in0=kmf, scalar1=1.0 / 256, scalar2=None, op0=ALU.mult),
                       reads=[kmT_r], writes=[kmT_r], partial=True)
            topk = b > 3
            if topk:
                KV_ = os.environ.get("KVAR", "")
                for par in range(2):
                    pgp = (pg0, pg1)[par]
                    for q2 in range(2):
                        for hh in range(4):
                            if KV_ == "q0" and q2 == 1: continue
                            if KV_ == "p0" and par == 1: continue
                            if KV_ == "h0" and hh > 0: continue
                            base = par * 64
                            sc.add("pe", lambda e, pgp=pgp, q2=q2, hh=hh, base=base, qs=qs: e.matmul(pgp[:, q2 * 4 + hh, :], lhsT=QT[qs][base:base + 64, hh, q2 * 128:(q2 + 1) * 128], rhs=kmT[base:base + 64, hh, :], start=True, stop=True),
                                   reads=[QT_r[qs], kmT_r], writes=[bank_r[(4, 6)[par]]])
                KT_ = os.environ.get("KTOPK", "full")
                if KT_ in ("gc", "gcm", "full"):
                    sc.add("dve", lambda e, b=b: e.tensor_copy(out=gsb[:, 0:8, 0:b], in_=pg0[:, :, 0:b]), reads=[bank_r[4]], writes=[gsb_r])
                    sc.add("dve", lambda e, b=b: e.tensor_copy(out=gsb[:, 8:16, 0:b], in_=pg1[:, :, 0:b]), reads=[bank_r[6]], writes=[gsb_r], partial=True)
                if KT_ in ("gcm", "full"):
                    for i16 in range(16):
                        sc.add("dve", lambda e, i16=i16: e.max(out=m8[:, i16, :], in_=gsb[:, i16, :]), reads=[gsb_r], writes=[sel_r], partial=True)
                if KT_ == "full":
                    sc.add("dve", lambda e: e.tensor_tensor(out=sel[:], in0=gsb[:], in1=m8[:, :, 2:3].to_broadcast([128, 16, 16]), op=ALU.is_ge),
                           reads=[gsb_r, sel_r], writes=[sel_r])
                else:
                    sc.add("dve", lambda e: e.memset(sel[:], 1.0), reads=[gsb_r, bank_r[4]], writes=[sel_r])
            for h in range(8 if KSTOP > 4 else 0):
                j = h // 2; base = (h % 2) * 64
                for n in [b] + list(range(b)):
                    si = SB[cnt["s"] % 4]; cnt["s"] += 1
                    pi = cnt["pt"] % NPT; cnt["pt"] += 1
                    oi = OB[cnt["o"] % 2]; cnt["o"] += 1
                    pss = bank[si][:].rearrange("p (k q) -> p k q", k=2)
                    pso = bank[oi][:, 0:130].rearrange("p (q d) -> p q d", q=2)
                    for kt in range(2):
                        ktile = 2 * n + kt
                        sc.add("pe", lambda e, pss=pss, kt=kt, j=j, base=base, ktile=ktile, qs=qs: e.matmul(pss[:, kt, :], lhsT=KT[base:base + 64, j, ktile * 128:(ktile + 1) * 128], rhs=QT[qs][base:base + 64, j, :], start=True, stop=True),
                               reads=[KT_r[ktile], QT_r[qs]], writes=[bank_r[si]])
                    sc.add("act", lambda e, pss=pss, pi=pi: e.activation(out=PT[pi][:], in_=pss, func=AF.Exp, scale=0.125, bias=cbias[:, 3:4]),
                           reads=[bank_r[si]], writes=[PT_r[pi]])
                    if n == b:
                        for kt in range(2):
                            sc.add("pool", lambda e, pi=pi, kt=kt: e.tensor_tensor(out=PT[pi][:, kt, kt * 128:(kt + 1) * 128], in0=PT[pi][:, kt, kt * 128:(kt + 1) * 128], in1=caus_b[:], op=ALU.mult),
                                   reads=[PT_r[pi], r_const], writes=[PT_r[pi]])
                    for q2 in range(2):
                        kts = [0] if (n == b and q2 == 0) else [0, 1]
                        for ii, kt in enumerate(kts):
                            sc.add("pe", lambda e, pso=pso, pi=pi, q2=q2, kt=kt, n=n, h=h, ii=ii, last=(ii == len(kts) - 1): e.matmul(pso[:, q2, :], lhsT=PT[pi][:, kt, q2 * 128:(q2 + 1) * 128], rhs=Vp[:, 2 * n + kt, h, :], start=(ii == 0), stop=last),
                                   reads=[PT_r[pi], Vp_r[2 * n + kt]], writes=[bank_r[oi]])
                    for q2 in range(2):
                        if n == b:
                            sc.add("dve", lambda e, pso=pso, q2=q2, h=h: e.tensor_copy(out=acc[:, q2, h, :], in_=pso[:, q2, :]),
                                   reads=[bank_r[oi]], writes=[acc_r[q2][h]])
                        elif topk:
                            sc.add("dve", lambda e, pso=pso, q2=q2, h=h, n=n: e.scalar_tensor_tensor(out=acc[:, q2, h, :], in0=pso[:, q2, :], scalar=sel[:, (h % 2) * 8 + q2 * 4 + h // 2, n:n + 1], in1=acc[:, q2, h, :], op0=ALU.mult, op1=ALU.add),
                                   reads=[bank_r[oi], sel_r, acc_r[q2][h]], writes=[acc_r[q2][h]])
                        else:
                            sc.add("dve", lambda e, pso=pso, q2=q2, h=h: e.tensor_tensor(out=acc[:, q2, h, :], in0=pso[:, q2, :], in1=acc[:, q2, h, :], op=ALU.add),
                                   reads=[bank_r[oi], acc_r[q2][h]], writes=[acc_r[q2][h]])
            if KSTOP <= 5:
                continue
            allacc = [acc_r[q][h] for q in range(2) for h in range(8)]
            sc.add("dve", lambda e: e.reciprocal(out=rec, in_=acc[:].rearrange("p q h d -> p (q h) d")[:, :, 64]), reads=allacc, writes=[ob_r])
            sc.add("dve", lambda e: e.tensor_tensor(out=ob[:].rearrange("p q (h d) -> p (q h) d", h=8), in0=acc[:].rearrange("p q h d -> p (q h) d")[:, :, 0:64], in1=rec.unsqueeze(2).to_broadcast([128, 16, 64]), op=ALU.mult),
                   reads=allacc + [ob_r], writes=[ob_r])
            os_ = b % 2
            for q2 in range(2):
                for j in range(4):
                    sc.add("pe", lambda e, q2=q2, j=j: e.transpose(out=ptr[:, q2 * 4 + j, :], in_=ob[:, q2, j * 128:(j + 1) * 128], identity=ident_b[:]),
                           reads=[ob_r, r_const], writes=[bank_r[3]])
            sc.add("dve", lambda e, os_=os_: e.tensor_copy(out=obT[os_][:].rearrange("p j (q c) -> p q j c", q=2), in_=ptr.rearrange("p (q j) c -> p q j c", q=2)),
                   reads=[bank_r[3]], writes=[obT_r[os_]])
            sc.add("sp", lambda e, os_=os_, b=b: e.dma_start(out=oT_d[4:8, :, b * 256:(b + 1) * 256].rearrange("j p c -> p j c"), in_=obT[os_][:]),
                   reads=[obT_r[os_]], writes=[], dma=True, key="obT%d" % os_)

    def phaseB(l):
        cv.reset()
        for _ in range(int(os.environ.get("ACTPAD", "0"))):
            sc.add("act", lambda e: e.copy(out=arena[:, 0:8], in_=ident_f[:, 0:8]))
        NBLK = S // 512
        wB = cv.get([128, 8, 2056], BF16)
        wB_r = [sc.res("wB%d" % k) for k in range(8)]
        gcw = cv.get([128, 12, 4]); dtb = cv.get([128, 4]); nexpA = cv.get([128, 4]); gng = cv.get([128, 128])
        mstrict = cv.get([128, 128]); mincl = cv.get([128, 128]); glc = cv.get([128, 4, 128])
        boff = cv.get([128, 6, 128]); aoff1 = cv.get([128, 128]); ones_f = cv.get([128, 128])
        pc_r = sc.res("pconst")
        rawb = cv.get([128, 2, 515]); rawb_r = [sc.res("rawb%d" % f) for f in range(2)]
        halo = cv.get([128, 12, 3]); halo_r = [sc.res("halo%d" % f) for f in range(12)]
        cacc = [cv.get([128, 512]) for _ in range(2)]; cacc_r = [sc.res("cacc%d" % i) for i in range(2)]
        cT = cv.get([128, 12, 512]); cT_r = [sc.res("cT%d" % f) for f in range(12)]
        Sst = [cv.get([128, 4, 128]) for _ in range(2)]
        S_r = [[sc.res("S%d_%d" % (i, h)) for h in range(4)] for i in range(2)]

        def tmp(name, shape, dt=F32, n=2):
            return [(cv.get(shape, dt), sc.res("%s%d" % (name, i))) for i in range(n)]

        Qtm = tmp("Qtm", [128, 4, 128]); Ktm = tmp("Ktm", [128, 4, 128]); Vtm = tmp("Vtm", [128, 4, 128])
        ssq = tmp("ssq", [128, 8]); rn = tmp("rn", [128, 8])
        sm = tmp("sm", [128, 64])
        zs = tmp("zs", [128, 512])
        junk = tmp("junk", [128, 128], n=1)[0]
        HT = 2
        QTh = tmp("QTh", [128, 128], F32, HT); KTh = tmp("KTh", [128, 128], F32, HT)
        GR = tmp("GR", [128, 128], F32, HT); Dm = tmp("Dm", [128, 128], F32, HT); Ds = tmp("Ds", [128, 128], F32, HT)
        Am = tmp("Am", [128, 128], F32, 2 * HT); attn = tmp("attn", [128, 128], F32, 2 * HT); attnT = tmp("attnT", [128, 128], F32, HT)
        Boall = tmp("Boall", [128, 6, 128], F32, HT); Em = tmp("Em", [128, 128], F32, 2 * HT); Dk = tmp("Dk", [128, 128], F32, 2 * HT)
        Xm = tmp("Xm", [128, 128], F32, HT); Rm = tmp("Rm", [128, 256], F32, HT); UW = tmp("UW", [128, 256], F32, HT)
        Kd = tmp("Kd", [128, 128], F32, HT); Qd = tmp("Qd", [128, 128], F32, HT); QpT = tmp("QpT", [128, 128], F32, HT)
        MpT = tmp("MpT", [128, 2, 128], F32, HT)
        osb = tmp("osb", [128, 4, 128], F32, 2); oss = tmp("oss", [128, 8], F32, 2)
        oab = tmp("oab", [128, 512], BF16, 2); oaT = tmp("oaT", [128, 4, 128], BF16, 2)
        ctr = {}

        def nxt(lst, key):
            i = ctr.get(key, 0); ctr[key] = i + 1
            return lst[i % len(lst)]

        PB = tuple(int(x) for x in os.environ.get("PB", "2,3,4,6,7").split(","))

        def pbank():
            i = ctr.get("pb", 0); ctr["pb"] = i + 1
            bi = PB[i % len(PB)]
            return bank[bi], bank_r[bi]

        for kc in range(8):
            sc.add("pool", lambda e, kc=kc: e.dma_start(out=wB[:, kc, :], in_=w_in_d[l, kc * 128:(kc + 1) * 128, 0:2056]),
                   writes=[wB_r[kc]], dma=True, key="wB%d" % kc)
        w4 = cT.rearrange("p f t -> p (f t)")[0:4, 0:1536]; w4_r = sc.res("w4")
        sc.add("sp", lambda e: e.dma_start(out=w4, in_=gconv_d[l, :, :]), writes=[w4_r, cT_r[0], cT_r[1], cT_r[2]], dma=True, key="w4")
        for f in range(12):
            sc.pe32(lambda e, f=f: e.transpose(out=bank[0][:, f * 4:(f + 1) * 4], in_=w4[0:4, f * 128:(f + 1) * 128], identity=ident_f[0:4, 0:4]), reads=[w4_r, cT_r[0], cT_r[1], cT_r[2], r_const], writes=[bank_r[0]])
        sc.add("dve", lambda e: e.tensor_copy(out=gcw.rearrange("p f j -> p (f j)"), in_=bank[0][:, 0:48]), reads=[bank_r[0]], writes=[pc_r], partial=True)
        sc.add("sp", lambda e: e.dma_start(out=dtb, in_=dtb_d[l, :].partition_broadcast(128)), writes=[pc_r], dma=True, key="pc", partial=True)
        sc.add("sp", lambda e: e.dma_start(out=nexpA, in_=alog_d[l, :].partition_broadcast(128)), writes=[pc_r], dma=True, key="pc", partial=True)
        sc.add("sp", lambda e: e.dma_start(out=gng, in_=gng_d[l, :].partition_broadcast(128)), writes=[pc_r], dma=True, key="pc", partial=True)
        sc.add("sp", lambda e: e.dma_start(out=mstrict, in_=c_mstrict_d[:, :]), writes=[pc_r], dma=True, key="pc", partial=True)
        sc.add("sp", lambda e: e.dma_start(out=glc.rearrange("p a b -> p (a b)"), in_=c_gl_d[:, :]), writes=[pc_r], dma=True, key="pc", partial=True)
        sc.add("sp", lambda e: e.dma_start(out=boff.rearrange("p a b -> p (a b)"), in_=c_boff_d[:, :]), writes=[pc_r], dma=True, key="pc", partial=True)
        sc.add("sp", lambda e: e.dma_start(out=aoff1, in_=c_aoff1_d[:, :]), writes=[pc_r], dma=True, key="pc", partial=True)
        sc.add("sp", lambda e: e.dma_start(out=mincl, in_=c_negincl_d[:, :]), writes=[pc_r], dma=True, key="pc", partial=True)
        sc.add("act", lambda e: e.activation(out=nexpA, in_=nexpA, func=AF.Exp, bias=cbias[:, 3:4]), reads=[pc_r], writes=[pc_r])
        sc.add("dve", lambda e: e.tensor_scalar(out=nexpA, in0=nexpA, scalar1=-1.0, scalar2=None, op0=ALU.mult), reads=[pc_r], writes=[pc_r])
        sc.add("dve", lambda e: e.memset(ones_f, 1.0), writes=[pc_r], reads=[pc_r])
        sc.add("dve", lambda e: e.memset(Sst[0][:], 0.0), writes=S_r[0])
        sc.add("pool", lambda e: e.memset(halo, 0.0), writes=halo_r)
        LTc, UTc, CS0c, CS1c = (glc[:, i, :] for i in range(4))

        cur = 0

        KB = int(os.environ.get("KB", "99"))
        for c in range(NBLK):
            csl = slice(c * 512, (c + 1) * 512)
            for f in range(12):
                pb, pb_r = bank[f % 2], bank_r[f % 2]
                for kc in range(8):
                    sc.pe16(pb[:], lambda e, pb=pb, f=f, kc=kc, csl=csl: e.matmul(pb[:], lhsT=wB[:, kc, f * 128:(f + 1) * 128], rhs=xT[:, kc, csl], start=(kc == 0), stop=(kc == 7)),
                           reads=[wB_r[kc]] + xT_r[4 * c:4 * c + 4], writes=[pb_r])
                rs = f % 2
                sc.add("pool", lambda e, f=f, rs=rs: e.tensor_copy(out=rawb[:, rs, 0:3], in_=halo[:, f, :]), reads=[halo_r[f]], writes=[rawb_r[rs]])
                sc.add("act", lambda e, pb=pb, rs=rs: e.copy(out=rawb[:, rs, 3:515], in_=pb[:]), reads=[pb_r], writes=[rawb_r[rs]], partial=True)
                sc.add("pool", lambda e, f=f, rs=rs: e.tensor_copy(out=halo[:, f, :], in_=rawb[:, rs, 512:515]), reads=[rawb_r[rs]], writes=[halo_r[f]])
                ca, ca_r = cacc[f % 2], cacc_r[f % 2]
                sc.add("act", lambda e, pb=pb, f=f, ca=ca: e.activation(out=ca, in_=pb[:], func=AF.Copy, scale=gcw[:, f, 3:4]), reads=[pb_r, pc_r], writes=[ca_r])
                for j in (2, 1, 0):
                    sc.add("dve", lambda e, f=f, j=j, ca=ca, rs=rs: e.scalar_tensor_tensor(out=ca, in0=rawb[:, rs, j:j + 512], scalar=gcw[:, f, j:j + 1], in1=ca, op0=ALU.mult, op1=ALU.add),
                           reads=[rawb_r[rs], pc_r, ca_r], writes=[ca_r])
                sc.add("act", lambda e, f=f, ca=ca: e.activation(out=cT[:, f, :], in_=ca, func=AF.Silu, bias=cbias[:, 3:4]), reads=[ca_r], writes=[cT_r[f]])
            for tt in range(4 if KB > 1 else 0):
                t = c * 4 + tt
                tsl = slice(t * 128, (t + 1) * 128)
                lsl = slice(tt * 128, (tt + 1) * 128)
                (Qt, Qt_r) = nxt(Qtm, "Qtm"); (Kt, Kt_r) = nxt(Ktm, "Ktm"); (Vt, Vt_r) = nxt(Vtm, "Vtm")
                (sq, sq_r) = nxt(ssq, "ssq"); (rnn, rn_r) = nxt(rn, "rn"); (smt, sm_r) = nxt(sm, "sm"); (zst, zs_r) = nxt(zs, "zs")
                for g in range(3):
                    pb, pb_r = bank[2 + g], bank_r[2 + g]
                    for h in range(4):
                        sc.pe32(lambda e, pb=pb, g=g, h=h, lsl=lsl: e.transpose(out=pb[:, h * 128:(h + 1) * 128], in_=cT[:, g * 4 + h, lsl], identity=ident_f[:]),
                               reads=[cT_r[g * 4 + h], r_const], writes=[pb_r])
                for g in range(2):
                    for h in range(4):
                        sc.add("act", lambda e, g=g, h=h, sq=sq: e.activation(out=junk[0], in_=bank[2 + g][:, h * 128:(h + 1) * 128], func=AF.Square, bias=cbias[:, 3:4], accum_out=sq[:, g * 4 + h:g * 4 + h + 1]),
                               reads=[bank_r[2 + g]], writes=[sq_r, junk[1]], partial=True)
                sc.add("act", lambda e, sq=sq, rnn=rnn: e.activation(out=rnn, in_=sq, func=AF.Sqrt, bias=cbias[:, 0:1]), reads=[sq_r, r_const], writes=[rn_r])
                sc.add("dve", lambda e, rnn=rnn: e.reciprocal(out=rnn, in_=rnn), reads=[rn_r], writes=[rn_r])
                sc.add("dve", lambda e, rnn=rnn: e.tensor_scalar(out=rnn[:, 0:4], in0=rnn[:, 0:4], scalar1=float(128 ** -0.5), scalar2=None, op0=ALU.mult), reads=[rn_r], writes=[rn_r])
                sc.add("dve", lambda e, Qt=Qt, rnn=rnn: e.tensor_tensor(out=Qt, in0=bank[2][:].rearrange("p (h d) -> p h d", h=4), in1=rnn[:, 0:4].unsqueeze(2).to_broadcast([128, 4, 128]), op=ALU.mult),
                       reads=[bank_r[2], rn_r], writes=[Qt_r])
                sc.add("dve", lambda e, Kt=Kt, rnn=rnn: e.tensor_tensor(out=Kt, in0=bank[3][:].rearrange("p (h d) -> p h d", h=4), in1=rnn[:, 4:8].unsqueeze(2).to_broadcast([128, 4, 128]), op=ALU.mult),
                       reads=[bank_r[3], rn_r], writes=[Kt_r])
                sc.add("act", lambda e, Vt=Vt: e.copy(out=Vt, in_=bank[4][:].rearrange("p (h d) -> p h d", h=4)), reads=[bank_r[4]], writes=[Vt_r])
                if KB <= 2:
                    continue
                pab, pab_r = bank[5], bank_r[5]
                for kc in range(8):
                    sc.pe16(bank[5][:, 0:8], lambda e, kc=kc, tsl=tsl: e.matmul(bank[5][:, 0:8], lhsT=xT[:, kc, tsl], rhs=wB[:, kc, 1536:1544], start=(kc == 0), stop=(kc == 7)),
                           reads=[xT_r[t], wB_r[kc]], writes=[pab_r])
                sc.add("dve", lambda e, smt=smt: e.tensor_tensor(out=smt[:, 0:4], in0=bank[5][:, 0:4], in1=dtb, op=ALU.add), reads=[pab_r, pc_r], writes=[sm_r])
                sc.add("dve", lambda e, smt=smt: e.tensor_scalar(out=smt[:, 36:40], in0=smt[:, 0:4], scalar1=-1.0, scalar2=None, op0=ALU.mult), reads=[sm_r], writes=[sm_r])
                sc.add("dve", lambda e, smt=smt: e.tensor_tensor(out=smt[:, 4:8], in0=smt[:, 0:4], in1=smt[:, 36:40], op=ALU.min), reads=[sm_r], writes=[sm_r])
                sc.add("act", lambda e, smt=smt: e.activation(out=smt[:, 4:8], in_=smt[:, 4:8], func=AF.Exp, bias=cbias[:, 3:4]), reads=[sm_r], writes=[sm_r])
                sc.add("act", lambda e, smt=smt: e.activation(out=smt[:, 4:8], in_=smt[:, 4:8], func=AF.Ln, bias=cbias[:, 1:2]), reads=[sm_r, r_const], writes=[sm_r])
                sc.add("dve", lambda e, smt=smt: e.scalar_tensor_tensor(out=smt[:, 8:12], in0=smt[:, 0:4], scalar=0.0, in1=smt[:, 4:8], op0=ALU.max, op1=ALU.add), reads=[sm_r], writes=[sm_r])
                sc.add("dve", lambda e, smt=smt: e.tensor_tensor(out=smt[:, 8:12], in0=smt[:, 8:12], in1=nexpA, op=ALU.mult), reads=[sm_r, pc_r], writes=[sm_r])
                sc.add("act", lambda e, smt=smt: e.activation(out=smt[:, 12:16], in_=bank[5][:, 4:8], func=AF.Exp, scale=-1.0, bias=cbias[:, 3:4]), reads=[pab_r], writes=[sm_r])
                sc.add("dve", lambda e, smt=smt: e.tensor_scalar(out=smt[:, 12:16], in0=smt[:, 12:16], scalar1=1.0, scalar2=None, op0=ALU.add), reads=[sm_r], writes=[sm_r])
                sc.add("dve", lambda e, smt=smt: e.reciprocal(out=smt[:, 12:16], in_=smt[:, 12:16]), reads=[sm_r], writes=[sm_r])
                for kc in range(8):
                    sc.pe16(bank[5][:], lambda e, kc=kc, tsl=tsl: e.matmul(bank[5][:], lhsT=xT[:, kc, tsl], rhs=wB[:, kc, 1544:2056], start=(kc == 0), stop=(kc == 7)),
                           reads=[xT_r[t], wB_r[kc]], writes=[pab_r])
                sc.add("act", lambda e, zst=zst: e.activation(out=zst, in_=bank[5][:], func=AF.Silu, bias=cbias[:, 3:4]), reads=[pab_r], writes=[zs_r])
                sc.add("pool", lambda e, zst=zst: e.tensor_tensor(out=zst.rearrange("p (h d) -> p h d", h=4), in0=zst.rearrange("p (h d) -> p h d", h=4), in1=gng.unsqueeze(1).to_broadcast([128, 4, 128]), op=ALU.mult),
                       reads=[zs_r, pc_r], writes=[zs_r])
                for i4, lt in enumerate((LTc, UTc, CS0c, CS1c)):
                    sc.pe32(lambda e, i4=i4, lt=lt, smt=smt: e.matmul(bank[5][:, 16 + 4 * i4:20 + 4 * i4], lhsT=lt, rhs=smt[:, 8:12], start=True, stop=True),
                           reads=[sm_r, pc_r], writes=[pab_r])
                sc.add("dve", lambda e, smt=smt: e.tensor_copy(out=smt[:, 40:44], in_=bank[5][:, 16:20]), reads=[pab_r], writes=[sm_r])
                sc.add("dve", lambda e, smt=smt: e.tensor_scalar(out=smt[:, 16:32], in0=bank[5][:, 16:32], scalar1=-60.0, scalar2=None, op0=ALU.max), reads=[pab_r], writes=[sm_r])
                sc.add("act", lambda e, smt=smt: e.activation(out=smt[:, 16:32], in_=smt[:, 16:32], func=AF.Exp, bias=cbias[:, 3:4]), reads=[sm_r], writes=[sm_r])
                sc.add("dve", lambda e, smt=smt: e.tensor_tensor(out=smt[:, 32:36], in0=smt[:, 12:16], in1=smt[:, 16:20], op=ALU.mult), reads=[sm_r], writes=[sm_r])
                (osb_t, osb_r) = nxt(osb, "osb"); (oss_t, oss_r) = nxt(oss, "oss")
                if KB <= 3:
                    continue
                sc.safe = os.environ.get("SAFE", "1") == "1"
                for h in range(4):
                    (QT_, QT_r_) = nxt(QTh, "QTh"); (KT_, KT_r_) = nxt(KTh, "KTh"); (GR_, GR_r_) = nxt(GR, "GR")
                    (D_, D_r) = nxt(Dm, "Dm"); (Ds_, Ds_r) = nxt(Ds, "Ds"); (A_, A_r) = nxt(Am, "Am")
                    (at_, at_r) = nxt(attn, "attn"); (atT_, atT_r) = nxt(attnT, "attnT"); (Bo_, Bo_r) = nxt(Boall, "Bo")
                    (X_, X_r) = nxt(Xm, "X"); (R_, R_r) = nxt(Rm, "R"); (UW_, UW_r) = nxt(UW, "UW")
                    (Kd_, Kd_r) = nxt(Kd, "Kd"); (Qd_, Qd_r) = nxt(Qd, "Qd"); (QpT_, QpT_r) = nxt(QpT, "QpT"); (Mp_, Mp_r) = nxt(MpT, "MpT")
                    beta_h = smt[:, 12 + h:13 + h]; egc_h = smt[:, 16 + h:17 + h]; egu_h = smt[:, 20 + h:21 + h]
                    gcum_h = smt[:, 40 + h:41 + h]; bk_h = smt[:, 32 + h:33 + h]
                    pb, pb_r = pbank()
                    sc.pe32(lambda e, pb=pb, Qt=Qt, h=h: e.transpose(out=pb[:, 0:128], in_=Qt[:, h, :], identity=ident_f[:]), reads=[Qt_r, r_const], writes=[pb_r])
                    sc.pe32(lambda e, pb=pb, Kt=Kt, h=h: e.transpose(out=pb[:, 128:256], in_=Kt[:, h, :], identity=ident_f[:]), reads=[Kt_r, r_const], writes=[pb_r])
                    sc.add("act", lambda e, pb=pb, QT_=QT_: e.copy(out=QT_, in_=pb[:, 0:128]), reads=[pb_r], writes=[QT_r_])
                    sc.add("dve", lambda e, pb=pb, KT_=KT_: e.tensor_copy(out=KT_, in_=pb[:, 128:256]), reads=[pb_r], writes=[KT_r_])
                    sc.add("act", lambda e, GR_=GR_, smt=smt, h=h: e.activation(out=GR_, in_=ones_f, func=AF.Copy, scale=smt[:, 8 + h:9 + h]), reads=[sm_r, pc_r], writes=[GR_r_])
                    K4 = os.environ.get("K4", "z")
                    if KB == 4 and K4 <= "a":
                        continue
                    pg_, pg_r = pbank()
                    sc.pe32(lambda e, pg_=pg_, GR_=GR_: e.matmul(pg_[:, 0:128], lhsT=GR_, rhs=LTc, start=True, stop=True), reads=[GR_r_, pc_r], writes=[pg_r])
                    sc.add("dve", lambda e, pg_=pg_, D_=D_, gcum_h=gcum_h: e.tensor_scalar(out=D_, in0=pg_[:, 0:128], scalar1=gcum_h, scalar2=0.0, op0=ALU.subtract, op1=ALU.max), reads=[pg_r, sm_r], writes=[D_r])
                    sc.add("dve", lambda e, D_=D_: e.tensor_scalar(out=D_, in0=D_, scalar1=60.0, scalar2=None, op0=ALU.min), reads=[D_r], writes=[D_r])
                    if os.environ.get("K5") == "waitD0":
                        sc.pe32(lambda e, pg_=pg_: e.transpose(out=pg_[:, 256:384], in_=ident_f[:], identity=ident_f[:]), reads=[r_const, D_r], writes=[])
                    sc.add("act", lambda e, D_=D_: e.activation(out=D_, in_=D_, func=AF.Exp, scale=-1.0, bias=cbias[:, 3:4]), reads=[D_r], writes=[D_r])
                    if os.environ.get("K5") == "waitD1":
                        sc.pe32(lambda e, pg_=pg_: e.transpose(out=pg_[:, 256:384], in_=ident_f[:], identity=ident_f[:]), reads=[r_const, D_r], writes=[])
                    PD = os.environ.get("PD", "dve")
                    sc.add(PD, lambda e, D_=D_, Ds_=Ds_: e.tensor_tensor(out=Ds_, in0=D_, in1=mstrict, op=ALU.mult), reads=[D_r, pc_r], writes=[Ds_r])
                    sc.add(PD, lambda e, D_=D_: e.tensor_tensor(out=D_, in0=D_, in1=mincl, op=ALU.mult), reads=[D_r, pc_r], writes=[D_r])
                    if os.environ.get("K5") == "waitD2":
                        sc.pe32(lambda e, pg_=pg_: e.transpose(out=pg_[:, 256:384], in_=ident_f[:], identity=ident_f[:]), reads=[r_const, Ds_r], writes=[])
                    if KB == 4 and K4 <= "b":
                        continue
                    pk_, pk_r = pbank()
                    K8 = os.environ.get("K8", "")
                    if K8 != "noKK" and K8 != "none":
                        sc.pe32(lambda e, pk_=pk_, KT_=KT_: e.matmul(pk_[:, 0:128], lhsT=KT_, rhs=KT_, start=True, stop=True), reads=[KT_r_], writes=[pk_r])
                    if K8 != "noQK" and K8 != "none":
                        sc.pe32(lambda e, pk_=pk_, QT_=QT_, KT_=KT_: e.matmul(pk_[:, 128:256], lhsT=QT_, rhs=KT_, start=True, stop=True), reads=[QT_r_, KT_r_], writes=[pk_r])
                    sc.add("act", lambda e, pk_=pk_, A_=A_, beta_h=beta_h: e.activation(out=A_, in_=pk_[:, 0:128], func=AF.Copy, scale=beta_h), reads=[pk_r, sm_r], writes=[A_r])
                    sc.add("act", lambda e, pk_=pk_, at_=at_: e.copy(out=at_, in_=pk_[:, 128:256]), reads=[pk_r], writes=[at_r])
                    (A0_, A0_r) = nxt(Am, "Am"); (at0_, at0_r) = nxt(attn, "attn")
                    sc.add("dve", lambda e, A_=A_, A0_=A0_, Ds_=Ds_: e.tensor_tensor(out=A0_, in0=A_, in1=Ds_, op=ALU.mult), reads=[A_r, Ds_r], writes=[A0_r])
                    sc.add("dve", lambda e, at_=at_, at0_=at0_, D_=D_: e.tensor_tensor(out=at0_, in0=at_, in1=D_, op=ALU.mult), reads=[at_r, D_r], writes=[at0_r])
                    A_, A_r, at_, at_r = A0_, A0_r, at0_, at0_r
                    if KB == 4 and K4 <= "c":
                        continue
                    if os.environ.get("HB", "0") == "1":
                        sc.barrier()
                    if os.environ.get("K6") == "samebank":
                        pt_, pt_r = pk_[:, 256:512], pk_r
                    else:
                        pt_, pt_r = pbank()
                    K5 = os.environ.get("K5", "")
                    if K5 == "waitonly":
                        sc.pe32(lambda e, pt_=pt_: e.transpose(out=pt_[:, 0:128], in_=ident_f[:], identity=ident_f[:]), reads=[r_const, A_r], writes=[pt_r])
                        continue
                    if K5 == "spin":
                        for _ in range(int(os.environ.get("NSPIN", "300"))):
                            sc.pe32(lambda e, pt_=pt_: e.transpose(out=pt_[:, 256:384], in_=ident_f[:], identity=ident_f[:]), reads=[r_const], writes=[pt_r])
                        sc.pe32(lambda e, pt_=pt_: e.transpose(out=pt_[:, 0:128], in_=ident_f[:], identity=ident_f[:]), reads=[r_const, A_r], writes=[pt_r])
                        continue
                    if K5 == "dummy":
                        sc.pe32(lambda e, pt_=pt_: e.transpose(out=pt_[:, 256:384], in_=ident_f[:], identity=ident_f[:]), reads=[r_const], writes=[pt_r])
                        sc.pe32(lambda e, pt_=pt_: e.transpose(out=pt_[:, 0:128], in_=ident_f[:], identity=ident_f[:]), reads=[r_const, A_r], writes=[pt_r])
                        continue
                    if K5 == "viaact2" and ((t * 4 + h) >= int(os.environ.get("KN", "999")) or (t * 4 + h) < int(os.environ.get("KN0", "0"))):
                        continue
                    if K5 == "viaact2":
                        sc.add("act", lambda e, X_=X_, A_=A_: e.copy(out=X_, in_=A_), reads=[A_r], writes=[X_r])
                        continue
                    if K5 == "viaact":
                        sc.add("act", lambda e, X_=X_, A_=A_: e.copy(out=X_, in_=A_), reads=[A_r], writes=[X_r])
                        sc.pe32(lambda e, pt_=pt_, X_=X_: e.transpose(out=pt_[:, 0:128], in_=X_, identity=ident_f[:]), reads=[r_const, X_r], writes=[pt_r])
                        continue
                    if K5 == "waitbf":
                        ptb_ = pt_.bitcast(BF16)
                        sc.pe16(ptb_[:, 0:128], lambda e, ptb_=ptb_: e.transpose(out=ptb_[:, 0:128], in_=ident_b[:], identity=ident_b[:]), reads=[r_const, A_r], writes=[pt_r])
                        continue
                    if K5 == "waitat":
                        sc.pe32(lambda e, pt_=pt_: e.transpose(out=pt_[:, 0:128], in_=ident_f[:], identity=ident_f[:]), reads=[r_const, at_r], writes=[pt_r])
                        continue
                    if K5 == "waitD":
                        sc.pe32(lambda e, pt_=pt_: e.transpose(out=pt_[:, 0:128], in_=ident_f[:], identity=ident_f[:]), reads=[r_const, Ds_r], writes=[pt_r])
                        continue
                    if K5 == "useGR":
                        sc.pe32(lambda e, pt_=pt_, GR_=GR_: e.transpose(out=pt_[:, 0:128], in_=GR_, identity=ident_f[:]), reads=[GR_r_, r_const] + ([A_r] if os.environ.get("K7") != "nodep" else []), writes=[pt_r])
                        continue
                    if K5 == "useDs":
                        sc.pe32(lambda e, pt_=pt_, Ds_=Ds_: e.transpose(out=pt_[:, 0:128], in_=Ds_, identity=ident_f[:]), reads=[Ds_r, A_r, r_const], writes=[pt_r])
                        continue
                    if K5 != "nope" and K5 != "pe2":
                        sc.pe32(lambda e, pt_=pt_, A_=A_: e.transpose(out=pt_[:, 0:128], in_=A_, identity=ident_f[:]), reads=[A_r, r_const], writes=[pt_r])
                    if K5 != "nope" and K5 != "pe1":
                        sc.pe32(lambda e, pt_=pt_, at_=at_: e.transpose(out=pt_[:, 128:256], in_=at_, identity=ident_f[:]), reads=[at_r, r_const], writes=[pt_r])
                    if K5 == "nodve":
                        continue
                    sc.add("dve", lambda e, pt_=pt_, X_=X_: e.tensor_copy(out=X_, in_=pt_[:, 0:128]), reads=[pt_r], writes=[X_r])
                    if KB == 4 and K4 <= "d":
                        continue
                    sc.add("pool", lambda e, X_=X_, Bo_=Bo_: e.tensor_tensor(out=Bo_, in0=X_.unsqueeze(1).to_broadcast([128, 6, 128]), in1=boff, op=ALU.mult), reads=[X_r, pc_r], writes=[Bo_r])
                    if KB == 4 and K4 <= "e":
                        continue
                    sc.add("act", lambda e, pt_=pt_, atT_=atT_: e.copy(out=atT_, in_=pt_[:, 128:256]), reads=[pt_r], writes=[atT_r])
                    if KB <= 4:
                        continue
                    (E_, E_r) = nxt(Em, "E"); (Dk_, Dk_r) = nxt(Dk, "Dk")
                    sc.add("pool", lambda e, E_=E_, Bo_=Bo_: e.tensor_tensor(out=E_, in0=ident_f[:], in1=Bo_[:, 0, :], op=ALU.subtract), reads=[Bo_r, r_const], writes=[E_r])
                    sc.add("pool", lambda e, Dk_=Dk_, A_=A_: e.tensor_tensor(out=Dk_, in0=A_, in1=aoff1, op=ALU.mult), reads=[A_r, pc_r], writes=[Dk_r])
                    sc.add("pool", lambda e, Dk_=Dk_: e.tensor_tensor(out=Dk_, in0=ident_f[:], in1=Dk_, op=ALU.subtract), reads=[Dk_r, r_const], writes=[Dk_r])
                    for lvl in range(1, 6):
                        px_, px_r = pbank()
                        sc.pe32(lambda e, px_=px_, Bo_=Bo_, lvl=lvl, Dk_=Dk_: e.matmul(px_[:, 0:128], lhsT=Bo_[:, lvl, :], rhs=Dk_, start=True, stop=True), reads=[Bo_r, Dk_r], writes=[px_r])
                        sc.add("act", lambda e, px_=px_, X_=X_: e.copy(out=X_, in_=px_[:, 0:128]), reads=[px_r], writes=[X_r])
                        py_, py_r = pbank()
                        sc.pe32(lambda e, py_=py_, X_=X_, E_=E_: e.matmul(py_[:, 0:128], lhsT=X_, rhs=E_, start=True, stop=True), reads=[X_r, E_r], writes=[py_r])
                        (E2_, E2_r) = nxt(Em, "E")
                        sc.add("dve", lambda e, py_=py_, E_=E_, E2_=E2_: e.tensor_tensor(out=E2_, in0=E_, in1=py_[:, 0:128], op=ALU.subtract), reads=[py_r, E_r], writes=[E2_r])
                        E_, E_r = E2_, E2_r
                        if lvl < 5:
                            pd_, pd_r = pbank()
                            sc.pe32(lambda e, pd_=pd_, E_=E_: e.transpose(out=pd_[:, 0:128], in_=E_, identity=ident_f[:]), reads=[E_r, r_const], writes=[pd_r])
                            (Dk_, Dk_r) = nxt(Dk, "Dk")
                            sc.add("act", lambda e, pd_=pd_, Dk_=Dk_: e.copy(out=Dk_, in_=pd_[:, 0:128]), reads=[pd_r], writes=[Dk_r])
                    if KB <= 5:
                        continue
                    if os.environ.get("HB", "0") == "1":
                        sc.barrier()
                    sc.add("pool", lambda e, R_=R_, Vt=Vt, h=h, beta_h=beta_h: e.tensor_scalar(out=R_[:, 0:128], in0=Vt[:, h, :], scalar1=beta_h, scalar2=None, op0=ALU.mult), reads=[Vt_r, sm_r], writes=[R_r])
                    sc.add("pool", lambda e, R_=R_, Kt=Kt, h=h, bk_h=bk_h: e.tensor_scalar(out=R_[:, 128:256], in0=Kt[:, h, :], scalar1=bk_h, scalar2=None, op0=ALU.mult), reads=[Kt_r, sm_r], writes=[R_r], partial=True)
                    sc.add("pool", lambda e, Kd_=Kd_, Kt=Kt, h=h, egu_h=egu_h: e.tensor_scalar(out=Kd_, in0=Kt[:, h, :], scalar1=egu_h, scalar2=None, op0=ALU.mult), reads=[Kt_r, sm_r], writes=[Kd_r])
                    sc.add("pool", lambda e, Qd_=Qd_, Qt=Qt, h=h, egc_h=egc_h: e.tensor_scalar(out=Qd_, in0=Qt[:, h, :], scalar1=egc_h, scalar2=None, op0=ALU.mult), reads=[Qt_r, sm_r], writes=[Qd_r])
                    pu_, pu_r = pbank()
                    sc.pe32(lambda e, pu_=pu_, E_=E_, R_=R_: e.matmul(pu_[:, 0:256], lhsT=E_, rhs=R_, start=True, stop=True), reads=[E_r, R_r], writes=[pu_r])
                    sc.add("act", lambda e, pu_=pu_, UW_=UW_: e.copy(out=UW_[:, 0:128], in_=pu_[:, 0:128]), reads=[pu_r], writes=[UW_r])
                    sc.add("dve", lambda e, pu_=pu_, UW_=UW_: e.tensor_scalar(out=UW_[:, 128:256], in0=pu_[:, 128:256], scalar1=-1.0, scalar2=None, op0=ALU.mult), reads=[pu_r], writes=[UW_r], partial=True)
                    pq_, pq_r = pbank()
                    sc.pe32(lambda e, pq_=pq_, Qd_=Qd_: e.matmul(pq_[:, 0:128], lhsT=Qd_, rhs=ident_f[:], start=True, stop=False), reads=[Qd_r, r_const], writes=[pq_r])
                    sc.pe32(lambda e, pq_=pq_, UW_=UW_, atT_=atT_: e.matmul(pq_[:, 0:128], lhsT=UW_[:, 128:256], rhs=atT_, start=False, stop=True), reads=[UW_r, atT_r], writes=[pq_r])
                    sc.add("act", lambda e, pq_=pq_, QpT_=QpT_: e.copy(out=QpT_, in_=pq_[:, 0:128]), reads=[pq_r], writes=[QpT_r])
                    for ci in range(2):
                        pm_, pm_r = pbank()
                        ps_ = slice(ci * 64, ci * 64 + 64)
                        sc.pe32(lambda e, pm_=pm_, UW_=UW_, Kd_=Kd_, ps_=ps_: e.matmul(pm_[:, 0:128], lhsT=UW_[ps_, 128:256], rhs=Kd_[ps_, :], start=True, stop=True), reads=[UW_r, Kd_r], writes=[pm_r])
                        if ci == 0:
                            sc.add("act", lambda e, pm_=pm_, Mp_=Mp_, ci=ci: e.copy(out=Mp_[:, ci, :], in_=pm_[:, 0:128]), reads=[pm_r], writes=[Mp_r])
                        else:
                            sc.add("dve", lambda e, pm_=pm_, Mp_=Mp_, ci=ci: e.tensor_copy(out=Mp_[:, ci, :], in_=pm_[:, 0:128]), reads=[pm_r], writes=[Mp_r], partial=True)
                    if os.environ.get("HB", "0") == "1":
                        sc.barrier()
                    for ci in range(2 if KB > 6 else 0):
                        ps_ = slice(ci * 64, ci * 64 + 64)
                        Sp, Sp_r = Sst[cur], S_r[cur][h]
                        Sn, Sn_r = Sst[1 - cur], S_r[1 - cur][h]
                        po_, po_r = pbank()
                        tp = (0, ci * 64)
                        sc.pe32(lambda e, po_=po_, QpT_=QpT_, Sp=Sp, h=h, ps_=ps_, tp=tp: e.matmul(po_[ps_, 0:128], lhsT=QpT_[:, ps_], rhs=Sp[:, h, :], start=True, stop=False, tile_position=tp), reads=[QpT_r, Sp_r], writes=[po_r])
                        sc.pe32(lambda e, po_=po_, atT_=atT_, UW_=UW_, ps_=ps_, tp=tp: e.matmul(po_[ps_, 0:128], lhsT=atT_[:, ps_], rhs=UW_[:, 0:128], start=False, stop=True, tile_position=tp), reads=[atT_r, UW_r], writes=[po_r])
                        sc.add("dve", lambda e, po_=po_, osb_t=osb_t, h=h, ps_=ps_: e.tensor_copy(out=osb_t[ps_, h, :], in_=po_[ps_, 0:128]), reads=[po_r], writes=[osb_r], partial=True)
                        sc.add("act", lambda e, po_=po_, oss_t=oss_t, h=h, ps_=ps_: e.activation(out=junk[0][ps_, :], in_=po_[ps_, 0:128], func=AF.Square, bias=cbias[ps_, 3:4], accum_out=oss_t[ps_, h:h + 1]), reads=[po_r], writes=[oss_r, junk[1]], partial=True)
                        pS_, pS_r = pbank()
                        sc.pe32(lambda e, pS_=pS_, Mp_=Mp_, ci=ci, Sp=Sp, h=h: e.matmul(pS_[:, 0:128], lhsT=Mp_[:, ci, :], rhs=Sp[:, h, :], start=True, stop=False), reads=[Mp_r, Sp_r], writes=[pS_r])
                        sc.pe32(lambda e, pS_=pS_, Kd_=Kd_, UW_=UW_, ps_=ps_: e.matmul(pS_[:, 0:128], lhsT=Kd_[ps_, :], rhs=UW_[ps_, 0:128], start=False, stop=True), reads=[Kd_r, UW_r], writes=[pS_r])
                        egl = smt[:, 24 + 4 * ci + h:25 + 4 * ci + h]
                        sc.add("dve", lambda e, pS_=pS_, Sp=Sp, Sn=Sn, h=h, egl=egl: e.scalar_tensor_tensor(out=Sn[:, h, :], in0=Sp[:, h, :], scalar=egl, in1=pS_[:, 0:128], op0=ALU.mult, op1=ALU.add), reads=[pS_r, Sp_r, sm_r], writes=[Sn_r])
                        cur = 1 - cur
                sc.safe = False
                if KB <= 7:
                    continue
                (oab_t, oab_r) = nxt(oab, "oab"); (oaT_t, oaT_r) = nxt(oaT, "oaT")
                sc.add("act", lambda e, oss_t=oss_t: e.activation(out=oss_t[:, 4:8], in_=oss_t[:, 0:4], func=AF.Sqrt, scale=1.0 / 128, bias=cbias[:, 0:1]), reads=[oss_r, r_const], writes=[oss_r])
                sc.add("dve", lambda e, oss_t=oss_t: e.reciprocal(out=oss_t[:, 4:8], in_=oss_t[:, 4:8]), reads=[oss_r], writes=[oss_r])
                sc.add("dve", lambda e, osb_t=osb_t, oss_t=oss_t: e.tensor_tensor(out=osb_t, in0=osb_t, in1=oss_t[:, 4:8].unsqueeze(2).to_broadcast([128, 4, 128]), op=ALU.mult), reads=[osb_r, oss_r], writes=[osb_r])
                sc.add("dve", lambda e, osb_t=osb_t, zst=zst, oab_t=oab_t: e.tensor_tensor(out=oab_t, in0=osb_t.rearrange("p h d -> p (h d)"), in1=zst, op=ALU.mult), reads=[osb_r, zs_r], writes=[oab_r])
                ptb = bank[5][:].bitcast(BF16).rearrange("p (k c) -> p k c", k=8)
                for h in range(4):
                    sc.pe16(ptb[:, h, :], lambda e, h=h, oab_t=oab_t: e.transpose(out=ptb[:, h, :], in_=oab_t[:, h * 128:(h + 1) * 128], identity=ident_b[:]), reads=[oab_r, r_const], writes=[bank_r[5]])
                sc.add("dve", lambda e, oaT_t=oaT_t: e.tensor_copy(out=oaT_t, in_=ptb[:, 0:4, :]), reads=[bank_r[5]], writes=[oaT_r])
                sc.add("sp", lambda e, oaT_t=oaT_t, tsl=tsl: e.dma_start(out=oT_d[0:4, :, tsl].rearrange("j p c -> p j c"), in_=oaT_t), reads=[oaT_r], writes=[], dma=True, key="oaT%d" % (ctr["oaT"] % 2))

    def ln_tile(L_, t, ps_lo, ps_lo_r, ps_hi, ps_hi_r, xr, xr_r, g_bc, b_bc, lnp_r, out_d, write_xT):
        tsl = slice(t * 128, (t + 1) * 128)
        (y, y_r) = L_["y"][t % 2]; (st, st_r) = L_["st"][t % 2]; (xb_, xb_r_) = L_["xb"][t % 2]
        for half, (pp, pp_r) in enumerate(((ps_lo, ps_lo_r), (ps_hi, ps_hi_r))):
            hs = slice(half * 512, (half + 1) * 512)
            sc.add("dve", lambda e, pp=pp, hs=hs, y=y, xr=xr: e.scalar_tensor_tensor(out=y[:, hs], in0=xr[:, hs], scalar=ALPHA, in1=pp[:, 0:512], op0=ALU.mult, op1=ALU.add),
                   reads=[pp_r, xr_r], writes=[y_r], partial=(half == 1))
        for half in range(2):
            hs = slice(half * 512, (half + 1) * 512)
            sc.add("dve", lambda e, half=half, hs=hs, y=y, st=st: e.bn_stats(out=st[:, half * 6:(half + 1) * 6], in_=y[:, hs]), reads=[y_r], writes=[st_r], partial=(half == 1))
        sc.add("dve", lambda e, st=st: e.bn_aggr(out=st[:, 12:14], in_=st[:, 0:12]), reads=[st_r], writes=[st_r])
        sc.add("act", lambda e, st=st: e.activation(out=st[:, 14:15], in_=st[:, 13:14], func=AF.Sqrt, bias=cbias[:, 2:3]), reads=[st_r, r_const], writes=[st_r])
        sc.add("dve", lambda e, st=st: e.reciprocal(out=st[:, 14:15], in_=st[:, 14:15]), reads=[st_r], writes=[st_r])
        sc.add("dve", lambda e, y=y, st=st: e.tensor_scalar(out=y, in0=y, scalar1=st[:, 12:13], scalar2=st[:, 14:15], op0=ALU.subtract, op1=ALU.mult), reads=[y_r, st_r], writes=[y_r])
        sc.add("pool", lambda e, y=y: e.tensor_tensor(out=y, in0=y, in1=g_bc, op=ALU.mult), reads=[y_r, lnp_r], writes=[y_r])
        sc.add("dve", lambda e, y=y: e.tensor_tensor(out=y, in0=y, in1=b_bc, op=ALU.add), reads=[y_r, lnp_r], writes=[y_r])
        sc.add("sp", lambda e, y=y, tsl=tsl: e.dma_start(out=out_d[tsl, :], in_=y), reads=[y_r], writes=[], dma=True, key="ysto%d" % (t % 2))
        if write_xT:
            sc.add("act", lambda e, y=y, xb_=xb_: e.copy(out=xb_, in_=y), reads=[y_r], writes=[xb_r_])
            ptb = bank[7][:].bitcast(BF16).rearrange("p (k c) -> p k c", k=8)
            for kc in range(8):
                sc.pe16(ptb[:, kc, :], lambda e, kc=kc, xb_=xb_: e.transpose(out=ptb[:, kc, :], in_=xb_[:, kc * 128:(kc + 1) * 128], identity=ident_b[:]), reads=[xb_r_, r_const], writes=[bank_r[7]])
            sc.add("dve", lambda e, tsl=tsl: e.tensor_copy(out=xT[:, :, tsl], in_=ptb), reads=[bank_r[7]], writes=[xT_r[t]])

    def ln_bufs(g_d, b_d, l):
        L_ = {}
        L_["y"] = [(cv.get([128, 1024]), sc.res("y%d" % i)) for i in range(2)]
        L_["st"] = [(cv.get([128, 16]), sc.res("st%d" % i)) for i in range(2)]
        L_["xb"] = [(cv.get([128, 1024], BF16), sc.res("xbln%d" % i)) for i in range(2)]
        g_bc = cv.get([128, 1024]); b_bc = cv.get([128, 1024]); lnp_r = sc.res("lnp")
        sc.add("sp", lambda e: e.dma_start(out=g_bc, in_=g_d[l, :].partition_broadcast(128)), writes=[lnp_r], dma=True, key="lnp", partial=True)
        sc.add("sp", lambda e: e.dma_start(out=b_bc, in_=b_d[l, :].partition_broadcast(128)), writes=[lnp_r], dma=True, key="lnp", partial=True)
        return L_, g_bc, b_bc, lnp_r

    def phaseC(l, xin_d):
        cv.reset()
        wO = cv.get([128, 8, 1024], BF16); wO_r = [sc.res("wO%d" % k) for k in range(8)]
        for kc in range(8):
            sc.add("pool", lambda e, kc=kc: e.dma_start(out=wO[:, kc, :], in_=w_out_d[l, kc * 128:(kc + 1) * 128, :]), writes=[wO_r[kc]], dma=True, key="wO%d" % kc)
        L_, g_bc, b_bc, lnp_r = ln_bufs(ln1g_d, ln1b_d, l)
        oTt = [(cv.get([128, 8, 128], BF16), sc.res("oTt%d" % i)) for i in range(3)]
        xrs = [(cv.get([128, 1024]), sc.res("xr%d" % i)) for i in range(3)]
        for t in range(NT):
            tsl = slice(t * 128, (t + 1) * 128)
            (ot, ot_r) = oTt[t % 3]; (xr, xr_r) = xrs[t % 3]
            sc.add("sp", lambda e, ot=ot, tsl=tsl: e.dma_start(out=ot, in_=oT_d[:, :, tsl].rearrange("k p c -> p k c")), writes=[ot_r], dma=True, key="oTt%d" % (t % 3))
            sc.add("sp", lambda e, xr=xr, tsl=tsl: e.dma_start(out=xr, in_=xin_d[tsl, :]), writes=[xr_r], dma=True, key="xr%d" % (t % 3))
            bl, bh = 2 * (t % 2), 2 * (t % 2) + 1
            for half, bi in ((0, bl), (1, bh)):
                for kc in range(8):
                    sc.pe16(bank[bi][:], lambda e, bi=bi, kc=kc, ot=ot, half=half: e.matmul(bank[bi][:], lhsT=ot[:, kc, :], rhs=wO[:, kc, half * 512:(half + 1) * 512], start=(kc == 0), stop=(kc == 7)),
                            reads=[ot_r, wO_r[kc]], writes=[bank_r[bi]])
            ln_tile(L_, t, bank[bl], bank_r[bl], bank[bh], bank_r[bh], xr, xr_r, g_bc, b_bc, lnp_r, x1_d, True)

    def phaseD(l, out_d, write_xT):
        cv.reset()
        NJ = DFF // 128
        NBLK = S // 512
        wD = cv.get([128, NJ, 1024], BF16); wD_r = [sc.res("wD%d" % j) for j in range(NJ)]
        for j in range(NJ):
            sc.add("pool", lambda e, j=j: e.dma_start(out=wD[:, j, :], in_=w_down_d[l, j * 128:(j + 1) * 128, :]), writes=[wD_r[j]], dma=True, key="wD%d" % (j % 4))
        L_, g_bc, b_bc, lnp_r = ln_bufs(ln2g_d, ln2b_d, l)
        fc4 = cv.get([128, 4, 44]); fc_r = sc.res("fc4")
        hT = cv.get([128, NJ, 512], BF16); hT_r = [sc.res("hT%d" % j) for j in range(NJ)]
        wU = [(cv.get([128, 8, 256], BF16), sc.res("wU%d" % i)) for i in range(3)]
        raw = [(cv.get([128, 2, 514]), sc.res("raw%d" % i)) for i in range(2)]
        acc = [(cv.get([128, 2, 512]), sc.res("facc%d" % i)) for i in range(2)]
        halo = cv.get([128, 44, 2]); halo_r = [sc.res("fhalo%d" % j) for j in range(44)]
        xrs = [(cv.get([128, 1024]), sc.res("xrD%d" % i)) for i in range(2)]
        w44 = hT.rearrange("p j t -> p (j t)").bitcast(F32)[0:44, 0:512].rearrange("p (a b) -> p a b", a=4)
        w44_r = sc.res("w44")
        for j3 in range(3):
            sc.add("sp", lambda e, j3=j3: e.dma_start(out=w44[:, j3, :], in_=fconvw_d[l, j3, :].rearrange("(f p) -> f p", p=128)), writes=[w44_r] + hT_r[0:2], dma=True, key="w44", partial=True)
        sc.add("sp", lambda e: e.dma_start(out=w44[:, 3, :], in_=fconvb_d[l, :].rearrange("(f p) -> f p", p=128)), writes=[w44_r], dma=True, key="w44", partial=True)
        for a4 in range(4):
            sc.pe32(lambda e, a4=a4: e.transpose(out=bank[0][:, a4 * 44:(a4 + 1) * 44], in_=w44[:, a4, :], identity=ident_f[0:44, 0:44]), reads=[w44_r, r_const], writes=[bank_r[0]])
        sc.add("dve", lambda e: e.tensor_copy(out=fc4.rearrange("p a f -> p (a f)"), in_=bank[0][:, 0:176]), reads=[bank_r[0]], writes=[fc_r])
        sc.add("pool", lambda e: e.memset(halo, 0.0), reads=[], writes=halo_r)
        sc.barrier()
        for c in range(NBLK):
            csl = slice(c * 512, (c + 1) * 512)
            for j in range(NJ):
                (wu, wu_r) = wU[j % 3]
                sc.add("pool", lambda e, wu=wu, j=j: e.dma_start(out=wu[:, :, 0:128], in_=w_up_d[l, :, j * 128:(j + 1) * 128].rearrange("(k p) f -> p k f", p=128)), writes=[wu_r], dma=True, key="wUa%d" % (j % 3))
                sc.add("pool", lambda e, wu=wu, j=j: e.dma_start(out=wu[:, :, 128:256], in_=w_up_d[l, :, DFF + j * 128:DFF + (j + 1) * 128].rearrange("(k p) f -> p k f", p=128)), writes=[wu_r], dma=True, key="wUb%d" % (j % 3), partial=True)
                (rw, rw_r) = raw[j % 2]; (ac, ac_r) = acc[j % 2]
                for gv in range(2):
                    f = gv * 22 + j
                    bi = 2 * (j % 2) + gv
                    for kc in range(8):
                        sc.pe16(bank[bi][:], lambda e, bi=bi, kc=kc, wu=wu, gv=gv, csl=csl: e.matmul(bank[bi][:], lhsT=wu[:, kc, gv * 128:(gv + 1) * 128], rhs=xT[:, kc, csl], start=(kc == 0), stop=(kc == 7)),
                                reads=[wu_r] + xT_r[4 * c:4 * c + 4], writes=[bank_r[bi]])
                    sc.add("pool", lambda e, rw=rw, gv=gv, f=f: e.tensor_copy(out=rw[:, gv, 0:2], in_=halo[:, f, :]), reads=[halo_r[f]], writes=[rw_r], partial=(gv == 1))
                    sc.add("act", lambda e, rw=rw, gv=gv, bi=bi: e.copy(out=rw[:, gv, 2:514], in_=bank[bi][:]), reads=[bank_r[bi]], writes=[rw_r], partial=True)
                    sc.add("pool", lambda e, rw=rw, gv=gv, f=f: e.tensor_copy(out=halo[:, f, :], in_=rw[:, gv, 512:514]), reads=[rw_r], writes=[halo_r[f]])
                    sc.add("act", lambda e, ac=ac, gv=gv, bi=bi, f=f: e.activation(out=ac[:, gv, :], in_=bank[bi][:], func=AF.Identity, scale=fc4[:, 2, f:f + 1], bias=fc4[:, 3, f:f + 1]), reads=[bank_r[bi], fc_r], writes=[ac_r], partial=(gv == 1))
                    eng = "dve"
                    for tap in (1, 0):
                        sc.add(eng, lambda e, ac=ac, rw=rw, gv=gv, tap=tap, f=f: e.scalar_tensor_tensor(out=ac[:, gv, :], in0=rw[:, gv, tap:tap + 512], scalar=fc4[:, tap, f:f + 1], in1=ac[:, gv, :], op0=ALU.mult, op1=ALU.add),
                               reads=[rw_r, fc_r, ac_r], writes=[ac_r])
                sc.add("act", lambda e, ac=ac: e.activation(out=ac[:, 0, :], in_=ac[:, 0, :], func=AF.Silu, bias=cbias[:, 3:4]), reads=[ac_r, r_const], writes=[ac_r])
                sc.add("dve", lambda e, ac=ac, j=j: e.tensor_tensor(out=hT[:, j, :], in0=ac[:, 0, :], in1=ac[:, 1, :], op=ALU.mult), reads=[ac_r], writes=[hT_r[j]])
            for tt in range(4):
                t = c * 4 + tt
                tsl = slice(t * 128, (t + 1) * 128)
                (xr, xr_r) = xrs[t % 2]
                sc.add("sp", lambda e, xr=xr, tsl=tsl: e.dma_start(out=xr, in_=x1_d[tsl, :]), writes=[xr_r], dma=True, key="xrD%d" % (t % 2))
                bl, bh = 4 + 2 * (t % 2), 5 + 2 * (t % 2)
                if bh == 7 and write_xT:
                    bl, bh = 4, 5
                for half, bi in ((0, bl), (1, bh)):
                    for j in range(NJ):
                        sc.pe16(bank[bi][:], lambda e, bi=bi, j=j, tt=tt, half=half: e.matmul(bank[bi][:], lhsT=hT[:, j, tt * 128:(tt + 1) * 128], rhs=wD[:, j, half * 512:(half + 1) * 512], start=(j == 0), stop=(j == NJ - 1)),
                                reads=[hT_r[j], wD_r[j]], writes=[bank_r[bi]])
                ln_tile(L_, t, bank[bl], bank_r[bl], bank[bh], bank_r[bh], xr, xr_r, g_bc, b_bc, lnp_r, out_d, write_xT)

    phase0(x_d)
    sc.barrier()
    if stop_after == "0":
        sc.add("sp", lambda e: e.dma_start(out=oT_d[:, :, :].rearrange("k p s -> p k s"), in_=xT[:]), reads=xT_r, writes=[], dma=True, key="dbg")
    elif stop_after == "A":
        phaseA(0)
    elif stop_after == "B":
        phaseB(0)
    else:
        for l in range(L):
            xin = x_d if l == 0 else x2_d
            last = (l == L - 1)
            phaseA(l)
            sc.barrier()
            phaseB(l)
            sc.barrier()
            phaseC(l, xin)
            sc.barrier()
            if stop_after == "C" and l == 0:
                break
            phaseD(l, y_d if last else x2_d, not last)
            sc.barrier()
            if stop_after == "D" and l == 0:
                break
    if dbg and os.environ.get("DUMPARENA") and not os.environ.get("SIM"):
        sc.barrier()
        dbg_arena = nc.dram_tensor("dbg_arena", [128, ARENA], F32, kind="ExternalOutput").ap()
        for q in range(4):
            sc.add("sp", lambda e, q=q: e.dma_start(out=dbg_arena[:, q * (ARENA // 4):(q + 1) * (ARENA // 4)], in_=arena[:, q * (ARENA // 4):(q + 1) * (ARENA // 4)]), dma=True, key="dbga")
    sc.emit(nc, es)
    es.close()
    return nc


_CACHE = {}


def kernel(**inputs):
    x = np.asarray(inputs["x"], dtype=np.float32)
    B, S, _ = x.shape
    L = int(np.asarray(inputs["w_in"]).shape[0])
    key = (S, L)
    if key not in _CACHE:
        _CACHE[key] = (build(S=S, L=L), make_consts(S))
    nc, consts = _CACHE[key]
    shared = {k: np.ascontiguousarray(np.asarray(v, dtype=np.float32)) for k, v in inputs.items() if k != "x"}
    for k, v in consts.items():
        shared["c_" + k] = v
    in_maps = []
    for b in range(B):
        m = dict(shared)
        m["x"] = np.ascontiguousarray(x[b])
        in_maps.append(m)
    res = run_bass_kernel_spmd(nc, in_maps, core_ids=list(range(B)))
    return np.stack([np.asarray(r["y"], dtype=np.float32) for r in res.results], axis=0)
```

```python
import os
import numpy as np
import ml_dtypes
from contextlib import ExitStack
import concourse.bass as bass
import concourse.mybir as mybir
from concourse.bass_utils import run_bass_kernel_spmd

F32 = mybir.dt.float32
BF16 = mybir.dt.bfloat16
AF = mybir.ActivationFunctionType
ALU = mybir.AluOpType
AX = mybir.AxisListType

D = 1024
NIN = 3592
DFF = 2816
ALPHA = float((2 * 2) ** 0.25)
NEG = -30000.0


class Res:
    __slots__ = ("name", "writers", "readers")

    def __init__(self, name):
        self.name = name
        self.writers = []
        self.readers = {}


class Op:
    __slots__ = ("eng", "fn", "dma", "key", "value", "deps", "signal", "barrier")

    def __init__(self, eng, fn, dma=False, key=None):
        self.eng = eng
        self.fn = fn
        self.dma = dma
        self.key = key
        self.value = None
        self.deps = []
        self.signal = False
        self.barrier = False


ENGS = ("pe", "act", "dve", "pool", "sp")


class Sched:
    def __init__(self):
        self.ops = {e: [] for e in ENGS}
        self.keycount = {}
        self.keylast = {}
        self.allres = []
        self.last_pe_f32 = False
        self.ident_b = None
        self.safe = False
        self.safecnt = 0
        self.safek = int(os.environ.get("SAFEK", "0"))
        self.safeeng = tuple(x for x in os.environ.get("SAFEENG", "act").split(",") if x)

    def res(self, name):
        r = Res(name)
        self.allres.append(r)
        return r

    def _dep(self, op, prod, raw):
        if prod is op:
            return
        if (not prod.dma) and (not op.dma) and prod.eng == op.eng:
            if op.eng == "pe":
                return
        op.deps.append(prod)
        prod.signal = True

    def add(self, eng, fn, reads=(), writes=(), dma=False, key=None, partial=False, f32=False, out=None):
        if eng == "pe":
            if (not f32) and self.last_pe_f32 and out is not None:
                fn0 = fn
                dmy = out.bitcast(F32) if out.dtype != F32 else out
                idb = self.ident_b

                def fn(e, fn0=fn0, dmy=dmy, idb=idb):
                    e.matmul(dmy[0:64, 0:8], lhsT=idb[:, 0:64], rhs=idb[:, 0:8], start=True, stop=True)
                    return fn0(e)
            self.last_pe_f32 = f32
        excl = self.safe and (not dma) and (eng in self.safeeng)
        if excl:
            self.barrier()
        op = Op(eng, fn, dma, key)
        for r in reads:
            for w in r.writers:
                self._dep(op, w, True)
        for r in writes:
            for w in r.writers:
                if not (partial and w.dma and op.dma):
                    self._dep(op, w, False)
            for rd in r.readers.values():
                if isinstance(rd, list):
                    for x in rd:
                        self._dep(op, x, False)
                else:
                    self._dep(op, rd, False)
        for r in reads:
            if dma:
                r.readers.setdefault("dma", []).append(op)
            else:
                r.readers[eng] = op
        for r in writes:
            if partial:
                r.writers = r.writers + [op]
            else:
                r.writers = [op]
            r.readers = {}
        if dma:
            assert key is not None
            self.keycount[key] = self.keycount.get(key, 0) + 16
            op.value = self.keycount[key]
            self.keylast[key] = op
        self.ops[eng].append(op)
        if excl:
            self.barrier()
        elif self.safe and not dma and self.safek > 0:
            self.safecnt += 1
            if self.safecnt % self.safek == 0:
                self.barrier()
        return op

    def pe32(self, fn, **kw):
        return self.add("pe", fn, f32=True, **kw)

    def pe16(self, out, fn, **kw):
        return self.add("pe", fn, out=out, **kw)

    def barrier(self):
        prods = []
        for e in ENGS:
            for o in reversed(self.ops[e]):
                if not o.dma and not o.barrier:
                    prods.append(o)
                    break
        prods += list(self.keylast.values())
        for e in ENGS:
            b = Op(e, None)
            b.barrier = True
            for p in prods:
                if p.dma or p.eng != e or e != "pe":
                    b.deps.append(p)
                    p.signal = True
            self.ops[e].append(b)
        for r in self.allres:
            r.writers = []
            r.readers = {}

    def emit(self, nc, es):
        esem = {e: es.enter_context(nc.semaphore("s_" + e)) for e in ENGS}
        ksem = {}
        for i, k in enumerate(self.keycount):
            ksem[k] = es.enter_context(nc.semaphore("k%d" % i))
        for e in ENGS:
            c = 0
            for o in self.ops[e]:
                if (not o.dma) and o.signal and not o.barrier:
                    c += 1
                    o.value = c
            if os.environ.get("SEMDBG"): print("SEM", e, "final", c, "nops", len(self.ops[e]))
        block = es.enter_context(nc.Block())
        hooks = {"pe": block.tensor, "act": block.scalar, "dve": block.vector,
                 "pool": block.gpsimd, "sp": block.sync}
        final_keys = dict(self.keycount)

        def mk(ename):
            def body(eng):
                waited = {}
                for o in self.ops[ename]:
                    need = {}
                    for p in o.deps:
                        s = ksem[p.key] if p.dma else esem[p.eng]
                        sid = id(s)
                        v = p.value
                        if waited.get(sid, 0) >= v:
                            continue
                        if sid not in need or need[sid][1] < v:
                            need[sid] = (s, v)
                    for sid, (s, v) in need.items():
                        eng.wait_ge(s, v)
                        waited[sid] = v
                    if o.fn is None:
                        continue
                    ins = o.fn(eng)
                    if o.dma:
                        ins.then_inc(ksem[o.key], 16)
                    elif o.signal:
                        ins.then_inc(esem[ename], 1)
                if ename == "sp":
                    for k, v in final_keys.items():
                        if waited.get(id(ksem[k]), 0) < v:
                            eng.wait_ge(ksem[k], v)
            return body

        for e in ENGS:
            hooks[e](mk(e))


def make_consts(S):
    i = np.arange(128)[:, None]
    j = np.arange(128)[None, :]
    same = (i // 64) == (j // 64)
    c = {}
    c["ident"] = np.eye(128, dtype=np.float32)
    c["caus01"] = (i <= j).astype(np.float32)
    c["mstrict"] = (same & (j < i)).astype(np.float32)
    c["negincl"] = (same & (j <= i)).astype(np.float32)
    LT = (same & (j <= i)).T.astype(np.float32)
    UT = (same & (j > i)).T.astype(np.float32)
    CS0 = np.zeros((128, 128), np.float32); CS0[:64, :] = 1.0
    CS1 = np.zeros((128, 128), np.float32); CS1[64:, :] = 1.0
    c["gl"] = np.concatenate([LT, UT, CS0, CS1], 1)
    offs = []
    for s in (1, 2, 4, 8, 16, 32):
        m = ((i // (2 * s)) == (j // (2 * s))) & ((i // s) != (j // s)) & (i > j)
        offs.append(m.T.astype(np.float32))
    c["boff"] = np.concatenate(offs, 1)
    c["aoff1"] = (((i // 2) == (j // 2)) & (i != j) & (i > j)).astype(np.float32)
    half = 8
    inv = 500000.0 ** (-np.arange(half, dtype=np.float32) / half)
    ang = np.arange(S, dtype=np.float32)[:, None] * inv[None, :]
    cos = np.cos(ang).astype(np.float32)
    sin = np.sin(ang).astype(np.float32)
    NT = S // 128
    cc = np.concatenate([cos, cos], 1).reshape(NT, 128, 16).transpose(1, 0, 2)
    ss = np.concatenate([sin, sin], 1).reshape(NT, 128, 16).transpose(1, 0, 2)
    c["rope"] = np.ascontiguousarray(np.concatenate([cc, ss], 2)).reshape(128, NT * 32)
    return c


def build(S=4096, L=2, dbg=False, stop_after=None):
    NT = S // 128
    NB = S // 256
    nc = bass.Bass("TRN2", target_bir_lowering=False)
    sc = Sched()
    es = ExitStack()

    def din(name, shape, dt=F32):
        return nc.dram_tensor(name, list(shape), dt, kind="ExternalInput").ap()

    def dscr(name, shape, dt=F32, out=False):
        kind = "ExternalOutput" if (out or dbg) else "Internal"
        return nc.dram_tensor(name, list(shape), dt, kind=kind).ap()

    x_d = din("x", [S, D])
    w_in_d = din("w_in", [L, D, NIN])
    gconv_d = din("gdn_conv_w", [L, 4, 1536])
    alog_d = din("gdn_a_log", [L, 4])
    dtb_d = din("gdn_dt_bias", [L, 4])
    gng_d = din("gdn_norm_g", [L, 128])
    w_out_d = din("w_out", [L, D, D])
    ln1g_d = din("ln1_g", [L, D])
    ln1b_d = din("ln1_b", [L, D])
    w_up_d = din("w_up", [L, D, 2 * DFF])
    fconvw_d = din("ffn_conv_w", [L, 3, 2 * DFF])
    fconvb_d = din("ffn_conv_b", [L, 2 * DFF])
    w_down_d = din("w_down", [L, DFF, D])
    ln2g_d = din("ln2_g", [L, D])
    ln2b_d = din("ln2_b", [L, D])
    c_ident_d = din("c_ident", [128, 128])
    c_caus_d = din("c_caus01", [128, 128])
    c_mstrict_d = din("c_mstrict", [128, 128])
    c_negincl_d = din("c_negincl", [128, 128])
    c_gl_d = din("c_gl", [128, 512])
    c_boff_d = din("c_boff", [128, 768])
    c_aoff1_d = din("c_aoff1", [128, 128])
    c_rope_d = din("c_rope", [128, NT * 32])

    y_d = dscr("y", [S, D], out=True)
    x1_d = dscr("x1res", [S, D])
    x2_d = dscr("x2res", [S, D]) if L > 1 else None
    oT_d = dscr("oT", [8, 128, S], BF16)

    def sb(name, shape, dt=F32):
        return es.enter_context(nc.sbuf_tensor(name, list(shape), dt))

    def ps(name, shape, dt=F32):
        return es.enter_context(nc.psum_tensor(name, list(shape), dt))

    xT = sb("xT", [128, 8, S], BF16)
    xT_r = [sc.res("xT%d" % t) for t in range(NT)]
    ident_f = sb("ident_f", [128, 128]); ident_b = sb("ident_b", [128, 128], BF16)
    caus_b = sb("caus_b", [128, 128], BF16)
    rope = sb("rope", [128, NT, 32])
    r_const = sc.res("consts")
    sc.ident_b = ident_b
    cbias = sb("cbias", [128, 4])
    sc.add("dve", lambda e: e.memset(cbias[:, 0:1], 1e-6), writes=[r_const], partial=True)
    sc.add("dve", lambda e: e.memset(cbias[:, 1:2], 1.0), writes=[r_const], partial=True)
    sc.add("dve", lambda e: e.memset(cbias[:, 2:3], 1e-5), writes=[r_const], partial=True)
    sc.add("dve", lambda e: e.memset(cbias[:, 3:4], 0.0), writes=[r_const], partial=True)

    bank = [ps("bank%d" % i, [128, 512]) for i in range(8)]
    bank_r = [sc.res("bank%d" % i) for i in range(8)]

    ARENA = 136 * 1024 // 4
    arena = sb("arena", [128, ARENA])

    class Carver:
        def __init__(self):
            self.off = 0

        def reset(self):
            self.off = 0

        def get(self, shape, dt=F32):
            n = int(np.prod(shape[1:]))
            nwords = n if dt == F32 else (n + 1) // 2
            a = arena[0:shape[0], self.off:self.off + nwords]
            self.off += (nwords + 15) // 16 * 16
            assert self.off <= ARENA, "arena overflow %d" % self.off
            if dt != F32:
                a = a.bitcast(dt)[:, 0:n]
            if len(shape) > 2:
                names = " ".join("d%d" % k for k in range(len(shape) - 1))
                kw = {"d%d" % k: shape[k + 1] for k in range(len(shape) - 2)}
                a = a.rearrange("p (%s) -> p %s" % (names, names), **kw)
            return a

    cv = Carver()

    sc.add("sp", lambda e: e.dma_start(out=ident_f[:], in_=c_ident_d[:, :]), writes=[r_const], dma=True, key="c0", partial=True)
    sc.add("pool", lambda e: e.dma_start(out=ident_b[:], in_=c_ident_d[:, :]), writes=[r_const], dma=True, key="c1", partial=True)
    sc.add("pool", lambda e: e.dma_start(out=caus_b[:], in_=c_caus_d[:, :]), writes=[r_const], dma=True, key="c1", partial=True)
    sc.add("sp", lambda e: e.dma_start(out=rope[:].rearrange("p t c -> p (t c)"), in_=c_rope_d[:, :]), writes=[r_const], dma=True, key="c0", partial=True)

    def phase0(src_d):
        cv.reset()
        xb = [cv.get([128, 1024], BF16) for _ in range(3)]
        xb_r = [sc.res("xb%d" % i) for i in range(3)]
        pst = [bank[0][:].bitcast(BF16), bank[1][:].bitcast(BF16)]
        for t in range(NT):
            s = t % 3
            sc.add("pool", lambda e, t=t, s=s: e.dma_start(out=xb[s], in_=src_d[t * 128:(t + 1) * 128, :]),
                   writes=[xb_r[s]], dma=True, key="xb%d" % s)
            p = t % 2
            pt = pst[p].rearrange("p (k c) -> p k c", k=8)
            for kc in range(8):
                sc.add("pe", lambda e, kc=kc, s=s, pt=pt: e.transpose(out=pt[:, kc, :], in_=xb[s][:, kc * 128:(kc + 1) * 128], identity=ident_b[:]),
                       reads=[xb_r[s], r_const], writes=[bank_r[p]])
            if t % 2 == 0:
                sc.add("act", lambda e, t=t, pt=pt: e.copy(out=xT[:, :, t * 128:(t + 1) * 128], in_=pt),
                       reads=[bank_r[p]], writes=[xT_r[t]])
            else:
                sc.add("dve", lambda e, t=t, pt=pt: e.tensor_copy(out=xT[:, :, t * 128:(t + 1) * 128], in_=pt),
                       reads=[bank_r[p]], writes=[xT_r[t]])

    def phaseA(l):
        cv.reset()
        wA = cv.get([128, 8, 1536], BF16)
        wA_r = [sc.res("wA%d" % k) for k in range(8)]
        KT = cv.get([128, 4, S], BF16)
        KT_r = [sc.res("KT%d" % t) for t in range(NT)]
        Vp = cv.get([128, NT, 8, 65], BF16)
        Vp_r = [sc.res("Vp%d" % t) for t in range(NT)]
        QT = [cv.get([128, 4, 256], BF16) for _ in range(2)]
        QT_r = [sc.res("QT%d" % i) for i in range(2)]
        kmT = cv.get([128, 4, 16], BF16)
        kmf = cv.get([128, 4])
        kmT_r = sc.res("kmT")
        qb = [cv.get([128, 512], BF16) for _ in range(2)]
        kb = [cv.get([128, 512], BF16) for _ in range(2)]
        qb_r = [sc.res("qb%d" % i) for i in range(2)]
        kb_r = [sc.res("kb%d" % i) for i in range(2)]
        t1 = cv.get([128, 8, 16]); t2 = cv.get([128, 8, 16])
        t1_r = sc.res("t1"); t2_r = sc.res("t2")
        gsb = cv.get([128, 16, 16]); m8 = cv.get([128, 16, 8]); sel = cv.get([128, 16, 16])
        gsb_r = sc.res("gsb"); sel_r = sc.res("sel")
        NPT = 4
        PT = [cv.get([128, 2, 256], BF16) for _ in range(NPT)]
        PT_r = [sc.res("PT%d" % i) for i in range(NPT)]
        acc = cv.get([128, 2, 8, 65])
        acc_r = [[sc.res("acc%d_%d" % (q, h)) for h in range(8)] for q in range(2)]
        rec = cv.get([128, 16])
        ob = cv.get([128, 2, 512], BF16)
        ob_r = sc.res("ob")
        obT = [cv.get([128, 4, 256], BF16) for _ in range(2)]
        obT_r = [sc.res("obT%d" % i) for i in range(2)]

        for kc in range(8):
            sc.add("pool", lambda e, kc=kc: e.dma_start(out=wA[:, kc, :], in_=w_in_d[l, kc * 128:(kc + 1) * 128, 2056:3592]),
                   writes=[wA_r[kc]], dma=True, key="wA%d" % kc)
        sc.add("pool", lambda e: e.memset(Vp[:, :, :, 64:65], 1.0), writes=Vp_r)
        sc.add("pool", lambda e: e.memset(gsb[:], -1e30), writes=[gsb_r])
        sc.add("pool", lambda e: e.memset(kmT[:], 0.0), writes=[kmT_r])

        pq, pk, pv = bank[0], bank[1], bank[2]
        ptr = bank[3][:].bitcast(BF16).rearrange("p (k c) -> p k c", k=8)
        pg0 = bank[4][:, 0:128].rearrange("p (a b) -> p a b", a=8)
        pg1 = bank[6][:, 0:128].rearrange("p (a b) -> p a b", a=8)
        SB = (0, 1, 2, 5)
        OB = (6, 7)
        cnt = {"s": 0, "o": 0, "pt": 0, "ev": 0}


        KSTOP = int(os.environ.get("KSTOP", "99"))
        for t in range(NT):
            b = t // 2
            qt_ = t % 2
            tsl = slice(t * 128, (t + 1) * 128)
            if KSTOP <= 0:
                break
            for g, pp in enumerate((pq, pk, pv)):
                for kc in range(8):
                    sc.add("pe", lambda e, g=g, kc=kc, pp=pp, tsl=tsl: e.matmul(pp[:], lhsT=xT[:, kc, tsl], rhs=wA[:, kc, g * 512:(g + 1) * 512], start=(kc == 0), stop=(kc == 7)),
                           reads=[xT_r[t], wA_r[kc]], writes=[bank_r[g]])
            if KSTOP <= 1:
                continue
            sc.add("act", lambda e, t=t: e.copy(out=Vp[:, t, :, 0:64], in_=pv[:].rearrange("p (h d) -> p h d", h=8)),
                   reads=[bank_r[2]], writes=[Vp_r[t]])
            s2 = t % 2
            for (pp, dst, dst_r, bi) in ((pq, qb[s2], qb_r[s2], 0), (pk, kb[s2], kb_r[s2], 1)):
                p3 = pp[:].rearrange("p (h d) -> p h d", h=8)
                d3 = dst.rearrange("p (h d) -> p h d", h=8)
                sc.add("act", lambda e, p3=p3, d3=d3: e.copy(out=d3[:, :, 16:64], in_=p3[:, :, 16:64]),
                       reads=[bank_r[bi]], writes=[dst_r])
                ccb = rope[:, t, 0:16].unsqueeze(1).to_broadcast([128, 8, 16])
                ssb = rope[:, t, 16:32].unsqueeze(1).to_broadcast([128, 8, 16])
                sc.add("dve", lambda e, p3=p3, ccb=ccb: e.tensor_tensor(out=t1, in0=p3[:, :, 0:16], in1=ccb, op=ALU.mult),
                       reads=[bank_r[bi], r_const], writes=[t1_r])
                sc.add("dve", lambda e, p3=p3, ssb=ssb: e.tensor_tensor(out=t2, in0=p3[:, :, 0:16], in1=ssb, op=ALU.mult),
                       reads=[bank_r[bi], r_const], writes=[t2_r])
                sc.add("dve", lambda e, d3=d3: e.tensor_tensor(out=d3[:, :, 0:8], in0=t1[:, :, 0:8], in1=t2[:, :, 8:16], op=ALU.subtract),
                       reads=[t1_r, t2_r], writes=[dst_r], partial=True)
                sc.add("dve", lambda e, d3=d3: e.tensor_tensor(out=d3[:, :, 8:16], in0=t1[:, :, 8:16], in1=t2[:, :, 0:8], op=ALU.add),
                       reads=[t1_r, t2_r], writes=[dst_r], partial=True)
            if KSTOP <= 2:
                continue
            for j in range(4):
                sc.add("pe", lambda e, j=j, s2=s2: e.transpose(out=ptr[:, j, :], in_=qb[s2][:, j * 128:(j + 1) * 128], identity=ident_b[:]),
                       reads=[qb_r[s2], r_const], writes=[bank_r[3]])
            for j in range(4):
                sc.add("pe", lambda e, j=j, s2=s2: e.transpose(out=ptr[:, 4 + j, :], in_=kb[s2][:, j * 128:(j + 1) * 128], identity=ident_b[:]),
                       reads=[kb_r[s2], r_const], writes=[bank_r[3]])
            qs = b % 2
            if KSTOP == 3 and os.environ.get("KSUB") == "a":
                continue
            sc.add("dve", lambda e, qs=qs, qt_=qt_: e.tensor_copy(out=QT[qs][:, :, qt_ * 128:(qt_ + 1) * 128], in_=ptr[:, 0:4, :]),
                   reads=[bank_r[3]], writes=[QT_r[qs]], partial=(qt_ == 1))
            if KSTOP == 3 and os.environ.get("KSUB") == "b":
                continue
            sc.add("dve", lambda e, tsl=tsl: e.tensor_copy(out=KT[:, :, tsl], in_=ptr[:, 4:8, :]),
                   reads=[bank_r[3]], writes=[KT_r[t]])
            if qt_ == 0 or KSTOP <= 3:
                continue
            if b + 1 < NB:
                sc.add("dve", lambda e, b=b: e.tensor_reduce(out=kmf, in_=KT[:, :, b * 256:(b + 1) * 256], axis=AX.X, op=ALU.add),
                       reads=[KT_r[t - 1], KT_r[t]], writes=[kmT_r])
                sc.add("dve", lambda e, b=b: e.tensor_scalar(out=kmT[:, :, b], in0=kmf, scalar1=1.0 / 256, scalar2=None, op0=ALU.mult),
                       reads=[kmT_r], writes=[kmT_r], partial=True)
            topk = b > 3
            if topk:
                KV_ = os.environ.get("KVAR", "")
                for par in range(2):
                    pgp = (pg0, pg1)[par]
                    for q2 in range(2):
                        for hh in range(4):
                            if KV_ == "q0" and q2 == 1: continue
                            if KV_ == "p0" and par == 1: continue
                            if KV_ == "h0" and hh > 0: continue
                            base = par * 64
                            sc.add("pe", lambda e, pgp=pgp, q2=q2, hh=hh, base=base, qs=qs: e.matmul(pgp[:, q2 * 4 + hh, :], lhsT=QT[qs][base:base + 64, hh, q2 * 128:(q2 + 1) * 128], rhs=kmT[base:base + 64, hh, :], start=True, stop=True),
                                   reads=[QT_r[qs], kmT_r], writes=[bank_r[(4, 6)[par]]])
                KT_ = os.environ.get("KTOPK", "full")
                if KT_ in ("gc", "gcm", "full"):
                    sc.add("dve", lambda e, b=b: e.tensor_copy(out=gsb[:, 0:8, 0:b], in_=pg0[:, :, 0:b]), reads=[bank_r[4]], writes=[gsb_r])
                    sc.add("dve", lambda e, b=b: e.tensor_copy(out=gsb[:, 8:16, 0:b], in_=pg1[:, :, 0:b]), reads=[bank_r[6]], writes=[gsb_r], partial=True)
                if KT_ in ("gcm", "full"):
                    for i16 in range(16):
                        sc.add("dve", lambda e, i16=i16: e.max(out=m8[:, i16, :], in_=gsb[:, i16, :]), reads=[gsb_r], writes=[sel_r], partial=True)
                if KT_ == "full":
                    sc.add("dve", lambda e: e.tensor_tensor(out=sel[:], in0=gsb[:], in1=m8[:, :, 2:3].to_broadcast([128, 16, 16]), op=ALU.is_ge),
                           reads=[gsb_r, sel_r], writes=[sel_r])
                else:
                    sc.add("dve", lambda e: e.memset(sel[:], 1.0), reads=[gsb_r, bank_r[4]], writes=[sel_r])
            for h in range(8 if KSTOP > 4 else 0):
                j = h // 2; base = (h % 2) * 64
                for n in [b] + list(range(b)):
                    si = SB[cnt["s"] % 4]; cnt["s"] += 1
                    pi = cnt["pt"] % NPT; cnt["pt"] += 1
                    oi = OB[cnt["o"] % 2]; cnt["o"] += 1
                    pss = bank[si][:].rearrange("p (k q) -> p k q", k=2)
                    pso = bank[oi][:, 0:130].rearrange("p (q d) -> p q d", q=2)
                    for kt in range(2):
                        ktile = 2 * n + kt
                        sc.add("pe", lambda e, pss=pss, kt=kt, j=j, base=base, ktile=ktile, qs=qs: e.matmul(pss[:, kt, :], lhsT=KT[base:base + 64, j, ktile * 128:(ktile + 1) * 128], rhs=QT[qs][base:base + 64, j, :], start=True, stop=True),
                               reads=[KT_r[ktile], QT_r[qs]], writes=[bank_r[si]])
                    sc.add("act", lambda e, pss=pss, pi=pi: e.activation(out=PT[pi][:], in_=pss, func=AF.Exp, scale=0.125, bias=cbias[:, 3:4]),
                           reads=[bank_r[si]], writes=[PT_r[pi]])
                    if n == b:
                        for kt in range(2):
                            sc.add("pool", lambda e, pi=pi, kt=kt: e.tensor_tensor(out=PT[pi][:, kt, kt * 128:(kt + 1) * 128], in0=PT[pi][:, kt, kt * 128:(kt + 1) * 128], in1=caus_b[:], op=ALU.mult),
                                   reads=[PT_r[pi], r_const], writes=[PT_r[pi]])
                    for q2 in range(2):
                        kts = [0] if (n == b and q2 == 0) else [0, 1]
                        for ii, kt in enumerate(kts):
                            sc.add("pe", lambda e, pso=pso, pi=pi, q2=q2, kt=kt, n=n, h=h, ii=ii, last=(ii == len(kts) - 1): e.matmul(pso[:, q2, :], lhsT=PT[pi][:, kt, q2 * 128:(q2 + 1) * 128], rhs=Vp[:, 2 * n + kt, h, :], start=(ii == 0), stop=last),
                                   reads=[PT_r[pi], Vp_r[2 * n + kt]], writes=[bank_r[oi]])
                    for q2 in range(2):
                        if n == b:
                            sc.add("dve", lambda e, pso=pso, q2=q2, h=h: e.tensor_copy(out=acc[:, q2, h, :], in_=pso[:, q2, :]),
                                   reads=[bank_r[oi]], writes=[acc_r[q2][h]])
                        elif topk:
                            sc.add("dve", lambda e, pso=pso, q2=q2, h=h, n=n: e.scalar_tensor_tensor(out=acc[:, q2, h, :], in0=pso[:, q2, :], scalar=sel[:, (h % 2) * 8 + q2 * 4 + h // 2, n:n + 1], in1=acc[:, q2, h, :], op0=ALU.mult, op1=ALU.add),
                                   reads=[bank_r[oi], sel_r, acc_r[q2][h]], writes=[acc_r[q2][h]])
                        else:
                            sc.add("dve", lambda e, pso=pso, q2=q2, h=h: e.tensor_tensor(out=acc[:, q2, h, :], in0=pso[:, q2, :], in1=acc[:, q2, h, :], op=ALU.add),
                                   reads=[bank_r[oi], acc_r[q2][h]], writes=[acc_r[q2][h]])
            if KSTOP <= 5:
                continue
            allacc = [acc_r[q][h] for q in range(2) for h in range(8)]
            sc.add("dve", lambda e: e.reciprocal(out=rec, in_=acc[:].rearrange("p q h d -> p (q h) d")[:, :, 64]), reads=allacc, writes=[ob_r])
            sc.add("dve", lambda e: e.tensor_tensor(out=ob[:].rearrange("p q (h d) -> p (q h) d", h=8), in0=acc[:].rearrange("p q h d -> p (q h) d")[:, :, 0:64], in1=rec.unsqueeze(2).to_broadcast([128, 16, 64]), op=ALU.mult),
                   reads=allacc + [ob_r], writes=[ob_r])
            os_ = b % 2
            for q2 in range(2):
                for j in range(4):
                    sc.add("pe", lambda e, q2=q2, j=j: e.transpose(out=ptr[:, q2 * 4 + j, :], in_=ob[:, q2, j * 128:(j + 1) * 128], identity=ident_b[:]),
                           reads=[ob_r, r_const], writes=[bank_r[3]])
            sc.add("dve", lambda e, os_=os_: e.tensor_copy(out=obT[os_][:].rearrange("p j (q c) -> p q j c", q=2), in_=ptr.rearrange("p (q j) c -> p q j c", q=2)),
                   reads=[bank_r[3]], writes=[obT_r[os_]])
            sc.add("sp", lambda e, os_=os_, b=b: e.dma_start(out=oT_d[4:8, :, b * 256:(b + 1) * 256].rearrange("j p c -> p j c"), in_=obT[os_][:]),
                   reads=[obT_r[os_]], writes=[], dma=True, key="obT%d" % os_)

    def phaseB(l):
        cv.reset()
        for _ in range(int(os.environ.get("ACTPAD", "0"))):
            sc.add("act", lambda e: e.copy(out=arena[:, 0:8], in_=ident_f[:, 0:8]))
        NBLK = S // 512
        wB = cv.get([128, 8, 2056], BF16)
        wB_r = [sc.res("wB%d" % k) for k in range(8)]
        gcw = cv.get([128, 12, 4]); dtb = cv.get([128, 4]); nexpA = cv.get([128, 4]); gng = cv.get([128, 128])
        mstrict = cv.get([128, 128]); mincl = cv.get([128, 128]); glc = cv.get([128, 4, 128])
        boff = cv.get([128, 6, 128]); aoff1 = cv.get([128, 128]); ones_f = cv.get([128, 128])
        pc_r = sc.res("pconst")
        rawb = cv.get([128, 2, 515]); rawb_r = [sc.res("rawb%d" % f) for f in range(2)]
        halo = cv.get([128, 12, 3]); halo_r = [sc.res("halo%d" % f) for f in range(12)]
        cacc = [cv.get([128, 512]) for _ in range(2)]; cacc_r = [sc.res("cacc%d" % i) for i in range(2)]
        cT = cv.get([128, 12, 512]); cT_r = [sc.res("cT%d" % f) for f in range(12)]
        Sst = [cv.get([128, 4, 128]) for _ in range(2)]
        S_r = [[sc.res("S%d_%d" % (i, h)) for h in range(4)] for i in range(2)]

        def tmp(name, shape, dt=F32, n=2):
            return [(cv.get(shape, dt), sc.res("%s%d" % (name, i))) for i in range(n)]

        Qtm = tmp("Qtm", [128, 4, 128]); Ktm = tmp("Ktm", [128, 4, 128]); Vtm = tmp("Vtm", [128, 4, 128])
        ssq = tmp("ssq", [128, 8]); rn = tmp("rn", [128, 8])
        sm = tmp("sm", [128, 64])
        zs = tmp("zs", [128, 512])
        junk = tmp("junk", [128, 128], n=1)[0]
        HT = 2
        QTh = tmp("QTh", [128, 128], F32, HT); KTh = tmp("KTh", [128, 128], F32, HT)
        GR = tmp("GR", [128, 128], F32, HT); Dm = tmp("Dm", [128, 128], F32, HT); Ds = tmp("Ds", [128, 128], F32, HT)
        Am = tmp("Am", [128, 128], F32, 2 * HT); attn = tmp("attn", [128, 128], F32, 2 * HT); attnT = tmp("attnT", [128, 128], F32, HT)
        Boall = tmp("Boall", [128, 6, 128], F32, HT); Em = tmp("Em", [128, 128], F32, 2 * HT); Dk = tmp("Dk", [128, 128], F32, 2 * HT)
        Xm = tmp("Xm", [128, 128], F32, HT); Rm = tmp("Rm", [128, 256], F32, HT); UW = tmp("UW", [128, 256], F32, HT)
        Kd = tmp("Kd", [128, 128], F32, HT); Qd = tmp("Qd", [128, 128], F32, HT); QpT = tmp("QpT", [128, 128], F32, HT)
        MpT = tmp("MpT", [128, 2, 128], F32, HT)
        osb = tmp("osb", [128, 4, 128], F32, 2); oss = tmp("oss", [128, 8], F32, 2)
        oab = tmp("oab", [128, 512], BF16, 2); oaT = tmp("oaT", [128, 4, 128], BF16, 2)
        ctr = {}

        def nxt(lst, key):
            i = ctr.get(key, 0); ctr[key] = i + 1
            return lst[i % len(lst)]

        PB = tuple(int(x) for x in os.environ.get("PB", "2,3,4,6,7").split(","))

        def pbank():
            i = ctr.get("pb", 0); ctr["pb"] = i + 1
            bi = PB[i % len(PB)]
            return bank[bi], bank_r[bi]

        for kc in range(8):
            sc.add("pool", lambda e, kc=kc: e.dma_start(out=wB[:, kc, :], in_=w_in_d[l, kc * 128:(kc + 1) * 128, 0:2056]),
                   writes=[wB_r[kc]], dma=True, key="wB%d" % kc)
        w4 = cT.rearrange("p f t -> p (f t)")[0:4, 0:1536]; w4_r = sc.res("w4")
        sc.add("sp", lambda e: e.dma_start(out=w4, in_=gconv_d[l, :, :]), writes=[w4_r, cT_r[0], cT_r[1], cT_r[2]], dma=True, key="w4")
        for f in range(12):
            sc.pe32(lambda e, f=f: e.transpose(out=bank[0][:, f * 4:(f + 1) * 4], in_=w4[0:4, f * 128:(f + 1) * 128], identity=ident_f[0:4, 0:4]), reads=[w4_r, cT_r[0], cT_r[1], cT_r[2], r_const], writes=[bank_r[0]])
        sc.add("dve", lambda e: e.tensor_copy(out=gcw.rearrange("p f j -> p (f j)"), in_=bank[0][:, 0:48]), reads=[bank_r[0]], writes=[pc_r], partial=True)
        sc.add("sp", lambda e: e.dma_start(out=dtb, in_=dtb_d[l, :].partition_broadcast(128)), writes=[pc_r], dma=True, key="pc", partial=True)
        sc.add("sp", lambda e: e.dma_start(out=nexpA, in_=alog_d[l, :].partition_broadcast(128)), writes=[pc_r], dma=True, key="pc", partial=True)
        sc.add("sp", lambda e: e.dma_start(out=gng, in_=gng_d[l, :].partition_broadcast(128)), writes=[pc_r], dma=True, key="pc", partial=True)
        sc.add("sp", lambda e: e.dma_start(out=mstrict, in_=c_mstrict_d[:, :]), writes=[pc_r], dma=True, key="pc", partial=True)
        sc.add("sp", lambda e: e.dma_start(out=glc.rearrange("p a b -> p (a b)"), in_=c_gl_d[:, :]), writes=[pc_r], dma=True, key="pc", partial=True)
        sc.add("sp", lambda e: e.dma_start(out=boff.rearrange("p a b -> p (a b)"), in_=c_boff_d[:, :]), writes=[pc_r], dma=True, key="pc", partial=True)
        sc.add("sp", lambda e: e.dma_start(out=aoff1, in_=c_aoff1_d[:, :]), writes=[pc_r], dma=True, key="pc", partial=True)
        sc.add("sp", lambda e: e.dma_start(out=mincl, in_=c_negincl_d[:, :]), writes=[pc_r], dma=True, key="pc", partial=True)
        sc.add("act", lambda e: e.activation(out=nexpA, in_=nexpA, func=AF.Exp, bias=cbias[:, 3:4]), reads=[pc_r], writes=[pc_r])
        sc.add("dve", lambda e: e.tensor_scalar(out=nexpA, in0=nexpA, scalar1=-1.0, scalar2=None, op0=ALU.mult), reads=[pc_r], writes=[pc_r])
        sc.add("dve", lambda e: e.memset(ones_f, 1.0), writes=[pc_r], reads=[pc_r])
        sc.add("dve", lambda e: e.memset(Sst[0][:], 0.0), writes=S_r[0])
        sc.add("pool", lambda e: e.memset(halo, 0.0), writes=halo_r)
        LTc, UTc, CS0c, CS1c = (glc[:, i, :] for i in range(4))

        cur = 0

        KB = int(os.environ.get("KB", "99"))
        for c in range(NBLK):
            csl = slice(c * 512, (c + 1) * 512)
            for f in range(12):
                pb, pb_r = bank[f % 2], bank_r[f % 2]
                for kc in range(8):
                    sc.pe16(pb[:], lambda e, pb=pb, f=f, kc=kc, csl=csl: e.matmul(pb[:], lhsT=wB[:, kc, f * 128:(f + 1) * 128], rhs=xT[:, kc, csl], start=(kc == 0), stop=(kc == 7)),
                           reads=[wB_r[kc]] + xT_r[4 * c:4 * c + 4], writes=[pb_r])
                rs = f % 2
                sc.add("pool", lambda e, f=f, rs=rs: e.tensor_copy(out=rawb[:, rs, 0:3], in_=halo[:, f, :]), reads=[halo_r[f]], writes=[rawb_r[rs]])
                sc.add("act", lambda e, pb=pb, rs=rs: e.copy(out=rawb[:, rs, 3:515], in_=pb[:]), reads=[pb_r], writes=[rawb_r[rs]], partial=True)
                sc.add("pool", lambda e, f=f, rs=rs: e.tensor_copy(out=halo[:, f, :], in_=rawb[:, rs, 512:515]), reads=[rawb_r[rs]], writes=[halo_r[f]])
                ca, ca_r = cacc[f % 2], cacc_r[f % 2]
                sc.add("act", lambda e, pb=pb, f=f, ca=ca: e.activation(out=ca, in_=pb[:], func=AF.Copy, scale=gcw[:, f, 3:4]), reads=[pb_r, pc_r], writes=[ca_r])
                for j in (2, 1, 0):
                    sc.add("dve", lambda e, f=f, j=j, ca=ca, rs=rs: e.scalar_tensor_tensor(out=ca, in0=rawb[:, rs, j:j + 512], scalar=gcw[:, f, j:j + 1], in1=ca, op0=ALU.mult, op1=ALU.add),
                           reads=[rawb_r[rs], pc_r, ca_r], writes=[ca_r])
                sc.add("act", lambda e, f=f, ca=ca: e.activation(out=cT[:, f, :], in_=ca, func=AF.Silu, bias=cbias[:, 3:4]), reads=[ca_r], writes=[cT_r[f]])
            for tt in range(4 if KB > 1 else 0):
                t = c * 4 + tt
                tsl = slice(t * 128, (t + 1) * 128)
                lsl = slice(tt * 128, (tt + 1) * 128)
                (Qt, Qt_r) = nxt(Qtm, "Qtm"); (Kt, Kt_r) = nxt(Ktm, "Ktm"); (Vt, Vt_r) = nxt(Vtm, "Vtm")
                (sq, sq_r) = nxt(ssq, "ssq"); (rnn, rn_r) = nxt(rn, "rn"); (smt, sm_r) = nxt(sm, "sm"); (zst, zs_r) = nxt(zs, "zs")
                for g in range(3):
                    pb, pb_r = bank[2 + g], bank_r[2 + g]
                    for h in range(4):
                        sc.pe32(lambda e, pb=pb, g=g, h=h, lsl=lsl: e.transpose(out=pb[:, h * 128:(h + 1) * 128], in_=cT[:, g * 4 + h, lsl], identity=ident_f[:]),
                               reads=[cT_r[g * 4 + h], r_const], writes=[pb_r])
                for g in range(2):
                    for h in range(4):
                        sc.add("act", lambda e, g=g, h=h, sq=sq: e.activation(out=junk[0], in_=bank[2 + g][:, h * 128:(h + 1) * 128], func=AF.Square, bias=cbias[:, 3:4], accum_out=sq[:, g * 4 + h:g * 4 + h + 1]),
                               reads=[bank_r[2 + g]], writes=[sq_r, junk[1]], partial=True)
                sc.add("act", lambda e, sq=sq, rnn=rnn: e.activation(out=rnn, in_=sq, func=AF.Sqrt, bias=cbias[:, 0:1]), reads=[sq_r, r_const], writes=[rn_r])
                sc.add("dve", lambda e, rnn=rnn: e.reciprocal(out=rnn, in_=rnn), reads=[rn_r], writes=[rn_r])
                sc.add("dve", lambda e, rnn=rnn: e.tensor_scalar(out=rnn[:, 0:4], in0=rnn[:, 0:4], scalar1=float(128 ** -0.5), scalar2=None, op0=ALU.mult), reads=[rn_r], writes=[rn_r])
                sc.add("dve", lambda e, Qt=Qt, rnn=rnn: e.tensor_tensor(out=Qt, in0=bank[2][:].rearrange("p (h d) -> p h d", h=4), in1=rnn[:, 0:4].unsqueeze(2).to_broadcast([128, 4, 128]), op=ALU.mult),
                       reads=[bank_r[2], rn_r], writes=[Qt_r])
                sc.add("dve", lambda e, Kt=Kt, rnn=rnn: e.tensor_tensor(out=Kt, in0=bank[3][:].rearrange("p (h d) -> p h d", h=4), in1=rnn[:, 4:8].unsqueeze(2).to_broadcast([128, 4, 128]), op=ALU.mult),
                       reads=[bank_r[3], rn_r], writes=[Kt_r])
                sc.add("act", lambda e, Vt=Vt: e.copy(out=Vt, in_=bank[4][:].rearrange("p (h d) -> p h d", h=4)), reads=[bank_r[4]], writes=[Vt_r])
                if KB <= 2:
                    continue
                pab, pab_r = bank[5], bank_r[5]
                for kc in range(8):
                    sc.pe16(bank[5][:, 0:8], lambda e, kc=kc, tsl=tsl: e.matmul(bank[5][:, 0:8], lhsT=xT[:, kc, tsl], rhs=wB[:, kc, 1536:1544], start=(kc == 0), stop=(kc == 7)),
                           reads=[xT_r[t], wB_r[kc]], writes=[pab_r])
                sc.add("dve", lambda e, smt=smt: e.tensor_tensor(out=smt[:, 0:4], in0=bank[5][:, 0:4], in1=dtb, op=ALU.add), reads=[pab_r, pc_r], writes=[sm_r])
                sc.add("dve", lambda e, smt=smt: e.tensor_scalar(out=smt[:, 36:40], in0=smt[:, 0:4], scalar1=-1.0, scalar2=None, op0=ALU.mult), reads=[sm_r], writes=[sm_r])
                sc.add("dve", lambda e, smt=smt: e.tensor_tensor(out=smt[:, 4:8], in0=smt[:, 0:4], in1=smt[:, 36:40], op=ALU.min), reads=[sm_r], writes=[sm_r])
                sc.add("act", lambda e, smt=smt: e.activation(out=smt[:, 4:8], in_=smt[:, 4:8], func=AF.Exp, bias=cbias[:, 3:4]), reads=[sm_r], writes=[sm_r])
                sc.add("act", lambda e, smt=smt: e.activation(out=smt[:, 4:8], in_=smt[:, 4:8], func=AF.Ln, bias=cbias[:, 1:2]), reads=[sm_r, r_const], writes=[sm_r])
                sc.add("dve", lambda e, smt=smt: e.scalar_tensor_tensor(out=smt[:, 8:12], in0=smt[:, 0:4], scalar=0.0, in1=smt[:, 4:8], op0=ALU.max, op1=ALU.add), reads=[sm_r], writes=[sm_r])
                sc.add("dve", lambda e, smt=smt: e.tensor_tensor(out=smt[:, 8:12], in0=smt[:, 8:12], in1=nexpA, op=ALU.mult), reads=[sm_r, pc_r], writes=[sm_r])
                sc.add("act", lambda e, smt=smt: e.activation(out=smt[:, 12:16], in_=bank[5][:, 4:8], func=AF.Exp, scale=-1.0, bias=cbias[:, 3:4]), reads=[pab_r], writes=[sm_r])
                sc.add("dve", lambda e, smt=smt: e.tensor_scalar(out=smt[:, 12:16], in0=smt[:, 12:16], scalar1=1.0, scalar2=None, op0=ALU.add), reads=[sm_r], writes=[sm_r])
                sc.add("dve", lambda e, smt=smt: e.reciprocal(out=smt[:, 12:16], in_=smt[:, 12:16]), reads=[sm_r], writes=[sm_r])
                for kc in range(8):
                    sc.pe16(bank[5][:], lambda e, kc=kc, tsl=tsl: e.matmul(bank[5][:], lhsT=xT[:, kc, tsl], rhs=wB[:, kc, 1544:2056], start=(kc == 0), stop=(kc == 7)),
                           reads=[xT_r[t], wB_r[kc]], writes=[pab_r])
                sc.add("act", lambda e, zst=zst: e.activation(out=zst, in_=bank[5][:], func=AF.Silu, bias=cbias[:, 3:4]), reads=[pab_r], writes=[zs_r])
                sc.add("pool", lambda e, zst=zst: e.tensor_tensor(out=zst.rearrange("p (h d) -> p h d", h=4), in0=zst.rearrange("p (h d) -> p h d", h=4), in1=gng.unsqueeze(1).to_broadcast([128, 4, 128]), op=ALU.mult),
                       reads=[zs_r, pc_r], writes=[zs_r])
                for i4, lt in enumerate((LTc, UTc, CS0c, CS1c)):
                    sc.pe32(lambda e, i4=i4, lt=lt, smt=smt: e.matmul(bank[5][:, 16 + 4 * i4:20 + 4 * i4], lhsT=lt, rhs=smt[:, 8:12], start=True, stop=True),
                           reads=[sm_r, pc_r], writes=[pab_r])
                sc.add("dve", lambda e, smt=smt: e.tensor_copy(out=smt[:, 40:44], in_=bank[5][:, 16:20]), reads=[pab_r], writes=[sm_r])
                sc.add("dve", lambda e, smt=smt: e.tensor_scalar(out=smt[:, 16:32], in0=bank[5][:, 16:32], scalar1=-60.0, scalar2=None, op0=ALU.max), reads=[pab_r], writes=[sm_r])
                sc.add("act", lambda e, smt=smt: e.activation(out=smt[:, 16:32], in_=smt[:, 16:32], func=AF.Exp, bias=cbias[:, 3:4]), reads=[sm_r], writes=[sm_r])
                sc.add("dve", lambda e, smt=smt: e.tensor_tensor(out=smt[:, 32:36], in0=smt[:, 12:16], in1=smt[:, 16:20], op=ALU.mult), reads=[sm_r], writes=[sm_r])
                (osb_t, osb_r) = nxt(osb, "osb"); (oss_t, oss_r) = nxt(oss, "oss")
                if KB <= 3:
                    continue
                sc.safe = os.environ.get("SAFE", "1") == "1"
                for h in range(4):
                    (QT_, QT_r_) = nxt(QTh, "QTh"); (KT_, KT_r_) = nxt(KTh, "KTh"); (GR_, GR_r_) = nxt(GR, "GR")
                    (D_, D_r) = nxt(Dm, "Dm"); (Ds_, Ds_r) = nxt(Ds, "Ds"); (A_, A_r) = nxt(Am, "Am")
                    (at_, at_r) = nxt(attn, "attn"); (atT_, atT_r) = nxt(attnT, "attnT"); (Bo_, Bo_r) = nxt(Boall, "Bo")
                    (X_, X_r) = nxt(Xm, "X"); (R_, R_r) = nxt(Rm, "R"); (UW_, UW_r) = nxt(UW, "UW")
                    (Kd_, Kd_r) = nxt(Kd, "Kd"); (Qd_, Qd_r) = nxt(Qd, "Qd"); (QpT_, QpT_r) = nxt(QpT, "QpT"); (Mp_, Mp_r) = nxt(MpT, "MpT")
                    beta_h = smt[:, 12 + h:13 + h]; egc_h = smt[:, 16 + h:17 + h]; egu_h = smt[:, 20 + h:21 + h]
                    gcum_h = smt[:, 40 + h:41 + h]; bk_h = smt[:, 32 + h:33 + h]
                    pb, pb_r = pbank()
                    sc.pe32(lambda e, pb=pb, Qt=Qt, h=h: e.transpose(out=pb[:, 0:128], in_=Qt[:, h, :], identity=ident_f[:]), reads=[Qt_r, r_const], writes=[pb_r])
                    sc.pe32(lambda e, pb=pb, Kt=Kt, h=h: e.transpose(out=pb[:, 128:256], in_=Kt[:, h, :], identity=ident_f[:]), reads=[Kt_r, r_const], writes=[pb_r])
                    sc.add("dve", lambda e, pb=pb, QT_=QT_: e.tensor_copy(out=QT_, in_=pb[:, 0:128]), reads=[pb_r], writes=[QT_r_])
                    sc.add("dve", lambda e, pb=pb, KT_=KT_: e.tensor_copy(out=KT_, in_=pb[:, 128:256]), reads=[pb_r], writes=[KT_r_])
                    sc.add("dve", lambda e, GR_=GR_, smt=smt, h=h: e.tensor_scalar(out=GR_, in0=ones_f, scalar1=smt[:, 8 + h:9 + h], scalar2=None, op0=ALU.mult), reads=[sm_r, pc_r], writes=[GR_r_])
                    K4 = os.environ.get("K4", "z")
                    if KB == 4 and K4 <= "a":
                        continue
                    pg_, pg_r = pbank()
                    sc.pe32(lambda e, pg_=pg_, GR_=GR_: e.matmul(pg_[:, 0:128], lhsT=GR_, rhs=LTc, start=True, stop=True), reads=[GR_r_, pc_r], writes=[pg_r])
                    sc.add("dve", lambda e, pg_=pg_, D_=D_, gcum_h=gcum_h: e.tensor_scalar(out=D_, in0=pg_[:, 0:128], scalar1=gcum_h, scalar2=0.0, op0=ALU.subtract, op1=ALU.max), reads=[pg_r, sm_r], writes=[D_r])
                    sc.add("dve", lambda e, D_=D_: e.tensor_scalar(out=D_, in0=D_, scalar1=60.0, scalar2=None, op0=ALU.min), reads=[D_r], writes=[D_r])
                    if os.environ.get("K5") == "waitD0":
                        sc.pe32(lambda e, pg_=pg_: e.transpose(out=pg_[:, 256:384], in_=ident_f[:], identity=ident_f[:]), reads=[r_const, D_r], writes=[])
                    sc.add("act", lambda e, D_=D_: e.activation(out=D_, in_=D_, func=AF.Exp, scale=-1.0, bias=cbias[:, 3:4]), reads=[D_r], writes=[D_r])
                    if os.environ.get("K5") == "waitD1":
                        sc.pe32(lambda e, pg_=pg_: e.transpose(out=pg_[:, 256:384], in_=ident_f[:], identity=ident_f[:]), reads=[r_const, D_r], writes=[])
                    PD = os.environ.get("PD", "dve")
                    sc.add(PD, lambda e, D_=D_, Ds_=Ds_: e.tensor_tensor(out=Ds_, in0=D_, in1=mstrict, op=ALU.mult), reads=[D_r, pc_r], writes=[Ds_r])
                    sc.add(PD, lambda e, D_=D_: e.tensor_tensor(out=D_, in0=D_, in1=mincl, op=ALU.mult), reads=[D_r, pc_r], writes=[D_r])
                    if os.environ.get("K5") == "waitD2":
                        sc.pe32(lambda e, pg_=pg_: e.transpose(out=pg_[:, 256:384], in_=ident_f[:], identity=ident_f[:]), reads=[r_const, Ds_r], writes=[])
                    if KB == 4 and K4 <= "b":
                        continue
                    pk_, pk_r = pbank()
                    K8 = os.environ.get("K8", "")
                    if K8 != "noKK" and K8 != "none":
                        sc.pe32(lambda e, pk_=pk_, KT_=KT_: e.matmul(pk_[:, 0:128], lhsT=KT_, rhs=KT_, start=True, stop=True), reads=[KT_r_], writes=[pk_r])
                    if K8 != "noQK" and K8 != "none":
                        sc.pe32(lambda e, pk_=pk_, QT_=QT_, KT_=KT_: e.matmul(pk_[:, 128:256], lhsT=QT_, rhs=KT_, start=True, stop=True), reads=[QT_r_, KT_r_], writes=[pk_r])
                    sc.add("dve", lambda e, pk_=pk_, A_=A_, beta_h=beta_h: e.tensor_scalar(out=A_, in0=pk_[:, 0:128], scalar1=beta_h, scalar2=None, op0=ALU.mult), reads=[pk_r, sm_r], writes=[A_r])
                    sc.add("dve", lambda e, pk_=pk_, at_=at_: e.tensor_copy(out=at_, in_=pk_[:, 128:256]), reads=[pk_r], writes=[at_r])
                    (A0_, A0_r) = nxt(Am, "Am"); (at0_, at0_r) = nxt(attn, "attn")
                    sc.add("dve", lambda e, A_=A_, A0_=A0_, Ds_=Ds_: e.tensor_tensor(out=A0_, in0=A_, in1=Ds_, op=ALU.mult), reads=[A_r, Ds_r], writes=[A0_r])
                    sc.add("dve", lambda e, at_=at_, at0_=at0_, D_=D_: e.tensor_tensor(out=at0_, in0=at_, in1=D_, op=ALU.mult), reads=[at_r, D_r], writes=[at0_r])
                    A_, A_r, at_, at_r = A0_, A0_r, at0_, at0_r
                    if KB == 4 and K4 <= "c":
                        continue
                    if os.environ.get("HB", "0") == "1":
                        sc.barrier()
                    if os.environ.get("K6") == "samebank":
                        pt_, pt_r = pk_[:, 256:512], pk_r
                    else:
                        pt_, pt_r = pbank()
                    K5 = os.environ.get("K5", "")
                    if K5 == "waitonly":
                        sc.pe32(lambda e, pt_=pt_: e.transpose(out=pt_[:, 0:128], in_=ident_f[:], identity=ident_f[:]), reads=[r_const, A_r], writes=[pt_r])
                        continue
                    if K5 == "spin":
                        for _ in range(int(os.environ.get("NSPIN", "300"))):
                            sc.pe32(lambda e, pt_=pt_: e.transpose(out=pt_[:, 256:384], in_=ident_f[:], identity=ident_f[:]), reads=[r_const], writes=[pt_r])
                        sc.pe32(lambda e, pt_=pt_: e.transpose(out=pt_[:, 0:128], in_=ident_f[:], identity=ident_f[:]), reads=[r_const, A_r], writes=[pt_r])
                        continue
                    if K5 == "dummy":
                        sc.pe32(lambda e, pt_=pt_: e.transpose(out=pt_[:, 256:384], in_=ident_f[:], identity=ident_f[:]), reads=[r_const], writes=[pt_r])
                        sc.pe32(lambda e, pt_=pt_: e.transpose(out=pt_[:, 0:128], in_=ident_f[:], identity=ident_f[:]), reads=[r_const, A_r], writes=[pt_r])
                        continue
                    if K5 == "viaact2" and ((t * 4 + h) >= int(os.environ.get("KN", "999")) or (t * 4 + h) < int(os.environ.get("KN0", "0"))):
                        continue
                    if K5 == "viaact2":
                        sc.add("dve", lambda e, X_=X_, A_=A_: e.tensor_copy(out=X_, in_=A_), reads=[A_r], writes=[X_r])
                        continue
                    if K5 == "viaact":
                        sc.add("dve", lambda e, X_=X_, A_=A_: e.tensor_copy(out=X_, in_=A_), reads=[A_r], writes=[X_r])
                        sc.pe32(lambda e, pt_=pt_, X_=X_: e.transpose(out=pt_[:, 0:128], in_=X_, identity=ident_f[:]), reads=[r_const, X_r], writes=[pt_r])
                        continue
                    if K5 == "waitbf":
                        ptb_ = pt_.bitcast(BF16)
                        sc.pe16(ptb_[:, 0:128], lambda e, ptb_=ptb_: e.transpose(out=ptb_[:, 0:128], in_=ident_b[:], identity=ident_b[:]), reads=[r_const, A_r], writes=[pt_r])
                        continue
                    if K5 == "waitat":
                        sc.pe32(lambda e, pt_=pt_: e.transpose(out=pt_[:, 0:128], in_=ident_f[:], identity=ident_f[:]), reads=[r_const, at_r], writes=[pt_r])
                        continue
                    if K5 == "waitD":
                        sc.pe32(lambda e, pt_=pt_: e.transpose(out=pt_[:, 0:128], in_=ident_f[:], identity=ident_f[:]), reads=[r_const, Ds_r], writes=[pt_r])
                        continue
                    if K5 == "useGR":
                        sc.pe32(lambda e, pt_=pt_, GR_=GR_: e.transpose(out=pt_[:, 0:128], in_=GR_, identity=ident_f[:]), reads=[GR_r_, r_const] + ([A_r] if os.environ.get("K7") != "nodep" else []), writes=[pt_r])
                        continue
                    if K5 == "useDs":
                        sc.pe32(lambda e, pt_=pt_, Ds_=Ds_: e.transpose(out=pt_[:, 0:128], in_=Ds_, identity=ident_f[:]), reads=[Ds_r, A_r, r_const], writes=[pt_r])
                        continue
                    if K5 != "nope" and K5 != "pe2":
                        sc.pe32(lambda e, pt_=pt_, A_=A_: e.transpose(out=pt_[:, 0:128], in_=A_, identity=ident_f[:]), reads=[A_r, r_const], writes=[pt_r])
                    if K5 != "nope" and K5 != "pe1":
                        sc.pe32(lambda e, pt_=pt_, at_=at_: e.transpose(out=pt_[:, 128:256], in_=at_, identity=ident_f[:]), reads=[at_r, r_const], writes=[pt_r])
                    if K5 == "nodve":
                        continue
                    sc.add("dve", lambda e, pt_=pt_, X_=X_: e.tensor_copy(out=X_, in_=pt_[:, 0:128]), reads=[pt_r], writes=[X_r])
                    if KB == 4 and K4 <= "d":
                        continue
                    sc.add("pool", lambda e, X_=X_, Bo_=Bo_: e.tensor_tensor(out=Bo_, in0=X_.unsqueeze(1).to_broadcast([128, 6, 128]), in1=boff, op=ALU.mult), reads=[X_r, pc_r], writes=[Bo_r])
                    if KB == 4 and K4 <= "e":
                        continue
                    sc.add("dve", lambda e, pt_=pt_, atT_=atT_: e.tensor_copy(out=atT_, in_=pt_[:, 128:256]), reads=[pt_r], writes=[atT_r])
                    if KB <= 4:
                        continue
                    (E_, E_r) = nxt(Em, "E"); (Dk_, Dk_r) = nxt(Dk, "Dk")
                    sc.add("pool", lambda e, E_=E_, Bo_=Bo_: e.tensor_tensor(out=E_, in0=ident_f[:], in1=Bo_[:, 0, :], op=ALU.subtract), reads=[Bo_r, r_const], writes=[E_r])
                    sc.add("pool", lambda e, Dk_=Dk_, A_=A_: e.tensor_tensor(out=Dk_, in0=A_, in1=aoff1, op=ALU.mult), reads=[A_r, pc_r], writes=[Dk_r])
                    sc.add("pool", lambda e, Dk_=Dk_: e.tensor_tensor(out=Dk_, in0=ident_f[:], in1=Dk_, op=ALU.subtract), reads=[Dk_r, r_const], writes=[Dk_r])
                    for lvl in range(1, 6):
                        px_, px_r = pbank()
                        sc.pe32(lambda e, px_=px_, Bo_=Bo_, lvl=lvl, Dk_=Dk_: e.matmul(px_[:, 0:128], lhsT=Bo_[:, lvl, :], rhs=Dk_, start=True, stop=True), reads=[Bo_r, Dk_r], writes=[px_r])
                        sc.add("dve", lambda e, px_=px_, X_=X_: e.tensor_copy(out=X_, in_=px_[:, 0:128]), reads=[px_r], writes=[X_r])
                        py_, py_r = pbank()
                        sc.pe32(lambda e, py_=py_, X_=X_, E_=E_: e.matmul(py_[:, 0:128], lhsT=X_, rhs=E_, start=True, stop=True), reads=[X_r, E_r], writes=[py_r])
                        (E2_, E2_r) = nxt(Em, "E")
                        sc.add("dve", lambda e, py_=py_, E_=E_, E2_=E2_: e.tensor_tensor(out=E2_, in0=E_, in1=py_[:, 0:128], op=ALU.subtract), reads=[py_r, E_r], writes=[E2_r])
                        E_, E_r = E2_, E2_r
                        if lvl < 5:
                            pd_, pd_r = pbank()
                            sc.pe32(lambda e, pd_=pd_, E_=E_: e.transpose(out=pd_[:, 0:128], in_=E_, identity=ident_f[:]), reads=[E_r, r_const], writes=[pd_r])
                            (Dk_, Dk_r) = nxt(Dk, "Dk")
                            sc.add("dve", lambda e, pd_=pd_, Dk_=Dk_: e.tensor_copy(out=Dk_, in_=pd_[:, 0:128]), reads=[pd_r], writes=[Dk_r])
                    if KB <= 5:
                        continue
                    if os.environ.get("HB", "0") == "1":
                        sc.barrier()
                    sc.add("pool", lambda e, R_=R_, Vt=Vt, h=h, beta_h=beta_h: e.tensor_scalar(out=R_[:, 0:128], in0=Vt[:, h, :], scalar1=beta_h, scalar2=None, op0=ALU.mult), reads=[Vt_r, sm_r], writes=[R_r])
                    sc.add("pool", lambda e, R_=R_, Kt=Kt, h=h, bk_h=bk_h: e.tensor_scalar(out=R_[:, 128:256], in0=Kt[:, h, :], scalar1=bk_h, scalar2=None, op0=ALU.mult), reads=[Kt_r, sm_r], writes=[R_r], partial=True)
                    sc.add("pool", lambda e, Kd_=Kd_, Kt=Kt, h=h, egu_h=egu_h: e.tensor_scalar(out=Kd_, in0=Kt[:, h, :], scalar1=egu_h, scalar2=None, op0=ALU.mult), reads=[Kt_r, sm_r], writes=[Kd_r])
                    sc.add("pool", lambda e, Qd_=Qd_, Qt=Qt, h=h, egc_h=egc_h: e.tensor_scalar(out=Qd_, in0=Qt[:, h, :], scalar1=egc_h, scalar2=None, op0=ALU.mult), reads=[Qt_r, sm_r], writes=[Qd_r])
                    pu_, pu_r = pbank()
                    sc.pe32(lambda e, pu_=pu_, E_=E_, R_=R_: e.matmul(pu_[:, 0:256], lhsT=E_, rhs=R_, start=True, stop=True), reads=[E_r, R_r], writes=[pu_r])
                    sc.add("dve", lambda e, pu_=pu_, UW_=UW_: e.tensor_copy(out=UW_[:, 0:128], in_=pu_[:, 0:128]), reads=[pu_r], writes=[UW_r])
                    sc.add("dve", lambda e, pu_=pu_, UW_=UW_: e.tensor_scalar(out=UW_[:, 128:256], in0=pu_[:, 128:256], scalar1=-1.0, scalar2=None, op0=ALU.mult), reads=[pu_r], writes=[UW_r], partial=True)
                    pq_, pq_r = pbank()
                    sc.pe32(lambda e, pq_=pq_, Qd_=Qd_: e.matmul(pq_[:, 0:128], lhsT=Qd_, rhs=ident_f[:], start=True, stop=False), reads=[Qd_r, r_const], writes=[pq_r])
                    sc.pe32(lambda e, pq_=pq_, UW_=UW_, atT_=atT_: e.matmul(pq_[:, 0:128], lhsT=UW_[:, 128:256], rhs=atT_, start=False, stop=True), reads=[UW_r, atT_r], writes=[pq_r])
                    sc.add("dve", lambda e, pq_=pq_, QpT_=QpT_: e.tensor_copy(out=QpT_, in_=pq_[:, 0:128]), reads=[pq_r], writes=[QpT_r])
                    for ci in range(2):
                        pm_, pm_r = pbank()
                        ps_ = slice(ci * 64, ci * 64 + 64)
                        sc.pe32(lambda e, pm_=pm_, UW_=UW_, Kd_=Kd_, ps_=ps_: e.matmul(pm_[:, 0:128], lhsT=UW_[ps_, 128:256], rhs=Kd_[ps_, :], start=True, stop=True), reads=[UW_r, Kd_r], writes=[pm_r])
                        if ci == 0:
                            sc.add("dve", lambda e, pm_=pm_, Mp_=Mp_, ci=ci: e.tensor_copy(out=Mp_[:, ci, :], in_=pm_[:, 0:128]), reads=[pm_r], writes=[Mp_r])
                        else:
                            sc.add("dve", lambda e, pm_=pm_, Mp_=Mp_, ci=ci: e.tensor_copy(out=Mp_[:, ci, :], in_=pm_[:, 0:128]), reads=[pm_r], writes=[Mp_r], partial=True)
                    if os.environ.get("HB", "0") == "1":
                        sc.barrier()
                    for ci in range(2 if KB > 6 else 0):
                        ps_ = slice(ci * 64, ci * 64 + 64)
                        Sp, Sp_r = Sst[cur], S_r[cur][h]
                        Sn, Sn_r = Sst[1 - cur], S_r[1 - cur][h]
                        po_, po_r = pbank()
                        tp = (0, ci * 64)
                        sc.pe32(lambda e, po_=po_, QpT_=QpT_, Sp=Sp, h=h, ps_=ps_, tp=tp: e.matmul(po_[ps_, 0:128], lhsT=QpT_[:, ps_], rhs=Sp[:, h, :], start=True, stop=False, tile_position=tp), reads=[QpT_r, Sp_r], writes=[po_r])
                        sc.pe32(lambda e, po_=po_, atT_=atT_, UW_=UW_, ps_=ps_, tp=tp: e.matmul(po_[ps_, 0:128], lhsT=atT_[:, ps_], rhs=UW_[:, 0:128], start=False, stop=True, tile_position=tp), reads=[atT_r, UW_r], writes=[po_r])
                        sc.add("dve", lambda e, po_=po_, osb_t=osb_t, h=h, ps_=ps_: e.tensor_copy(out=osb_t[ps_, h, :], in_=po_[ps_, 0:128]), reads=[po_r], writes=[osb_r], partial=True)
                        sc.add("act", lambda e, po_=po_, oss_t=oss_t, h=h, ps_=ps_: e.activation(out=junk[0][ps_, :], in_=po_[ps_, 0:128], func=AF.Square, bias=cbias[ps_, 3:4], accum_out=oss_t[ps_, h:h + 1]), reads=[po_r], writes=[oss_r, junk[1]], partial=True)
                        pS_, pS_r = pbank()
                        sc.pe32(lambda e, pS_=pS_, Mp_=Mp_, ci=ci, Sp=Sp, h=h: e.matmul(pS_[:, 0:128], lhsT=Mp_[:, ci, :], rhs=Sp[:, h, :], start=True, stop=False), reads=[Mp_r, Sp_r], writes=[pS_r])
                        sc.pe32(lambda e, pS_=pS_, Kd_=Kd_, UW_=UW_, ps_=ps_: e.matmul(pS_[:, 0:128], lhsT=Kd_[ps_, :], rhs=UW_[ps_, 0:128], start=False, stop=True), reads=[Kd_r, UW_r], writes=[pS_r])
                        egl = smt[:, 24 + 4 * ci + h:25 + 4 * ci + h]
                        sc.add("dve", lambda e, pS_=pS_, Sp=Sp, Sn=Sn, h=h, egl=egl: e.scalar_tensor_tensor(out=Sn[:, h, :], in0=Sp[:, h, :], scalar=egl, in1=pS_[:, 0:128], op0=ALU.mult, op1=ALU.add), reads=[pS_r, Sp_r, sm_r], writes=[Sn_r])
                        cur = 1 - cur
                sc.safe = False
                if KB <= 7:
                    continue
                (oab_t, oab_r) = nxt(oab, "oab"); (oaT_t, oaT_r) = nxt(oaT, "oaT")
                sc.add("act", lambda e, oss_t=oss_t: e.activation(out=oss_t[:, 4:8], in_=oss_t[:, 0:4], func=AF.Sqrt, scale=1.0 / 128, bias=cbias[:, 0:1]), reads=[oss_r, r_const], writes=[oss_r])
                sc.add("dve", lambda e, oss_t=oss_t: e.reciprocal(out=oss_t[:, 4:8], in_=oss_t[:, 4:8]), reads=[oss_r], writes=[oss_r])
                sc.add("dve", lambda e, osb_t=osb_t, oss_t=oss_t: e.tensor_tensor(out=osb_t, in0=osb_t, in1=oss_t[:, 4:8].unsqueeze(2).to_broadcast([128, 4, 128]), op=ALU.mult), reads=[osb_r, oss_r], writes=[osb_r])
                sc.add("dve", lambda e, osb_t=osb_t, zst=zst, oab_t=oab_t: e.tensor_tensor(out=oab_t, in0=osb_t.rearrange("p h d -> p (h d)"), in1=zst, op=ALU.mult), reads=[osb_r, zs_r], writes=[oab_r])
                ptb = bank[5][:].bitcast(BF16).rearrange("p (k c) -> p k c", k=8)
                for h in range(4):
                    sc.pe16(ptb[:, h, :], lambda e, h=h, oab_t=oab_t: e.transpose(out=ptb[:, h, :], in_=oab_t[:, h * 128:(h + 1) * 128], identity=ident_b[:]), reads=[oab_r, r_const], writes=[bank_r[5]])
                sc.add("dve", lambda e, oaT_t=oaT_t: e.tensor_copy(out=oaT_t, in_=ptb[:, 0:4, :]), reads=[bank_r[5]], writes=[oaT_r])
                sc.add("sp", lambda e, oaT_t=oaT_t, tsl=tsl: e.dma_start(out=oT_d[0:4, :, tsl].rearrange("j p c -> p j c"), in_=oaT_t), reads=[oaT_r], writes=[], dma=True, key="oaT%d" % (ctr["oaT"] % 2))

    def ln_tile(L_, t, ps_lo, ps_lo_r, ps_hi, ps_hi_r, xr, xr_r, g_bc, b_bc, lnp_r, out_d, write_xT):
        tsl = slice(t * 128, (t + 1) * 128)
        (y, y_r) = L_["y"][t % 2]; (st, st_r) = L_["st"][t % 2]; (xb_, xb_r_) = L_["xb"][t % 2]
        for half, (pp, pp_r) in enumerate(((ps_lo, ps_lo_r), (ps_hi, ps_hi_r))):
            hs = slice(half * 512, (half + 1) * 512)
            sc.add("dve", lambda e, pp=pp, hs=hs, y=y, xr=xr: e.scalar_tensor_tensor(out=y[:, hs], in0=xr[:, hs], scalar=ALPHA, in1=pp[:, 0:512], op0=ALU.mult, op1=ALU.add),
                   reads=[pp_r, xr_r], writes=[y_r], partial=(half == 1))
        for half in range(2):
            hs = slice(half * 512, (half + 1) * 512)
            sc.add("dve", lambda e, half=half, hs=hs, y=y, st=st: e.bn_stats(out=st[:, half * 6:(half + 1) * 6], in_=y[:, hs]), reads=[y_r], writes=[st_r], partial=(half == 1))
        sc.add("dve", lambda e, st=st: e.bn_aggr(out=st[:, 12:14], in_=st[:, 0:12]), reads=[st_r], writes=[st_r])
        sc.add("act", lambda e, st=st: e.activation(out=st[:, 14:15], in_=st[:, 13:14], func=AF.Sqrt, bias=cbias[:, 2:3]), reads=[st_r, r_const], writes=[st_r])
        sc.add("dve", lambda e, st=st: e.reciprocal(out=st[:, 14:15], in_=st[:, 14:15]), reads=[st_r], writes=[st_r])
        sc.add("dve", lambda e, y=y, st=st: e.tensor_scalar(out=y, in0=y, scalar1=st[:, 12:13], scalar2=st[:, 14:15], op0=ALU.subtract, op1=ALU.mult), reads=[y_r, st_r], writes=[y_r])
        sc.add("pool", lambda e, y=y: e.tensor_tensor(out=y, in0=y, in1=g_bc, op=ALU.mult), reads=[y_r, lnp_r], writes=[y_r])
        sc.add("dve", lambda e, y=y: e.tensor_tensor(out=y, in0=y, in1=b_bc, op=ALU.add), reads=[y_r, lnp_r], writes=[y_r])
        sc.add("sp", lambda e, y=y, tsl=tsl: e.dma_start(out=out_d[tsl, :], in_=y), reads=[y_r], writes=[], dma=True, key="ysto%d" % (t % 2))
        if write_xT:
            sc.add("act", lambda e, y=y, xb_=xb_: e.copy(out=xb_, in_=y), reads=[y_r], writes=[xb_r_])
            ptb = bank[7][:].bitcast(BF16).rearrange("p (k c) -> p k c", k=8)
            for kc in range(8):
                sc.pe16(ptb[:, kc, :], lambda e, kc=kc, xb_=xb_: e.transpose(out=ptb[:, kc, :], in_=xb_[:, kc * 128:(kc + 1) * 128], identity=ident_b[:]), reads=[xb_r_, r_const], writes=[bank_r[7]])
            sc.add("dve", lambda e, tsl=tsl: e.tensor_copy(out=xT[:, :, tsl], in_=ptb), reads=[bank_r[7]], writes=[xT_r[t]])

    def ln_bufs(g_d, b_d, l):
        L_ = {}
        L_["y"] = [(cv.get([128, 1024]), sc.res("y%d" % i)) for i in range(2)]
        L_["st"] = [(cv.get([128, 16]), sc.res("st%d" % i)) for i in range(2)]
        L_["xb"] = [(cv.get([128, 1024], BF16), sc.res("xbln%d" % i)) for i in range(2)]
        g_bc = cv.get([128, 1024]); b_bc = cv.get([128, 1024]); lnp_r = sc.res("lnp")
        sc.add("sp", lambda e: e.dma_start(out=g_bc, in_=g_d[l, :].partition_broadcast(128)), writes=[lnp_r], dma=True, key="lnp", partial=True)
        sc.add("sp", lambda e: e.dma_start(out=b_bc, in_=b_d[l, :].partition_broadcast(128)), writes=[lnp_r], dma=True, key="lnp", partial=True)
        return L_, g_bc, b_bc, lnp_r

    def phaseC(l, xin_d):
        cv.reset()
        wO = cv.get([128, 8, 1024], BF16); wO_r = [sc.res("wO%d" % k) for k in range(8)]
        for kc in range(8):
            sc.add("pool", lambda e, kc=kc: e.dma_start(out=wO[:, kc, :], in_=w_out_d[l, kc * 128:(kc + 1) * 128, :]), writes=[wO_r[kc]], dma=True, key="wO%d" % kc)
        L_, g_bc, b_bc, lnp_r = ln_bufs(ln1g_d, ln1b_d, l)
        oTt = [(cv.get([128, 8, 128], BF16), sc.res("oTt%d" % i)) for i in range(3)]
        xrs = [(cv.get([128, 1024]), sc.res("xr%d" % i)) for i in range(3)]
        for t in range(NT):
            tsl = slice(t * 128, (t + 1) * 128)
            (ot, ot_r) = oTt[t % 3]; (xr, xr_r) = xrs[t % 3]
            sc.add("sp", lambda e, ot=ot, tsl=tsl: e.dma_start(out=ot, in_=oT_d[:, :, tsl].rearrange("k p c -> p k c")), writes=[ot_r], dma=True, key="oTt%d" % (t % 3))
            sc.add("sp", lambda e, xr=xr, tsl=tsl: e.dma_start(out=xr, in_=xin_d[tsl, :]), writes=[xr_r], dma=True, key="xr%d" % (t % 3))
            bl, bh = 2 * (t % 2), 2 * (t % 2) + 1
            for half, bi in ((0, bl), (1, bh)):
                for kc in range(8):
                    sc.pe16(bank[bi][:], lambda e, bi=bi, kc=kc, ot=ot, half=half: e.matmul(bank[bi][:], lhsT=ot[:, kc, :], rhs=wO[:, kc, half * 512:(half + 1) * 512], start=(kc == 0), stop=(kc == 7)),
                            reads=[ot_r, wO_r[kc]], writes=[bank_r[bi]])
            ln_tile(L_, t, bank[bl], bank_r[bl], bank[bh], bank_r[bh], xr, xr_r, g_bc, b_bc, lnp_r, x1_d, True)

    def phaseD(l, out_d, write_xT):
        cv.reset()
        NJ = DFF // 128
        NBLK = S // 512
        wD = cv.get([128, NJ, 1024], BF16); wD_r = [sc.res("wD%d" % j) for j in range(NJ)]
        for j in range(NJ):
            sc.add("pool", lambda e, j=j: e.dma_start(out=wD[:, j, :], in_=w_down_d[l, j * 128:(j + 1) * 128, :]), writes=[wD_r[j]], dma=True, key="wD%d" % (j % 4))
        L_, g_bc, b_bc, lnp_r = ln_bufs(ln2g_d, ln2b_d, l)
        fc4 = cv.get([128, 4, 44]); fc_r = sc.res("fc4")
        hT = cv.get([128, NJ, 512], BF16); hT_r = [sc.res("hT%d" % j) for j in range(NJ)]
        wU = [(cv.get([128, 8, 256], BF16), sc.res("wU%d" % i)) for i in range(3)]
        raw = [(cv.get([128, 2, 514]), sc.res("raw%d" % i)) for i in range(2)]
        acc = [(cv.get([128, 2, 512]), sc.res("facc%d" % i)) for i in range(2)]
        halo = cv.get([128, 44, 2]); halo_r = [sc.res("fhalo%d" % j) for j in range(44)]
        xrs = [(cv.get([128, 1024]), sc.res("xrD%d" % i)) for i in range(2)]
        w44 = hT.rearrange("p j t -> p (j t)").bitcast(F32)[0:44, 0:512].rearrange("p (a b) -> p a b", a=4)
        w44_r = sc.res("w44")
        for j3 in range(3):
            sc.add("sp", lambda e, j3=j3: e.dma_start(out=w44[:, j3, :], in_=fconvw_d[l, j3, :].rearrange("(f p) -> f p", p=128)), writes=[w44_r] + hT_r[0:2], dma=True, key="w44", partial=True)
        sc.add("sp", lambda e: e.dma_start(out=w44[:, 3, :], in_=fconvb_d[l, :].rearrange("(f p) -> f p", p=128)), writes=[w44_r], dma=True, key="w44", partial=True)
        for a4 in range(4):
            sc.pe32(lambda e, a4=a4: e.transpose(out=bank[0][:, a4 * 44:(a4 + 1) * 44], in_=w44[:, a4, :], identity=ident_f[0:44, 0:44]), reads=[w44_r, r_const], writes=[bank_r[0]])
        sc.add("dve", lambda e: e.tensor_copy(out=fc4.rearrange("p a f -> p (a f)"), in_=bank[0][:, 0:176]), reads=[bank_r[0]], writes=[fc_r])
        sc.add("pool", lambda e: e.memset(halo, 0.0), reads=[], writes=halo_r)
        sc.barrier()
        for c in range(NBLK):
            csl = slice(c * 512, (c + 1) * 512)
            for j in range(NJ):
                (wu, wu_r) = wU[j % 3]
                sc.add("pool", lambda e, wu=wu, j=j: e.dma_start(out=wu[:, :, 0:128], in_=w_up_d[l, :, j * 128:(j + 1) * 128].rearrange("(k p) f -> p k f", p=128)), writes=[wu_r], dma=True, key="wUa%d" % (j % 3))
                sc.add("pool", lambda e, wu=wu, j=j: e.dma_start(out=wu[:, :, 128:256], in_=w_up_d[l, :, DFF + j * 128:DFF + (j + 1) * 128].rearrange("(k p) f -> p k f", p=128)), writes=[wu_r], dma=True, key="wUb%d" % (j % 3), partial=True)
                (rw, rw_r) = raw[j % 2]; (ac, ac_r) = acc[j % 2]
                for gv in range(2):
                    f = gv * 22 + j
                    bi = 2 * (j % 2) + gv
                    for kc in range(8):
                        sc.pe16(bank[bi][:], lambda e, bi=bi, kc=kc, wu=wu, gv=gv, csl=csl: e.matmul(bank[bi][:], lhsT=wu[:, kc, gv * 128:(gv + 1) * 128], rhs=xT[:, kc, csl], start=(kc == 0), stop=(kc == 7)),
                                reads=[wu_r] + xT_r[4 * c:4 * c + 4], writes=[bank_r[bi]])
                    sc.add("pool", lambda e, rw=rw, gv=gv, f=f: e.tensor_copy(out=rw[:, gv, 0:2], in_=halo[:, f, :]), reads=[halo_r[f]], writes=[rw_r], partial=(gv == 1))
                    sc.add("act", lambda e, rw=rw, gv=gv, bi=bi: e.copy(out=rw[:, gv, 2:514], in_=bank[bi][:]), reads=[bank_r[bi]], writes=[rw_r], partial=True)
                    sc.add("pool", lambda e, rw=rw, gv=gv, f=f: e.tensor_copy(out=halo[:, f, :], in_=rw[:, gv, 512:514]), reads=[rw_r], writes=[halo_r[f]])
                    sc.add("act", lambda e, ac=ac, gv=gv, bi=bi, f=f: e.activation(out=ac[:, gv, :], in_=bank[bi][:], func=AF.Identity, scale=fc4[:, 2, f:f + 1], bias=fc4[:, 3, f:f + 1]), reads=[bank_r[bi], fc_r], writes=[ac_r], partial=(gv == 1))
                    eng = "dve"
                    for tap in (1, 0):
                        sc.add(eng, lambda e, ac=ac, rw=rw, gv=gv, tap=tap, f=f: e.scalar_tensor_tensor(out=ac[:, gv, :], in0=rw[:, gv, tap:tap + 512], scalar=fc4[:, tap, f:f + 1], in1=ac[:, gv, :], op0=ALU.mult, op1=ALU.add),
                               reads=[rw_r, fc_r, ac_r], writes=[ac_r])
                sc.add("act", lambda e, ac=ac: e.activation(out=ac[:, 0, :], in_=ac[:, 0, :], func=AF.Silu, bias=cbias[:, 3:4]), reads=[ac_r, r_const], writes=[ac_r])
                sc.add("dve", lambda e, ac=ac, j=j: e.tensor_tensor(out=hT[:, j, :], in0=ac[:, 0, :], in1=ac[:, 1, :], op=ALU.mult), reads=[ac_r], writes=[hT_r[j]])
            for tt in range(4):
                t = c * 4 + tt
                tsl = slice(t * 128, (t + 1) * 128)
                (xr, xr_r) = xrs[t % 2]
                sc.add("sp", lambda e, xr=xr, tsl=tsl: e.dma_start(out=xr, in_=x1_d[tsl, :]), writes=[xr_r], dma=True, key="xrD%d" % (t % 2))
                bl, bh = 4 + 2 * (t % 2), 5 + 2 * (t % 2)
                if bh == 7 and write_xT:
                    bl, bh = 4, 5
                for half, bi in ((0, bl), (1, bh)):
                    for j in range(NJ):
                        sc.pe16(bank[bi][:], lambda e, bi=bi, j=j, tt=tt, half=half: e.matmul(bank[bi][:], lhsT=hT[:, j, tt * 128:(tt + 1) * 128], rhs=wD[:, j, half * 512:(half + 1) * 512], start=(j == 0), stop=(j == NJ - 1)),
                                reads=[hT_r[j], wD_r[j]], writes=[bank_r[bi]])
                ln_tile(L_, t, bank[bl], bank_r[bl], bank[bh], bank_r[bh], xr, xr_r, g_bc, b_bc, lnp_r, out_d, write_xT)

    phase0(x_d)
    sc.barrier()
    if stop_after == "0":
        sc.add("sp", lambda e: e.dma_start(out=oT_d[:, :, :].rearrange("k p s -> p k s"), in_=xT[:]), reads=xT_r, writes=[], dma=True, key="dbg")
    elif stop_after == "A":
        phaseA(0)
    elif stop_after == "B":
        phaseB(0)
    else:
        for l in range(L):
            xin = x_d if l == 0 else x2_d
            last = (l == L - 1)
            phaseA(l)
            sc.barrier()
            phaseB(l)
            sc.barrier()
            phaseC(l, xin)
            sc.barrier()
            if stop_after == "C" and l == 0:
                break
            phaseD(l, y_d if last else x2_d, not last)
            sc.barrier()
            if stop_after == "D" and l == 0:
                break
    if dbg and os.environ.get("DUMPARENA") and not os.environ.get("SIM"):
        sc.barrier()
        dbg_arena = nc.dram_tensor("dbg_arena", [128, ARENA], F32, kind="ExternalOutput").ap()
        for q in range(4):
            sc.add("sp", lambda e, q=q: e.dma_start(out=dbg_arena[:, q * (ARENA // 4):(q + 1) * (ARENA // 4)], in_=arena[:, q * (ARENA // 4):(q + 1) * (ARENA // 4)]), dma=True, key="dbga")
    sc.emit(nc, es)
    es.close()
    return nc


_CACHE = {}


def kernel(**inputs):
    x = np.asarray(inputs["x"], dtype=np.float32)
    B, S, _ = x.shape
    L = int(np.asarray(inputs["w_in"]).shape[0])
    key = (S, L)
    if key not in _CACHE:
        _CACHE[key] = (build(S=S, L=L), make_consts(S))
    nc, consts = _CACHE[key]
    shared = {k: np.ascontiguousarray(np.asarray(v, dtype=np.float32)) for k, v in inputs.items() if k != "x"}
    for k, v in consts.items():
        shared["c_" + k] = v
    in_maps = []
    for b in range(B):
        m = dict(shared)
        m["x"] = np.ascontiguousarray(x[b])
        in_maps.append(m)
    res = run_bass_kernel_spmd(nc, in_maps, core_ids=list(range(B)))
    return np.stack([np.asarray(r["y"], dtype=np.float32) for r in res.results], axis=0)
```

```python
import os
import numpy as np
import ml_dtypes
from contextlib import ExitStack
import concourse.bass as bass
import concourse.mybir as mybir
from concourse.bass_utils import run_bass_kernel_spmd

F32 = mybir.dt.float32
BF16 = mybir.dt.bfloat16
AF = mybir.ActivationFunctionType
ALU = mybir.AluOpType
AX = mybir.AxisListType

D = 1024
NIN = 3592
DFF = 2816
ALPHA = float((2 * 2) ** 0.25)
NEG = -30000.0


class Res:
    __slots__ = ("name", "writers", "readers")

    def __init__(self, name):
        self.name = name
        self.writers = []
        self.readers = {}


class Op:
    __slots__ = ("eng", "fn", "dma", "key", "value", "deps", "signal", "barrier")

    def __init__(self, eng, fn, dma=False, key=None):
        self.eng = eng
        self.fn = fn
        self.dma = dma
        self.key = key
        self.value = None
        self.deps = []
        self.signal = False
        self.barrier = False


ENGS = ("pe", "act", "dve", "pool", "sp")


class Sched:
    def __init__(self):
        self.ops = {e: [] for e in ENGS}
        self.keycount = {}
        self.keylast = {}
        self.allres = []
        self.last_pe_f32 = False
        self.ident_b = None
        self.safe = False
        self.safecnt = 0
        self.safek = int(os.environ.get("SAFEK", "0"))
        self.safeeng = tuple(x for x in os.environ.get("SAFEENG", "act").split(",") if x)

    def res(self, name):
        r = Res(name)
        self.allres.append(r)
        return r

    def _dep(self, op, prod, raw):
        if prod is op:
            return
        if (not prod.dma) and (not op.dma) and prod.eng == op.eng:
            if op.eng == "pe":
                return
        op.deps.append(prod)
        prod.signal = True

    def add(self, eng, fn, reads=(), writes=(), dma=False, key=None, partial=False, f32=False, out=None):
        if eng == "pe":
            if (not f32) and self.last_pe_f32 and out is not None:
                fn0 = fn
                dmy = out.bitcast(F32) if out.dtype != F32 else out
                idb = self.ident_b

                def fn(e, fn0=fn0, dmy=dmy, idb=idb):
                    e.matmul(dmy[0:64, 0:8], lhsT=idb[:, 0:64], rhs=idb[:, 0:8], start=True, stop=True)
                    return fn0(e)
            self.last_pe_f32 = f32
        excl = self.safe and (not dma) and (eng in self.safeeng)
        if excl:
            self.barrier()
        op = Op(eng, fn, dma, key)
        for r in reads:
            for w in r.writers:
                self._dep(op, w, True)
        for r in writes:
            for w in r.writers:
                if not (partial and w.dma and op.dma):
                    self._dep(op, w, False)
            for rd in r.readers.values():
                if isinstance(rd, list):
                    for x in rd:
                        self._dep(op, x, False)
                else:
                    self._dep(op, rd, False)
        for r in reads:
            if dma:
                r.readers.setdefault("dma", []).append(op)
            else:
                r.readers[eng] = op
        for r in writes:
            if partial:
                r.writers = r.writers + [op]
            else:
                r.writers = [op]
            r.readers = {}
        if dma:
            assert key is not None
            self.keycount[key] = self.keycount.get(key, 0) + 16
            op.value = self.keycount[key]
            self.keylast[key] = op
        self.ops[eng].append(op)
        if excl:
            self.barrier()
        elif self.safe and not dma and self.safek > 0:
            self.safecnt += 1
            if self.safecnt % self.safek == 0:
                self.barrier()
        return op

    def pe32(self, fn, **kw):
        return self.add("pe", fn, f32=True, **kw)

    def pe16(self, out, fn, **kw):
        return self.add("pe", fn, out=out, **kw)

    def barrier(self):
        prods = []
        for e in ENGS:
            for o in reversed(self.ops[e]):
                if not o.dma and not o.barrier:
                    prods.append(o)
                    break
        prods += list(self.keylast.values())
        for e in ENGS:
            b = Op(e, None)
            b.barrier = True
            for p in prods:
                if p.dma or p.eng != e or e != "pe":
                    b.deps.append(p)
                    p.signal = True
            self.ops[e].append(b)
        for r in self.allres:
            r.writers = []
            r.readers = {}

    def emit(self, nc, es):
        esem = {e: es.enter_context(nc.semaphore("s_" + e)) for e in ENGS}
        ksem = {}
        for i, k in enumerate(self.keycount):
            ksem[k] = es.enter_context(nc.semaphore("k%d" % i))
        for e in ENGS:
            c = 0
            for o in self.ops[e]:
                if (not o.dma) and o.signal and not o.barrier:
                    c += 1
                    o.value = c
            if os.environ.get("SEMDBG"): print("SEM", e, "final", c, "nops", len(self.ops[e]))
        block = es.enter_context(nc.Block())
        hooks = {"pe": block.tensor, "act": block.scalar, "dve": block.vector,
                 "pool": block.gpsimd, "sp": block.sync}
        final_keys = dict(self.keycount)

        def mk(ename):
            def body(eng):
                waited = {}
                for o in self.ops[ename]:
                    need = {}
                    for p in o.deps:
                        s = ksem[p.key] if p.dma else esem[p.eng]
                        sid = id(s)
                        v = p.value
                        if waited.get(sid, 0) >= v:
                            continue
                        if sid not in need or need[sid][1] < v:
                            need[sid] = (s, v)
                    for sid, (s, v) in need.items():
                        eng.wait_ge(s, v)
                        waited[sid] = v
                    if o.fn is None:
                        continue
                    ins = o.fn(eng)
                    if o.dma:
                        ins.then_inc(ksem[o.key], 16)
                    elif o.signal:
                        ins.then_inc(esem[ename], 1)
                if ename == "sp":
                    for k, v in final_keys.items():
                        if waited.get(id(ksem[k]), 0) < v:
                            eng.wait_ge(ksem[k], v)
            return body

        for e in ENGS:
            hooks[e](mk(e))


def make_consts(S):
    i = np.arange(128)[:, None]
    j = np.arange(128)[None, :]
    same = (i // 64) == (j // 64)
    c = {}
    c["ident"] = np.eye(128, dtype=np.float32)
    c["caus01"] = (i <= j).astype(np.float32)
    c["mstrict"] = (same & (j < i)).astype(np.float32)
    c["negincl"] = (same & (j <= i)).astype(np.float32)
    LT = (same & (j <= i)).T.astype(np.float32)
    UT = (same & (j > i)).T.astype(np.float32)
    CS0 = np.zeros((128, 128), np.float32); CS0[:64, :] = 1.0
    CS1 = np.zeros((128, 128), np.float32); CS1[64:, :] = 1.0
    c["gl"] = np.concatenate([LT, UT, CS0, CS1], 1)
    offs = []
    for s in (1, 2, 4, 8, 16, 32):
        m = ((i // (2 * s)) == (j // (2 * s))) & ((i // s) != (j // s)) & (i > j)
        offs.append(m.T.astype(np.float32))
    c["boff"] = np.concatenate(offs, 1)
    c["aoff1"] = (((i // 2) == (j // 2)) & (i != j) & (i > j)).astype(np.float32)
    half = 8
    inv = 500000.0 ** (-np.arange(half, dtype=np.float32) / half)
    ang = np.arange(S, dtype=np.float32)[:, None] * inv[None, :]
    cos = np.cos(ang).astype(np.float32)
    sin = np.sin(ang).astype(np.float32)
    NT = S // 128
    cc = np.concatenate([cos, cos], 1).reshape(NT, 128, 16).transpose(1, 0, 2)
    ss = np.concatenate([sin, sin], 1).reshape(NT, 128, 16).transpose(1, 0, 2)
    c["rope"] = np.ascontiguousarray(np.concatenate([cc, ss], 2)).reshape(128, NT * 32)
    return c


def build(S=4096, L=2, dbg=False, stop_after=None):
    NT = S // 128
    NB = S // 256
    nc = bass.Bass("TRN2", target_bir_lowering=False)
    sc = Sched()
    es = ExitStack()

    def din(name, shape, dt=F32):
        return nc.dram_tensor(name, list(shape), dt, kind="ExternalInput").ap()

    def dscr(name, shape, dt=F32, out=False):
        kind = "ExternalOutput" if (out or dbg) else "Internal"
        return nc.dram_tensor(name, list(shape), dt, kind=kind).ap()

    x_d = din("x", [S, D])
    w_in_d = din("w_in", [L, D, NIN])
    gconv_d = din("gdn_conv_w", [L, 4, 1536])
    alog_d = din("gdn_a_log", [L, 4])
    dtb_d = din("gdn_dt_bias", [L, 4])
    gng_d = din("gdn_norm_g", [L, 128])
    w_out_d = din("w_out", [L, D, D])
    ln1g_d = din("ln1_g", [L, D])
    ln1b_d = din("ln1_b", [L, D])
    w_up_d = din("w_up", [L, D, 2 * DFF])
    fconvw_d = din("ffn_conv_w", [L, 3, 2 * DFF])
    fconvb_d = din("ffn_conv_b", [L, 2 * DFF])
    w_down_d = din("w_down", [L, DFF, D])
    ln2g_d = din("ln2_g", [L, D])
    ln2b_d = din("ln2_b", [L, D])
    c_ident_d = din("c_ident", [128, 128])
    c_caus_d = din("c_caus01", [128, 128])
    c_mstrict_d = din("c_mstrict", [128, 128])
    c_negincl_d = din("c_negincl", [128, 128])
    c_gl_d = din("c_gl", [128, 512])
    c_boff_d = din("c_boff", [128, 768])
    c_aoff1_d = din("c_aoff1", [128, 128])
    c_rope_d = din("c_rope", [128, NT * 32])

    y_d = dscr("y", [S, D], out=True)
    x1_d = dscr("x1res", [S, D])
    x2_d = dscr("x2res", [S, D]) if L > 1 else None
    oT_d = dscr("oT", [8, 128, S], BF16)
    wupbf_d = [nc.dram_tensor("wupbf%d" % l_, [44, 128, 8, 128], BF16, kind="Internal").ap() for l_ in range(L)]
    wupbf_r = [sc.res("wupbf%d" % l_) for l_ in range(L)]

    def sb(name, shape, dt=F32):
        return es.enter_context(nc.sbuf_tensor(name, list(shape), dt))

    def ps(name, shape, dt=F32):
        return es.enter_context(nc.psum_tensor(name, list(shape), dt))

    xT = sb("xT", [128, 8, S], BF16)
    xT_r = [sc.res("xT%d" % t) for t in range(NT)]
    ident_f = sb("ident_f", [128, 128]); ident_b = sb("ident_b", [128, 128], BF16)
    caus_b = sb("caus_b", [128, 128], BF16)
    rope = sb("rope", [128, NT, 32])
    r_const = sc.res("consts")
    sc.ident_b = ident_b
    cbias = sb("cbias", [128, 4])
    sc.add("dve", lambda e: e.memset(cbias[:, 0:1], 1e-6), writes=[r_const], partial=True)
    sc.add("dve", lambda e: e.memset(cbias[:, 1:2], 1.0), writes=[r_const], partial=True)
    sc.add("dve", lambda e: e.memset(cbias[:, 2:3], 1e-5), writes=[r_const], partial=True)
    sc.add("dve", lambda e: e.memset(cbias[:, 3:4], 0.0), writes=[r_const], partial=True)

    bank = [ps("bank%d" % i, [128, 512]) for i in range(8)]
    bank_r = [sc.res("bank%d" % i) for i in range(8)]

    ARENA = 136 * 1024 // 4
    arena = sb("arena", [128, ARENA])

    class Carver:
        def __init__(self):
            self.off = 0

        def reset(self):
            self.off = 0

        def get(self, shape, dt=F32):
            n = int(np.prod(shape[1:]))
            nwords = n if dt == F32 else (n + 1) // 2
            a = arena[0:shape[0], self.off:self.off + nwords]
            self.off += (nwords + 15) // 16 * 16
            assert self.off <= ARENA, "arena overflow %d" % self.off
            if dt != F32:
                a = a.bitcast(dt)[:, 0:n]
            if len(shape) > 2:
                names = " ".join("d%d" % k for k in range(len(shape) - 1))
                kw = {"d%d" % k: shape[k + 1] for k in range(len(shape) - 2)}
                a = a.rearrange("p (%s) -> p %s" % (names, names), **kw)
            return a

    cv = Carver()

    sc.add("sp", lambda e: e.dma_start(out=ident_f[:], in_=c_ident_d[:, :]), writes=[r_const], dma=True, key="c0", partial=True)
    sc.add("pool", lambda e: e.dma_start(out=ident_b[:], in_=c_ident_d[:, :]), writes=[r_const], dma=True, key="c1", partial=True)
    sc.add("pool", lambda e: e.dma_start(out=caus_b[:], in_=c_caus_d[:, :]), writes=[r_const], dma=True, key="c1", partial=True)
    sc.add("sp", lambda e: e.dma_start(out=rope[:].rearrange("p t c -> p (t c)"), in_=c_rope_d[:, :]), writes=[r_const], dma=True, key="c0", partial=True)

    def phase0(src_d):
        cv.reset()
        xb = [cv.get([128, 1024], BF16) for _ in range(3)]
        xb_r = [sc.res("xb%d" % i) for i in range(3)]
        pst = [bank[0][:].bitcast(BF16), bank[1][:].bitcast(BF16)]
        for t in range(NT):
            s = t % 3
            sc.add("pool", lambda e, t=t, s=s: e.dma_start(out=xb[s], in_=src_d[t * 128:(t + 1) * 128, :]),
                   writes=[xb_r[s]], dma=True, key="xb%d" % s)
            p = t % 2
            pt = pst[p].rearrange("p (k c) -> p k c", k=8)
            for kc in range(8):
                sc.add("pe", lambda e, kc=kc, s=s, pt=pt: e.transpose(out=pt[:, kc, :], in_=xb[s][:, kc * 128:(kc + 1) * 128], identity=ident_b[:]),
                       reads=[xb_r[s], r_const], writes=[bank_r[p]])
            if t % 2 == 0:
                sc.add("act", lambda e, t=t, pt=pt: e.copy(out=xT[:, :, t * 128:(t + 1) * 128], in_=pt),
                       reads=[bank_r[p]], writes=[xT_r[t]])
            else:
                sc.add("dve", lambda e, t=t, pt=pt: e.tensor_copy(out=xT[:, :, t * 128:(t + 1) * 128], in_=pt),
                       reads=[bank_r[p]], writes=[xT_r[t]])

    def phaseA(l):
        cv.reset()
        wA = cv.get([128, 8, 1536], BF16)
        wA_r = [sc.res("wA%d" % k) for k in range(8)]
        KT = cv.get([128, 4, S], BF16)
        KT_r = [sc.res("KT%d" % t) for t in range(NT)]
        Vp = cv.get([128, NT, 8, 65], BF16)
        Vp_r = [sc.res("Vp%d" % t) for t in range(NT)]
        QT = [cv.get([128, 4, 256], BF16) for _ in range(2)]
        QT_r = [sc.res("QT%d" % i) for i in range(2)]
        kmT = cv.get([128, 4, 16], BF16)
        kmf = cv.get([128, 4])
        kmT_r = sc.res("kmT")
        qb = [cv.get([128, 512], BF16) for _ in range(2)]
        kb = [cv.get([128, 512], BF16) for _ in range(2)]
        qb_r = [sc.res("qb%d" % i) for i in range(2)]
        kb_r = [sc.res("kb%d" % i) for i in range(2)]
        t1 = cv.get([128, 8, 16]); t2 = cv.get([128, 8, 16])
        t1_r = sc.res("t1"); t2_r = sc.res("t2")
        gsb = cv.get([128, 16, 16]); m8 = cv.get([128, 16, 8]); sel = cv.get([128, 16, 16])
        gsb_r = sc.res("gsb"); sel_r = sc.res("sel")
        NPT = 4
        PT = [cv.get([128, 2, 256], BF16) for _ in range(NPT)]
        PT_r = [sc.res("PT%d" % i) for i in range(NPT)]
        acc = cv.get([128, 2, 8, 65])
        acc_r = [[sc.res("acc%d_%d" % (q, h)) for h in range(8)] for q in range(2)]
        rec = cv.get([128, 16])
        ob = cv.get([128, 2, 512], BF16)
        ob_r = sc.res("ob")
        obT = [cv.get([128, 4, 256], BF16) for _ in range(2)]
        obT_r = [sc.res("obT%d" % i) for i in range(2)]

        for kc in range(8):
            sc.add("pool", lambda e, kc=kc: e.dma_start(out=wA[:, kc, :], in_=w_in_d[l, kc * 128:(kc + 1) * 128, 2056:3592]),
                   writes=[wA_r[kc]], dma=True, key="wA%d" % kc)
        sc.add("pool", lambda e: e.memset(Vp[:, :, :, 64:65], 1.0), writes=Vp_r)
        sc.add("pool", lambda e: e.memset(gsb[:], -1e30), writes=[gsb_r])
        sc.add("pool", lambda e: e.memset(kmT[:], 0.0), writes=[kmT_r])

        pq, pk, pv = bank[0], bank[1], bank[2]
        ptr = bank[3][:].bitcast(BF16).rearrange("p (k c) -> p k c", k=8)
        pg0 = bank[4][:, 0:128].rearrange("p (a b) -> p a b", a=8)
        pg1 = bank[6][:, 0:128].rearrange("p (a b) -> p a b", a=8)
        SB = (0, 1, 2, 5)
        OB = (6, 7)
        cnt = {"s": 0, "o": 0, "pt": 0, "ev": 0}


        KSTOP = int(os.environ.get("KSTOP", "99"))
        for t in range(NT):
            b = t // 2
            qt_ = t % 2
            tsl = slice(t * 128, (t + 1) * 128)
            if KSTOP <= 0:
                break
            for g, pp in enumerate((pq, pk, pv)):
                for kc in range(8):
                    sc.add("pe", lambda e, g=g, kc=kc, pp=pp, tsl=tsl: e.matmul(pp[:], lhsT=xT[:, kc, tsl], rhs=wA[:, kc, g * 512:(g + 1) * 512], start=(kc == 0), stop=(kc == 7)),
                           reads=[xT_r[t], wA_r[kc]], writes=[bank_r[g]])
            if KSTOP <= 1:
                continue
            sc.add("act", lambda e, t=t: e.copy(out=Vp[:, t, :, 0:64], in_=pv[:].rearrange("p (h d) -> p h d", h=8)),
                   reads=[bank_r[2]], writes=[Vp_r[t]])
            s2 = t % 2
            for (pp, dst, dst_r, bi) in ((pq, qb[s2], qb_r[s2], 0), (pk, kb[s2], kb_r[s2], 1)):
                p3 = pp[:].rearrange("p (h d) -> p h d", h=8)
                d3 = dst.rearrange("p (h d) -> p h d", h=8)
                sc.add("act", lambda e, p3=p3, d3=d3: e.copy(out=d3[:, :, 16:64], in_=p3[:, :, 16:64]),
                       reads=[bank_r[bi]], writes=[dst_r])
                ccb = rope[:, t, 0:16].unsqueeze(1).to_broadcast([128, 8, 16])
                ssb = rope[:, t, 16:32].unsqueeze(1).to_broadcast([128, 8, 16])
                sc.add("dve", lambda e, p3=p3, ccb=ccb: e.tensor_tensor(out=t1, in0=p3[:, :, 0:16], in1=ccb, op=ALU.mult),
                       reads=[bank_r[bi], r_const], writes=[t1_r])
                sc.add("dve", lambda e, p3=p3, ssb=ssb: e.tensor_tensor(out=t2, in0=p3[:, :, 0:16], in1=ssb, op=ALU.mult),
                       reads=[bank_r[bi], r_const], writes=[t2_r])
                sc.add("dve", lambda e, d3=d3: e.tensor_tensor(out=d3[:, :, 0:8], in0=t1[:, :, 0:8], in1=t2[:, :, 8:16], op=ALU.subtract),
                       reads=[t1_r, t2_r], writes=[dst_r], partial=True)
                sc.add("dve", lambda e, d3=d3: e.tensor_tensor(out=d3[:, :, 8:16], in0=t1[:, :, 8:16], in1=t2[:, :, 0:8], op=ALU.add),
                       reads=[t1_r, t2_r], writes=[dst_r], partial=True)
            if KSTOP <= 2:
                continue
            for j in range(4):
                sc.add("pe", lambda e, j=j, s2=s2: e.transpose(out=ptr[:, j, :], in_=qb[s2][:, j * 128:(j + 1) * 128], identity=ident_b[:]),
                       reads=[qb_r[s2], r_const], writes=[bank_r[3]])
            for j in range(4):
                sc.add("pe", lambda e, j=j, s2=s2: e.transpose(out=ptr[:, 4 + j, :], in_=kb[s2][:, j * 128:(j + 1) * 128], identity=ident_b[:]),
                       reads=[kb_r[s2], r_const], writes=[bank_r[3]])
            qs = b % 2
            if KSTOP == 3 and os.environ.get("KSUB") == "a":
                continue
            sc.add("dve", lambda e, qs=qs, qt_=qt_: e.tensor_copy(out=QT[qs][:, :, qt_ * 128:(qt_ + 1) * 128], in_=ptr[:, 0:4, :]),
                   reads=[bank_r[3]], writes=[QT_r[qs]], partial=(qt_ == 1))
            if KSTOP == 3 and os.environ.get("KSUB") == "b":
                continue
            sc.add("dve", lambda e, tsl=tsl: e.tensor_copy(out=KT[:, :, tsl], in_=ptr[:, 4:8, :]),
                   reads=[bank_r[3]], writes=[KT_r[t]])
            if qt_ == 0 or KSTOP <= 3:
                continue
            if b + 1 < NB:
                sc.add("dve", lambda e, b=b: e.tensor_reduce(out=kmf, in_=KT[:, :, b * 256:(b + 1) * 256], axis=AX.X, op=ALU.add),
                       reads=[KT_r[t - 1], KT_r[t]], writes=[kmT_r])
                sc.add("dve", lambda e, b=b: e.tensor_scalar(out=kmT[:, :, b], in0=kmf, scalar1=1.0 / 256, scalar2=None, op0=ALU.mult),
                       reads=[kmT_r], writes=[kmT_r], partial=True)
            topk = b > 3
            if topk:
                KV_ = os.environ.get("KVAR", "")
                for par in range(2):
                    pgp = (pg0, pg1)[par]
                    for q2 in range(2):
                        for hh in range(4):
                            if KV_ == "q0" and q2 == 1: continue
                            if KV_ == "p0" and par == 1: continue
                            if KV_ == "h0" and hh > 0: continue
                            base = par * 64
                            sc.add("pe", lambda e, pgp=pgp, q2=q2, hh=hh, base=base, qs=qs: e.matmul(pgp[:, q2 * 4 + hh, :], lhsT=QT[qs][base:base + 64, hh, q2 * 128:(q2 + 1) * 128], rhs=kmT[base:base + 64, hh, :], start=True, stop=True),
                                   reads=[QT_r[qs], kmT_r], writes=[bank_r[(4, 6)[par]]])
                KT_ = os.environ.get("KTOPK", "full")
                if KT_ in ("gc", "gcm", "full"):
                    sc.add("dve", lambda e, b=b: e.tensor_copy(out=gsb[:, 0:8, 0:b], in_=pg0[:, :, 0:b]), reads=[bank_r[4]], writes=[gsb_r])
                    sc.add("dve", lambda e, b=b: e.tensor_copy(out=gsb[:, 8:16, 0:b], in_=pg1[:, :, 0:b]), reads=[bank_r[6]], writes=[gsb_r], partial=True)
                if KT_ in ("gcm", "full"):
                    for i16 in range(16):
                        sc.add("dve", lambda e, i16=i16: e.max(out=m8[:, i16, :], in_=gsb[:, i16, :]), reads=[gsb_r], writes=[sel_r], partial=True)
                if KT_ == "full":
                    sc.add("dve", lambda e: e.tensor_tensor(out=sel[:], in0=gsb[:], in1=m8[:, :, 2:3].to_broadcast([128, 16, 16]), op=ALU.is_ge),
                           reads=[gsb_r, sel_r], writes=[sel_r])
                else:
                    sc.add("dve", lambda e: e.memset(sel[:], 1.0), reads=[gsb_r, bank_r[4]], writes=[sel_r])
            for h in range(8 if KSTOP > 4 else 0):
                j = h // 2; base = (h % 2) * 64
                for n in [b] + list(range(b)):
                    si = SB[cnt["s"] % 4]; cnt["s"] += 1
                    pi = cnt["pt"] % NPT; cnt["pt"] += 1
                    oi = OB[cnt["o"] % 2]; cnt["o"] += 1
                    pss = bank[si][:].rearrange("p (k q) -> p k q", k=2)
                    pso = bank[oi][:, 0:130].rearrange("p (q d) -> p q d", q=2)
                    for kt in range(2):
                        ktile = 2 * n + kt
                        sc.add("pe", lambda e, pss=pss, kt=kt, j=j, base=base, ktile=ktile, qs=qs: e.matmul(pss[:, kt, :], lhsT=KT[base:base + 64, j, ktile * 128:(ktile + 1) * 128], rhs=QT[qs][base:base + 64, j, :], start=True, stop=True),
                               reads=[KT_r[ktile], QT_r[qs]], writes=[bank_r[si]])
                    sc.add("act", lambda e, pss=pss, pi=pi: e.activation(out=PT[pi][:], in_=pss, func=AF.Exp, scale=0.125, bias=cbias[:, 3:4]),
                           reads=[bank_r[si]], writes=[PT_r[pi]])
                    if n == b:
                        for kt in range(2):
                            sc.add("pool", lambda e, pi=pi, kt=kt: e.tensor_tensor(out=PT[pi][:, kt, kt * 128:(kt + 1) * 128], in0=PT[pi][:, kt, kt * 128:(kt + 1) * 128], in1=caus_b[:], op=ALU.mult),
                                   reads=[PT_r[pi], r_const], writes=[PT_r[pi]])
                    for q2 in range(2):
                        kts = [0] if (n == b and q2 == 0) else [0, 1]
                        for ii, kt in enumerate(kts):
                            sc.add("pe", lambda e, pso=pso, pi=pi, q2=q2, kt=kt, n=n, h=h, ii=ii, last=(ii == len(kts) - 1): e.matmul(pso[:, q2, :], lhsT=PT[pi][:, kt, q2 * 128:(q2 + 1) * 128], rhs=Vp[:, 2 * n + kt, h, :], start=(ii == 0), stop=last),
                                   reads=[PT_r[pi], Vp_r[2 * n + kt]], writes=[bank_r[oi]])
                    for q2 in range(2):
                        if n == b:
                            sc.add("dve", lambda e, pso=pso, q2=q2, h=h: e.tensor_copy(out=acc[:, q2, h, :], in_=pso[:, q2, :]),
                                   reads=[bank_r[oi]], writes=[acc_r[q2][h]])
                        elif topk:
                            sc.add("dve", lambda e, pso=pso, q2=q2, h=h, n=n: e.scalar_tensor_tensor(out=acc[:, q2, h, :], in0=pso[:, q2, :], scalar=sel[:, (h % 2) * 8 + q2 * 4 + h // 2, n:n + 1], in1=acc[:, q2, h, :], op0=ALU.mult, op1=ALU.add),
                                   reads=[bank_r[oi], sel_r, acc_r[q2][h]], writes=[acc_r[q2][h]])
                        else:
                            sc.add("dve", lambda e, pso=pso, q2=q2, h=h: e.tensor_tensor(out=acc[:, q2, h, :], in0=pso[:, q2, :], in1=acc[:, q2, h, :], op=ALU.add),
                                   reads=[bank_r[oi], acc_r[q2][h]], writes=[acc_r[q2][h]])
            if KSTOP <= 5:
                continue
            allacc = [acc_r[q][h] for q in range(2) for h in range(8)]
            sc.add("dve", lambda e: e.reciprocal(out=rec, in_=acc[:].rearrange("p q h d -> p (q h) d")[:, :, 64]), reads=allacc, writes=[ob_r])
            sc.add("dve", lambda e: e.tensor_tensor(out=ob[:].rearrange("p q (h d) -> p (q h) d", h=8), in0=acc[:].rearrange("p q h d -> p (q h) d")[:, :, 0:64], in1=rec.unsqueeze(2).to_broadcast([128, 16, 64]), op=ALU.mult),
                   reads=allacc + [ob_r], writes=[ob_r])
            os_ = b % 2
            for q2 in range(2):
                for j in range(4):
                    sc.add("pe", lambda e, q2=q2, j=j: e.transpose(out=ptr[:, q2 * 4 + j, :], in_=ob[:, q2, j * 128:(j + 1) * 128], identity=ident_b[:]),
                           reads=[ob_r, r_const], writes=[bank_r[3]])
            sc.add("dve", lambda e, os_=os_: e.tensor_copy(out=obT[os_][:].rearrange("p j (q c) -> p q j c", q=2), in_=ptr.rearrange("p (q j) c -> p q j c", q=2)),
                   reads=[bank_r[3]], writes=[obT_r[os_]])
            sc.add("sp", lambda e, os_=os_, b=b: e.dma_start(out=oT_d[4:8, :, b * 256:(b + 1) * 256].rearrange("j p c -> p j c"), in_=obT[os_][:]),
                   reads=[obT_r[os_]], writes=[], dma=True, key="obT%d" % os_)

    def phaseB(l):
        cv.reset()
        for _ in range(int(os.environ.get("ACTPAD", "0"))):
            sc.add("act", lambda e: e.copy(out=arena[:, 0:8], in_=ident_f[:, 0:8]))
        NBLK = S // 512
        wB = cv.get([128, 8, 2056], BF16)
        wB_r = [sc.res("wB%d" % k) for k in range(8)]
        gcw = cv.get([128, 12, 4]); dtb = cv.get([128, 4]); nexpA = cv.get([128, 4]); gng = cv.get([128, 128])
        mstrict = cv.get([128, 128]); mincl = cv.get([128, 128]); glc = cv.get([128, 4, 128])
        boff = cv.get([128, 6, 128]); aoff1 = cv.get([128, 128]); ones_f = cv.get([128, 128])
        pc_r = sc.res("pconst")
        rawb = cv.get([128, 2, 515]); rawb_r = [sc.res("rawb%d" % f) for f in range(2)]
        halo = cv.get([128, 12, 3]); halo_r = [sc.res("halo%d" % f) for f in range(12)]
        cacc = [cv.get([128, 512]) for _ in range(2)]; cacc_r = [sc.res("cacc%d" % i) for i in range(2)]
        cT = cv.get([128, 12, 512]); cT_r = [sc.res("cT%d" % f) for f in range(12)]
        Sst = [cv.get([128, 4, 128]) for _ in range(2)]
        S_r = [[sc.res("S%d_%d" % (i, h)) for h in range(4)] for i in range(2)]

        def tmp(name, shape, dt=F32, n=2):
            return [(cv.get(shape, dt), sc.res("%s%d" % (name, i))) for i in range(n)]

        Qtm = tmp("Qtm", [128, 4, 128]); Ktm = tmp("Ktm", [128, 4, 128]); Vtm = tmp("Vtm", [128, 4, 128])
        ssq = tmp("ssq", [128, 8]); rn = tmp("rn", [128, 8])
        sm = tmp("sm", [128, 64])
        zs = tmp("zs", [128, 512])
        junk = tmp("junk", [128, 128], n=1)[0]
        HT = 2
        QTh = tmp("QTh", [128, 128], F32, HT); KTh = tmp("KTh", [128, 128], F32, HT)
        GR = tmp("GR", [128, 128], F32, HT); Dm = tmp("Dm", [128, 128], F32, HT); Ds = tmp("Ds", [128, 128], F32, HT)
        Am = tmp("Am", [128, 128], F32, 2 * HT); attn = tmp("attn", [128, 128], F32, 2 * HT); attnT = tmp("attnT", [128, 128], F32, HT)
        Boall = tmp("Boall", [128, 6, 128], F32, HT); Em = tmp("Em", [128, 128], F32, 2 * HT); Dk = tmp("Dk", [128, 128], F32, 2 * HT)
        Xm = tmp("Xm", [128, 128], F32, HT); Rm = tmp("Rm", [128, 256], F32, HT); UW = tmp("UW", [128, 256], F32, HT)
        Kd = tmp("Kd", [128, 128], F32, HT); Qd = tmp("Qd", [128, 128], F32, HT); QpT = tmp("QpT", [128, 128], F32, HT)
        MpT = tmp("MpT", [128, 2, 128], F32, HT)
        osb = tmp("osb", [128, 4, 128], F32, 2); oss = tmp("oss", [128, 8], F32, 2)
        oab = tmp("oab", [128, 512], BF16, 2); oaT = tmp("oaT", [128, 4, 128], BF16, 2)
        ctr = {}

        def nxt(lst, key):
            i = ctr.get(key, 0); ctr[key] = i + 1
            return lst[i % len(lst)]

        PB = tuple(int(x) for x in os.environ.get("PB", "2,3,4,6,7").split(","))

        def pbank():
            i = ctr.get("pb", 0); ctr["pb"] = i + 1
            bi = PB[i % len(PB)]
            return bank[bi], bank_r[bi]

        for kc in range(8):
            sc.add("pool", lambda e, kc=kc: e.dma_start(out=wB[:, kc, :], in_=w_in_d[l, kc * 128:(kc + 1) * 128, 0:2056]),
                   writes=[wB_r[kc]], dma=True, key="wB%d" % kc)
        w4 = cT.rearrange("p f t -> p (f t)")[0:4, 0:1536]; w4_r = sc.res("w4")
        sc.add("sp", lambda e: e.dma_start(out=w4, in_=gconv_d[l, :, :]), writes=[w4_r, cT_r[0], cT_r[1], cT_r[2]], dma=True, key="w4")
        for f in range(12):
            sc.pe32(lambda e, f=f: e.transpose(out=bank[0][:, f * 4:(f + 1) * 4], in_=w4[0:4, f * 128:(f + 1) * 128], identity=ident_f[0:4, 0:4]), reads=[w4_r, cT_r[0], cT_r[1], cT_r[2], r_const], writes=[bank_r[0]])
        sc.add("dve", lambda e: e.tensor_copy(out=gcw.rearrange("p f j -> p (f j)"), in_=bank[0][:, 0:48]), reads=[bank_r[0]], writes=[pc_r], partial=True)
        sc.add("sp", lambda e: e.dma_start(out=dtb, in_=dtb_d[l, :].partition_broadcast(128)), writes=[pc_r], dma=True, key="pc", partial=True)
        sc.add("sp", lambda e: e.dma_start(out=nexpA, in_=alog_d[l, :].partition_broadcast(128)), writes=[pc_r], dma=True, key="pc", partial=True)
        sc.add("sp", lambda e: e.dma_start(out=gng, in_=gng_d[l, :].partition_broadcast(128)), writes=[pc_r], dma=True, key="pc", partial=True)
        sc.add("sp", lambda e: e.dma_start(out=mstrict, in_=c_mstrict_d[:, :]), writes=[pc_r], dma=True, key="pc", partial=True)
        sc.add("sp", lambda e: e.dma_start(out=glc.rearrange("p a b -> p (a b)"), in_=c_gl_d[:, :]), writes=[pc_r], dma=True, key="pc", partial=True)
        sc.add("sp", lambda e: e.dma_start(out=boff.rearrange("p a b -> p (a b)"), in_=c_boff_d[:, :]), writes=[pc_r], dma=True, key="pc", partial=True)
        sc.add("sp", lambda e: e.dma_start(out=aoff1, in_=c_aoff1_d[:, :]), writes=[pc_r], dma=True, key="pc", partial=True)
        sc.add("sp", lambda e: e.dma_start(out=mincl, in_=c_negincl_d[:, :]), writes=[pc_r], dma=True, key="pc", partial=True)
        sc.add("act", lambda e: e.activation(out=nexpA, in_=nexpA, func=AF.Exp, bias=cbias[:, 3:4]), reads=[pc_r], writes=[pc_r])
        sc.add("dve", lambda e: e.tensor_scalar(out=nexpA, in0=nexpA, scalar1=-1.0, scalar2=None, op0=ALU.mult), reads=[pc_r], writes=[pc_r])
        sc.add("dve", lambda e: e.memset(ones_f, 1.0), writes=[pc_r], reads=[pc_r])
        sc.add("dve", lambda e: e.memset(Sst[0][:], 0.0), writes=S_r[0])
        sc.add("pool", lambda e: e.memset(halo, 0.0), writes=halo_r)
        LTc, UTc, CS0c, CS1c = (glc[:, i, :] for i in range(4))

        cur = 0

        KB = int(os.environ.get("KB", "99"))
        for c in range(NBLK):
            csl = slice(c * 512, (c + 1) * 512)
            for f in range(12):
                pb, pb_r = bank[f % 2], bank_r[f % 2]
                for kc in range(8):
                    sc.pe16(pb[:], lambda e, pb=pb, f=f, kc=kc, csl=csl: e.matmul(pb[:], lhsT=wB[:, kc, f * 128:(f + 1) * 128], rhs=xT[:, kc, csl], start=(kc == 0), stop=(kc == 7)),
                           reads=[wB_r[kc]] + xT_r[4 * c:4 * c + 4], writes=[pb_r])
                rs = f % 2
                sc.add("pool", lambda e, f=f, rs=rs: e.tensor_copy(out=rawb[:, rs, 0:3], in_=halo[:, f, :]), reads=[halo_r[f]], writes=[rawb_r[rs]])
                sc.add("act", lambda e, pb=pb, rs=rs: e.copy(out=rawb[:, rs, 3:515], in_=pb[:]), reads=[pb_r], writes=[rawb_r[rs]], partial=True)
                sc.add("pool", lambda e, f=f, rs=rs: e.tensor_copy(out=halo[:, f, :], in_=rawb[:, rs, 512:515]), reads=[rawb_r[rs]], writes=[halo_r[f]])
                ca, ca_r = cacc[f % 2], cacc_r[f % 2]
                sc.add("act", lambda e, pb=pb, f=f, ca=ca: e.activation(out=ca, in_=pb[:], func=AF.Copy, scale=gcw[:, f, 3:4]), reads=[pb_r, pc_r], writes=[ca_r])
                for j in (2, 1, 0):
                    sc.add("dve", lambda e, f=f, j=j, ca=ca, rs=rs: e.scalar_tensor_tensor(out=ca, in0=rawb[:, rs, j:j + 512], scalar=gcw[:, f, j:j + 1], in1=ca, op0=ALU.mult, op1=ALU.add),
                           reads=[rawb_r[rs], pc_r, ca_r], writes=[ca_r])
                sc.add("act", lambda e, f=f, ca=ca: e.activation(out=cT[:, f, :], in_=ca, func=AF.Silu, bias=cbias[:, 3:4]), reads=[ca_r], writes=[cT_r[f]])
            for tt in range(4 if KB > 1 else 0):
                t = c * 4 + tt
                tsl = slice(t * 128, (t + 1) * 128)
                lsl = slice(tt * 128, (tt + 1) * 128)
                (Qt, Qt_r) = nxt(Qtm, "Qtm"); (Kt, Kt_r) = nxt(Ktm, "Ktm"); (Vt, Vt_r) = nxt(Vtm, "Vtm")
                (sq, sq_r) = nxt(ssq, "ssq"); (rnn, rn_r) = nxt(rn, "rn"); (smt, sm_r) = nxt(sm, "sm"); (zst, zs_r) = nxt(zs, "zs")
                for g in range(3):
                    pb, pb_r = bank[2 + g], bank_r[2 + g]
                    for h in range(4):
                        sc.pe32(lambda e, pb=pb, g=g, h=h, lsl=lsl: e.transpose(out=pb[:, h * 128:(h + 1) * 128], in_=cT[:, g * 4 + h, lsl], identity=ident_f[:]),
                               reads=[cT_r[g * 4 + h], r_const], writes=[pb_r])
                for g in range(2):
                    for h in range(4):
                        sc.add("act", lambda e, g=g, h=h, sq=sq: e.activation(out=junk[0], in_=bank[2 + g][:, h * 128:(h + 1) * 128], func=AF.Square, bias=cbias[:, 3:4], accum_out=sq[:, g * 4 + h:g * 4 + h + 1]),
                               reads=[bank_r[2 + g]], writes=[sq_r, junk[1]], partial=True)
                sc.add("act", lambda e, sq=sq, rnn=rnn: e.activation(out=rnn, in_=sq, func=AF.Sqrt, bias=cbias[:, 0:1]), reads=[sq_r, r_const], writes=[rn_r])
                sc.add("dve", lambda e, rnn=rnn: e.reciprocal(out=rnn, in_=rnn), reads=[rn_r], writes=[rn_r])
                sc.add("dve", lambda e, rnn=rnn: e.tensor_scalar(out=rnn[:, 0:4], in0=rnn[:, 0:4], scalar1=float(128 ** -0.5), scalar2=None, op0=ALU.mult), reads=[rn_r], writes=[rn_r])
                sc.add("dve", lambda e, Qt=Qt, rnn=rnn: e.tensor_tensor(out=Qt, in0=bank[2][:].rearrange("p (h d) -> p h d", h=4), in1=rnn[:, 0:4].unsqueeze(2).to_broadcast([128, 4, 128]), op=ALU.mult),
                       reads=[bank_r[2], rn_r], writes=[Qt_r])
                sc.add("dve", lambda e, Kt=Kt, rnn=rnn: e.tensor_tensor(out=Kt, in0=bank[3][:].rearrange("p (h d) -> p h d", h=4), in1=rnn[:, 4:8].unsqueeze(2).to_broadcast([128, 4, 128]), op=ALU.mult),
                       reads=[bank_r[3], rn_r], writes=[Kt_r])
                sc.add("act", lambda e, Vt=Vt: e.copy(out=Vt, in_=bank[4][:].rearrange("p (h d) -> p h d", h=4)), reads=[bank_r[4]], writes=[Vt_r])
                if KB <= 2:
                    continue
                pab, pab_r = bank[5], bank_r[5]
                for kc in range(8):
                    sc.pe16(bank[5][:, 0:8], lambda e, kc=kc, tsl=tsl: e.matmul(bank[5][:, 0:8], lhsT=xT[:, kc, tsl], rhs=wB[:, kc, 1536:1544], start=(kc == 0), stop=(kc == 7)),
                           reads=[xT_r[t], wB_r[kc]], writes=[pab_r])
                sc.add("dve", lambda e, smt=smt: e.tensor_tensor(out=smt[:, 0:4], in0=bank[5][:, 0:4], in1=dtb, op=ALU.add), reads=[pab_r, pc_r], writes=[sm_r])
                sc.add("dve", lambda e, smt=smt: e.tensor_scalar(out=smt[:, 36:40], in0=smt[:, 0:4], scalar1=-1.0, scalar2=None, op0=ALU.mult), reads=[sm_r], writes=[sm_r])
                sc.add("dve", lambda e, smt=smt: e.tensor_tensor(out=smt[:, 4:8], in0=smt[:, 0:4], in1=smt[:, 36:40], op=ALU.min), reads=[sm_r], writes=[sm_r])
                sc.add("act", lambda e, smt=smt: e.activation(out=smt[:, 4:8], in_=smt[:, 4:8], func=AF.Exp, bias=cbias[:, 3:4]), reads=[sm_r], writes=[sm_r])
                sc.add("act", lambda e, smt=smt: e.activation(out=smt[:, 4:8], in_=smt[:, 4:8], func=AF.Ln, bias=cbias[:, 1:2]), reads=[sm_r, r_const], writes=[sm_r])
                sc.add("dve", lambda e, smt=smt: e.scalar_tensor_tensor(out=smt[:, 8:12], in0=smt[:, 0:4], scalar=0.0, in1=smt[:, 4:8], op0=ALU.max, op1=ALU.add), reads=[sm_r], writes=[sm_r])
                sc.add("dve", lambda e, smt=smt: e.tensor_tensor(out=smt[:, 8:12], in0=smt[:, 8:12], in1=nexpA, op=ALU.mult), reads=[sm_r, pc_r], writes=[sm_r])
                sc.add("act", lambda e, smt=smt: e.activation(out=smt[:, 12:16], in_=bank[5][:, 4:8], func=AF.Exp, scale=-1.0, bias=cbias[:, 3:4]), reads=[pab_r], writes=[sm_r])
                sc.add("dve", lambda e, smt=smt: e.tensor_scalar(out=smt[:, 12:16], in0=smt[:, 12:16], scalar1=1.0, scalar2=None, op0=ALU.add), reads=[sm_r], writes=[sm_r])
                sc.add("dve", lambda e, smt=smt: e.reciprocal(out=smt[:, 12:16], in_=smt[:, 12:16]), reads=[sm_r], writes=[sm_r])
                for kc in range(8):
                    sc.pe16(bank[5][:], lambda e, kc=kc, tsl=tsl: e.matmul(bank[5][:], lhsT=xT[:, kc, tsl], rhs=wB[:, kc, 1544:2056], start=(kc == 0), stop=(kc == 7)),
                           reads=[xT_r[t], wB_r[kc]], writes=[pab_r])
                sc.add("act", lambda e, zst=zst: e.activation(out=zst, in_=bank[5][:], func=AF.Silu, bias=cbias[:, 3:4]), reads=[pab_r], writes=[zs_r])
                sc.add("pool", lambda e, zst=zst: e.tensor_tensor(out=zst.rearrange("p (h d) -> p h d", h=4), in0=zst.rearrange("p (h d) -> p h d", h=4), in1=gng.unsqueeze(1).to_broadcast([128, 4, 128]), op=ALU.mult),
                       reads=[zs_r, pc_r], writes=[zs_r])
                for i4, lt in enumerate((LTc, UTc, CS0c, CS1c)):
                    sc.pe32(lambda e, i4=i4, lt=lt, smt=smt: e.matmul(bank[5][:, 16 + 4 * i4:20 + 4 * i4], lhsT=lt, rhs=smt[:, 8:12], start=True, stop=True),
                           reads=[sm_r, pc_r], writes=[pab_r])
                sc.add("dve", lambda e, smt=smt: e.tensor_copy(out=smt[:, 40:44], in_=bank[5][:, 16:20]), reads=[pab_r], writes=[sm_r])
                sc.add("dve", lambda e, smt=smt: e.tensor_scalar(out=smt[:, 16:32], in0=bank[5][:, 16:32], scalar1=-60.0, scalar2=None, op0=ALU.max), reads=[pab_r], writes=[sm_r])
                sc.add("act", lambda e, smt=smt: e.activation(out=smt[:, 16:32], in_=smt[:, 16:32], func=AF.Exp, bias=cbias[:, 3:4]), reads=[sm_r], writes=[sm_r])
                sc.add("dve", lambda e, smt=smt: e.tensor_tensor(out=smt[:, 32:36], in0=smt[:, 12:16], in1=smt[:, 16:20], op=ALU.mult), reads=[sm_r], writes=[sm_r])
                (osb_t, osb_r) = nxt(osb, "osb"); (oss_t, oss_r) = nxt(oss, "oss")
                if KB <= 3:
                    continue
                sc.safe = os.environ.get("SAFE", "1") == "1"
                for h in range(4):
                    (QT_, QT_r_) = nxt(QTh, "QTh"); (KT_, KT_r_) = nxt(KTh, "KTh"); (GR_, GR_r_) = nxt(GR, "GR")
                    (D_, D_r) = nxt(Dm, "Dm"); (Ds_, Ds_r) = nxt(Ds, "Ds"); (A_, A_r) = nxt(Am, "Am")
                    (at_, at_r) = nxt(attn, "attn"); (atT_, atT_r) = nxt(attnT, "attnT"); (Bo_, Bo_r) = nxt(Boall, "Bo")
                    (X_, X_r) = nxt(Xm, "X"); (R_, R_r) = nxt(Rm, "R"); (UW_, UW_r) = nxt(UW, "UW")
                    (Kd_, Kd_r) = nxt(Kd, "Kd"); (Qd_, Qd_r) = nxt(Qd, "Qd"); (QpT_, QpT_r) = nxt(QpT, "QpT"); (Mp_, Mp_r) = nxt(MpT, "MpT")
                    beta_h = smt[:, 12 + h:13 + h]; egc_h = smt[:, 16 + h:17 + h]; egu_h = smt[:, 20 + h:21 + h]
                    gcum_h = smt[:, 40 + h:41 + h]; bk_h = smt[:, 32 + h:33 + h]
                    pb, pb_r = pbank()
                    sc.pe32(lambda e, pb=pb, Qt=Qt, h=h: e.transpose(out=pb[:, 0:128], in_=Qt[:, h, :], identity=ident_f[:]), reads=[Qt_r, r_const], writes=[pb_r])
                    sc.pe32(lambda e, pb=pb, Kt=Kt, h=h: e.transpose(out=pb[:, 128:256], in_=Kt[:, h, :], identity=ident_f[:]), reads=[Kt_r, r_const], writes=[pb_r])
                    sc.add("dve", lambda e, pb=pb, QT_=QT_: e.tensor_copy(out=QT_, in_=pb[:, 0:128]), reads=[pb_r], writes=[QT_r_])
                    sc.add("dve", lambda e, pb=pb, KT_=KT_: e.tensor_copy(out=KT_, in_=pb[:, 128:256]), reads=[pb_r], writes=[KT_r_])
                    sc.add("dve", lambda e, GR_=GR_, smt=smt, h=h: e.tensor_scalar(out=GR_, in0=ones_f, scalar1=smt[:, 8 + h:9 + h], scalar2=None, op0=ALU.mult), reads=[sm_r, pc_r], writes=[GR_r_])
                    K4 = os.environ.get("K4", "z")
                    if KB == 4 and K4 <= "a":
                        continue
                    pg_, pg_r = pbank()
                    sc.pe32(lambda e, pg_=pg_, GR_=GR_: e.matmul(pg_[:, 0:128], lhsT=GR_, rhs=LTc, start=True, stop=True), reads=[GR_r_, pc_r], writes=[pg_r])
                    sc.add("dve", lambda e, pg_=pg_, D_=D_, gcum_h=gcum_h: e.tensor_scalar(out=D_, in0=pg_[:, 0:128], scalar1=gcum_h, scalar2=0.0, op0=ALU.subtract, op1=ALU.max), reads=[pg_r, sm_r], writes=[D_r])
                    sc.add("dve", lambda e, D_=D_: e.tensor_scalar(out=D_, in0=D_, scalar1=60.0, scalar2=None, op0=ALU.min), reads=[D_r], writes=[D_r])
                    if os.environ.get("K5") == "waitD0":
                        sc.pe32(lambda e, pg_=pg_: e.transpose(out=pg_[:, 256:384], in_=ident_f[:], identity=ident_f[:]), reads=[r_const, D_r], writes=[])
                    sc.add("act", lambda e, D_=D_: e.activation(out=D_, in_=D_, func=AF.Exp, scale=-1.0, bias=cbias[:, 3:4]), reads=[D_r], writes=[D_r])
                    if os.environ.get("K5") == "waitD1":
                        sc.pe32(lambda e, pg_=pg_: e.transpose(out=pg_[:, 256:384], in_=ident_f[:], identity=ident_f[:]), reads=[r_const, D_r], writes=[])
                    PD = os.environ.get("PD", "dve")
                    sc.add(PD, lambda e, D_=D_, Ds_=Ds_: e.tensor_tensor(out=Ds_, in0=D_, in1=mstrict, op=ALU.mult), reads=[D_r, pc_r], writes=[Ds_r])
                    sc.add(PD, lambda e, D_=D_: e.tensor_tensor(out=D_, in0=D_, in1=mincl, op=ALU.mult), reads=[D_r, pc_r], writes=[D_r])
                    if os.environ.get("K5") == "waitD2":
                        sc.pe32(lambda e, pg_=pg_: e.transpose(out=pg_[:, 256:384], in_=ident_f[:], identity=ident_f[:]), reads=[r_const, Ds_r], writes=[])
                    if KB == 4 and K4 <= "b":
                        continue
                    pk_, pk_r = pbank()
                    K8 = os.environ.get("K8", "")
                    if K8 != "noKK" and K8 != "none":
                        sc.pe32(lambda e, pk_=pk_, KT_=KT_: e.matmul(pk_[:, 0:128], lhsT=KT_, rhs=KT_, start=True, stop=True), reads=[KT_r_], writes=[pk_r])
                    if K8 != "noQK" and K8 != "none":
                        sc.pe32(lambda e, pk_=pk_, QT_=QT_, KT_=KT_: e.matmul(pk_[:, 128:256], lhsT=QT_, rhs=KT_, start=True, stop=True), reads=[QT_r_, KT_r_], writes=[pk_r])
                    sc.add("dve", lambda e, pk_=pk_, A_=A_, beta_h=beta_h: e.tensor_scalar(out=A_, in0=pk_[:, 0:128], scalar1=beta_h, scalar2=None, op0=ALU.mult), reads=[pk_r, sm_r], writes=[A_r])
                    sc.add("dve", lambda e, pk_=pk_, at_=at_: e.tensor_copy(out=at_, in_=pk_[:, 128:256]), reads=[pk_r], writes=[at_r])
                    (A0_, A0_r) = nxt(Am, "Am"); (at0_, at0_r) = nxt(attn, "attn")
                    sc.add("dve", lambda e, A_=A_, A0_=A0_, Ds_=Ds_: e.tensor_tensor(out=A0_, in0=A_, in1=Ds_, op=ALU.mult), reads=[A_r, Ds_r], writes=[A0_r])
                    sc.add("dve", lambda e, at_=at_, at0_=at0_, D_=D_: e.tensor_tensor(out=at0_, in0=at_, in1=D_, op=ALU.mult), reads=[at_r, D_r], writes=[at0_r])
                    A_, A_r, at_, at_r = A0_, A0_r, at0_, at0_r
                    if KB == 4 and K4 <= "c":
                        continue
                    if os.environ.get("HB", "0") == "1":
                        sc.barrier()
                    if os.environ.get("K6") == "samebank":
                        pt_, pt_r = pk_[:, 256:512], pk_r
                    else:
                        pt_, pt_r = pbank()
                    K5 = os.environ.get("K5", "")
                    if K5 == "waitonly":
                        sc.pe32(lambda e, pt_=pt_: e.transpose(out=pt_[:, 0:128], in_=ident_f[:], identity=ident_f[:]), reads=[r_const, A_r], writes=[pt_r])
                        continue
                    if K5 == "spin":
                        for _ in range(int(os.environ.get("NSPIN", "300"))):
                            sc.pe32(lambda e, pt_=pt_: e.transpose(out=pt_[:, 256:384], in_=ident_f[:], identity=ident_f[:]), reads=[r_const], writes=[pt_r])
                        sc.pe32(lambda e, pt_=pt_: e.transpose(out=pt_[:, 0:128], in_=ident_f[:], identity=ident_f[:]), reads=[r_const, A_r], writes=[pt_r])
                        continue
                    if K5 == "dummy":
                        sc.pe32(lambda e, pt_=pt_: e.transpose(out=pt_[:, 256:384], in_=ident_f[:], identity=ident_f[:]), reads=[r_const], writes=[pt_r])
                        sc.pe32(lambda e, pt_=pt_: e.transpose(out=pt_[:, 0:128], in_=ident_f[:], identity=ident_f[:]), reads=[r_const, A_r], writes=[pt_r])
                        continue
                    if K5 == "viaact2" and ((t * 4 + h) >= int(os.environ.get("KN", "999")) or (t * 4 + h) < int(os.environ.get("KN0", "0"))):
                        continue
                    if K5 == "viaact2":
                        sc.add("dve", lambda e, X_=X_, A_=A_: e.tensor_copy(out=X_, in_=A_), reads=[A_r], writes=[X_r])
                        continue
                    if K5 == "viaact":
                        sc.add("dve", lambda e, X_=X_, A_=A_: e.tensor_copy(out=X_, in_=A_), reads=[A_r], writes=[X_r])
                        sc.pe32(lambda e, pt_=pt_, X_=X_: e.transpose(out=pt_[:, 0:128], in_=X_, identity=ident_f[:]), reads=[r_const, X_r], writes=[pt_r])
                        continue
                    if K5 == "waitbf":
                        ptb_ = pt_.bitcast(BF16)
                        sc.pe16(ptb_[:, 0:128], lambda e, ptb_=ptb_: e.transpose(out=ptb_[:, 0:128], in_=ident_b[:], identity=ident_b[:]), reads=[r_const, A_r], writes=[pt_r])
                        continue
                    if K5 == "waitat":
                        sc.pe32(lambda e, pt_=pt_: e.transpose(out=pt_[:, 0:128], in_=ident_f[:], identity=ident_f[:]), reads=[r_const, at_r], writes=[pt_r])
                        continue
                    if K5 == "waitD":
                        sc.pe32(lambda e, pt_=pt_: e.transpose(out=pt_[:, 0:128], in_=ident_f[:], identity=ident_f[:]), reads=[r_const, Ds_r], writes=[pt_r])
                        continue
                    if K5 == "useGR":
                        sc.pe32(lambda e, pt_=pt_, GR_=GR_: e.transpose(out=pt_[:, 0:128], in_=GR_, identity=ident_f[:]), reads=[GR_r_, r_const] + ([A_r] if os.environ.get("K7") != "nodep" else []), writes=[pt_r])
                        continue
                    if K5 == "useDs":
                        sc.pe32(lambda e, pt_=pt_, Ds_=Ds_: e.transpose(out=pt_[:, 0:128], in_=Ds_, identity=ident_f[:]), reads=[Ds_r, A_r, r_const], writes=[pt_r])
                        continue
                    if K5 != "nope" and K5 != "pe2":
                        sc.pe32(lambda e, pt_=pt_, A_=A_: e.transpose(out=pt_[:, 0:128], in_=A_, identity=ident_f[:]), reads=[A_r, r_const], writes=[pt_r])
                    if K5 != "nope" and K5 != "pe1":
                        sc.pe32(lambda e, pt_=pt_, at_=at_: e.transpose(out=pt_[:, 128:256], in_=at_, identity=ident_f[:]), reads=[at_r, r_const], writes=[pt_r])
                    if K5 == "nodve":
                        continue
                    sc.add("dve", lambda e, pt_=pt_, X_=X_: e.tensor_copy(out=X_, in_=pt_[:, 0:128]), reads=[pt_r], writes=[X_r])
                    if KB == 4 and K4 <= "d":
                        continue
                    sc.add("pool", lambda e, X_=X_, Bo_=Bo_: e.tensor_tensor(out=Bo_, in0=X_.unsqueeze(1).to_broadcast([128, 6, 128]), in1=boff, op=ALU.mult), reads=[X_r, pc_r], writes=[Bo_r])
                    if KB == 4 and K4 <= "e":
                        continue
                    sc.add("dve", lambda e, pt_=pt_, atT_=atT_: e.tensor_copy(out=atT_, in_=pt_[:, 128:256]), reads=[pt_r], writes=[atT_r])
                    if KB <= 4:
                        continue
                    (E_, E_r) = nxt(Em, "E"); (Dk_, Dk_r) = nxt(Dk, "Dk")
                    sc.add("pool", lambda e, E_=E_, Bo_=Bo_: e.tensor_tensor(out=E_, in0=ident_f[:], in1=Bo_[:, 0, :], op=ALU.subtract), reads=[Bo_r, r_const], writes=[E_r])
                    sc.add("pool", lambda e, Dk_=Dk_, A_=A_: e.tensor_tensor(out=Dk_, in0=A_, in1=aoff1, op=ALU.mult), reads=[A_r, pc_r], writes=[Dk_r])
                    sc.add("pool", lambda e, Dk_=Dk_: e.tensor_tensor(out=Dk_, in0=ident_f[:], in1=Dk_, op=ALU.subtract), reads=[Dk_r, r_const], writes=[Dk_r])
                    for lvl in range(1, 6):
                        px_, px_r = pbank()
                        sc.pe32(lambda e, px_=px_, Bo_=Bo_, lvl=lvl, Dk_=Dk_: e.matmul(px_[:, 0:128], lhsT=Bo_[:, lvl, :], rhs=Dk_, start=True, stop=True), reads=[Bo_r, Dk_r], writes=[px_r])
                        sc.add("dve", lambda e, px_=px_, X_=X_: e.tensor_copy(out=X_, in_=px_[:, 0:128]), reads=[px_r], writes=[X_r])
                        py_, py_r = pbank()
                        sc.pe32(lambda e, py_=py_, X_=X_, E_=E_: e.matmul(py_[:, 0:128], lhsT=X_, rhs=E_, start=True, stop=True), reads=[X_r, E_r], writes=[py_r])
                        (E2_, E2_r) = nxt(Em, "E")
                        sc.add("dve", lambda e, py_=py_, E_=E_, E2_=E2_: e.tensor_tensor(out=E2_, in0=E_, in1=py_[:, 0:128], op=ALU.subtract), reads=[py_r, E_r], writes=[E2_r])
                        E_, E_r = E2_, E2_r
                        if lvl < 5:
                            pd_, pd_r = pbank()
                            sc.pe32(lambda e, pd_=pd_, E_=E_: e.transpose(out=pd_[:, 0:128], in_=E_, identity=ident_f[:]), reads=[E_r, r_const], writes=[pd_r])
                            (Dk_, Dk_r) = nxt(Dk, "Dk")
                            sc.add("dve", lambda e, pd_=pd_, Dk_=Dk_: e.tensor_copy(out=Dk_, in_=pd_[:, 0:128]), reads=[pd_r], writes=[Dk_r])
                    if KB <= 5:
                        continue
                    if os.environ.get("HB", "0") == "1":
                        sc.barrier()
                    sc.add("pool", lambda e, R_=R_, Vt=Vt, h=h, beta_h=beta_h: e.tensor_scalar(out=R_[:, 0:128], in0=Vt[:, h, :], scalar1=beta_h, scalar2=None, op0=ALU.mult), reads=[Vt_r, sm_r], writes=[R_r])
                    sc.add("pool", lambda e, R_=R_, Kt=Kt, h=h, bk_h=bk_h: e.tensor_scalar(out=R_[:, 128:256], in0=Kt[:, h, :], scalar1=bk_h, scalar2=None, op0=ALU.mult), reads=[Kt_r, sm_r], writes=[R_r], partial=True)
                    sc.add("pool", lambda e, Kd_=Kd_, Kt=Kt, h=h, egu_h=egu_h: e.tensor_scalar(out=Kd_, in0=Kt[:, h, :], scalar1=egu_h, scalar2=None, op0=ALU.mult), reads=[Kt_r, sm_r], writes=[Kd_r])
                    sc.add("pool", lambda e, Qd_=Qd_, Qt=Qt, h=h, egc_h=egc_h: e.tensor_scalar(out=Qd_, in0=Qt[:, h, :], scalar1=egc_h, scalar2=None, op0=ALU.mult), reads=[Qt_r, sm_r], writes=[Qd_r])
                    pu_, pu_r = pbank()
                    sc.pe32(lambda e, pu_=pu_, E_=E_, R_=R_: e.matmul(pu_[:, 0:256], lhsT=E_, rhs=R_, start=True, stop=True), reads=[E_r, R_r], writes=[pu_r])
                    sc.add("dve", lambda e, pu_=pu_, UW_=UW_: e.tensor_copy(out=UW_[:, 0:128], in_=pu_[:, 0:128]), reads=[pu_r], writes=[UW_r])
                    sc.add("dve", lambda e, pu_=pu_, UW_=UW_: e.tensor_scalar(out=UW_[:, 128:256], in0=pu_[:, 128:256], scalar1=-1.0, scalar2=None, op0=ALU.mult), reads=[pu_r], writes=[UW_r], partial=True)
                    pq_, pq_r = pbank()
                    sc.pe32(lambda e, pq_=pq_, Qd_=Qd_: e.matmul(pq_[:, 0:128], lhsT=Qd_, rhs=ident_f[:], start=True, stop=False), reads=[Qd_r, r_const], writes=[pq_r])
                    sc.pe32(lambda e, pq_=pq_, UW_=UW_, atT_=atT_: e.matmul(pq_[:, 0:128], lhsT=UW_[:, 128:256], rhs=atT_, start=False, stop=True), reads=[UW_r, atT_r], writes=[pq_r])
                    sc.add("dve", lambda e, pq_=pq_, QpT_=QpT_: e.tensor_copy(out=QpT_, in_=pq_[:, 0:128]), reads=[pq_r], writes=[QpT_r])
                    for ci in range(2):
                        pm_, pm_r = pbank()
                        ps_ = slice(ci * 64, ci * 64 + 64)
                        sc.pe32(lambda e, pm_=pm_, UW_=UW_, Kd_=Kd_, ps_=ps_: e.matmul(pm_[:, 0:128], lhsT=UW_[ps_, 128:256], rhs=Kd_[ps_, :], start=True, stop=True), reads=[UW_r, Kd_r], writes=[pm_r])
                        if ci == 0:
                            sc.add("dve", lambda e, pm_=pm_, Mp_=Mp_, ci=ci: e.tensor_copy(out=Mp_[:, ci, :], in_=pm_[:, 0:128]), reads=[pm_r], writes=[Mp_r])
                        else:
                            sc.add("dve", lambda e, pm_=pm_, Mp_=Mp_, ci=ci: e.tensor_copy(out=Mp_[:, ci, :], in_=pm_[:, 0:128]), reads=[pm_r], writes=[Mp_r], partial=True)
                    if os.environ.get("HB", "0") == "1":
                        sc.barrier()
                    for ci in range(2 if KB > 6 else 0):
                        ps_ = slice(ci * 64, ci * 64 + 64)
                        Sp, Sp_r = Sst[cur], S_r[cur][h]
                        Sn, Sn_r = Sst[1 - cur], S_r[1 - cur][h]
                        po_, po_r = pbank()
                        tp = (0, ci * 64)
                        sc.pe32(lambda e, po_=po_, QpT_=QpT_, Sp=Sp, h=h, ps_=ps_, tp=tp: e.matmul(po_[ps_, 0:128], lhsT=QpT_[:, ps_], rhs=Sp[:, h, :], start=True, stop=False, tile_position=tp), reads=[QpT_r, Sp_r], writes=[po_r])
                        sc.pe32(lambda e, po_=po_, atT_=atT_, UW_=UW_, ps_=ps_, tp=tp: e.matmul(po_[ps_, 0:128], lhsT=atT_[:, ps_], rhs=UW_[:, 0:128], start=False, stop=True, tile_position=tp), reads=[atT_r, UW_r], writes=[po_r])
                        sc.add("dve", lambda e, po_=po_, osb_t=osb_t, h=h, ps_=ps_: e.tensor_copy(out=osb_t[ps_, h, :], in_=po_[ps_, 0:128]), reads=[po_r], writes=[osb_r], partial=True)
                        sc.add("act", lambda e, po_=po_, oss_t=oss_t, h=h, ps_=ps_: e.activation(out=junk[0][ps_, :], in_=po_[ps_, 0:128], func=AF.Square, bias=cbias[ps_, 3:4], accum_out=oss_t[ps_, h:h + 1]), reads=[po_r], writes=[oss_r, junk[1]], partial=True)
                        pS_, pS_r = pbank()
                        sc.pe32(lambda e, pS_=pS_, Mp_=Mp_, ci=ci, Sp=Sp, h=h: e.matmul(pS_[:, 0:128], lhsT=Mp_[:, ci, :], rhs=Sp[:, h, :], start=True, stop=False), reads=[Mp_r, Sp_r], writes=[pS_r])
                        sc.pe32(lambda e, pS_=pS_, Kd_=Kd_, UW_=UW_, ps_=ps_: e.matmul(pS_[:, 0:128], lhsT=Kd_[ps_, :], rhs=UW_[ps_, 0:128], start=False, stop=True), reads=[Kd_r, UW_r], writes=[pS_r])
                        egl = smt[:, 24 + 4 * ci + h:25 + 4 * ci + h]
                        sc.add("dve", lambda e, pS_=pS_, Sp=Sp, Sn=Sn, h=h, egl=egl: e.scalar_tensor_tensor(out=Sn[:, h, :], in0=Sp[:, h, :], scalar=egl, in1=pS_[:, 0:128], op0=ALU.mult, op1=ALU.add), reads=[pS_r, Sp_r, sm_r], writes=[Sn_r])
                        cur = 1 - cur
                sc.safe = False
                if KB <= 7:
                    continue
                (oab_t, oab_r) = nxt(oab, "oab"); (oaT_t, oaT_r) = nxt(oaT, "oaT")
                sc.add("act", lambda e, oss_t=oss_t: e.activation(out=oss_t[:, 4:8], in_=oss_t[:, 0:4], func=AF.Sqrt, scale=1.0 / 128, bias=cbias[:, 0:1]), reads=[oss_r, r_const], writes=[oss_r])
                sc.add("dve", lambda e, oss_t=oss_t: e.reciprocal(out=oss_t[:, 4:8], in_=oss_t[:, 4:8]), reads=[oss_r], writes=[oss_r])
                sc.add("dve", lambda e, osb_t=osb_t, oss_t=oss_t: e.tensor_tensor(out=osb_t, in0=osb_t, in1=oss_t[:, 4:8].unsqueeze(2).to_broadcast([128, 4, 128]), op=ALU.mult), reads=[osb_r, oss_r], writes=[osb_r])
                sc.add("dve", lambda e, osb_t=osb_t, zst=zst, oab_t=oab_t: e.tensor_tensor(out=oab_t, in0=osb_t.rearrange("p h d -> p (h d)"), in1=zst, op=ALU.mult), reads=[osb_r, zs_r], writes=[oab_r])
                ptb = bank[5][:].bitcast(BF16).rearrange("p (k c) -> p k c", k=8)
                for h in range(4):
                    sc.pe16(ptb[:, h, :], lambda e, h=h, oab_t=oab_t: e.transpose(out=ptb[:, h, :], in_=oab_t[:, h * 128:(h + 1) * 128], identity=ident_b[:]), reads=[oab_r, r_const], writes=[bank_r[5]])
                sc.add("dve", lambda e, oaT_t=oaT_t: e.tensor_copy(out=oaT_t, in_=ptb[:, 0:4, :]), reads=[bank_r[5]], writes=[oaT_r])
                sc.add("sp", lambda e, oaT_t=oaT_t, tsl=tsl: e.dma_start(out=oT_d[0:4, :, tsl].rearrange("j p c -> p j c"), in_=oaT_t), reads=[oaT_r], writes=[], dma=True, key="oaT%d" % (ctr["oaT"] % 2))

    def ln_tile(L_, t, ps_lo, ps_lo_r, ps_hi, ps_hi_r, xr, xr_r, g_bc, b_bc, lnp_r, out_d, write_xT):
        tsl = slice(t * 128, (t + 1) * 128)
        (y, y_r) = L_["y"][t % 2]; (st, st_r) = L_["st"][t % 2]; (xb_, xb_r_) = L_["xb"][t % 2]
        for half, (pp, pp_r) in enumerate(((ps_lo, ps_lo_r), (ps_hi, ps_hi_r))):
            hs = slice(half * 512, (half + 1) * 512)
            sc.add("dve", lambda e, pp=pp, hs=hs, y=y, xr=xr: e.scalar_tensor_tensor(out=y[:, hs], in0=xr[:, hs], scalar=ALPHA, in1=pp[:, 0:512], op0=ALU.mult, op1=ALU.add),
                   reads=[pp_r, xr_r], writes=[y_r], partial=(half == 1))
        for half in range(2):
            hs = slice(half * 512, (half + 1) * 512)
            sc.add("dve", lambda e, half=half, hs=hs, y=y, st=st: e.bn_stats(out=st[:, half * 6:(half + 1) * 6], in_=y[:, hs]), reads=[y_r], writes=[st_r], partial=(half == 1))
        sc.add("dve", lambda e, st=st: e.bn_aggr(out=st[:, 12:14], in_=st[:, 0:12]), reads=[st_r], writes=[st_r])
        sc.add("act", lambda e, st=st: e.activation(out=st[:, 14:15], in_=st[:, 13:14], func=AF.Sqrt, bias=cbias[:, 2:3]), reads=[st_r, r_const], writes=[st_r])
        sc.add("dve", lambda e, st=st: e.reciprocal(out=st[:, 14:15], in_=st[:, 14:15]), reads=[st_r], writes=[st_r])
        sc.add("dve", lambda e, y=y, st=st: e.tensor_scalar(out=y, in0=y, scalar1=st[:, 12:13], scalar2=st[:, 14:15], op0=ALU.subtract, op1=ALU.mult), reads=[y_r, st_r], writes=[y_r])
        sc.add("pool", lambda e, y=y: e.tensor_tensor(out=y, in0=y, in1=g_bc, op=ALU.mult), reads=[y_r, lnp_r], writes=[y_r])
        sc.add("dve", lambda e, y=y: e.tensor_tensor(out=y, in0=y, in1=b_bc, op=ALU.add), reads=[y_r, lnp_r], writes=[y_r])
        sc.add("sp", lambda e, y=y, tsl=tsl: e.dma_start(out=out_d[tsl, :], in_=y), reads=[y_r], writes=[], dma=True, key="ysto%d" % (t % 2))
        if write_xT:
            sc.add("act", lambda e, y=y, xb_=xb_: e.copy(out=xb_, in_=y), reads=[y_r], writes=[xb_r_])
            ptb = bank[7][:].bitcast(BF16).rearrange("p (k c) -> p k c", k=8)
            for kc in range(8):
                sc.pe16(ptb[:, kc, :], lambda e, kc=kc, xb_=xb_: e.transpose(out=ptb[:, kc, :], in_=xb_[:, kc * 128:(kc + 1) * 128], identity=ident_b[:]), reads=[xb_r_, r_const], writes=[bank_r[7]])
            sc.add("dve", lambda e, tsl=tsl: e.tensor_copy(out=xT[:, :, tsl], in_=ptb), reads=[bank_r[7]], writes=[xT_r[t]])

    def ln_bufs(g_d, b_d, l):
        L_ = {}
        L_["y"] = [(cv.get([128, 1024]), sc.res("y%d" % i)) for i in range(2)]
        L_["st"] = [(cv.get([128, 16]), sc.res("st%d" % i)) for i in range(2)]
        L_["xb"] = [(cv.get([128, 1024], BF16), sc.res("xbln%d" % i)) for i in range(2)]
        g_bc = cv.get([128, 1024]); b_bc = cv.get([128, 1024]); lnp_r = sc.res("lnp")
        sc.add("sp", lambda e: e.dma_start(out=g_bc, in_=g_d[l, :].partition_broadcast(128)), writes=[lnp_r], dma=True, key="lnp", partial=True)
        sc.add("sp", lambda e: e.dma_start(out=b_bc, in_=b_d[l, :].partition_broadcast(128)), writes=[lnp_r], dma=True, key="lnp", partial=True)
        return L_, g_bc, b_bc, lnp_r

    def phaseC(l, xin_d):
        cv.reset()
        wO = cv.get([128, 8, 1024], BF16); wO_r = [sc.res("wO%d" % k) for k in range(8)]
        for kc in range(8):
            sc.add("pool", lambda e, kc=kc: e.dma_start(out=wO[:, kc, :], in_=w_out_d[l, kc * 128:(kc + 1) * 128, :]), writes=[wO_r[kc]], dma=True, key="wO%d" % kc)
        for kc in range(8):
            sc.add("pool", lambda e, kc=kc: e.dma_start(out=wupbf_d[l].rearrange("t p k f -> p t k f")[:, :, kc, :], in_=w_up_d[l, kc * 128:(kc + 1) * 128, :].rearrange("p (t f) -> p t f", f=128)),
                   writes=[wupbf_r[l]], dma=True, key="wcv%d" % (kc % 4), partial=True)
        L_, g_bc, b_bc, lnp_r = ln_bufs(ln1g_d, ln1b_d, l)
        oTt = [(cv.get([128, 8, 128], BF16), sc.res("oTt%d" % i)) for i in range(3)]
        xrs = [(cv.get([128, 1024]), sc.res("xr%d" % i)) for i in range(3)]
        for t in range(NT):
            tsl = slice(t * 128, (t + 1) * 128)
            (ot, ot_r) = oTt[t % 3]; (xr, xr_r) = xrs[t % 3]
            sc.add("sp", lambda e, ot=ot, tsl=tsl: e.dma_start(out=ot, in_=oT_d[:, :, tsl].rearrange("k p c -> p k c")), writes=[ot_r], dma=True, key="oTt%d" % (t % 3))
            sc.add("sp", lambda e, xr=xr, tsl=tsl: e.dma_start(out=xr, in_=xin_d[tsl, :]), writes=[xr_r], dma=True, key="xr%d" % (t % 3))
            bl, bh = 2 * (t % 2), 2 * (t % 2) + 1
            for half, bi in ((0, bl), (1, bh)):
                for kc in range(8):
                    sc.pe16(bank[bi][:], lambda e, bi=bi, kc=kc, ot=ot, half=half: e.matmul(bank[bi][:], lhsT=ot[:, kc, :], rhs=wO[:, kc, half * 512:(half + 1) * 512], start=(kc == 0), stop=(kc == 7)),
                            reads=[ot_r, wO_r[kc]], writes=[bank_r[bi]])
            ln_tile(L_, t, bank[bl], bank_r[bl], bank[bh], bank_r[bh], xr, xr_r, g_bc, b_bc, lnp_r, x1_d, True)

    def phaseD(l, out_d, write_xT):
        cv.reset()
        NJ = DFF // 128
        NBLK = S // 512
        wD = cv.get([128, NJ, 1024], BF16); wD_r = [sc.res("wD%d" % j) for j in range(NJ)]
        for j in range(NJ):
            sc.add("pool", lambda e, j=j: e.dma_start(out=wD[:, j, :], in_=w_down_d[l, j * 128:(j + 1) * 128, :]), writes=[wD_r[j]], dma=True, key="wD%d" % (j % 4))
        L_, g_bc, b_bc, lnp_r = ln_bufs(ln2g_d, ln2b_d, l)
        fc4 = cv.get([128, 4, 44]); fc_r = sc.res("fc4")
        hT = cv.get([128, NJ, 512], BF16); hT_r = [sc.res("hT%d" % j) for j in range(NJ)]
        wU = [(cv.get([128, 8, 256], BF16), sc.res("wU%d" % i)) for i in range(3)]
        raw = [(cv.get([128, 2, 514]), sc.res("raw%d" % i)) for i in range(2)]
        acc = [(cv.get([128, 2, 512]), sc.res("facc%d" % i)) for i in range(2)]
        halo = cv.get([128, 44, 2]); halo_r = [sc.res("fhalo%d" % j) for j in range(44)]
        xrs = [(cv.get([128, 1024]), sc.res("xrD%d" % i)) for i in range(2)]
        w44 = hT.rearrange("p j t -> p (j t)").bitcast(F32)[0:44, 0:512].rearrange("p (a b) -> p a b", a=4)
        w44_r = sc.res("w44")
        for j3 in range(3):
            sc.add("sp", lambda e, j3=j3: e.dma_start(out=w44[:, j3, :], in_=fconvw_d[l, j3, :].rearrange("(f p) -> f p", p=128)), writes=[w44_r] + hT_r[0:2], dma=True, key="w44", partial=True)
        sc.add("sp", lambda e: e.dma_start(out=w44[:, 3, :], in_=fconvb_d[l, :].rearrange("(f p) -> f p", p=128)), writes=[w44_r], dma=True, key="w44", partial=True)
        for a4 in range(4):
            sc.pe32(lambda e, a4=a4: e.transpose(out=bank[0][:, a4 * 44:(a4 + 1) * 44], in_=w44[:, a4, :], identity=ident_f[0:44, 0:44]), reads=[w44_r, r_const], writes=[bank_r[0]])
        sc.add("dve", lambda e: e.tensor_copy(out=fc4.rearrange("p a f -> p (a f)"), in_=bank[0][:, 0:176]), reads=[bank_r[0]], writes=[fc_r])
        sc.add("pool", lambda e: e.memset(halo, 0.0), reads=[], writes=halo_r)
        sc.barrier()
        for c in range(NBLK):
            csl = slice(c * 512, (c + 1) * 512)
            for j in range(NJ):
                (wu, wu_r) = wU[j % 3]
                sc.add("sp", lambda e, wu=wu, j=j: e.dma_start(out=wu[:, :, 0:128], in_=wupbf_d[l][j, :, :, :]), reads=[wupbf_r[l]], writes=[wu_r], dma=True, key="wUa%d" % (j % 3))
                sc.add("sp", lambda e, wu=wu, j=j: e.dma_start(out=wu[:, :, 128:256], in_=wupbf_d[l][22 + j, :, :, :]), reads=[wupbf_r[l]], writes=[wu_r], dma=True, key="wUb%d" % (j % 3), partial=True)
                (rw, rw_r) = raw[j % 2]; (ac, ac_r) = acc[j % 2]
                for gv in range(2):
                    f = gv * 22 + j
                    bi = 2 * (j % 2) + gv
                    for kc in range(8):
                        sc.pe16(bank[bi][:], lambda e, bi=bi, kc=kc, wu=wu, gv=gv, csl=csl: e.matmul(bank[bi][:], lhsT=wu[:, kc, gv * 128:(gv + 1) * 128], rhs=xT[:, kc, csl], start=(kc == 0), stop=(kc == 7)),
                                reads=[wu_r] + xT_r[4 * c:4 * c + 4], writes=[bank_r[bi]])
                    sc.add("pool", lambda e, rw=rw, gv=gv, f=f: e.tensor_copy(out=rw[:, gv, 0:2], in_=halo[:, f, :]), reads=[halo_r[f]], writes=[rw_r], partial=(gv == 1))
                    sc.add("act", lambda e, rw=rw, gv=gv, bi=bi: e.copy(out=rw[:, gv, 2:514], in_=bank[bi][:]), reads=[bank_r[bi]], writes=[rw_r], partial=True)
                    sc.add("pool", lambda e, rw=rw, gv=gv, f=f: e.tensor_copy(out=halo[:, f, :], in_=rw[:, gv, 512:514]), reads=[rw_r], writes=[halo_r[f]])
                    sc.add("act", lambda e, ac=ac, gv=gv, bi=bi, f=f: e.activation(out=ac[:, gv, :], in_=bank[bi][:], func=AF.Identity, scale=fc4[:, 2, f:f + 1], bias=fc4[:, 3, f:f + 1]), reads=[bank_r[bi], fc_r], writes=[ac_r], partial=(gv == 1))
                    eng = "dve"
                    for tap in (1, 0):
                        sc.add(eng, lambda e, ac=ac, rw=rw, gv=gv, tap=tap, f=f: e.scalar_tensor_tensor(out=ac[:, gv, :], in0=rw[:, gv, tap:tap + 512], scalar=fc4[:, tap, f:f + 1], in1=ac[:, gv, :], op0=ALU.mult, op1=ALU.add),
                               reads=[rw_r, fc_r, ac_r], writes=[ac_r])
                sc.add("act", lambda e, ac=ac: e.activation(out=ac[:, 0, :], in_=ac[:, 0, :], func=AF.Silu, bias=cbias[:, 3:4]), reads=[ac_r, r_const], writes=[ac_r])
                sc.add("dve", lambda e, ac=ac, j=j: e.tensor_tensor(out=hT[:, j, :], in0=ac[:, 0, :], in1=ac[:, 1, :], op=ALU.mult), reads=[ac_r], writes=[hT_r[j]])
            for tt in range(4):
                t = c * 4 + tt
                tsl = slice(t * 128, (t + 1) * 128)
                (xr, xr_r) = xrs[t % 2]
                sc.add("sp", lambda e, xr=xr, tsl=tsl: e.dma_start(out=xr, in_=x1_d[tsl, :]), writes=[xr_r], dma=True, key="xrD%d" % (t % 2))
                bl, bh = 4 + 2 * (t % 2), 5 + 2 * (t % 2)
                if bh == 7 and write_xT:
                    bl, bh = 4, 5
                for half, bi in ((0, bl), (1, bh)):
                    for j in range(NJ):
                        sc.pe16(bank[bi][:], lambda e, bi=bi, j=j, tt=tt, half=half: e.matmul(bank[bi][:], lhsT=hT[:, j, tt * 128:(tt + 1) * 128], rhs=wD[:, j, half * 512:(half + 1) * 512], start=(j == 0), stop=(j == NJ - 1)),
                                reads=[hT_r[j], wD_r[j]], writes=[bank_r[bi]])
                ln_tile(L_, t, bank[bl], bank_r[bl], bank[bh], bank_r[bh], xr, xr_r, g_bc, b_bc, lnp_r, out_d, write_xT)

    phase0(x_d)
    sc.barrier()
    if stop_after == "0":
        sc.add("sp", lambda e: e.dma_start(out=oT_d[:, :, :].rearrange("k p s -> p k s"), in_=xT[:]), reads=xT_r, writes=[], dma=True, key="dbg")
    elif stop_after == "A":
        phaseA(0)
    elif stop_after == "B":
        phaseB(0)
    else:
        for l in range(L):
            xin = x_d if l == 0 else x2_d
            last = (l == L - 1)
            phaseA(l)
            sc.barrier()
            phaseB(l)
            sc.barrier()
            phaseC(l, xin)
            sc.barrier()
            if stop_after == "C" and l == 0:
                break
            phaseD(l, y_d if last else x2_d, not last)
            sc.barrier()
            if stop_after == "D" and l == 0:
                break
    if dbg and os.environ.get("DUMPARENA") and not os.environ.get("SIM"):
        sc.barrier()
        dbg_arena = nc.dram_tensor("dbg_arena", [128, ARENA], F32, kind="ExternalOutput").ap()
        for q in range(4):
            sc.add("sp", lambda e, q=q: e.dma_start(out=dbg_arena[:, q * (ARENA // 4):(q + 1) * (ARENA // 4)], in_=arena[:, q * (ARENA // 4):(q + 1) * (ARENA // 4)]), dma=True, key="dbga")
    sc.emit(nc, es)
    es.close()
    return nc


_CACHE = {}


def kernel(**inputs):
    x = np.asarray(inputs["x"], dtype=np.float32)
    B, S, _ = x.shape
    L = int(np.asarray(inputs["w_in"]).shape[0])
    key = (S, L)
    if key not in _CACHE:
        _CACHE[key] = (build(S=S, L=L), make_consts(S))
    nc, consts = _CACHE[key]
    shared = {k: np.ascontiguousarray(np.asarray(v, dtype=np.float32)) for k, v in inputs.items() if k != "x"}
    for k, v in consts.items():
        shared["c_" + k] = v
    in_maps = []
    for b in range(B):
        m = dict(shared)
        m["x"] = np.ascontiguousarray(x[b])
        in_maps.append(m)
    res = run_bass_kernel_spmd(nc, in_maps, core_ids=list(range(B)))
    return np.stack([np.asarray(r["y"], dtype=np.float32) for r in res.results], axis=0)
```

```python
import os
import numpy as np
import ml_dtypes
from contextlib import ExitStack
import concourse.bass as bass
import concourse.mybir as mybir
from concourse.bass_utils import run_bass_kernel_spmd

F32 = mybir.dt.float32
BF16 = mybir.dt.bfloat16
AF = mybir.ActivationFunctionType
ALU = mybir.AluOpType
AX = mybir.AxisListType

D = 1024
NIN = 3592
DFF = 2816
ALPHA = float((2 * 2) ** 0.25)
NEG = -30000.0


class Res:
    __slots__ = ("name", "writers", "readers")

    def __init__(self, name):
        self.name = name
        self.writers = []
        self.readers = {}


class Op:
    __slots__ = ("eng", "fn", "dma", "key", "value", "deps", "signal", "barrier")

    def __init__(self, eng, fn, dma=False, key=None):
        self.eng = eng
        self.fn = fn
        self.dma = dma
        self.key = key
        self.value = None
        self.deps = []
        self.signal = False
        self.barrier = False


ENGS = ("pe", "act", "dve", "pool", "sp")


class Sched:
    def __init__(self):
        self.ops = {e: [] for e in ENGS}
        self.keycount = {}
        self.keylast = {}
        self.allres = []
        self.last_pe_f32 = False
        self.ident_b = None
        self.safe = False
        self.safecnt = 0
        self.safek = int(os.environ.get("SAFEK", "0"))
        self.safeeng = tuple(x for x in os.environ.get("SAFEENG", "act").split(",") if x)

    def res(self, name):
        r = Res(name)
        self.allres.append(r)
        return r

    def _dep(self, op, prod, raw):
        if prod is op:
            return
        if (not prod.dma) and (not op.dma) and prod.eng == op.eng:
            if op.eng == "pe":
                return
        op.deps.append(prod)
        prod.signal = True

    def add(self, eng, fn, reads=(), writes=(), dma=False, key=None, partial=False, f32=False, out=None):
        if eng == "pe":
            if (not f32) and self.last_pe_f32 and out is not None:
                fn0 = fn
                dmy = out.bitcast(F32) if out.dtype != F32 else out
                idb = self.ident_b

                def fn(e, fn0=fn0, dmy=dmy, idb=idb):
                    e.matmul(dmy[0:64, 0:8], lhsT=idb[:, 0:64], rhs=idb[:, 0:8], start=True, stop=True)
                    return fn0(e)
            self.last_pe_f32 = f32
        excl = self.safe and (not dma) and (eng in self.safeeng)
        if excl:
            self.barrier()
        op = Op(eng, fn, dma, key)
        for r in reads:
            for w in r.writers:
                self._dep(op, w, True)
        for r in writes:
            for w in r.writers:
                if not (partial and w.dma and op.dma):
                    self._dep(op, w, False)
            for rd in r.readers.values():
                if isinstance(rd, list):
                    for x in rd:
                        self._dep(op, x, False)
                else:
                    self._dep(op, rd, False)
        for r in reads:
            if dma:
                r.readers.setdefault("dma", []).append(op)
            else:
                r.readers[eng] = op
        for r in writes:
            if partial:
                r.writers = r.writers + [op]
            else:
                r.writers = [op]
            r.readers = {}
        if dma:
            assert key is not None
            self.keycount[key] = self.keycount.get(key, 0) + 16
            op.value = self.keycount[key]
            self.keylast[key] = op
        self.ops[eng].append(op)
        if excl:
            self.barrier()
        elif self.safe and not dma and self.safek > 0:
            self.safecnt += 1
            if self.safecnt % self.safek == 0:
                self.barrier()
        return op

    def pe32(self, fn, **kw):
        return self.add("pe", fn, f32=True, **kw)

    def pe16(self, out, fn, **kw):
        return self.add("pe", fn, out=out, **kw)

    def barrier(self):
        prods = []
        for e in ENGS:
            for o in reversed(self.ops[e]):
                if not o.dma and not o.barrier:
                    prods.append(o)
                    break
        prods += list(self.keylast.values())
        for e in ENGS:
            b = Op(e, None)
            b.barrier = True
            for p in prods:
                if p.dma or p.eng != e or e != "pe":
                    b.deps.append(p)
                    p.signal = True
            self.ops[e].append(b)
        for r in self.allres:
            r.writers = []
            r.readers = {}

    def emit(self, nc, es):
        esem = {e: es.enter_context(nc.semaphore("s_" + e)) for e in ENGS}
        ksem = {}
        for i, k in enumerate(self.keycount):
            ksem[k] = es.enter_context(nc.semaphore("k%d" % i))
        for e in ENGS:
            c = 0
            for o in self.ops[e]:
                if (not o.dma) and o.signal and not o.barrier:
                    c += 1
                    o.value = c
            if os.environ.get("SEMDBG"): print("SEM", e, "final", c, "nops", len(self.ops[e]))
        block = es.enter_context(nc.Block())
        hooks = {"pe": block.tensor, "act": block.scalar, "dve": block.vector,
                 "pool": block.gpsimd, "sp": block.sync}
        final_keys = dict(self.keycount)

        def mk(ename):
            def body(eng):
                waited = {}
                for o in self.ops[ename]:
                    need = {}
                    for p in o.deps:
                        s = ksem[p.key] if p.dma else esem[p.eng]
                        sid = id(s)
                        v = p.value
                        if waited.get(sid, 0) >= v:
                            continue
                        if sid not in need or need[sid][1] < v:
                            need[sid] = (s, v)
                    for sid, (s, v) in need.items():
                        eng.wait_ge(s, v)
                        waited[sid] = v
                    if o.fn is None:
                        continue
                    ins = o.fn(eng)
                    if o.dma:
                        ins.then_inc(ksem[o.key], 16)
                    elif o.signal:
                        ins.then_inc(esem[ename], 1)
                if ename == "sp":
                    for k, v in final_keys.items():
                        if waited.get(id(ksem[k]), 0) < v:
                            eng.wait_ge(ksem[k], v)
            return body

        for e in ENGS:
            hooks[e](mk(e))


def make_consts(S):
    i = np.arange(128)[:, None]
    j = np.arange(128)[None, :]
    same = (i // 64) == (j // 64)
    c = {}
    c["ident"] = np.eye(128, dtype=np.float32)
    c["caus01"] = (i <= j).astype(np.float32)
    c["mstrict"] = (same & (j < i)).astype(np.float32)
    c["negincl"] = (same & (j <= i)).astype(np.float32)
    LT = (same & (j <= i)).T.astype(np.float32)
    UT = (same & (j > i)).T.astype(np.float32)
    CS0 = np.zeros((128, 128), np.float32); CS0[:64, :] = 1.0
    CS1 = np.zeros((128, 128), np.float32); CS1[64:, :] = 1.0
    c["gl"] = np.concatenate([LT, UT, CS0, CS1], 1)
    offs = []
    for s in (1, 2, 4, 8, 16, 32):
        m = ((i // (2 * s)) == (j // (2 * s))) & ((i // s) != (j // s)) & (i > j)
        offs.append(m.T.astype(np.float32))
    c["boff"] = np.concatenate(offs, 1)
    c["aoff1"] = (((i // 2) == (j // 2)) & (i != j) & (i > j)).astype(np.float32)
    half = 8
    inv = 500000.0 ** (-np.arange(half, dtype=np.float32) / half)
    ang = np.arange(S, dtype=np.float32)[:, None] * inv[None, :]
    cos = np.cos(ang).astype(np.float32)
    sin = np.sin(ang).astype(np.float32)
    NT = S // 128
    cc = np.concatenate([cos, cos], 1).reshape(NT, 128, 16).transpose(1, 0, 2)
    ss = np.concatenate([sin, sin], 1).reshape(NT, 128, 16).transpose(1, 0, 2)
    c["rope"] = np.ascontiguousarray(np.concatenate([cc, ss], 2)).reshape(128, NT * 32)
    return c


def build(S=4096, L=2, dbg=False, stop_after=None):
    NT = S // 128
    NB = S // 256
    nc = bass.Bass("TRN2", target_bir_lowering=False)
    sc = Sched()
    es = ExitStack()

    def din(name, shape, dt=F32):
        return nc.dram_tensor(name, list(shape), dt, kind="ExternalInput").ap()

    def dscr(name, shape, dt=F32, out=False):
        kind = "ExternalOutput" if (out or dbg) else "Internal"
        return nc.dram_tensor(name, list(shape), dt, kind=kind).ap()

    x_d = din("x", [S, D])
    w_in_d = din("w_in", [L, D, NIN])
    gconv_d = din("gdn_conv_w", [L, 4, 1536])
    alog_d = din("gdn_a_log", [L, 4])
    dtb_d = din("gdn_dt_bias", [L, 4])
    gng_d = din("gdn_norm_g", [L, 128])
    w_out_d = din("w_out", [L, D, D])
    ln1g_d = din("ln1_g", [L, D])
    ln1b_d = din("ln1_b", [L, D])
    w_up_d = din("w_up", [L, D, 2 * DFF])
    fconvw_d = din("ffn_conv_w", [L, 3, 2 * DFF])
    fconvb_d = din("ffn_conv_b", [L, 2 * DFF])
    w_down_d = din("w_down", [L, DFF, D])
    ln2g_d = din("ln2_g", [L, D])
    ln2b_d = din("ln2_b", [L, D])
    c_ident_d = din("c_ident", [128, 128])
    c_caus_d = din("c_caus01", [128, 128])
    c_mstrict_d = din("c_mstrict", [128, 128])
    c_negincl_d = din("c_negincl", [128, 128])
    c_gl_d = din("c_gl", [128, 512])
    c_boff_d = din("c_boff", [128, 768])
    c_aoff1_d = din("c_aoff1", [128, 128])
    c_rope_d = din("c_rope", [128, NT * 32])

    y_d = dscr("y", [S, D], out=True)
    x1_d = dscr("x1res", [S, D])
    x2_d = dscr("x2res", [S, D]) if L > 1 else None
    oT_d = dscr("oT", [8, 128, S], BF16)
    wupbf_d = [nc.dram_tensor("wupbf%d" % l_, [44, 128, 8, 128], BF16, kind="Internal").ap() for l_ in range(L)]
    wupbf_r = [sc.res("wupbf%d" % l_) for l_ in range(L)]

    def sb(name, shape, dt=F32):
        return es.enter_context(nc.sbuf_tensor(name, list(shape), dt))

    def ps(name, shape, dt=F32):
        return es.enter_context(nc.psum_tensor(name, list(shape), dt))

    xT = sb("xT", [128, 8, S], BF16)
    xT_r = [sc.res("xT%d" % t) for t in range(NT)]
    ident_f = sb("ident_f", [128, 128]); ident_b = sb("ident_b", [128, 128], BF16)
    caus_b = sb("caus_b", [128, 128], BF16)
    rope = sb("rope", [128, NT, 32])
    r_const = sc.res("consts")
    sc.ident_b = ident_b
    cbias = sb("cbias", [128, 4])
    sc.add("dve", lambda e: e.memset(cbias[:, 0:1], 1e-6), writes=[r_const], partial=True)
    sc.add("dve", lambda e: e.memset(cbias[:, 1:2], 1.0), writes=[r_const], partial=True)
    sc.add("dve", lambda e: e.memset(cbias[:, 2:3], 1e-5), writes=[r_const], partial=True)
    sc.add("dve", lambda e: e.memset(cbias[:, 3:4], 0.0), writes=[r_const], partial=True)

    bank = [ps("bank%d" % i, [128, 512]) for i in range(8)]
    bank_r = [sc.res("bank%d" % i) for i in range(8)]

    ARENA = 136 * 1024 // 4
    arena = sb("arena", [128, ARENA])

    class Carver:
        def __init__(self):
            self.off = 0

        def reset(self):
            self.off = 0

        def get(self, shape, dt=F32):
            n = int(np.prod(shape[1:]))
            nwords = n if dt == F32 else (n + 1) // 2
            a = arena[0:shape[0], self.off:self.off + nwords]
            self.off += (nwords + 15) // 16 * 16
            assert self.off <= ARENA, "arena overflow %d" % self.off
            if dt != F32:
                a = a.bitcast(dt)[:, 0:n]
            if len(shape) > 2:
                names = " ".join("d%d" % k for k in range(len(shape) - 1))
                kw = {"d%d" % k: shape[k + 1] for k in range(len(shape) - 2)}
                a = a.rearrange("p (%s) -> p %s" % (names, names), **kw)
            return a

    cv = Carver()

    sc.add("sp", lambda e: e.dma_start(out=ident_f[:], in_=c_ident_d[:, :]), writes=[r_const], dma=True, key="c0", partial=True)
    sc.add("pool", lambda e: e.dma_start(out=ident_b[:], in_=c_ident_d[:, :]), writes=[r_const], dma=True, key="c1", partial=True)
    sc.add("pool", lambda e: e.dma_start(out=caus_b[:], in_=c_caus_d[:, :]), writes=[r_const], dma=True, key="c1", partial=True)
    sc.add("sp", lambda e: e.dma_start(out=rope[:].rearrange("p t c -> p (t c)"), in_=c_rope_d[:, :]), writes=[r_const], dma=True, key="c0", partial=True)

    def phase0(src_d):
        cv.reset()
        xb = [cv.get([128, 1024], BF16) for _ in range(3)]
        xb_r = [sc.res("xb%d" % i) for i in range(3)]
        pst = [bank[0][:].bitcast(BF16), bank[1][:].bitcast(BF16)]
        for t in range(NT):
            s = t % 3
            sc.add("pool", lambda e, t=t, s=s: e.dma_start(out=xb[s], in_=src_d[t * 128:(t + 1) * 128, :]),
                   writes=[xb_r[s]], dma=True, key="xb%d" % s)
            p = t % 2
            pt = pst[p].rearrange("p (k c) -> p k c", k=8)
            for kc in range(8):
                sc.add("pe", lambda e, kc=kc, s=s, pt=pt: e.transpose(out=pt[:, kc, :], in_=xb[s][:, kc * 128:(kc + 1) * 128], identity=ident_b[:]),
                       reads=[xb_r[s], r_const], writes=[bank_r[p]])
            if t % 2 == 0:
                sc.add("act", lambda e, t=t, pt=pt: e.copy(out=xT[:, :, t * 128:(t + 1) * 128], in_=pt),
                       reads=[bank_r[p]], writes=[xT_r[t]])
            else:
                sc.add("dve", lambda e, t=t, pt=pt: e.tensor_copy(out=xT[:, :, t * 128:(t + 1) * 128], in_=pt),
                       reads=[bank_r[p]], writes=[xT_r[t]])

    def phaseA(l):
        cv.reset()
        wA = cv.get([128, 8, 1536], BF16)
        wA_r = [sc.res("wA%d" % k) for k in range(8)]
        KT = cv.get([128, 4, S], BF16)
        KT_r = [sc.res("KT%d" % t) for t in range(NT)]
        Vp = cv.get([128, NT, 8, 65], BF16)
        Vp_r = [sc.res("Vp%d" % t) for t in range(NT)]
        QT = [cv.get([128, 4, 256], BF16) for _ in range(2)]
        QT_r = [sc.res("QT%d" % i) for i in range(2)]
        kmT = cv.get([128, 4, 16], BF16)
        kmf = cv.get([128, 4])
        kmT_r = sc.res("kmT")
        qb = [cv.get([128, 512], BF16) for _ in range(2)]
        kb = [cv.get([128, 512], BF16) for _ in range(2)]
        qb_r = [sc.res("qb%d" % i) for i in range(2)]
        kb_r = [sc.res("kb%d" % i) for i in range(2)]
        t1 = cv.get([128, 8, 16]); t2 = cv.get([128, 8, 16])
        t1_r = sc.res("t1"); t2_r = sc.res("t2")
        gsb = cv.get([128, 16, 16]); m8 = cv.get([128, 16, 8]); sel = cv.get([128, 16, 16])
        gsb_r = sc.res("gsb"); sel_r = sc.res("sel")
        NPT = 4
        PT = [cv.get([128, 2, 256], BF16) for _ in range(NPT)]
        PT_r = [sc.res("PT%d" % i) for i in range(NPT)]
        acc = cv.get([128, 2, 8, 65])
        acc_r = [[sc.res("acc%d_%d" % (q, h)) for h in range(8)] for q in range(2)]
        rec = cv.get([128, 16])
        ob = cv.get([128, 2, 512], BF16)
        ob_r = sc.res("ob")
        obT = [cv.get([128, 4, 256], BF16) for _ in range(2)]
        obT_r = [sc.res("obT%d" % i) for i in range(2)]

        for kc in range(8):
            sc.add("pool", lambda e, kc=kc: e.dma_start(out=wA[:, kc, :], in_=w_in_d[l, kc * 128:(kc + 1) * 128, 2056:3592]),
                   writes=[wA_r[kc]], dma=True, key="wA%d" % kc)
        sc.add("pool", lambda e: e.memset(Vp[:, :, :, 64:65], 1.0), writes=Vp_r)
        sc.add("pool", lambda e: e.memset(gsb[:], -1e30), writes=[gsb_r])
        sc.add("pool", lambda e: e.memset(kmT[:], 0.0), writes=[kmT_r])

        pq, pk, pv = bank[0], bank[1], bank[2]
        ptr = bank[3][:].bitcast(BF16).rearrange("p (k c) -> p k c", k=8)
        pg0 = bank[4][:, 0:128].rearrange("p (a b) -> p a b", a=8)
        pg1 = bank[6][:, 0:128].rearrange("p (a b) -> p a b", a=8)
        SB = (0, 1, 2, 5)
        OB = (6, 7)
        cnt = {"s": 0, "o": 0, "pt": 0, "ev": 0}


        KSTOP = int(os.environ.get("KSTOP", "99"))
        for t in range(NT):
            b = t // 2
            qt_ = t % 2
            tsl = slice(t * 128, (t + 1) * 128)
            if KSTOP <= 0:
                break
            for g, pp in enumerate((pq, pk, pv)):
                for kc in range(8):
                    sc.add("pe", lambda e, g=g, kc=kc, pp=pp, tsl=tsl: e.matmul(pp[:], lhsT=xT[:, kc, tsl], rhs=wA[:, kc, g * 512:(g + 1) * 512], start=(kc == 0), stop=(kc == 7)),
                           reads=[xT_r[t], wA_r[kc]], writes=[bank_r[g]])
            if KSTOP <= 1:
                continue
            sc.add("act", lambda e, t=t: e.copy(out=Vp[:, t, :, 0:64], in_=pv[:].rearrange("p (h d) -> p h d", h=8)),
                   reads=[bank_r[2]], writes=[Vp_r[t]])
            s2 = t % 2
            for (pp, dst, dst_r, bi) in ((pq, qb[s2], qb_r[s2], 0), (pk, kb[s2], kb_r[s2], 1)):
                p3 = pp[:].rearrange("p (h d) -> p h d", h=8)
                d3 = dst.rearrange("p (h d) -> p h d", h=8)
                sc.add("act", lambda e, p3=p3, d3=d3: e.copy(out=d3[:, :, 16:64], in_=p3[:, :, 16:64]),
                       reads=[bank_r[bi]], writes=[dst_r])
                ccb = rope[:, t, 0:16].unsqueeze(1).to_broadcast([128, 8, 16])
                ssb = rope[:, t, 16:32].unsqueeze(1).to_broadcast([128, 8, 16])
                sc.add("dve", lambda e, p3=p3, ccb=ccb: e.tensor_tensor(out=t1, in0=p3[:, :, 0:16], in1=ccb, op=ALU.mult),
                       reads=[bank_r[bi], r_const], writes=[t1_r])
                sc.add("dve", lambda e, p3=p3, ssb=ssb: e.tensor_tensor(out=t2, in0=p3[:, :, 0:16], in1=ssb, op=ALU.mult),
                       reads=[bank_r[bi], r_const], writes=[t2_r])
                sc.add("dve", lambda e, d3=d3: e.tensor_tensor(out=d3[:, :, 0:8], in0=t1[:, :, 0:8], in1=t2[:, :, 8:16], op=ALU.subtract),
                       reads=[t1_r, t2_r], writes=[dst_r], partial=True)
                sc.add("dve", lambda e, d3=d3: e.tensor_tensor(out=d3[:, :, 8:16], in0=t1[:, :, 8:16], in1=t2[:, :, 0:8], op=ALU.add),
                       reads=[t1_r, t2_r], writes=[dst_r], partial=True)
            if KSTOP <= 2:
                continue
            for j in range(4):
                sc.add("pe", lambda e, j=j, s2=s2: e.transpose(out=ptr[:, j, :], in_=qb[s2][:, j * 128:(j + 1) * 128], identity=ident_b[:]),
                       reads=[qb_r[s2], r_const], writes=[bank_r[3]])
            for j in range(4):
                sc.add("pe", lambda e, j=j, s2=s2: e.transpose(out=ptr[:, 4 + j, :], in_=kb[s2][:, j * 128:(j + 1) * 128], identity=ident_b[:]),
                       reads=[kb_r[s2], r_const], writes=[bank_r[3]])
            qs = b % 2
            if KSTOP == 3 and os.environ.get("KSUB") == "a":
                continue
            sc.add("dve", lambda e, qs=qs, qt_=qt_: e.tensor_copy(out=QT[qs][:, :, qt_ * 128:(qt_ + 1) * 128], in_=ptr[:, 0:4, :]),
                   reads=[bank_r[3]], writes=[QT_r[qs]], partial=(qt_ == 1))
            if KSTOP == 3 and os.environ.get("KSUB") == "b":
                continue
            sc.add("dve", lambda e, tsl=tsl: e.tensor_copy(out=KT[:, :, tsl], in_=ptr[:, 4:8, :]),
                   reads=[bank_r[3]], writes=[KT_r[t]])
            if qt_ == 0 or KSTOP <= 3:
                continue
            if b + 1 < NB:
                sc.add("dve", lambda e, b=b: e.tensor_reduce(out=kmf, in_=KT[:, :, b * 256:(b + 1) * 256], axis=AX.X, op=ALU.add),
                       reads=[KT_r[t - 1], KT_r[t]], writes=[kmT_r])
                sc.add("dve", lambda e, b=b: e.tensor_scalar(out=kmT[:, :, b], in0=kmf, scalar1=1.0 / 256, scalar2=None, op0=ALU.mult),
                       reads=[kmT_r], writes=[kmT_r], partial=True)
            topk = b > 3
            if topk:
                KV_ = os.environ.get("KVAR", "")
                for par in range(2):
                    pgp = (pg0, pg1)[par]
                    for q2 in range(2):
                        for hh in range(4):
                            if KV_ == "q0" and q2 == 1: continue
                            if KV_ == "p0" and par == 1: continue
                            if KV_ == "h0" and hh > 0: continue
                            base = par * 64
                            sc.add("pe", lambda e, pgp=pgp, q2=q2, hh=hh, base=base, qs=qs: e.matmul(pgp[:, q2 * 4 + hh, :], lhsT=QT[qs][base:base + 64, hh, q2 * 128:(q2 + 1) * 128], rhs=kmT[base:base + 64, hh, :], start=True, stop=True),
                                   reads=[QT_r[qs], kmT_r], writes=[bank_r[(4, 6)[par]]])
                KT_ = os.environ.get("KTOPK", "full")
                if KT_ in ("gc", "gcm", "full"):
                    sc.add("dve", lambda e, b=b: e.tensor_copy(out=gsb[:, 0:8, 0:b], in_=pg0[:, :, 0:b]), reads=[bank_r[4]], writes=[gsb_r])
                    sc.add("dve", lambda e, b=b: e.tensor_copy(out=gsb[:, 8:16, 0:b], in_=pg1[:, :, 0:b]), reads=[bank_r[6]], writes=[gsb_r], partial=True)
                if KT_ in ("gcm", "full"):
                    for i16 in range(16):
                        sc.add("dve", lambda e, i16=i16: e.max(out=m8[:, i16, :], in_=gsb[:, i16, :]), reads=[gsb_r], writes=[sel_r], partial=True)
                if KT_ == "full":
                    sc.add("dve", lambda e: e.tensor_tensor(out=sel[:], in0=gsb[:], in1=m8[:, :, 2:3].to_broadcast([128, 16, 16]), op=ALU.is_ge),
                           reads=[gsb_r, sel_r], writes=[sel_r])
                else:
                    sc.add("dve", lambda e: e.memset(sel[:], 1.0), reads=[gsb_r, bank_r[4]], writes=[sel_r])
            units = [(h, n) for h in range(8 if KSTOP > 4 else 0) for n in ([b] + list(range(b)))]
            ust = {}

            def emit_st(u, b=b, qs=qs):
                h, n = u
                j = h // 2; base = (h % 2) * 64
                si = SB[cnt["s"] % 4]; cnt["s"] += 1
                pi = cnt["pt"] % NPT; cnt["pt"] += 1
                pss = bank[si][:].rearrange("p (k q) -> p k q", k=2)
                for kt in range(2):
                    ktile = 2 * n + kt
                    sc.add("pe", lambda e, pss=pss, kt=kt, j=j, base=base, ktile=ktile, qs=qs: e.matmul(pss[:, kt, :], lhsT=KT[base:base + 64, j, ktile * 128:(ktile + 1) * 128], rhs=QT[qs][base:base + 64, j, :], start=True, stop=True),
                           reads=[KT_r[ktile], QT_r[qs]], writes=[bank_r[si]])
                sc.add("act", lambda e, pss=pss, pi=pi: e.activation(out=PT[pi][:], in_=pss, func=AF.Exp, scale=0.125, bias=cbias[:, 3:4]),
                       reads=[bank_r[si], r_const], writes=[PT_r[pi]])
                if n == b:
                    for kt in range(2):
                        sc.add("pool", lambda e, pi=pi, kt=kt: e.tensor_tensor(out=PT[pi][:, kt, kt * 128:(kt + 1) * 128], in0=PT[pi][:, kt, kt * 128:(kt + 1) * 128], in1=caus_b[:], op=ALU.mult),
                               reads=[PT_r[pi], r_const], writes=[PT_r[pi]])
                ust[u] = pi

            def emit_pv(u, b=b, topk=topk):
                h, n = u
                pi = ust.pop(u)
                oi = OB[cnt["o"] % 2]; cnt["o"] += 1
                pso = bank[oi][:, 0:130].rearrange("p (q d) -> p q d", q=2)
                for q2 in range(2):
                    kts = [0] if (n == b and q2 == 0) else [0, 1]
                    for ii, kt in enumerate(kts):
                        sc.add("pe", lambda e, pso=pso, pi=pi, q2=q2, kt=kt, n=n, h=h, ii=ii, last=(ii == len(kts) - 1): e.matmul(pso[:, q2, :], lhsT=PT[pi][:, kt, q2 * 128:(q2 + 1) * 128], rhs=Vp[:, 2 * n + kt, h, :], start=(ii == 0), stop=last),
                               reads=[PT_r[pi], Vp_r[2 * n + kt]], writes=[bank_r[oi]])
                for q2 in range(2):
                    if n == b:
                        sc.add("dve", lambda e, pso=pso, q2=q2, h=h: e.tensor_copy(out=acc[:, q2, h, :], in_=pso[:, q2, :]),
                               reads=[bank_r[oi]], writes=[acc_r[q2][h]])
                    elif topk:
                        sc.add("dve", lambda e, pso=pso, q2=q2, h=h, n=n: e.scalar_tensor_tensor(out=acc[:, q2, h, :], in0=pso[:, q2, :], scalar=sel[:, (h % 2) * 8 + q2 * 4 + h // 2, n:n + 1], in1=acc[:, q2, h, :], op0=ALU.mult, op1=ALU.add),
                               reads=[bank_r[oi], sel_r, acc_r[q2][h]], writes=[acc_r[q2][h]])
                    else:
                        sc.add("dve", lambda e, pso=pso, q2=q2, h=h: e.tensor_tensor(out=acc[:, q2, h, :], in0=pso[:, q2, :], in1=acc[:, q2, h, :], op=ALU.add),
                               reads=[bank_r[oi], acc_r[q2][h]], writes=[acc_r[q2][h]])

            LOOK = 2
            for i in range(min(LOOK, len(units))):
                emit_st(units[i])
            for i, u in enumerate(units):
                if i + LOOK < len(units):
                    emit_st(units[i + LOOK])
                emit_pv(u)
            if KSTOP <= 5:
                continue
            allacc = [acc_r[q][h] for q in range(2) for h in range(8)]
            sc.add("dve", lambda e: e.reciprocal(out=rec, in_=acc[:].rearrange("p q h d -> p (q h) d")[:, :, 64]), reads=allacc, writes=[ob_r])
            sc.add("dve", lambda e: e.tensor_tensor(out=ob[:].rearrange("p q (h d) -> p (q h) d", h=8), in0=acc[:].rearrange("p q h d -> p (q h) d")[:, :, 0:64], in1=rec.unsqueeze(2).to_broadcast([128, 16, 64]), op=ALU.mult),
                   reads=allacc + [ob_r], writes=[ob_r])
            os_ = b % 2
            for q2 in range(2):
                for j in range(4):
                    sc.add("pe", lambda e, q2=q2, j=j: e.transpose(out=ptr[:, q2 * 4 + j, :], in_=ob[:, q2, j * 128:(j + 1) * 128], identity=ident_b[:]),
                           reads=[ob_r, r_const], writes=[bank_r[3]])
            sc.add("dve", lambda e, os_=os_: e.tensor_copy(out=obT[os_][:].rearrange("p j (q c) -> p q j c", q=2), in_=ptr.rearrange("p (q j) c -> p q j c", q=2)),
                   reads=[bank_r[3]], writes=[obT_r[os_]])
            sc.add("sp", lambda e, os_=os_, b=b: e.dma_start(out=oT_d[4:8, :, b * 256:(b + 1) * 256].rearrange("j p c -> p j c"), in_=obT[os_][:]),
                   reads=[obT_r[os_]], writes=[], dma=True, key="obT%d" % os_)

    def phaseB(l):
        cv.reset()
        for _ in range(int(os.environ.get("ACTPAD", "0"))):
            sc.add("act", lambda e: e.copy(out=arena[:, 0:8], in_=ident_f[:, 0:8]))
        NBLK = S // 512
        wB = cv.get([128, 8, 2056], BF16)
        wB_r = [sc.res("wB%d" % k) for k in range(8)]
        gcw = cv.get([128, 12, 4]); dtb = cv.get([128, 4]); nexpA = cv.get([128, 4]); gng = cv.get([128, 128])
        mstrict = cv.get([128, 128]); mincl = cv.get([128, 128]); glc = cv.get([128, 4, 128])
        boff = cv.get([128, 6, 128]); aoff1 = cv.get([128, 128]); ones_f = cv.get([128, 128])
        pc_r = sc.res("pconst")
        rawb = cv.get([128, 2, 515]); rawb_r = [sc.res("rawb%d" % f) for f in range(2)]
        halo = cv.get([128, 12, 3]); halo_r = [sc.res("halo%d" % f) for f in range(12)]
        cacc = [cv.get([128, 512]) for _ in range(2)]; cacc_r = [sc.res("cacc%d" % i) for i in range(2)]
        cT = cv.get([128, 12, 512]); cT_r = [sc.res("cT%d" % f) for f in range(12)]
        Sst = [cv.get([128, 4, 128]) for _ in range(2)]
        S_r = [[sc.res("S%d_%d" % (i, h)) for h in range(4)] for i in range(2)]

        def tmp(name, shape, dt=F32, n=2):
            return [(cv.get(shape, dt), sc.res("%s%d" % (name, i))) for i in range(n)]

        Qtm = tmp("Qtm", [128, 4, 128]); Ktm = tmp("Ktm", [128, 4, 128]); Vtm = tmp("Vtm", [128, 4, 128])
        ssq = tmp("ssq", [128, 8]); rn = tmp("rn", [128, 8])
        sm = tmp("sm", [128, 64])
        zs = tmp("zs", [128, 512])
        junk = tmp("junk", [128, 128], n=1)[0]
        HT = 2
        QTh = tmp("QTh", [128, 128], F32, HT); KTh = tmp("KTh", [128, 128], F32, HT)
        GR = tmp("GR", [128, 128], F32, HT); Dm = tmp("Dm", [128, 128], F32, HT); Ds = tmp("Ds", [128, 128], F32, HT)
        Am = tmp("Am", [128, 128], F32, 2 * HT); attn = tmp("attn", [128, 128], F32, 2 * HT); attnT = tmp("attnT", [128, 128], F32, HT)
        Boall = tmp("Boall", [128, 6, 128], F32, HT); Em = tmp("Em", [128, 128], F32, 2 * HT); Dk = tmp("Dk", [128, 128], F32, 2 * HT)
        Xm = tmp("Xm", [128, 128], F32, HT); Rm = tmp("Rm", [128, 256], F32, HT); UW = tmp("UW", [128, 256], F32, HT)
        Kd = tmp("Kd", [128, 128], F32, HT); Qd = tmp("Qd", [128, 128], F32, HT); QpT = tmp("QpT", [128, 128], F32, HT)
        MpT = tmp("MpT", [128, 2, 128], F32, HT)
        osb = tmp("osb", [128, 4, 128], F32, 2); oss = tmp("oss", [128, 8], F32, 2)
        oab = tmp("oab", [128, 512], BF16, 2); oaT = tmp("oaT", [128, 4, 128], BF16, 2)
        ctr = {}

        def nxt(lst, key):
            i = ctr.get(key, 0); ctr[key] = i + 1
            return lst[i % len(lst)]

        PB = tuple(int(x) for x in os.environ.get("PB", "2,3,4,6,7").split(","))

        def pbank():
            i = ctr.get("pb", 0); ctr["pb"] = i + 1
            bi = PB[i % len(PB)]
            return bank[bi], bank_r[bi]

        for kc in range(8):
            sc.add("pool", lambda e, kc=kc: e.dma_start(out=wB[:, kc, :], in_=w_in_d[l, kc * 128:(kc + 1) * 128, 0:2056]),
                   writes=[wB_r[kc]], dma=True, key="wB%d" % kc)
        w4 = cT.rearrange("p f t -> p (f t)")[0:4, 0:1536]; w4_r = sc.res("w4")
        sc.add("sp", lambda e: e.dma_start(out=w4, in_=gconv_d[l, :, :]), writes=[w4_r, cT_r[0], cT_r[1], cT_r[2]], dma=True, key="w4")
        for f in range(12):
            sc.pe32(lambda e, f=f: e.transpose(out=bank[0][:, f * 4:(f + 1) * 4], in_=w4[0:4, f * 128:(f + 1) * 128], identity=ident_f[0:4, 0:4]), reads=[w4_r, cT_r[0], cT_r[1], cT_r[2], r_const], writes=[bank_r[0]])
        sc.add("dve", lambda e: e.tensor_copy(out=gcw.rearrange("p f j -> p (f j)"), in_=bank[0][:, 0:48]), reads=[bank_r[0]], writes=[pc_r], partial=True)
        sc.add("sp", lambda e: e.dma_start(out=dtb, in_=dtb_d[l, :].partition_broadcast(128)), writes=[pc_r], dma=True, key="pc", partial=True)
        sc.add("sp", lambda e: e.dma_start(out=nexpA, in_=alog_d[l, :].partition_broadcast(128)), writes=[pc_r], dma=True, key="pc", partial=True)
        sc.add("sp", lambda e: e.dma_start(out=gng, in_=gng_d[l, :].partition_broadcast(128)), writes=[pc_r], dma=True, key="pc", partial=True)
        sc.add("sp", lambda e: e.dma_start(out=mstrict, in_=c_mstrict_d[:, :]), writes=[pc_r], dma=True, key="pc", partial=True)
        sc.add("sp", lambda e: e.dma_start(out=glc.rearrange("p a b -> p (a b)"), in_=c_gl_d[:, :]), writes=[pc_r], dma=True, key="pc", partial=True)
        sc.add("sp", lambda e: e.dma_start(out=boff.rearrange("p a b -> p (a b)"), in_=c_boff_d[:, :]), writes=[pc_r], dma=True, key="pc", partial=True)
        sc.add("sp", lambda e: e.dma_start(out=aoff1, in_=c_aoff1_d[:, :]), writes=[pc_r], dma=True, key="pc", partial=True)
        sc.add("sp", lambda e: e.dma_start(out=mincl, in_=c_negincl_d[:, :]), writes=[pc_r], dma=True, key="pc", partial=True)
        sc.add("act", lambda e: e.activation(out=nexpA, in_=nexpA, func=AF.Exp, bias=cbias[:, 3:4]), reads=[pc_r], writes=[pc_r])
        sc.add("dve", lambda e: e.tensor_scalar(out=nexpA, in0=nexpA, scalar1=-1.0, scalar2=None, op0=ALU.mult), reads=[pc_r], writes=[pc_r])
        sc.add("dve", lambda e: e.memset(ones_f, 1.0), writes=[pc_r], reads=[pc_r])
        sc.add("dve", lambda e: e.memset(Sst[0][:], 0.0), writes=S_r[0])
        sc.add("pool", lambda e: e.memset(halo, 0.0), writes=halo_r)
        LTc, UTc, CS0c, CS1c = (glc[:, i, :] for i in range(4))

        cur = 0

        KB = int(os.environ.get("KB", "99"))
        for c in range(NBLK):
            csl = slice(c * 512, (c + 1) * 512)
            for f in range(12):
                pb, pb_r = bank[f % 2], bank_r[f % 2]
                for kc in range(8):
                    sc.pe16(pb[:], lambda e, pb=pb, f=f, kc=kc, csl=csl: e.matmul(pb[:], lhsT=wB[:, kc, f * 128:(f + 1) * 128], rhs=xT[:, kc, csl], start=(kc == 0), stop=(kc == 7)),
                           reads=[wB_r[kc]] + xT_r[4 * c:4 * c + 4], writes=[pb_r])
                rs = f % 2
                sc.add("pool", lambda e, f=f, rs=rs: e.tensor_copy(out=rawb[:, rs, 0:3], in_=halo[:, f, :]), reads=[halo_r[f]], writes=[rawb_r[rs]])
                sc.add("act", lambda e, pb=pb, rs=rs: e.copy(out=rawb[:, rs, 3:515], in_=pb[:]), reads=[pb_r], writes=[rawb_r[rs]], partial=True)
                sc.add("pool", lambda e, f=f, rs=rs: e.tensor_copy(out=halo[:, f, :], in_=rawb[:, rs, 512:515]), reads=[rawb_r[rs]], writes=[halo_r[f]])
                ca, ca_r = cacc[f % 2], cacc_r[f % 2]
                sc.add("act", lambda e, pb=pb, f=f, ca=ca: e.activation(out=ca, in_=pb[:], func=AF.Copy, scale=gcw[:, f, 3:4]), reads=[pb_r, pc_r], writes=[ca_r])
                for j in (2, 1, 0):
                    sc.add("dve", lambda e, f=f, j=j, ca=ca, rs=rs: e.scalar_tensor_tensor(out=ca, in0=rawb[:, rs, j:j + 512], scalar=gcw[:, f, j:j + 1], in1=ca, op0=ALU.mult, op1=ALU.add),
                           reads=[rawb_r[rs], pc_r, ca_r], writes=[ca_r])
                sc.add("act", lambda e, f=f, ca=ca: e.activation(out=cT[:, f, :], in_=ca, func=AF.Silu, bias=cbias[:, 3:4]), reads=[ca_r], writes=[cT_r[f]])
            for tt in range(4 if KB > 1 else 0):
                t = c * 4 + tt
                tsl = slice(t * 128, (t + 1) * 128)
                lsl = slice(tt * 128, (tt + 1) * 128)
                (Qt, Qt_r) = nxt(Qtm, "Qtm"); (Kt, Kt_r) = nxt(Ktm, "Ktm"); (Vt, Vt_r) = nxt(Vtm, "Vtm")
                (sq, sq_r) = nxt(ssq, "ssq"); (rnn, rn_r) = nxt(rn, "rn"); (smt, sm_r) = nxt(sm, "sm"); (zst, zs_r) = nxt(zs, "zs")
                for g in range(3):
                    pb, pb_r = bank[2 + g], bank_r[2 + g]
                    for h in range(4):
                        sc.pe32(lambda e, pb=pb, g=g, h=h, lsl=lsl: e.transpose(out=pb[:, h * 128:(h + 1) * 128], in_=cT[:, g * 4 + h, lsl], identity=ident_f[:]),
                               reads=[cT_r[g * 4 + h], r_const], writes=[pb_r])
                for g in range(2):
                    for h in range(4):
                        sc.add("act", lambda e, g=g, h=h, sq=sq: e.activation(out=junk[0], in_=bank[2 + g][:, h * 128:(h + 1) * 128], func=AF.Square, bias=cbias[:, 3:4], accum_out=sq[:, g * 4 + h:g * 4 + h + 1]),
                               reads=[bank_r[2 + g]], writes=[sq_r, junk[1]], partial=True)
                sc.add("act", lambda e, sq=sq, rnn=rnn: e.activation(out=rnn, in_=sq, func=AF.Sqrt, bias=cbias[:, 0:1]), reads=[sq_r, r_const], writes=[rn_r])
                sc.add("dve", lambda e, rnn=rnn: e.reciprocal(out=rnn, in_=rnn), reads=[rn_r], writes=[rn_r])
                sc.add("dve", lambda e, rnn=rnn: e.tensor_scalar(out=rnn[:, 0:4], in0=rnn[:, 0:4], scalar1=float(128 ** -0.5), scalar2=None, op0=ALU.mult), reads=[rn_r], writes=[rn_r])
                sc.add("dve", lambda e, Qt=Qt, rnn=rnn: e.tensor_tensor(out=Qt, in0=bank[2][:].rearrange("p (h d) -> p h d", h=4), in1=rnn[:, 0:4].unsqueeze(2).to_broadcast([128, 4, 128]), op=ALU.mult),
                       reads=[bank_r[2], rn_r], writes=[Qt_r])
                sc.add("dve", lambda e, Kt=Kt, rnn=rnn: e.tensor_tensor(out=Kt, in0=bank[3][:].rearrange("p (h d) -> p h d", h=4), in1=rnn[:, 4:8].unsqueeze(2).to_broadcast([128, 4, 128]), op=ALU.mult),
                       reads=[bank_r[3], rn_r], writes=[Kt_r])
                sc.add("act", lambda e, Vt=Vt: e.copy(out=Vt, in_=bank[4][:].rearrange("p (h d) -> p h d", h=4)), reads=[bank_r[4]], writes=[Vt_r])
                if KB <= 2:
                    continue
                pab, pab_r = bank[5], bank_r[5]
                for kc in range(8):
                    sc.pe16(bank[5][:, 0:8], lambda e, kc=kc, tsl=tsl: e.matmul(bank[5][:, 0:8], lhsT=xT[:, kc, tsl], rhs=wB[:, kc, 1536:1544], start=(kc == 0), stop=(kc == 7)),
                           reads=[xT_r[t], wB_r[kc]], writes=[pab_r])
                sc.add("dve", lambda e, smt=smt: e.tensor_tensor(out=smt[:, 0:4], in0=bank[5][:, 0:4], in1=dtb, op=ALU.add), reads=[pab_r, pc_r], writes=[sm_r])
                sc.add("dve", lambda e, smt=smt: e.tensor_scalar(out=smt[:, 36:40], in0=smt[:, 0:4], scalar1=-1.0, scalar2=None, op0=ALU.mult), reads=[sm_r], writes=[sm_r])
                sc.add("dve", lambda e, smt=smt: e.tensor_tensor(out=smt[:, 4:8], in0=smt[:, 0:4], in1=smt[:, 36:40], op=ALU.min), reads=[sm_r], writes=[sm_r])
                sc.add("act", lambda e, smt=smt: e.activation(out=smt[:, 4:8], in_=smt[:, 4:8], func=AF.Exp, bias=cbias[:, 3:4]), reads=[sm_r], writes=[sm_r])
                sc.add("act", lambda e, smt=smt: e.activation(out=smt[:, 4:8], in_=smt[:, 4:8], func=AF.Ln, bias=cbias[:, 1:2]), reads=[sm_r, r_const], writes=[sm_r])
                sc.add("dve", lambda e, smt=smt: e.scalar_tensor_tensor(out=smt[:, 8:12], in0=smt[:, 0:4], scalar=0.0, in1=smt[:, 4:8], op0=ALU.max, op1=ALU.add), reads=[sm_r], writes=[sm_r])
                sc.add("dve", lambda e, smt=smt: e.tensor_tensor(out=smt[:, 8:12], in0=smt[:, 8:12], in1=nexpA, op=ALU.mult), reads=[sm_r, pc_r], writes=[sm_r])
                sc.add("act", lambda e, smt=smt: e.activation(out=smt[:, 12:16], in_=bank[5][:, 4:8], func=AF.Exp, scale=-1.0, bias=cbias[:, 3:4]), reads=[pab_r], writes=[sm_r])
                sc.add("dve", lambda e, smt=smt: e.tensor_scalar(out=smt[:, 12:16], in0=smt[:, 12:16], scalar1=1.0, scalar2=None, op0=ALU.add), reads=[sm_r], writes=[sm_r])
                sc.add("dve", lambda e, smt=smt: e.reciprocal(out=smt[:, 12:16], in_=smt[:, 12:16]), reads=[sm_r], writes=[sm_r])
                for kc in range(8):
                    sc.pe16(bank[5][:], lambda e, kc=kc, tsl=tsl: e.matmul(bank[5][:], lhsT=xT[:, kc, tsl], rhs=wB[:, kc, 1544:2056], start=(kc == 0), stop=(kc == 7)),
                           reads=[xT_r[t], wB_r[kc]], writes=[pab_r])
                sc.add("act", lambda e, zst=zst: e.activation(out=zst, in_=bank[5][:], func=AF.Silu, bias=cbias[:, 3:4]), reads=[pab_r], writes=[zs_r])
                sc.add("pool", lambda e, zst=zst: e.tensor_tensor(out=zst.rearrange("p (h d) -> p h d", h=4), in0=zst.rearrange("p (h d) -> p h d", h=4), in1=gng.unsqueeze(1).to_broadcast([128, 4, 128]), op=ALU.mult),
                       reads=[zs_r, pc_r], writes=[zs_r])
                for i4, lt in enumerate((LTc, UTc, CS0c, CS1c)):
                    sc.pe32(lambda e, i4=i4, lt=lt, smt=smt: e.matmul(bank[5][:, 16 + 4 * i4:20 + 4 * i4], lhsT=lt, rhs=smt[:, 8:12], start=True, stop=True),
                           reads=[sm_r, pc_r], writes=[pab_r])
                sc.add("dve", lambda e, smt=smt: e.tensor_copy(out=smt[:, 40:44], in_=bank[5][:, 16:20]), reads=[pab_r], writes=[sm_r])
                sc.add("dve", lambda e, smt=smt: e.tensor_scalar(out=smt[:, 16:32], in0=bank[5][:, 16:32], scalar1=-60.0, scalar2=None, op0=ALU.max), reads=[pab_r], writes=[sm_r])
                sc.add("act", lambda e, smt=smt: e.activation(out=smt[:, 16:32], in_=smt[:, 16:32], func=AF.Exp, bias=cbias[:, 3:4]), reads=[sm_r], writes=[sm_r])
                sc.add("dve", lambda e, smt=smt: e.tensor_tensor(out=smt[:, 32:36], in0=smt[:, 12:16], in1=smt[:, 16:20], op=ALU.mult), reads=[sm_r], writes=[sm_r])
                (osb_t, osb_r) = nxt(osb, "osb"); (oss_t, oss_r) = nxt(oss, "oss")
                if KB <= 3:
                    continue
                sc.safe = os.environ.get("SAFE", "1") == "1"
                for h in range(4):
                    (QT_, QT_r_) = nxt(QTh, "QTh"); (KT_, KT_r_) = nxt(KTh, "KTh"); (GR_, GR_r_) = nxt(GR, "GR")
                    (D_, D_r) = nxt(Dm, "Dm"); (Ds_, Ds_r) = nxt(Ds, "Ds"); (A_, A_r) = nxt(Am, "Am")
                    (at_, at_r) = nxt(attn, "attn"); (atT_, atT_r) = nxt(attnT, "attnT"); (Bo_, Bo_r) = nxt(Boall, "Bo")
                    (X_, X_r) = nxt(Xm, "X"); (R_, R_r) = nxt(Rm, "R"); (UW_, UW_r) = nxt(UW, "UW")
                    (Kd_, Kd_r) = nxt(Kd, "Kd"); (Qd_, Qd_r) = nxt(Qd, "Qd"); (QpT_, QpT_r) = nxt(QpT, "QpT"); (Mp_, Mp_r) = nxt(MpT, "MpT")
                    beta_h = smt[:, 12 + h:13 + h]; egc_h = smt[:, 16 + h:17 + h]; egu_h = smt[:, 20 + h:21 + h]
                    gcum_h = smt[:, 40 + h:41 + h]; bk_h = smt[:, 32 + h:33 + h]
                    pb, pb_r = pbank()
                    sc.pe32(lambda e, pb=pb, Qt=Qt, h=h: e.transpose(out=pb[:, 0:128], in_=Qt[:, h, :], identity=ident_f[:]), reads=[Qt_r, r_const], writes=[pb_r])
                    sc.pe32(lambda e, pb=pb, Kt=Kt, h=h: e.transpose(out=pb[:, 128:256], in_=Kt[:, h, :], identity=ident_f[:]), reads=[Kt_r, r_const], writes=[pb_r])
                    sc.add("dve", lambda e, pb=pb, QT_=QT_: e.tensor_copy(out=QT_, in_=pb[:, 0:128]), reads=[pb_r], writes=[QT_r_])
                    sc.add("dve", lambda e, pb=pb, KT_=KT_: e.tensor_copy(out=KT_, in_=pb[:, 128:256]), reads=[pb_r], writes=[KT_r_])
                    sc.add("dve", lambda e, GR_=GR_, smt=smt, h=h: e.tensor_scalar(out=GR_, in0=ones_f, scalar1=smt[:, 8 + h:9 + h], scalar2=None, op0=ALU.mult), reads=[sm_r, pc_r], writes=[GR_r_])
                    K4 = os.environ.get("K4", "z")
                    if KB == 4 and K4 <= "a":
                        continue
                    pg_, pg_r = pbank()
                    sc.pe32(lambda e, pg_=pg_, GR_=GR_: e.matmul(pg_[:, 0:128], lhsT=GR_, rhs=LTc, start=True, stop=True), reads=[GR_r_, pc_r], writes=[pg_r])
                    sc.add("dve", lambda e, pg_=pg_, D_=D_, gcum_h=gcum_h: e.tensor_scalar(out=D_, in0=pg_[:, 0:128], scalar1=gcum_h, scalar2=0.0, op0=ALU.subtract, op1=ALU.max), reads=[pg_r, sm_r], writes=[D_r])
                    sc.add("dve", lambda e, D_=D_: e.tensor_scalar(out=D_, in0=D_, scalar1=60.0, scalar2=None, op0=ALU.min), reads=[D_r], writes=[D_r])
                    if os.environ.get("K5") == "waitD0":
                        sc.pe32(lambda e, pg_=pg_: e.transpose(out=pg_[:, 256:384], in_=ident_f[:], identity=ident_f[:]), reads=[r_const, D_r], writes=[])
                    sc.add("act", lambda e, D_=D_: e.activation(out=D_, in_=D_, func=AF.Exp, scale=-1.0, bias=cbias[:, 3:4]), reads=[D_r], writes=[D_r])
                    if os.environ.get("K5") == "waitD1":
                        sc.pe32(lambda e, pg_=pg_: e.transpose(out=pg_[:, 256:384], in_=ident_f[:], identity=ident_f[:]), reads=[r_const, D_r], writes=[])
                    PD = os.environ.get("PD", "dve")
                    sc.add(PD, lambda e, D_=D_, Ds_=Ds_: e.tensor_tensor(out=Ds_, in0=D_, in1=mstrict, op=ALU.mult), reads=[D_r, pc_r], writes=[Ds_r])
                    sc.add(PD, lambda e, D_=D_: e.tensor_tensor(out=D_, in0=D_, in1=mincl, op=ALU.mult), reads=[D_r, pc_r], writes=[D_r])
                    if os.environ.get("K5") == "waitD2":
                        sc.pe32(lambda e, pg_=pg_: e.transpose(out=pg_[:, 256:384], in_=ident_f[:], identity=ident_f[:]), reads=[r_const, Ds_r], writes=[])
                    if KB == 4 and K4 <= "b":
                        continue
                    pk_, pk_r = pbank()
                    K8 = os.environ.get("K8", "")
                    if K8 != "noKK" and K8 != "none":
                        sc.pe32(lambda e, pk_=pk_, KT_=KT_: e.matmul(pk_[:, 0:128], lhsT=KT_, rhs=KT_, start=True, stop=True), reads=[KT_r_], writes=[pk_r])
                    if K8 != "noQK" and K8 != "none":
                        sc.pe32(lambda e, pk_=pk_, QT_=QT_, KT_=KT_: e.matmul(pk_[:, 128:256], lhsT=QT_, rhs=KT_, start=True, stop=True), reads=[QT_r_, KT_r_], writes=[pk_r])
                    sc.add("dve", lambda e, pk_=pk_, A_=A_, beta_h=beta_h: e.tensor_scalar(out=A_, in0=pk_[:, 0:128], scalar1=beta_h, scalar2=None, op0=ALU.mult), reads=[pk_r, sm_r], writes=[A_r])
                    sc.add("dve", lambda e, pk_=pk_, at_=at_: e.tensor_copy(out=at_, in_=pk_[:, 128:256]), reads=[pk_r], writes=[at_r])
                    (A0_, A0_r) = nxt(Am, "Am"); (at0_, at0_r) = nxt(attn, "attn")
                    sc.add("dve", lambda e, A_=A_, A0_=A0_, Ds_=Ds_: e.tensor_tensor(out=A0_, in0=A_, in1=Ds_, op=ALU.mult), reads=[A_r, Ds_r], writes=[A0_r])
                    sc.add("dve", lambda e, at_=at_, at0_=at0_, D_=D_: e.tensor_tensor(out=at0_, in0=at_, in1=D_, op=ALU.mult), reads=[at_r, D_r], writes=[at0_r])
                    A_, A_r, at_, at_r = A0_, A0_r, at0_, at0_r
                    if KB == 4 and K4 <= "c":
                        continue
                    if os.environ.get("HB", "0") == "1":
                        sc.barrier()
                    if os.environ.get("K6") == "samebank":
                        pt_, pt_r = pk_[:, 256:512], pk_r
                    else:
                        pt_, pt_r = pbank()
                    K5 = os.environ.get("K5", "")
                    if K5 == "waitonly":
                        sc.pe32(lambda e, pt_=pt_: e.transpose(out=pt_[:, 0:128], in_=ident_f[:], identity=ident_f[:]), reads=[r_const, A_r], writes=[pt_r])
                        continue
                    if K5 == "spin":
                        for _ in range(int(os.environ.get("NSPIN", "300"))):
                            sc.pe32(lambda e, pt_=pt_: e.transpose(out=pt_[:, 256:384], in_=ident_f[:], identity=ident_f[:]), reads=[r_const], writes=[pt_r])
                        sc.pe32(lambda e, pt_=pt_: e.transpose(out=pt_[:, 0:128], in_=ident_f[:], identity=ident_f[:]), reads=[r_const, A_r], writes=[pt_r])
                        continue
                    if K5 == "dummy":
                        sc.pe32(lambda e, pt_=pt_: e.transpose(out=pt_[:, 256:384], in_=ident_f[:], identity=ident_f[:]), reads=[r_const], writes=[pt_r])
                        sc.pe32(lambda e, pt_=pt_: e.transpose(out=pt_[:, 0:128], in_=ident_f[:], identity=ident_f[:]), reads=[r_const, A_r], writes=[pt_r])
                        continue
                    if K5 == "viaact2" and ((t * 4 + h) >= int(os.environ.get("KN", "999")) or (t * 4 + h) < int(os.environ.get("KN0", "0"))):
                        continue
                    if K5 == "viaact2":
                        sc.add("dve", lambda e, X_=X_, A_=A_: e.tensor_copy(out=X_, in_=A_), reads=[A_r], writes=[X_r])
                        continue
                    if K5 == "viaact":
                        sc.add("dve", lambda e, X_=X_, A_=A_: e.tensor_copy(out=X_, in_=A_), reads=[A_r], writes=[X_r])
                        sc.pe32(lambda e, pt_=pt_, X_=X_: e.transpose(out=pt_[:, 0:128], in_=X_, identity=ident_f[:]), reads=[r_const, X_r], writes=[pt_r])
                        continue
                    if K5 == "waitbf":
                        ptb_ = pt_.bitcast(BF16)
                        sc.pe16(ptb_[:, 0:128], lambda e, ptb_=ptb_: e.transpose(out=ptb_[:, 0:128], in_=ident_b[:], identity=ident_b[:]), reads=[r_const, A_r], writes=[pt_r])
                        continue
                    if K5 == "waitat":
                        sc.pe32(lambda e, pt_=pt_: e.transpose(out=pt_[:, 0:128], in_=ident_f[:], identity=ident_f[:]), reads=[r_const, at_r], writes=[pt_r])
                        continue
                    if K5 == "waitD":
                        sc.pe32(lambda e, pt_=pt_: e.transpose(out=pt_[:, 0:128], in_=ident_f[:], identity=ident_f[:]), reads=[r_const, Ds_r], writes=[pt_r])
                        continue
                    if K5 == "useGR":
                        sc.pe32(lambda e, pt_=pt_, GR_=GR_: e.transpose(out=pt_[:, 0:128], in_=GR_, identity=ident_f[:]), reads=[GR_r_, r_const] + ([A_r] if os.environ.get("K7") != "nodep" else []), writes=[pt_r])
                        continue
                    if K5 == "useDs":
                        sc.pe32(lambda e, pt_=pt_, Ds_=Ds_: e.transpose(out=pt_[:, 0:128], in_=Ds_, identity=ident_f[:]), reads=[Ds_r, A_r, r_const], writes=[pt_r])
                        continue
                    if K5 != "nope" and K5 != "pe2":
                        sc.pe32(lambda e, pt_=pt_, A_=A_: e.transpose(out=pt_[:, 0:128], in_=A_, identity=ident_f[:]), reads=[A_r, r_const], writes=[pt_r])
                    if K5 != "nope" and K5 != "pe1":
                        sc.pe32(lambda e, pt_=pt_, at_=at_: e.transpose(out=pt_[:, 128:256], in_=at_, identity=ident_f[:]), reads=[at_r, r_const], writes=[pt_r])
                    if K5 == "nodve":
                        continue
                    sc.add("dve", lambda e, pt_=pt_, X_=X_: e.tensor_copy(out=X_, in_=pt_[:, 0:128]), reads=[pt_r], writes=[X_r])
                    if KB == 4 and K4 <= "d":
                        continue
                    sc.add("pool", lambda e, X_=X_, Bo_=Bo_: e.tensor_tensor(out=Bo_, in0=X_.unsqueeze(1).to_broadcast([128, 6, 128]), in1=boff, op=ALU.mult), reads=[X_r, pc_r], writes=[Bo_r])
                    if KB == 4 and K4 <= "e":
                        continue
                    sc.add("dve", lambda e, pt_=pt_, atT_=atT_: e.tensor_copy(out=atT_, in_=pt_[:, 128:256]), reads=[pt_r], writes=[atT_r])
                    if KB <= 4:
                        continue
                    (E_, E_r) = nxt(Em, "E"); (Dk_, Dk_r) = nxt(Dk, "Dk")
                    sc.add("pool", lambda e, E_=E_, Bo_=Bo_: e.tensor_tensor(out=E_, in0=ident_f[:], in1=Bo_[:, 0, :], op=ALU.subtract), reads=[Bo_r, r_const], writes=[E_r])
                    sc.add("pool", lambda e, Dk_=Dk_, A_=A_: e.tensor_tensor(out=Dk_, in0=A_, in1=aoff1, op=ALU.mult), reads=[A_r, pc_r], writes=[Dk_r])
                    sc.add("pool", lambda e, Dk_=Dk_: e.tensor_tensor(out=Dk_, in0=ident_f[:], in1=Dk_, op=ALU.subtract), reads=[Dk_r, r_const], writes=[Dk_r])
                    for lvl in range(1, 6):
                        px_, px_r = pbank()
                        sc.pe32(lambda e, px_=px_, Bo_=Bo_, lvl=lvl, Dk_=Dk_: e.matmul(px_[:, 0:128], lhsT=Bo_[:, lvl, :], rhs=Dk_, start=True, stop=True), reads=[Bo_r, Dk_r], writes=[px_r])
                        sc.add("dve", lambda e, px_=px_, X_=X_: e.tensor_copy(out=X_, in_=px_[:, 0:128]), reads=[px_r], writes=[X_r])
                        py_, py_r = pbank()
                        sc.pe32(lambda e, py_=py_, X_=X_, E_=E_: e.matmul(py_[:, 0:128], lhsT=X_, rhs=E_, start=True, stop=True), reads=[X_r, E_r], writes=[py_r])
                        (E2_, E2_r) = nxt(Em, "E")
                        sc.add("dve", lambda e, py_=py_, E_=E_, E2_=E2_: e.tensor_tensor(out=E2_, in0=E_, in1=py_[:, 0:128], op=ALU.subtract), reads=[py_r, E_r], writes=[E2_r])
                        E_, E_r = E2_, E2_r
                        if lvl < 5:
                            pd_, pd_r = pbank()
                            sc.pe32(lambda e, pd_=pd_, E_=E_: e.transpose(out=pd_[:, 0:128], in_=E_, identity=ident_f[:]), reads=[E_r, r_const], writes=[pd_r])
                            (Dk_, Dk_r) = nxt(Dk, "Dk")
                            sc.add("dve", lambda e, pd_=pd_, Dk_=Dk_: e.tensor_copy(out=Dk_, in_=pd_[:, 0:128]), reads=[pd_r], writes=[Dk_r])
                    if KB <= 5:
                        continue
                    if os.environ.get("HB", "0") == "1":
                        sc.barrier()
                    sc.add("pool", lambda e, R_=R_, Vt=Vt, h=h, beta_h=beta_h: e.tensor_scalar(out=R_[:, 0:128], in0=Vt[:, h, :], scalar1=beta_h, scalar2=None, op0=ALU.mult), reads=[Vt_r, sm_r], writes=[R_r])
                    sc.add("pool", lambda e, R_=R_, Kt=Kt, h=h, bk_h=bk_h: e.tensor_scalar(out=R_[:, 128:256], in0=Kt[:, h, :], scalar1=bk_h, scalar2=None, op0=ALU.mult), reads=[Kt_r, sm_r], writes=[R_r], partial=True)
                    sc.add("pool", lambda e, Kd_=Kd_, Kt=Kt, h=h, egu_h=egu_h: e.tensor_scalar(out=Kd_, in0=Kt[:, h, :], scalar1=egu_h, scalar2=None, op0=ALU.mult), reads=[Kt_r, sm_r], writes=[Kd_r])
                    sc.add("pool", lambda e, Qd_=Qd_, Qt=Qt, h=h, egc_h=egc_h: e.tensor_scalar(out=Qd_, in0=Qt[:, h, :], scalar1=egc_h, scalar2=None, op0=ALU.mult), reads=[Qt_r, sm_r], writes=[Qd_r])
                    pu_, pu_r = pbank()
                    sc.pe32(lambda e, pu_=pu_, E_=E_, R_=R_: e.matmul(pu_[:, 0:256], lhsT=E_, rhs=R_, start=True, stop=True), reads=[E_r, R_r], writes=[pu_r])
                    sc.add("dve", lambda e, pu_=pu_, UW_=UW_: e.tensor_copy(out=UW_[:, 0:128], in_=pu_[:, 0:128]), reads=[pu_r], writes=[UW_r])
                    sc.add("dve", lambda e, pu_=pu_, UW_=UW_: e.tensor_scalar(out=UW_[:, 128:256], in0=pu_[:, 128:256], scalar1=-1.0, scalar2=None, op0=ALU.mult), reads=[pu_r], writes=[UW_r], partial=True)
                    pq_, pq_r = pbank()
                    sc.pe32(lambda e, pq_=pq_, Qd_=Qd_: e.matmul(pq_[:, 0:128], lhsT=Qd_, rhs=ident_f[:], start=True, stop=False), reads=[Qd_r, r_const], writes=[pq_r])
                    sc.pe32(lambda e, pq_=pq_, UW_=UW_, atT_=atT_: e.matmul(pq_[:, 0:128], lhsT=UW_[:, 128:256], rhs=atT_, start=False, stop=True), reads=[UW_r, atT_r], writes=[pq_r])
                    sc.add("dve", lambda e, pq_=pq_, QpT_=QpT_: e.tensor_copy(out=QpT_, in_=pq_[:, 0:128]), reads=[pq_r], writes=[QpT_r])
                    for ci in range(2):
                        pm_, pm_r = pbank()
                        ps_ = slice(ci * 64, ci * 64 + 64)
                        sc.pe32(lambda e, pm_=pm_, UW_=UW_, Kd_=Kd_, ps_=ps_: e.matmul(pm_[:, 0:128], lhsT=UW_[ps_, 128:256], rhs=Kd_[ps_, :], start=True, stop=True), reads=[UW_r, Kd_r], writes=[pm_r])
                        if ci == 0:
                            sc.add("dve", lambda e, pm_=pm_, Mp_=Mp_, ci=ci: e.tensor_copy(out=Mp_[:, ci, :], in_=pm_[:, 0:128]), reads=[pm_r], writes=[Mp_r])
                        else:
                            sc.add("dve", lambda e, pm_=pm_, Mp_=Mp_, ci=ci: e.tensor_copy(out=Mp_[:, ci, :], in_=pm_[:, 0:128]), reads=[pm_r], writes=[Mp_r], partial=True)
                    if os.environ.get("HB", "0") == "1":
                        sc.barrier()
                    for ci in range(2 if KB > 6 else 0):
                        ps_ = slice(ci * 64, ci * 64 + 64)
                        Sp, Sp_r = Sst[cur], S_r[cur][h]
                        Sn, Sn_r = Sst[1 - cur], S_r[1 - cur][h]
                        po_, po_r = pbank()
                        tp = (0, ci * 64)
                        sc.pe32(lambda e, po_=po_, QpT_=QpT_, Sp=Sp, h=h, ps_=ps_, tp=tp: e.matmul(po_[ps_, 0:128], lhsT=QpT_[:, ps_], rhs=Sp[:, h, :], start=True, stop=False, tile_position=tp), reads=[QpT_r, Sp_r], writes=[po_r])
                        sc.pe32(lambda e, po_=po_, atT_=atT_, UW_=UW_, ps_=ps_, tp=tp: e.matmul(po_[ps_, 0:128], lhsT=atT_[:, ps_], rhs=UW_[:, 0:128], start=False, stop=True, tile_position=tp), reads=[atT_r, UW_r], writes=[po_r])
                        sc.add("dve", lambda e, po_=po_, osb_t=osb_t, h=h, ps_=ps_: e.tensor_copy(out=osb_t[ps_, h, :], in_=po_[ps_, 0:128]), reads=[po_r], writes=[osb_r], partial=True)
                        sc.add("act", lambda e, po_=po_, oss_t=oss_t, h=h, ps_=ps_: e.activation(out=junk[0][ps_, :], in_=po_[ps_, 0:128], func=AF.Square, bias=cbias[ps_, 3:4], accum_out=oss_t[ps_, h:h + 1]), reads=[po_r], writes=[oss_r, junk[1]], partial=True)
                        pS_, pS_r = pbank()
                        sc.pe32(lambda e, pS_=pS_, Mp_=Mp_, ci=ci, Sp=Sp, h=h: e.matmul(pS_[:, 0:128], lhsT=Mp_[:, ci, :], rhs=Sp[:, h, :], start=True, stop=False), reads=[Mp_r, Sp_r], writes=[pS_r])
                        sc.pe32(lambda e, pS_=pS_, Kd_=Kd_, UW_=UW_, ps_=ps_: e.matmul(pS_[:, 0:128], lhsT=Kd_[ps_, :], rhs=UW_[ps_, 0:128], start=False, stop=True), reads=[Kd_r, UW_r], writes=[pS_r])
                        egl = smt[:, 24 + 4 * ci + h:25 + 4 * ci + h]
                        sc.add("dve", lambda e, pS_=pS_, Sp=Sp, Sn=Sn, h=h, egl=egl: e.scalar_tensor_tensor(out=Sn[:, h, :], in0=Sp[:, h, :], scalar=egl, in1=pS_[:, 0:128], op0=ALU.mult, op1=ALU.add), reads=[pS_r, Sp_r, sm_r], writes=[Sn_r])
                        cur = 1 - cur
                sc.safe = False
                if KB <= 7:
                    continue
                (oab_t, oab_r) = nxt(oab, "oab"); (oaT_t, oaT_r) = nxt(oaT, "oaT")
                sc.add("act", lambda e, oss_t=oss_t: e.activation(out=oss_t[:, 4:8], in_=oss_t[:, 0:4], func=AF.Sqrt, scale=1.0 / 128, bias=cbias[:, 0:1]), reads=[oss_r, r_const], writes=[oss_r])
                sc.add("dve", lambda e, oss_t=oss_t: e.reciprocal(out=oss_t[:, 4:8], in_=oss_t[:, 4:8]), reads=[oss_r], writes=[oss_r])
                sc.add("dve", lambda e, osb_t=osb_t, oss_t=oss_t: e.tensor_tensor(out=osb_t, in0=osb_t, in1=oss_t[:, 4:8].unsqueeze(2).to_broadcast([128, 4, 128]), op=ALU.mult), reads=[osb_r, oss_r], writes=[osb_r])
                sc.add("dve", lambda e, osb_t=osb_t, zst=zst, oab_t=oab_t: e.tensor_tensor(out=oab_t, in0=osb_t.rearrange("p h d -> p (h d)"), in1=zst, op=ALU.mult), reads=[osb_r, zs_r], writes=[oab_r])
                ptb = bank[5][:].bitcast(BF16).rearrange("p (k c) -> p k c", k=8)
                for h in range(4):
                    sc.pe16(ptb[:, h, :], lambda e, h=h, oab_t=oab_t: e.transpose(out=ptb[:, h, :], in_=oab_t[:, h * 128:(h + 1) * 128], identity=ident_b[:]), reads=[oab_r, r_const], writes=[bank_r[5]])
                sc.add("dve", lambda e, oaT_t=oaT_t: e.tensor_copy(out=oaT_t, in_=ptb[:, 0:4, :]), reads=[bank_r[5]], writes=[oaT_r])
                sc.add("sp", lambda e, oaT_t=oaT_t, tsl=tsl: e.dma_start(out=oT_d[0:4, :, tsl].rearrange("j p c -> p j c"), in_=oaT_t), reads=[oaT_r], writes=[], dma=True, key="oaT%d" % (ctr["oaT"] % 2))

    def ln_tile(L_, t, ps_lo, ps_lo_r, ps_hi, ps_hi_r, xr, xr_r, g_bc, b_bc, lnp_r, out_d, write_xT):
        tsl = slice(t * 128, (t + 1) * 128)
        (y, y_r) = L_["y"][t % 2]; (st, st_r) = L_["st"][t % 2]; (xb_, xb_r_) = L_["xb"][t % 2]
        for half, (pp, pp_r) in enumerate(((ps_lo, ps_lo_r), (ps_hi, ps_hi_r))):
            hs = slice(half * 512, (half + 1) * 512)
            sc.add("dve", lambda e, pp=pp, hs=hs, y=y, xr=xr: e.scalar_tensor_tensor(out=y[:, hs], in0=xr[:, hs], scalar=ALPHA, in1=pp[:, 0:512], op0=ALU.mult, op1=ALU.add),
                   reads=[pp_r, xr_r], writes=[y_r], partial=(half == 1))
        for half in range(2):
            hs = slice(half * 512, (half + 1) * 512)
            sc.add("dve", lambda e, half=half, hs=hs, y=y, st=st: e.bn_stats(out=st[:, half * 6:(half + 1) * 6], in_=y[:, hs]), reads=[y_r], writes=[st_r], partial=(half == 1))
        sc.add("dve", lambda e, st=st: e.bn_aggr(out=st[:, 12:14], in_=st[:, 0:12]), reads=[st_r], writes=[st_r])
        sc.add("act", lambda e, st=st: e.activation(out=st[:, 14:15], in_=st[:, 13:14], func=AF.Sqrt, bias=cbias[:, 2:3]), reads=[st_r, r_const], writes=[st_r])
        sc.add("dve", lambda e, st=st: e.reciprocal(out=st[:, 14:15], in_=st[:, 14:15]), reads=[st_r], writes=[st_r])
        sc.add("dve", lambda e, y=y, st=st: e.tensor_scalar(out=y, in0=y, scalar1=st[:, 12:13], scalar2=st[:, 14:15], op0=ALU.subtract, op1=ALU.mult), reads=[y_r, st_r], writes=[y_r])
        sc.add("pool", lambda e, y=y: e.tensor_tensor(out=y, in0=y, in1=g_bc, op=ALU.mult), reads=[y_r, lnp_r], writes=[y_r])
        sc.add("dve", lambda e, y=y: e.tensor_tensor(out=y, in0=y, in1=b_bc, op=ALU.add), reads=[y_r, lnp_r], writes=[y_r])
        sc.add("sp", lambda e, y=y, tsl=tsl: e.dma_start(out=out_d[tsl, :], in_=y), reads=[y_r], writes=[], dma=True, key="ysto%d" % (t % 2))
        if write_xT:
            sc.add("act", lambda e, y=y, xb_=xb_: e.copy(out=xb_, in_=y), reads=[y_r], writes=[xb_r_])
            ptb = bank[7][:].bitcast(BF16).rearrange("p (k c) -> p k c", k=8)
            for kc in range(8):
                sc.pe16(ptb[:, kc, :], lambda e, kc=kc, xb_=xb_: e.transpose(out=ptb[:, kc, :], in_=xb_[:, kc * 128:(kc + 1) * 128], identity=ident_b[:]), reads=[xb_r_, r_const], writes=[bank_r[7]])
            sc.add("dve", lambda e, tsl=tsl: e.tensor_copy(out=xT[:, :, tsl], in_=ptb), reads=[bank_r[7]], writes=[xT_r[t]])

    def ln_bufs(g_d, b_d, l):
        L_ = {}
        L_["y"] = [(cv.get([128, 1024]), sc.res("y%d" % i)) for i in range(2)]
        L_["st"] = [(cv.get([128, 16]), sc.res("st%d" % i)) for i in range(2)]
        L_["xb"] = [(cv.get([128, 1024], BF16), sc.res("xbln%d" % i)) for i in range(2)]
        g_bc = cv.get([128, 1024]); b_bc = cv.get([128, 1024]); lnp_r = sc.res("lnp")
        sc.add("sp", lambda e: e.dma_start(out=g_bc, in_=g_d[l, :].partition_broadcast(128)), writes=[lnp_r], dma=True, key="lnp", partial=True)
        sc.add("sp", lambda e: e.dma_start(out=b_bc, in_=b_d[l, :].partition_broadcast(128)), writes=[lnp_r], dma=True, key="lnp", partial=True)
        return L_, g_bc, b_bc, lnp_r

    def phaseC(l, xin_d):
        cv.reset()
        wO = cv.get([128, 8, 1024], BF16); wO_r = [sc.res("wO%d" % k) for k in range(8)]
        for kc in range(8):
            sc.add("pool", lambda e, kc=kc: e.dma_start(out=wO[:, kc, :], in_=w_out_d[l, kc * 128:(kc + 1) * 128, :]), writes=[wO_r[kc]], dma=True, key="wO%d" % kc)
        for kc in range(8):
            sc.add("pool", lambda e, kc=kc: e.dma_start(out=wupbf_d[l].rearrange("t p k f -> p t k f")[:, :, kc, :], in_=w_up_d[l, kc * 128:(kc + 1) * 128, :].rearrange("p (t f) -> p t f", f=128)),
                   writes=[wupbf_r[l]], dma=True, key="wcv%d" % (kc % 4), partial=True)
        L_, g_bc, b_bc, lnp_r = ln_bufs(ln1g_d, ln1b_d, l)
        oTt = [(cv.get([128, 8, 128], BF16), sc.res("oTt%d" % i)) for i in range(3)]
        xrs = [(cv.get([128, 1024]), sc.res("xr%d" % i)) for i in range(3)]
        for t in range(NT):
            tsl = slice(t * 128, (t + 1) * 128)
            (ot, ot_r) = oTt[t % 3]; (xr, xr_r) = xrs[t % 3]
            sc.add("sp", lambda e, ot=ot, tsl=tsl: e.dma_start(out=ot, in_=oT_d[:, :, tsl].rearrange("k p c -> p k c")), writes=[ot_r], dma=True, key="oTt%d" % (t % 3))
            sc.add("sp", lambda e, xr=xr, tsl=tsl: e.dma_start(out=xr, in_=xin_d[tsl, :]), writes=[xr_r], dma=True, key="xr%d" % (t % 3))
            bl, bh = 2 * (t % 2), 2 * (t % 2) + 1
            for half, bi in ((0, bl), (1, bh)):
                for kc in range(8):
                    sc.pe16(bank[bi][:], lambda e, bi=bi, kc=kc, ot=ot, half=half: e.matmul(bank[bi][:], lhsT=ot[:, kc, :], rhs=wO[:, kc, half * 512:(half + 1) * 512], start=(kc == 0), stop=(kc == 7)),
                            reads=[ot_r, wO_r[kc]], writes=[bank_r[bi]])
            ln_tile(L_, t, bank[bl], bank_r[bl], bank[bh], bank_r[bh], xr, xr_r, g_bc, b_bc, lnp_r, x1_d, True)

    def phaseD(l, out_d, write_xT):
        cv.reset()
        NJ = DFF // 128
        NBLK = S // 512
        wD = cv.get([128, NJ, 1024], BF16); wD_r = [sc.res("wD%d" % j) for j in range(NJ)]
        for j in range(NJ):
            sc.add("pool", lambda e, j=j: e.dma_start(out=wD[:, j, :], in_=w_down_d[l, j * 128:(j + 1) * 128, :]), writes=[wD_r[j]], dma=True, key="wD%d" % (j % 4))
        L_, g_bc, b_bc, lnp_r = ln_bufs(ln2g_d, ln2b_d, l)
        fc4 = cv.get([128, 4, 44]); fc_r = sc.res("fc4")
        hT = cv.get([128, NJ, 512], BF16); hT_r = [sc.res("hT%d" % j) for j in range(NJ)]
        wU = [(cv.get([128, 8, 256], BF16), sc.res("wU%d" % i)) for i in range(3)]
        raw = [(cv.get([128, 2, 514]), sc.res("raw%d" % i)) for i in range(2)]
        acc = [(cv.get([128, 2, 512]), sc.res("facc%d" % i)) for i in range(2)]
        halo = cv.get([128, 44, 2]); halo_r = [sc.res("fhalo%d" % j) for j in range(44)]
        xrs = [(cv.get([128, 1024]), sc.res("xrD%d" % i)) for i in range(2)]
        w44 = hT.rearrange("p j t -> p (j t)").bitcast(F32)[0:44, 0:512].rearrange("p (a b) -> p a b", a=4)
        w44_r = sc.res("w44")
        for j3 in range(3):
            sc.add("sp", lambda e, j3=j3: e.dma_start(out=w44[:, j3, :], in_=fconvw_d[l, j3, :].rearrange("(f p) -> f p", p=128)), writes=[w44_r] + hT_r[0:2], dma=True, key="w44", partial=True)
        sc.add("sp", lambda e: e.dma_start(out=w44[:, 3, :], in_=fconvb_d[l, :].rearrange("(f p) -> f p", p=128)), writes=[w44_r], dma=True, key="w44", partial=True)
        for a4 in range(4):
            sc.pe32(lambda e, a4=a4: e.transpose(out=bank[0][:, a4 * 44:(a4 + 1) * 44], in_=w44[:, a4, :], identity=ident_f[0:44, 0:44]), reads=[w44_r, r_const], writes=[bank_r[0]])
        sc.add("dve", lambda e: e.tensor_copy(out=fc4.rearrange("p a f -> p (a f)"), in_=bank[0][:, 0:176]), reads=[bank_r[0]], writes=[fc_r])
        sc.add("pool", lambda e: e.memset(halo, 0.0), reads=[], writes=halo_r)
        sc.barrier()
        for c in range(NBLK):
            csl = slice(c * 512, (c + 1) * 512)
            for j in range(NJ):
                (wu, wu_r) = wU[j % 3]
                sc.add("sp", lambda e, wu=wu, j=j: e.dma_start(out=wu[:, :, 0:128], in_=wupbf_d[l][j, :, :, :]), reads=[wupbf_r[l]], writes=[wu_r], dma=True, key="wUa%d" % (j % 3))
                sc.add("sp", lambda e, wu=wu, j=j: e.dma_start(out=wu[:, :, 128:256], in_=wupbf_d[l][22 + j, :, :, :]), reads=[wupbf_r[l]], writes=[wu_r], dma=True, key="wUb%d" % (j % 3), partial=True)
                (rw, rw_r) = raw[j % 2]; (ac, ac_r) = acc[j % 2]
                for gv in range(2):
                    f = gv * 22 + j
                    bi = 2 * (j % 2) + gv
                    for kc in range(8):
                        sc.pe16(bank[bi][:], lambda e, bi=bi, kc=kc, wu=wu, gv=gv, csl=csl: e.matmul(bank[bi][:], lhsT=wu[:, kc, gv * 128:(gv + 1) * 128], rhs=xT[:, kc, csl], start=(kc == 0), stop=(kc == 7)),
                                reads=[wu_r] + xT_r[4 * c:4 * c + 4], writes=[bank_r[bi]])
                    sc.add("pool", lambda e, rw=rw, gv=gv, f=f: e.tensor_copy(out=rw[:, gv, 0:2], in_=halo[:, f, :]), reads=[halo_r[f]], writes=[rw_r], partial=(gv == 1))
                    sc.add("act", lambda e, rw=rw, gv=gv, bi=bi: e.copy(out=rw[:, gv, 2:514], in_=bank[bi][:]), reads=[bank_r[bi]], writes=[rw_r], partial=True)
                    sc.add("pool", lambda e, rw=rw, gv=gv, f=f: e.tensor_copy(out=halo[:, f, :], in_=rw[:, gv, 512:514]), reads=[rw_r], writes=[halo_r[f]])
                    sc.add("act", lambda e, ac=ac, gv=gv, bi=bi, f=f: e.activation(out=ac[:, gv, :], in_=bank[bi][:], func=AF.Identity, scale=fc4[:, 2, f:f + 1], bias=fc4[:, 3, f:f + 1]), reads=[bank_r[bi], fc_r], writes=[ac_r], partial=(gv == 1))
                    eng = "dve"
                    for tap in (1, 0):
                        sc.add(eng, lambda e, ac=ac, rw=rw, gv=gv, tap=tap, f=f: e.scalar_tensor_tensor(out=ac[:, gv, :], in0=rw[:, gv, tap:tap + 512], scalar=fc4[:, tap, f:f + 1], in1=ac[:, gv, :], op0=ALU.mult, op1=ALU.add),
                               reads=[rw_r, fc_r, ac_r], writes=[ac_r])
                sc.add("act", lambda e, ac=ac: e.activation(out=ac[:, 0, :], in_=ac[:, 0, :], func=AF.Silu, bias=cbias[:, 3:4]), reads=[ac_r, r_const], writes=[ac_r])
                sc.add("dve", lambda e, ac=ac, j=j: e.tensor_tensor(out=hT[:, j, :], in0=ac[:, 0, :], in1=ac[:, 1, :], op=ALU.mult), reads=[ac_r], writes=[hT_r[j]])
            for tt in range(4):
                t = c * 4 + tt
                tsl = slice(t * 128, (t + 1) * 128)
                (xr, xr_r) = xrs[t % 2]
                sc.add("sp", lambda e, xr=xr, tsl=tsl: e.dma_start(out=xr, in_=x1_d[tsl, :]), writes=[xr_r], dma=True, key="xrD%d" % (t % 2))
                bl, bh = 4 + 2 * (t % 2), 5 + 2 * (t % 2)
                if bh == 7 and write_xT:
                    bl, bh = 4, 5
                for half, bi in ((0, bl), (1, bh)):
                    for j in range(NJ):
                        sc.pe16(bank[bi][:], lambda e, bi=bi, j=j, tt=tt, half=half: e.matmul(bank[bi][:], lhsT=hT[:, j, tt * 128:(tt + 1) * 128], rhs=wD[:, j, half * 512:(half + 1) * 512], start=(j == 0), stop=(j == NJ - 1)),
                                reads=[hT_r[j], wD_r[j]], writes=[bank_r[bi]])
                ln_tile(L_, t, bank[bl], bank_r[bl], bank[bh], bank_r[bh], xr, xr_r, g_bc, b_bc, lnp_r, out_d, write_xT)

    phase0(x_d)
    sc.barrier()
    if stop_after == "0":
        sc.add("sp", lambda e: e.dma_start(out=oT_d[:, :, :].rearrange("k p s -> p k s"), in_=xT[:]), reads=xT_r, writes=[], dma=True, key="dbg")
    elif stop_after == "A":
        phaseA(0)
    elif stop_after == "B":
        phaseB(0)
    else:
        for l in range(L):
            xin = x_d if l == 0 else x2_d
            last = (l == L - 1)
            phaseA(l)
            sc.barrier()
            phaseB(l)
            sc.barrier()
            phaseC(l, xin)
            sc.barrier()
            if stop_after == "C" and l == 0:
                break
            phaseD(l, y_d if last else x2_d, not last)
            sc.barrier()
            if stop_after == "D" and l == 0:
                break
    if dbg and os.environ.get("DUMPARENA") and not os.environ.get("SIM"):
        sc.barrier()
        dbg_arena = nc.dram_tensor("dbg_arena", [128, ARENA], F32, kind="ExternalOutput").ap()
        for q in range(4):
            sc.add("sp", lambda e, q=q: e.dma_start(out=dbg_arena[:, q * (ARENA // 4):(q + 1) * (ARENA // 4)], in_=arena[:, q * (ARENA // 4):(q + 1) * (ARENA // 4)]), dma=True, key="dbga")
    sc.emit(nc, es)
    es.close()
    return nc


_CACHE = {}


def kernel(**inputs):
    x = np.asarray(inputs["x"], dtype=np.float32)
    B, S, _ = x.shape
    L = int(np.asarray(inputs["w_in"]).shape[0])
    key = (S, L)
    if key not in _CACHE:
        _CACHE[key] = (build(S=S, L=L), make_consts(S))
    nc, consts = _CACHE[key]
    shared = {k: np.ascontiguousarray(np.asarray(v, dtype=np.float32)) for k, v in inputs.items() if k != "x"}
    for k, v in consts.items():
        shared["c_" + k] = v
    in_maps = []
    for b in range(B):
        m = dict(shared)
        m["x"] = np.ascontiguousarray(x[b])
        in_maps.append(m)
    res = run_bass_kernel_spmd(nc, in_maps, core_ids=list(range(B)))
    return np.stack([np.asarray(r["y"], dtype=np.float32) for r in res.results], axis=0)
```

```python
import os
import numpy as np
import ml_dtypes
from contextlib import ExitStack
import concourse.bass as bass
import concourse.mybir as mybir
from concourse.bass_utils import run_bass_kernel_spmd

F32 = mybir.dt.float32
BF16 = mybir.dt.bfloat16
AF = mybir.ActivationFunctionType
ALU = mybir.AluOpType
AX = mybir.AxisListType

D = 1024
NIN = 3592
DFF = 2816
ALPHA = float((2 * 2) ** 0.25)
NEG = -30000.0


class Res:
    __slots__ = ("name", "writers", "readers")

    def __init__(self, name):
        self.name = name
        self.writers = []
        self.readers = {}


class Op:
    __slots__ = ("eng", "fn", "dma", "key", "value", "deps", "signal", "barrier")

    def __init__(self, eng, fn, dma=False, key=None):
        self.eng = eng
        self.fn = fn
        self.dma = dma
        self.key = key
        self.value = None
        self.deps = []
        self.signal = False
        self.barrier = False


ENGS = ("pe", "act", "dve", "pool", "sp")


class Sched:
    def __init__(self):
        self.ops = {e: [] for e in ENGS}
        self.keycount = {}
        self.keylast = {}
        self.allres = []
        self.last_pe_f32 = False
        self.ident_b = None
        self.safe = False
        self.safecnt = 0
        self.safek = int(os.environ.get("SAFEK", "0"))
        self.safeeng = tuple(x for x in os.environ.get("SAFEENG", "act").split(",") if x)

    def res(self, name):
        r = Res(name)
        self.allres.append(r)
        return r

    def _dep(self, op, prod, raw):
        if prod is op:
            return
        if (not prod.dma) and (not op.dma) and prod.eng == op.eng:
            if op.eng == "pe":
                return
        op.deps.append(prod)
        prod.signal = True

    def add(self, eng, fn, reads=(), writes=(), dma=False, key=None, partial=False, f32=False, out=None):
        if eng == "pe":
            if (not f32) and self.last_pe_f32 and out is not None:
                fn0 = fn
                dmy = out.bitcast(F32) if out.dtype != F32 else out
                idb = self.ident_b

                def fn(e, fn0=fn0, dmy=dmy, idb=idb):
                    e.matmul(dmy[0:64, 0:8], lhsT=idb[:, 0:64], rhs=idb[:, 0:8], start=True, stop=True)
                    return fn0(e)
            self.last_pe_f32 = f32
        excl = self.safe and (not dma) and (eng in self.safeeng)
        if excl:
            self.barrier()
        op = Op(eng, fn, dma, key)
        for r in reads:
            for w in r.writers:
                self._dep(op, w, True)
        for r in writes:
            for w in r.writers:
                if not (partial and w.dma and op.dma):
                    self._dep(op, w, False)
            for rd in r.readers.values():
                if isinstance(rd, list):
                    for x in rd:
                        self._dep(op, x, False)
                else:
                    self._dep(op, rd, False)
        for r in reads:
            if dma:
                r.readers.setdefault("dma", []).append(op)
            else:
                r.readers[eng] = op
        for r in writes:
            if partial:
                r.writers = r.writers + [op]
            else:
                r.writers = [op]
            r.readers = {}
        if dma:
            assert key is not None
            self.keycount[key] = self.keycount.get(key, 0) + 16
            op.value = self.keycount[key]
            self.keylast[key] = op
        self.ops[eng].append(op)
        if excl:
            self.barrier()
        elif self.safe and not dma and self.safek > 0:
            self.safecnt += 1
            if self.safecnt % self.safek == 0:
                self.barrier()
        return op

    def pe32(self, fn, **kw):
        return self.add("pe", fn, f32=True, **kw)

    def pe16(self, out, fn, **kw):
        return self.add("pe", fn, out=out, **kw)

    def barrier(self):
        prods = []
        for e in ENGS:
            for o in reversed(self.ops[e]):
                if not o.dma and not o.barrier:
                    prods.append(o)
                    break
        prods += list(self.keylast.values())
        for e in ENGS:
            b = Op(e, None)
            b.barrier = True
            for p in prods:
                if p.dma or p.eng != e or e != "pe":
                    b.deps.append(p)
                    p.signal = True
            self.ops[e].append(b)
        for r in self.allres:
            r.writers = []
            r.readers = {}

    def emit(self, nc, es):
        esem = {e: es.enter_context(nc.semaphore("s_" + e)) for e in ENGS}
        ksem = {}
        for i, k in enumerate(self.keycount):
            ksem[k] = es.enter_context(nc.semaphore("k%d" % i))
        for e in ENGS:
            c = 0
            for o in self.ops[e]:
                if (not o.dma) and o.signal and not o.barrier:
                    c += 1
                    o.value = c
            if os.environ.get("SEMDBG"): print("SEM", e, "final", c, "nops", len(self.ops[e]))
        block = es.enter_context(nc.Block())
        hooks = {"pe": block.tensor, "act": block.scalar, "dve": block.vector,
                 "pool": block.gpsimd, "sp": block.sync}
        final_keys = dict(self.keycount)

        def mk(ename):
            def body(eng):
                waited = {}
                for o in self.ops[ename]:
                    need = {}
                    for p in o.deps:
                        s = ksem[p.key] if p.dma else esem[p.eng]
                        sid = id(s)
                        v = p.value
                        if waited.get(sid, 0) >= v:
                            continue
                        if sid not in need or need[sid][1] < v:
                            need[sid] = (s, v)
                    for sid, (s, v) in need.items():
                        eng.wait_ge(s, v)
                        waited[sid] = v
                    if o.fn is None:
                        continue
                    ins = o.fn(eng)
                    if o.dma:
                        ins.then_inc(ksem[o.key], 16)
                    elif o.signal:
                        ins.then_inc(esem[ename], 1)
                if ename == "sp":
                    for k, v in final_keys.items():
                        if waited.get(id(ksem[k]), 0) < v:
                            eng.wait_ge(ksem[k], v)
            return body

        for e in ENGS:
            hooks[e](mk(e))


def make_consts(S):
    i = np.arange(128)[:, None]
    j = np.arange(128)[None, :]
    same = (i // 64) == (j // 64)
    c = {}
    c["ident"] = np.eye(128, dtype=np.float32)
    c["caus01"] = (i <= j).astype(np.float32)
    c["mstrict"] = (same & (j < i)).astype(np.float32)
    c["negincl"] = (same & (j <= i)).astype(np.float32)
    LT = (same & (j <= i)).T.astype(np.float32)
    UT = (same & (j > i)).T.astype(np.float32)
    CS0 = np.zeros((128, 128), np.float32); CS0[:64, :] = 1.0
    CS1 = np.zeros((128, 128), np.float32); CS1[64:, :] = 1.0
    c["gl"] = np.concatenate([LT, UT, CS0, CS1], 1)
    offs = []
    for s in (1, 2, 4, 8, 16, 32):
        m = ((i // (2 * s)) == (j // (2 * s))) & ((i // s) != (j // s)) & (i > j)
        offs.append(m.T.astype(np.float32))
    c["boff"] = np.concatenate(offs, 1)
    c["aoff1"] = (((i // 2) == (j // 2)) & (i != j) & (i > j)).astype(np.float32)
    half = 8
    inv = 500000.0 ** (-np.arange(half, dtype=np.float32) / half)
    ang = np.arange(S, dtype=np.float32)[:, None] * inv[None, :]
    cos = np.cos(ang).astype(np.float32)
    sin = np.sin(ang).astype(np.float32)
    NT = S // 128
    cc = np.concatenate([cos, cos], 1).reshape(NT, 128, 16).transpose(1, 0, 2)
    ss = np.concatenate([sin, sin], 1).reshape(NT, 128, 16).transpose(1, 0, 2)
    c["rope"] = np.ascontiguousarray(np.concatenate([cc, ss], 2)).reshape(128, NT * 32)
    return c


def build(S=4096, L=2, dbg=False, stop_after=None):
    NT = S // 128
    NB = S // 256
    nc = bass.Bass("TRN2", target_bir_lowering=False)
    sc = Sched()
    es = ExitStack()

    def din(name, shape, dt=F32):
        return nc.dram_tensor(name, list(shape), dt, kind="ExternalInput").ap()

    def dscr(name, shape, dt=F32, out=False):
        kind = "ExternalOutput" if (out or dbg) else "Internal"
        return nc.dram_tensor(name, list(shape), dt, kind=kind).ap()

    x_d = din("x", [S, D])
    w_in_d = din("w_in", [L, D, NIN])
    gconv_d = din("gdn_conv_w", [L, 4, 1536])
    alog_d = din("gdn_a_log", [L, 4])
    dtb_d = din("gdn_dt_bias", [L, 4])
    gng_d = din("gdn_norm_g", [L, 128])
    w_out_d = din("w_out", [L, D, D])
    ln1g_d = din("ln1_g", [L, D])
    ln1b_d = din("ln1_b", [L, D])
    w_up_d = din("w_up", [L, D, 2 * DFF])
    fconvw_d = din("ffn_conv_w", [L, 3, 2 * DFF])
    fconvb_d = din("ffn_conv_b", [L, 2 * DFF])
    w_down_d = din("w_down", [L, DFF, D])
    ln2g_d = din("ln2_g", [L, D])
    ln2b_d = din("ln2_b", [L, D])
    c_ident_d = din("c_ident", [128, 128])
    c_caus_d = din("c_caus01", [128, 128])
    c_mstrict_d = din("c_mstrict", [128, 128])
    c_negincl_d = din("c_negincl", [128, 128])
    c_gl_d = din("c_gl", [128, 512])
    c_boff_d = din("c_boff", [128, 768])
    c_aoff1_d = din("c_aoff1", [128, 128])
    c_rope_d = din("c_rope", [128, NT * 32])

    y_d = dscr("y", [S, D], out=True)
    x1_d = dscr("x1res", [S, D])
    x2_d = dscr("x2res", [S, D]) if L > 1 else None
    oT_d = dscr("oT", [8, 128, S], BF16)
    wupbf_d = [nc.dram_tensor("wupbf%d" % l_, [44, 128, 8, 128], BF16, kind="Internal").ap() for l_ in range(L)]
    wupbf_r = [sc.res("wupbf%d" % l_) for l_ in range(L)]

    def sb(name, shape, dt=F32):
        return es.enter_context(nc.sbuf_tensor(name, list(shape), dt))

    def ps(name, shape, dt=F32):
        return es.enter_context(nc.psum_tensor(name, list(shape), dt))

    xT = sb("xT", [128, 8, S], BF16)
    xT_r = [sc.res("xT%d" % t) for t in range(NT)]
    ident_f = sb("ident_f", [128, 128]); ident_b = sb("ident_b", [128, 128], BF16)
    caus_b = sb("caus_b", [128, 128], BF16)
    rope = sb("rope", [128, NT, 32])
    r_const = sc.res("consts")
    sc.ident_b = ident_b
    cbias = sb("cbias", [128, 4])
    sc.add("dve", lambda e: e.memset(cbias[:, 0:1], 1e-6), writes=[r_const], partial=True)
    sc.add("dve", lambda e: e.memset(cbias[:, 1:2], 1.0), writes=[r_const], partial=True)
    sc.add("dve", lambda e: e.memset(cbias[:, 2:3], 1e-5), writes=[r_const], partial=True)
    sc.add("dve", lambda e: e.memset(cbias[:, 3:4], 0.0), writes=[r_const], partial=True)

    bank = [ps("bank%d" % i, [128, 512]) for i in range(8)]
    bank_r = [sc.res("bank%d" % i) for i in range(8)]

    ARENA = 136 * 1024 // 4
    arena = sb("arena", [128, ARENA])

    class Carver:
        def __init__(self):
            self.off = 0

        def reset(self):
            self.off = 0

        def get(self, shape, dt=F32):
            n = int(np.prod(shape[1:]))
            nwords = n if dt == F32 else (n + 1) // 2
            a = arena[0:shape[0], self.off:self.off + nwords]
            self.off += (nwords + 15) // 16 * 16
            assert self.off <= ARENA, "arena overflow %d" % self.off
            if dt != F32:
                a = a.bitcast(dt)[:, 0:n]
            if len(shape) > 2:
                names = " ".join("d%d" % k for k in range(len(shape) - 1))
                kw = {"d%d" % k: shape[k + 1] for k in range(len(shape) - 2)}
                a = a.rearrange("p (%s) -> p %s" % (names, names), **kw)
            return a

    cv = Carver()

    sc.add("sp", lambda e: e.dma_start(out=ident_f[:], in_=c_ident_d[:, :]), writes=[r_const], dma=True, key="c0", partial=True)
    sc.add("pool", lambda e: e.dma_start(out=ident_b[:], in_=c_ident_d[:, :]), writes=[r_const], dma=True, key="c1", partial=True)
    sc.add("pool", lambda e: e.dma_start(out=caus_b[:], in_=c_caus_d[:, :]), writes=[r_const], dma=True, key="c1", partial=True)
    sc.add("sp", lambda e: e.dma_start(out=rope[:].rearrange("p t c -> p (t c)"), in_=c_rope_d[:, :]), writes=[r_const], dma=True, key="c0", partial=True)

    def phase0(src_d):
        cv.reset()
        xb = [cv.get([128, 1024], BF16) for _ in range(3)]
        xb_r = [sc.res("xb%d" % i) for i in range(3)]
        pst = [bank[0][:].bitcast(BF16), bank[1][:].bitcast(BF16)]
        for t in range(NT):
            s = t % 3
            sc.add("pool", lambda e, t=t, s=s: e.dma_start(out=xb[s], in_=src_d[t * 128:(t + 1) * 128, :]),
                   writes=[xb_r[s]], dma=True, key="xb%d" % s)
            p = t % 2
            pt = pst[p].rearrange("p (k c) -> p k c", k=8)
            for kc in range(8):
                sc.add("pe", lambda e, kc=kc, s=s, pt=pt: e.transpose(out=pt[:, kc, :], in_=xb[s][:, kc * 128:(kc + 1) * 128], identity=ident_b[:]),
                       reads=[xb_r[s], r_const], writes=[bank_r[p]])
            if t % 2 == 0:
                sc.add("act", lambda e, t=t, pt=pt: e.copy(out=xT[:, :, t * 128:(t + 1) * 128], in_=pt),
                       reads=[bank_r[p]], writes=[xT_r[t]])
            else:
                sc.add("dve", lambda e, t=t, pt=pt: e.tensor_copy(out=xT[:, :, t * 128:(t + 1) * 128], in_=pt),
                       reads=[bank_r[p]], writes=[xT_r[t]])

    def phaseA(l):
        cv.reset()
        wA = cv.get([128, 8, 1536], BF16)
        wA_r = [sc.res("wA%d" % k) for k in range(8)]
        KT = cv.get([128, 4, S], BF16)
        KT_r = [sc.res("KT%d" % t) for t in range(NT)]
        Vp = cv.get([128, NT, 8, 65], BF16)
        Vp_r = [sc.res("Vp%d" % t) for t in range(NT)]
        QT = [cv.get([128, 4, 256], BF16) for _ in range(2)]
        QT_r = [sc.res("QT%d" % i) for i in range(2)]
        kmT = cv.get([128, 4, 16], BF16)
        kmf = cv.get([128, 4])
        kmT_r = sc.res("kmT")
        qb = [cv.get([128, 512], BF16) for _ in range(2)]
        kb = [cv.get([128, 512], BF16) for _ in range(2)]
        qb_r = [sc.res("qb%d" % i) for i in range(2)]
        kb_r = [sc.res("kb%d" % i) for i in range(2)]
        t1 = cv.get([128, 8, 16]); t2 = cv.get([128, 8, 16])
        t1_r = sc.res("t1"); t2_r = sc.res("t2")
        gsb = cv.get([128, 16, 16]); m8 = cv.get([128, 16, 8]); sel = cv.get([128, 16, 16])
        gsb_r = sc.res("gsb"); sel_r = sc.res("sel")
        NPT = 4
        PT = [cv.get([128, 2, 256], BF16) for _ in range(NPT)]
        PT_r = [sc.res("PT%d" % i) for i in range(NPT)]
        acc = cv.get([128, 2, 8, 65])
        acc_r = [[sc.res("acc%d_%d" % (q, h)) for h in range(8)] for q in range(2)]
        rec = cv.get([128, 16])
        ob = cv.get([128, 2, 512], BF16)
        ob_r = sc.res("ob")
        obT = [cv.get([128, 4, 256], BF16) for _ in range(2)]
        obT_r = [sc.res("obT%d" % i) for i in range(2)]

        for kc in range(8):
            sc.add("pool", lambda e, kc=kc: e.dma_start(out=wA[:, kc, :], in_=w_in_d[l, kc * 128:(kc + 1) * 128, 2056:3592]),
                   writes=[wA_r[kc]], dma=True, key="wA%d" % kc)
        sc.add("pool", lambda e: e.memset(Vp[:, :, :, 64:65], 1.0), writes=Vp_r)
        sc.add("pool", lambda e: e.memset(gsb[:], -1e30), writes=[gsb_r])
        sc.add("pool", lambda e: e.memset(kmT[:], 0.0), writes=[kmT_r])

        pq, pk, pv = bank[0], bank[1], bank[2]
        ptr = bank[3][:].bitcast(BF16).rearrange("p (k c) -> p k c", k=8)
        pg0 = bank[4][:, 0:128].rearrange("p (a b) -> p a b", a=8)
        pg1 = bank[6][:, 0:128].rearrange("p (a b) -> p a b", a=8)
        SB = (0, 1, 2, 5)
        OB = (6, 7)
        cnt = {"s": 0, "o": 0, "pt": 0, "ev": 0}


        KSTOP = int(os.environ.get("KSTOP", "99"))
        for t in range(NT):
            b = t // 2
            qt_ = t % 2
            tsl = slice(t * 128, (t + 1) * 128)
            if KSTOP <= 0:
                break
            for g, pp in enumerate((pq, pk, pv)):
                for kc in range(8):
                    sc.add("pe", lambda e, g=g, kc=kc, pp=pp, tsl=tsl: e.matmul(pp[:], lhsT=xT[:, kc, tsl], rhs=wA[:, kc, g * 512:(g + 1) * 512], start=(kc == 0), stop=(kc == 7)),
                           reads=[xT_r[t], wA_r[kc]], writes=[bank_r[g]])
            if KSTOP <= 1:
                continue
            sc.add("act", lambda e, t=t: e.copy(out=Vp[:, t, :, 0:64], in_=pv[:].rearrange("p (h d) -> p h d", h=8)),
                   reads=[bank_r[2]], writes=[Vp_r[t]])
            s2 = t % 2
            for (pp, dst, dst_r, bi) in ((pq, qb[s2], qb_r[s2], 0), (pk, kb[s2], kb_r[s2], 1)):
                p3 = pp[:].rearrange("p (h d) -> p h d", h=8)
                d3 = dst.rearrange("p (h d) -> p h d", h=8)
                sc.add("act", lambda e, p3=p3, d3=d3: e.copy(out=d3[:, :, 16:64], in_=p3[:, :, 16:64]),
                       reads=[bank_r[bi]], writes=[dst_r])
                ccb = rope[:, t, 0:16].unsqueeze(1).to_broadcast([128, 8, 16])
                ssb = rope[:, t, 16:32].unsqueeze(1).to_broadcast([128, 8, 16])
                sc.add("dve", lambda e, p3=p3, ccb=ccb: e.tensor_tensor(out=t1, in0=p3[:, :, 0:16], in1=ccb, op=ALU.mult),
                       reads=[bank_r[bi], r_const], writes=[t1_r])
                sc.add("dve", lambda e, p3=p3, ssb=ssb: e.tensor_tensor(out=t2, in0=p3[:, :, 0:16], in1=ssb, op=ALU.mult),
                       reads=[bank_r[bi], r_const], writes=[t2_r])
                sc.add("dve", lambda e, d3=d3: e.tensor_tensor(out=d3[:, :, 0:8], in0=t1[:, :, 0:8], in1=t2[:, :, 8:16], op=ALU.subtract),
                       reads=[t1_r, t2_r], writes=[dst_r], partial=True)
                sc.add("dve", lambda e, d3=d3: e.tensor_tensor(out=d3[:, :, 8:16], in0=t1[:, :, 8:16], in1=t2[:, :, 0:8], op=ALU.add),
                       reads=[t1_r, t2_r], writes=[dst_r], partial=True)
            if KSTOP <= 2:
                continue
            for j in range(4):
                sc.add("pe", lambda e, j=j, s2=s2: e.transpose(out=ptr[:, j, :], in_=qb[s2][:, j * 128:(j + 1) * 128], identity=ident_b[:]),
                       reads=[qb_r[s2], r_const], writes=[bank_r[3]])
            for j in range(4):
                sc.add("pe", lambda e, j=j, s2=s2: e.transpose(out=ptr[:, 4 + j, :], in_=kb[s2][:, j * 128:(j + 1) * 128], identity=ident_b[:]),
                       reads=[kb_r[s2], r_const], writes=[bank_r[3]])
            qs = b % 2
            if KSTOP == 3 and os.environ.get("KSUB") == "a":
                continue
            sc.add("dve", lambda e, qs=qs, qt_=qt_: e.tensor_copy(out=QT[qs][:, :, qt_ * 128:(qt_ + 1) * 128], in_=ptr[:, 0:4, :]),
                   reads=[bank_r[3]], writes=[QT_r[qs]], partial=(qt_ == 1))
            if KSTOP == 3 and os.environ.get("KSUB") == "b":
                continue
            sc.add("dve", lambda e, tsl=tsl: e.tensor_copy(out=KT[:, :, tsl], in_=ptr[:, 4:8, :]),
                   reads=[bank_r[3]], writes=[KT_r[t]])
            if qt_ == 0 or KSTOP <= 3:
                continue
            if b + 1 < NB:
                sc.add("dve", lambda e, b=b: e.tensor_reduce(out=kmf, in_=KT[:, :, b * 256:(b + 1) * 256], axis=AX.X, op=ALU.add),
                       reads=[KT_r[t - 1], KT_r[t]], writes=[kmT_r])
                sc.add("dve", lambda e, b=b: e.tensor_scalar(out=kmT[:, :, b], in0=kmf, scalar1=1.0 / 256, scalar2=None, op0=ALU.mult),
                       reads=[kmT_r], writes=[kmT_r], partial=True)
            topk = b > 3
            if topk:
                KV_ = os.environ.get("KVAR", "")
                for par in range(2):
                    pgp = (pg0, pg1)[par]
                    for q2 in range(2):
                        for hh in range(4):
                            if KV_ == "q0" and q2 == 1: continue
                            if KV_ == "p0" and par == 1: continue
                            if KV_ == "h0" and hh > 0: continue
                            base = par * 64
                            sc.add("pe", lambda e, pgp=pgp, q2=q2, hh=hh, base=base, qs=qs: e.matmul(pgp[:, q2 * 4 + hh, :], lhsT=QT[qs][base:base + 64, hh, q2 * 128:(q2 + 1) * 128], rhs=kmT[base:base + 64, hh, :], start=True, stop=True),
                                   reads=[QT_r[qs], kmT_r], writes=[bank_r[(4, 6)[par]]])
                KT_ = os.environ.get("KTOPK", "full")
                if KT_ in ("gc", "gcm", "full"):
                    sc.add("dve", lambda e, b=b: e.tensor_copy(out=gsb[:, 0:8, 0:b], in_=pg0[:, :, 0:b]), reads=[bank_r[4]], writes=[gsb_r])
                    sc.add("dve", lambda e, b=b: e.tensor_copy(out=gsb[:, 8:16, 0:b], in_=pg1[:, :, 0:b]), reads=[bank_r[6]], writes=[gsb_r], partial=True)
                if KT_ in ("gcm", "full"):
                    for i16 in range(16):
                        sc.add("dve", lambda e, i16=i16: e.max(out=m8[:, i16, :], in_=gsb[:, i16, :]), reads=[gsb_r], writes=[sel_r], partial=True)
                if KT_ == "full":
                    sc.add("dve", lambda e: e.tensor_tensor(out=sel[:], in0=gsb[:], in1=m8[:, :, 2:3].to_broadcast([128, 16, 16]), op=ALU.is_ge),
                           reads=[gsb_r, sel_r], writes=[sel_r])
                else:
                    sc.add("dve", lambda e: e.memset(sel[:], 1.0), reads=[gsb_r, bank_r[4]], writes=[sel_r])
            units = [(h, n) for h in range(8 if KSTOP > 4 else 0) for n in ([b] + list(range(b)))]
            ust = {}

            def emit_st(u, b=b, qs=qs):
                h, n = u
                j = h // 2; base = (h % 2) * 64
                si = SB[cnt["s"] % 4]; cnt["s"] += 1
                pi = cnt["pt"] % NPT; cnt["pt"] += 1
                pss = bank[si][:].rearrange("p (k q) -> p k q", k=2)
                for kt in range(2):
                    ktile = 2 * n + kt
                    sc.add("pe", lambda e, pss=pss, kt=kt, j=j, base=base, ktile=ktile, qs=qs: e.matmul(pss[:, kt, :], lhsT=KT[base:base + 64, j, ktile * 128:(ktile + 1) * 128], rhs=QT[qs][base:base + 64, j, :], start=True, stop=True),
                           reads=[KT_r[ktile], QT_r[qs]], writes=[bank_r[si]])
                sc.add("act", lambda e, pss=pss, pi=pi: e.activation(out=PT[pi][:], in_=pss, func=AF.Exp, scale=0.125, bias=cbias[:, 3:4]),
                       reads=[bank_r[si], r_const], writes=[PT_r[pi]])
                if n == b:
                    for kt in range(2):
                        sc.add("pool", lambda e, pi=pi, kt=kt: e.tensor_tensor(out=PT[pi][:, kt, kt * 128:(kt + 1) * 128], in0=PT[pi][:, kt, kt * 128:(kt + 1) * 128], in1=caus_b[:], op=ALU.mult),
                               reads=[PT_r[pi], r_const], writes=[PT_r[pi]])
                ust[u] = pi

            def emit_pv(u, b=b, topk=topk):
                h, n = u
                pi = ust.pop(u)
                oi = OB[cnt["o"] % 2]; cnt["o"] += 1
                pso = bank[oi][:, 0:130].rearrange("p (q d) -> p q d", q=2)
                for q2 in range(2):
                    kts = [0] if (n == b and q2 == 0) else [0, 1]
                    for ii, kt in enumerate(kts):
                        sc.add("pe", lambda e, pso=pso, pi=pi, q2=q2, kt=kt, n=n, h=h, ii=ii, last=(ii == len(kts) - 1): e.matmul(pso[:, q2, :], lhsT=PT[pi][:, kt, q2 * 128:(q2 + 1) * 128], rhs=Vp[:, 2 * n + kt, h, :], start=(ii == 0), stop=last),
                               reads=[PT_r[pi], Vp_r[2 * n + kt]], writes=[bank_r[oi]])
                for q2 in range(2):
                    if n == b:
                        sc.add("dve", lambda e, pso=pso, q2=q2, h=h: e.tensor_copy(out=acc[:, q2, h, :], in_=pso[:, q2, :]),
                               reads=[bank_r[oi]], writes=[acc_r[q2][h]])
                    elif topk:
                        sc.add("dve", lambda e, pso=pso, q2=q2, h=h, n=n: e.scalar_tensor_tensor(out=acc[:, q2, h, :], in0=pso[:, q2, :], scalar=sel[:, (h % 2) * 8 + q2 * 4 + h // 2, n:n + 1], in1=acc[:, q2, h, :], op0=ALU.mult, op1=ALU.add),
                               reads=[bank_r[oi], sel_r, acc_r[q2][h]], writes=[acc_r[q2][h]])
                    else:
                        sc.add("dve", lambda e, pso=pso, q2=q2, h=h: e.tensor_tensor(out=acc[:, q2, h, :], in0=pso[:, q2, :], in1=acc[:, q2, h, :], op=ALU.add),
                               reads=[bank_r[oi], acc_r[q2][h]], writes=[acc_r[q2][h]])

            LOOK = 2
            for i in range(min(LOOK, len(units))):
                emit_st(units[i])
            for i, u in enumerate(units):
                if i + LOOK < len(units):
                    emit_st(units[i + LOOK])
                emit_pv(u)
            if KSTOP <= 5:
                continue
            allacc = [acc_r[q][h] for q in range(2) for h in range(8)]
            sc.add("dve", lambda e: e.reciprocal(out=rec, in_=acc[:].rearrange("p q h d -> p (q h) d")[:, :, 64]), reads=allacc, writes=[ob_r])
            sc.add("dve", lambda e: e.tensor_tensor(out=ob[:].rearrange("p q (h d) -> p (q h) d", h=8), in0=acc[:].rearrange("p q h d -> p (q h) d")[:, :, 0:64], in1=rec.unsqueeze(2).to_broadcast([128, 16, 64]), op=ALU.mult),
                   reads=allacc + [ob_r], writes=[ob_r])
            os_ = b % 2
            for q2 in range(2):
                for j in range(4):
                    sc.add("pe", lambda e, q2=q2, j=j: e.transpose(out=ptr[:, q2 * 4 + j, :], in_=ob[:, q2, j * 128:(j + 1) * 128], identity=ident_b[:]),
                           reads=[ob_r, r_const], writes=[bank_r[3]])
            sc.add("dve", lambda e, os_=os_: e.tensor_copy(out=obT[os_][:].rearrange("p j (q c) -> p q j c", q=2), in_=ptr.rearrange("p (q j) c -> p q j c", q=2)),
                   reads=[bank_r[3]], writes=[obT_r[os_]])
            sc.add("sp", lambda e, os_=os_, b=b: e.dma_start(out=oT_d[4:8, :, b * 256:(b + 1) * 256].rearrange("j p c -> p j c"), in_=obT[os_][:]),
                   reads=[obT_r[os_]], writes=[], dma=True, key="obT%d" % os_)

    def phaseB(l):
        cv.reset()
        for _ in range(int(os.environ.get("ACTPAD", "0"))):
            sc.add("act", lambda e: e.copy(out=arena[:, 0:8], in_=ident_f[:, 0:8]))
        NBLK = S // 512
        wB = cv.get([128, 8, 2056], BF16)
        wB_r = [sc.res("wB%d" % k) for k in range(8)]
        gcw = cv.get([128, 12, 4]); dtb = cv.get([128, 4]); nexpA = cv.get([128, 4]); gng = cv.get([128, 128])
        mstrict = cv.get([128, 128]); mincl = cv.get([128, 128]); glc = cv.get([128, 4, 128])
        boff = cv.get([128, 6, 128]); aoff1 = cv.get([128, 128]); ones_f = cv.get([128, 128])
        pc_r = sc.res("pconst")
        rawb = cv.get([128, 2, 515]); rawb_r = [sc.res("rawb%d" % f) for f in range(2)]
        halo = cv.get([128, 12, 3]); halo_r = [sc.res("halo%d" % f) for f in range(12)]
        cacc = [cv.get([128, 512]) for _ in range(2)]; cacc_r = [sc.res("cacc%d" % i) for i in range(2)]
        cT = cv.get([128, 12, 512]); cT_r = [sc.res("cT%d" % f) for f in range(12)]
        Sst = [cv.get([128, 4, 128]) for _ in range(2)]
        S_r = [[sc.res("S%d_%d" % (i, h)) for h in range(4)] for i in range(2)]

        def tmp(name, shape, dt=F32, n=2):
            return [(cv.get(shape, dt), sc.res("%s%d" % (name, i))) for i in range(n)]

        Qtm = tmp("Qtm", [128, 4, 128]); Ktm = tmp("Ktm", [128, 4, 128]); Vtm = tmp("Vtm", [128, 4, 128])
        ssq = tmp("ssq", [128, 8]); rn = tmp("rn", [128, 8])
        sm = tmp("sm", [128, 64])
        zs = tmp("zs", [128, 512])
        junk = tmp("junk", [128, 128], n=1)[0]
        HT = 2
        QTh = tmp("QTh", [128, 128], F32, HT); KTh = tmp("KTh", [128, 128], F32, HT)
        GR = tmp("GR", [128, 128], F32, HT); Dm = tmp("Dm", [128, 128], F32, HT); Ds = tmp("Ds", [128, 128], F32, HT)
        Am = tmp("Am", [128, 128], F32, 2 * HT); attn = tmp("attn", [128, 128], F32, 2 * HT); attnT = tmp("attnT", [128, 128], F32, HT)
        Boall = tmp("Boall", [128, 6, 128], F32, HT); Em = tmp("Em", [128, 128], F32, 2 * HT); Dk = tmp("Dk", [128, 128], F32, 2 * HT)
        Xm = tmp("Xm", [128, 128], F32, HT); Rm = tmp("Rm", [128, 256], F32, HT); UW = tmp("UW", [128, 256], F32, HT)
        Kd = tmp("Kd", [128, 128], F32, HT); Qd = tmp("Qd", [128, 128], F32, HT); QpT = tmp("QpT", [128, 128], F32, HT)
        MpT = tmp("MpT", [128, 2, 128], F32, HT)
        osb = tmp("osb", [128, 4, 128], F32, 2); oss = tmp("oss", [128, 8], F32, 2)
        oab = tmp("oab", [128, 512], BF16, 2); oaT = tmp("oaT", [128, 4, 128], BF16, 2)
        ctr = {}

        def nxt(lst, key):
            i = ctr.get(key, 0); ctr[key] = i + 1
            return lst[i % len(lst)]

        PB = tuple(int(x) for x in os.environ.get("PB", "2,3,4,6,7").split(","))

        def pbank():
            i = ctr.get("pb", 0); ctr["pb"] = i + 1
            bi = PB[i % len(PB)]
            return bank[bi], bank_r[bi]

        for kc in range(8):
            sc.add("pool", lambda e, kc=kc: e.dma_start(out=wB[:, kc, :], in_=w_in_d[l, kc * 128:(kc + 1) * 128, 0:2056]),
                   writes=[wB_r[kc]], dma=True, key="wB%d" % kc)
        w4 = cT.rearrange("p f t -> p (f t)")[0:4, 0:1536]; w4_r = sc.res("w4")
        sc.add("sp", lambda e: e.dma_start(out=w4, in_=gconv_d[l, :, :]), writes=[w4_r, cT_r[0], cT_r[1], cT_r[2]], dma=True, key="w4")
        for f in range(12):
            sc.pe32(lambda e, f=f: e.transpose(out=bank[0][:, f * 4:(f + 1) * 4], in_=w4[0:4, f * 128:(f + 1) * 128], identity=ident_f[0:4, 0:4]), reads=[w4_r, cT_r[0], cT_r[1], cT_r[2], r_const], writes=[bank_r[0]])
        sc.add("dve", lambda e: e.tensor_copy(out=gcw.rearrange("p f j -> p (f j)"), in_=bank[0][:, 0:48]), reads=[bank_r[0]], writes=[pc_r], partial=True)
        sc.add("sp", lambda e: e.dma_start(out=dtb, in_=dtb_d[l, :].partition_broadcast(128)), writes=[pc_r], dma=True, key="pc", partial=True)
        sc.add("sp", lambda e: e.dma_start(out=nexpA, in_=alog_d[l, :].partition_broadcast(128)), writes=[pc_r], dma=True, key="pc", partial=True)
        sc.add("sp", lambda e: e.dma_start(out=gng, in_=gng_d[l, :].partition_broadcast(128)), writes=[pc_r], dma=True, key="pc", partial=True)
        sc.add("sp", lambda e: e.dma_start(out=mstrict, in_=c_mstrict_d[:, :]), writes=[pc_r], dma=True, key="pc", partial=True)
        sc.add("sp", lambda e: e.dma_start(out=glc.rearrange("p a b -> p (a b)"), in_=c_gl_d[:, :]), writes=[pc_r], dma=True, key="pc", partial=True)
        sc.add("sp", lambda e: e.dma_start(out=boff.rearrange("p a b -> p (a b)"), in_=c_boff_d[:, :]), writes=[pc_r], dma=True, key="pc", partial=True)
        sc.add("sp", lambda e: e.dma_start(out=aoff1, in_=c_aoff1_d[:, :]), writes=[pc_r], dma=True, key="pc", partial=True)
        sc.add("sp", lambda e: e.dma_start(out=mincl, in_=c_negincl_d[:, :]), writes=[pc_r], dma=True, key="pc", partial=True)
        sc.add("act", lambda e: e.activation(out=nexpA, in_=nexpA, func=AF.Exp, bias=cbias[:, 3:4]), reads=[pc_r], writes=[pc_r])
        sc.add("dve", lambda e: e.tensor_scalar(out=nexpA, in0=nexpA, scalar1=-1.0, scalar2=None, op0=ALU.mult), reads=[pc_r], writes=[pc_r])
        sc.add("dve", lambda e: e.memset(ones_f, 1.0), writes=[pc_r], reads=[pc_r])
        sc.add("dve", lambda e: e.memset(Sst[0][:], 0.0), writes=S_r[0])
        sc.add("pool", lambda e: e.memset(halo, 0.0), writes=halo_r)
        LTc, UTc, CS0c, CS1c = (glc[:, i, :] for i in range(4))

        cur = 0

        KB = int(os.environ.get("KB", "99"))
        for c in range(NBLK):
            csl = slice(c * 512, (c + 1) * 512)
            for f in range(12):
                pb, pb_r = bank[f % 2], bank_r[f % 2]
                for kc in range(8):
                    sc.pe16(pb[:], lambda e, pb=pb, f=f, kc=kc, csl=csl: e.matmul(pb[:], lhsT=wB[:, kc, f * 128:(f + 1) * 128], rhs=xT[:, kc, csl], start=(kc == 0), stop=(kc == 7)),
                           reads=[wB_r[kc]] + xT_r[4 * c:4 * c + 4], writes=[pb_r])
                rs = f % 2
                sc.add("pool", lambda e, f=f, rs=rs: e.tensor_copy(out=rawb[:, rs, 0:3], in_=halo[:, f, :]), reads=[halo_r[f]], writes=[rawb_r[rs]])
                sc.add("act", lambda e, pb=pb, rs=rs: e.copy(out=rawb[:, rs, 3:515], in_=pb[:]), reads=[pb_r], writes=[rawb_r[rs]], partial=True)
                sc.add("pool", lambda e, f=f, rs=rs: e.tensor_copy(out=halo[:, f, :], in_=rawb[:, rs, 512:515]), reads=[rawb_r[rs]], writes=[halo_r[f]])
                ca, ca_r = cacc[f % 2], cacc_r[f % 2]
                sc.add("act", lambda e, pb=pb, f=f, ca=ca: e.activation(out=ca, in_=pb[:], func=AF.Copy, scale=gcw[:, f, 3:4]), reads=[pb_r, pc_r], writes=[ca_r])
                for j in (2, 1, 0):
                    sc.add("dve", lambda e, f=f, j=j, ca=ca, rs=rs: e.scalar_tensor_tensor(out=ca, in0=rawb[:, rs, j:j + 512], scalar=gcw[:, f, j:j + 1], in1=ca, op0=ALU.mult, op1=ALU.add),
                           reads=[rawb_r[rs], pc_r, ca_r], writes=[ca_r])
                sc.add("act", lambda e, f=f, ca=ca: e.activation(out=cT[:, f, :], in_=ca, func=AF.Silu, bias=cbias[:, 3:4]), reads=[ca_r], writes=[cT_r[f]])
            for tt in range(4 if KB > 1 else 0):
                t = c * 4 + tt
                tsl = slice(t * 128, (t + 1) * 128)
                lsl = slice(tt * 128, (tt + 1) * 128)
                (Qt, Qt_r) = nxt(Qtm, "Qtm"); (Kt, Kt_r) = nxt(Ktm, "Ktm"); (Vt, Vt_r) = nxt(Vtm, "Vtm")
                (sq, sq_r) = nxt(ssq, "ssq"); (rnn, rn_r) = nxt(rn, "rn"); (smt, sm_r) = nxt(sm, "sm"); (zst, zs_r) = nxt(zs, "zs")
                for g in range(3):
                    pb, pb_r = bank[2 + g], bank_r[2 + g]
                    for h in range(4):
                        sc.pe32(lambda e, pb=pb, g=g, h=h, lsl=lsl: e.transpose(out=pb[:, h * 128:(h + 1) * 128], in_=cT[:, g * 4 + h, lsl], identity=ident_f[:]),
                               reads=[cT_r[g * 4 + h], r_const], writes=[pb_r])
                for g in range(2):
                    for h in range(4):
                        sc.add("act", lambda e, g=g, h=h, sq=sq: e.activation(out=junk[0], in_=bank[2 + g][:, h * 128:(h + 1) * 128], func=AF.Square, bias=cbias[:, 3:4], accum_out=sq[:, g * 4 + h:g * 4 + h + 1]),
                               reads=[bank_r[2 + g]], writes=[sq_r, junk[1]], partial=True)
                sc.add("act", lambda e, sq=sq, rnn=rnn: e.activation(out=rnn, in_=sq, func=AF.Sqrt, bias=cbias[:, 0:1]), reads=[sq_r, r_const], writes=[rn_r])
                sc.add("dve", lambda e, rnn=rnn: e.reciprocal(out=rnn, in_=rnn), reads=[rn_r], writes=[rn_r])
                sc.add("dve", lambda e, rnn=rnn: e.tensor_scalar(out=rnn[:, 0:4], in0=rnn[:, 0:4], scalar1=float(128 ** -0.5), scalar2=None, op0=ALU.mult), reads=[rn_r], writes=[rn_r])
                sc.add("dve", lambda e, Qt=Qt, rnn=rnn: e.tensor_tensor(out=Qt, in0=bank[2][:].rearrange("p (h d) -> p h d", h=4), in1=rnn[:, 0:4].unsqueeze(2).to_broadcast([128, 4, 128]), op=ALU.mult),
                       reads=[bank_r[2], rn_r], writes=[Qt_r])
                sc.add("dve", lambda e, Kt=Kt, rnn=rnn: e.tensor_tensor(out=Kt, in0=bank[3][:].rearrange("p (h d) -> p h d", h=4), in1=rnn[:, 4:8].unsqueeze(2).to_broadcast([128, 4, 128]), op=ALU.mult),
                       reads=[bank_r[3], rn_r], writes=[Kt_r])
                sc.add("act", lambda e, Vt=Vt: e.copy(out=Vt, in_=bank[4][:].rearrange("p (h d) -> p h d", h=4)), reads=[bank_r[4]], writes=[Vt_r])
                if KB <= 2:
                    continue
                pab, pab_r = bank[5], bank_r[5]
                for kc in range(8):
                    sc.pe16(bank[5][:, 0:8], lambda e, kc=kc, tsl=tsl: e.matmul(bank[5][:, 0:8], lhsT=xT[:, kc, tsl], rhs=wB[:, kc, 1536:1544], start=(kc == 0), stop=(kc == 7)),
                           reads=[xT_r[t], wB_r[kc]], writes=[pab_r])
                sc.add("dve", lambda e, smt=smt: e.tensor_tensor(out=smt[:, 0:4], in0=bank[5][:, 0:4], in1=dtb, op=ALU.add), reads=[pab_r, pc_r], writes=[sm_r])
                sc.add("dve", lambda e, smt=smt: e.tensor_scalar(out=smt[:, 36:40], in0=smt[:, 0:4], scalar1=-1.0, scalar2=None, op0=ALU.mult), reads=[sm_r], writes=[sm_r])
                sc.add("dve", lambda e, smt=smt: e.tensor_tensor(out=smt[:, 4:8], in0=smt[:, 0:4], in1=smt[:, 36:40], op=ALU.min), reads=[sm_r], writes=[sm_r])
                sc.add("act", lambda e, smt=smt: e.activation(out=smt[:, 4:8], in_=smt[:, 4:8], func=AF.Exp, bias=cbias[:, 3:4]), reads=[sm_r], writes=[sm_r])
                sc.add("act", lambda e, smt=smt: e.activation(out=smt[:, 4:8], in_=smt[:, 4:8], func=AF.Ln, bias=cbias[:, 1:2]), reads=[sm_r, r_const], writes=[sm_r])
                sc.add("dve", lambda e, smt=smt: e.scalar_tensor_tensor(out=smt[:, 8:12], in0=smt[:, 0:4], scalar=0.0, in1=smt[:, 4:8], op0=ALU.max, op1=ALU.add), reads=[sm_r], writes=[sm_r])
                sc.add("dve", lambda e, smt=smt: e.tensor_tensor(out=smt[:, 8:12], in0=smt[:, 8:12], in1=nexpA, op=ALU.mult), reads=[sm_r, pc_r], writes=[sm_r])
                sc.add("act", lambda e, smt=smt: e.activation(out=smt[:, 12:16], in_=bank[5][:, 4:8], func=AF.Exp, scale=-1.0, bias=cbias[:, 3:4]), reads=[pab_r], writes=[sm_r])
                sc.add("dve", lambda e, smt=smt: e.tensor_scalar(out=smt[:, 12:16], in0=smt[:, 12:16], scalar1=1.0, scalar2=None, op0=ALU.add), reads=[sm_r], writes=[sm_r])
                sc.add("dve", lambda e, smt=smt: e.reciprocal(out=smt[:, 12:16], in_=smt[:, 12:16]), reads=[sm_r], writes=[sm_r])
                for kc in range(8):
                    sc.pe16(bank[5][:], lambda e, kc=kc, tsl=tsl: e.matmul(bank[5][:], lhsT=xT[:, kc, tsl], rhs=wB[:, kc, 1544:2056], start=(kc == 0), stop=(kc == 7)),
                           reads=[xT_r[t], wB_r[kc]], writes=[pab_r])
                sc.add("act", lambda e, zst=zst: e.activation(out=zst, in_=bank[5][:], func=AF.Silu, bias=cbias[:, 3:4]), reads=[pab_r], writes=[zs_r])
                sc.add("pool", lambda e, zst=zst: e.tensor_tensor(out=zst.rearrange("p (h d) -> p h d", h=4), in0=zst.rearrange("p (h d) -> p h d", h=4), in1=gng.unsqueeze(1).to_broadcast([128, 4, 128]), op=ALU.mult),
                       reads=[zs_r, pc_r], writes=[zs_r])
                for i4, lt in enumerate((LTc, UTc, CS0c, CS1c)):
                    sc.pe32(lambda e, i4=i4, lt=lt, smt=smt: e.matmul(bank[5][:, 16 + 4 * i4:20 + 4 * i4], lhsT=lt, rhs=smt[:, 8:12], start=True, stop=True),
                           reads=[sm_r, pc_r], writes=[pab_r])
                sc.add("dve", lambda e, smt=smt: e.tensor_copy(out=smt[:, 40:44], in_=bank[5][:, 16:20]), reads=[pab_r], writes=[sm_r])
                sc.add("dve", lambda e, smt=smt: e.tensor_scalar(out=smt[:, 16:32], in0=bank[5][:, 16:32], scalar1=-60.0, scalar2=None, op0=ALU.max), reads=[pab_r], writes=[sm_r])
                sc.add("act", lambda e, smt=smt: e.activation(out=smt[:, 16:32], in_=smt[:, 16:32], func=AF.Exp, bias=cbias[:, 3:4]), reads=[sm_r], writes=[sm_r])
                sc.add("dve", lambda e, smt=smt: e.tensor_tensor(out=smt[:, 32:36], in0=smt[:, 12:16], in1=smt[:, 16:20], op=ALU.mult), reads=[sm_r], writes=[sm_r])
                (osb_t, osb_r) = nxt(osb, "osb"); (oss_t, oss_r) = nxt(oss, "oss")
                if KB <= 3:
                    continue
                sc.safe = os.environ.get("SAFE", "1") == "1"
                for h in range(4):
                    (QT_, QT_r_) = nxt(QTh, "QTh"); (KT_, KT_r_) = nxt(KTh, "KTh"); (GR_, GR_r_) = nxt(GR, "GR")
                    (D_, D_r) = nxt(Dm, "Dm"); (Ds_, Ds_r) = nxt(Ds, "Ds"); (A_, A_r) = nxt(Am, "Am")
                    (at_, at_r) = nxt(attn, "attn"); (atT_, atT_r) = nxt(attnT, "attnT"); (Bo_, Bo_r) = nxt(Boall, "Bo")
                    (X_, X_r) = nxt(Xm, "X"); (R_, R_r) = nxt(Rm, "R"); (UW_, UW_r) = nxt(UW, "UW")
                    (Kd_, Kd_r) = nxt(Kd, "Kd"); (Qd_, Qd_r) = nxt(Qd, "Qd"); (QpT_, QpT_r) = nxt(QpT, "QpT"); (Mp_, Mp_r) = nxt(MpT, "MpT")
                    beta_h = smt[:, 12 + h:13 + h]; egc_h = smt[:, 16 + h:17 + h]; egu_h = smt[:, 20 + h:21 + h]
                    gcum_h = smt[:, 40 + h:41 + h]; bk_h = smt[:, 32 + h:33 + h]
                    pb, pb_r = pbank()
                    sc.pe32(lambda e, pb=pb, Qt=Qt, h=h: e.transpose(out=pb[:, 0:128], in_=Qt[:, h, :], identity=ident_f[:]), reads=[Qt_r, r_const], writes=[pb_r])
                    sc.pe32(lambda e, pb=pb, Kt=Kt, h=h: e.transpose(out=pb[:, 128:256], in_=Kt[:, h, :], identity=ident_f[:]), reads=[Kt_r, r_const], writes=[pb_r])
                    sc.add("dve", lambda e, pb=pb, QT_=QT_: e.tensor_copy(out=QT_, in_=pb[:, 0:128]), reads=[pb_r], writes=[QT_r_])
                    sc.add("dve", lambda e, pb=pb, KT_=KT_: e.tensor_copy(out=KT_, in_=pb[:, 128:256]), reads=[pb_r], writes=[KT_r_])
                    sc.add("dve", lambda e, GR_=GR_, smt=smt, h=h: e.tensor_scalar(out=GR_, in0=ones_f, scalar1=smt[:, 8 + h:9 + h], scalar2=None, op0=ALU.mult), reads=[sm_r, pc_r], writes=[GR_r_])
                    K4 = os.environ.get("K4", "z")
                    if KB == 4 and K4 <= "a":
                        continue
                    pg_, pg_r = pbank()
                    sc.pe32(lambda e, pg_=pg_, GR_=GR_: e.matmul(pg_[:, 0:128], lhsT=GR_, rhs=LTc, start=True, stop=True), reads=[GR_r_, pc_r], writes=[pg_r])
                    sc.add("dve", lambda e, pg_=pg_, D_=D_, gcum_h=gcum_h: e.tensor_scalar(out=D_, in0=pg_[:, 0:128], scalar1=gcum_h, scalar2=0.0, op0=ALU.subtract, op1=ALU.max), reads=[pg_r, sm_r], writes=[D_r])
                    sc.add("dve", lambda e, D_=D_: e.tensor_scalar(out=D_, in0=D_, scalar1=60.0, scalar2=None, op0=ALU.min), reads=[D_r], writes=[D_r])
                    if os.environ.get("K5") == "waitD0":
                        sc.pe32(lambda e, pg_=pg_: e.transpose(out=pg_[:, 256:384], in_=ident_f[:], identity=ident_f[:]), reads=[r_const, D_r], writes=[])
                    sc.add("act", lambda e, D_=D_: e.activation(out=D_, in_=D_, func=AF.Exp, scale=-1.0, bias=cbias[:, 3:4]), reads=[D_r], writes=[D_r])
                    if os.environ.get("K5") == "waitD1":
                        sc.pe32(lambda e, pg_=pg_: e.transpose(out=pg_[:, 256:384], in_=ident_f[:], identity=ident_f[:]), reads=[r_const, D_r], writes=[])
                    PD = os.environ.get("PD", "dve")
                    sc.add(PD, lambda e, D_=D_, Ds_=Ds_: e.tensor_tensor(out=Ds_, in0=D_, in1=mstrict, op=ALU.mult), reads=[D_r, pc_r], writes=[Ds_r])
                    sc.add(PD, lambda e, D_=D_: e.tensor_tensor(out=D_, in0=D_, in1=mincl, op=ALU.mult), reads=[D_r, pc_r], writes=[D_r])
                    if os.environ.get("K5") == "waitD2":
                        sc.pe32(lambda e, pg_=pg_: e.transpose(out=pg_[:, 256:384], in_=ident_f[:], identity=ident_f[:]), reads=[r_const, Ds_r], writes=[])
                    if KB == 4 and K4 <= "b":
                        continue
                    pk_, pk_r = pbank()
                    K8 = os.environ.get("K8", "")
                    if K8 != "noKK" and K8 != "none":
                        sc.pe32(lambda e, pk_=pk_, KT_=KT_: e.matmul(pk_[:, 0:128], lhsT=KT_, rhs=KT_, start=True, stop=True), reads=[KT_r_], writes=[pk_r])
                    if K8 != "noQK" and K8 != "none":
                        sc.pe32(lambda e, pk_=pk_, QT_=QT_, KT_=KT_: e.matmul(pk_[:, 128:256], lhsT=QT_, rhs=KT_, start=True, stop=True), reads=[QT_r_, KT_r_], writes=[pk_r])
                    sc.add("dve", lambda e, pk_=pk_, A_=A_, beta_h=beta_h: e.tensor_scalar(out=A_, in0=pk_[:, 0:128], scalar1=beta_h, scalar2=None, op0=ALU.mult), reads=[pk_r, sm_r], writes=[A_r])
                    sc.add("dve", lambda e, pk_=pk_, at_=at_: e.tensor_copy(out=at_, in_=pk_[:, 128:256]), reads=[pk_r], writes=[at_r])
                    (A0_, A0_r) = nxt(Am, "Am"); (at0_, at0_r) = nxt(attn, "attn")
                    sc.add("dve", lambda e, A_=A_, A0_=A0_, Ds_=Ds_: e.tensor_tensor(out=A0_, in0=A_, in1=Ds_, op=ALU.mult), reads=[A_r, Ds_r], writes=[A0_r])
                    sc.add("dve", lambda e, at_=at_, at0_=at0_, D_=D_: e.tensor_tensor(out=at0_, in0=at_, in1=D_, op=ALU.mult), reads=[at_r, D_r], writes=[at0_r])
                    A_, A_r, at_, at_r = A0_, A0_r, at0_, at0_r
                    if KB == 4 and K4 <= "c":
                        continue
                    if os.environ.get("HB", "0") == "1":
                        sc.barrier()
                    if os.environ.get("K6") == "samebank":
                        pt_, pt_r = pk_[:, 256:512], pk_r
                    else:
                        pt_, pt_r = pbank()
                    K5 = os.environ.get("K5", "")
                    if K5 == "waitonly":
                        sc.pe32(lambda e, pt_=pt_: e.transpose(out=pt_[:, 0:128], in_=ident_f[:], identity=ident_f[:]), reads=[r_const, A_r], writes=[pt_r])
                        continue
                    if K5 == "spin":
                        for _ in range(int(os.environ.get("NSPIN", "300"))):
                            sc.pe32(lambda e, pt_=pt_: e.transpose(out=pt_[:, 256:384], in_=ident_f[:], identity=ident_f[:]), reads=[r_const], writes=[pt_r])
                        sc.pe32(lambda e, pt_=pt_: e.transpose(out=pt_[:, 0:128], in_=ident_f[:], identity=ident_f[:]), reads=[r_const, A_r], writes=[pt_r])
                        continue
                    if K5 == "dummy":
                        sc.pe32(lambda e, pt_=pt_: e.transpose(out=pt_[:, 256:384], in_=ident_f[:], identity=ident_f[:]), reads=[r_const], writes=[pt_r])
                        sc.pe32(lambda e, pt_=pt_: e.transpose(out=pt_[:, 0:128], in_=ident_f[:], identity=ident_f[:]), reads=[r_const, A_r], writes=[pt_r])
                        continue
                    if K5 == "viaact2" and ((t * 4 + h) >= int(os.environ.get("KN", "999")) or (t * 4 + h) < int(os.environ.get("KN0", "0"))):
                        continue
                    if K5 == "viaact2":
                        sc.add("dve", lambda e, X_=X_, A_=A_: e.tensor_copy(out=X_, in_=A_), reads=[A_r], writes=[X_r])
                        continue
                    if K5 == "viaact":
                        sc.add("dve", lambda e, X_=X_, A_=A_: e.tensor_copy(out=X_, in_=A_), reads=[A_r], writes=[X_r])
                        sc.pe32(lambda e, pt_=pt_, X_=X_: e.transpose(out=pt_[:, 0:128], in_=X_, identity=ident_f[:]), reads=[r_const, X_r], writes=[pt_r])
                        continue
                    if K5 == "waitbf":
                        ptb_ = pt_.bitcast(BF16)
                        sc.pe16(ptb_[:, 0:128], lambda e, ptb_=ptb_: e.transpose(out=ptb_[:, 0:128], in_=ident_b[:], identity=ident_b[:]), reads=[r_const, A_r], writes=[pt_r])
                        continue
                    if K5 == "waitat":
                        sc.pe32(lambda e, pt_=pt_: e.transpose(out=pt_[:, 0:128], in_=ident_f[:], identity=ident_f[:]), reads=[r_const, at_r], writes=[pt_r])
                        continue
                    if K5 == "waitD":
                        sc.pe32(lambda e, pt_=pt_: e.transpose(out=pt_[:, 0:128], in_=ident_f[:], identity=ident_f[:]), reads=[r_const, Ds_r], writes=[pt_r])
                        continue
                    if K5 == "useGR":
                        sc.pe32(lambda e, pt_=pt_, GR_=GR_: e.transpose(out=pt_[:, 0:128], in_=GR_, identity=ident_f[:]), reads=[GR_r_, r_const] + ([A_r] if os.environ.get("K7") != "nodep" else []), writes=[pt_r])
                        continue
                    if K5 == "useDs":
                        sc.pe32(lambda e, pt_=pt_, Ds_=Ds_: e.transpose(out=pt_[:, 0:128], in_=Ds_, identity=ident_f[:]), reads=[Ds_r, A_r, r_const], writes=[pt_r])
                        continue
                    if K5 != "nope" and K5 != "pe2":
                        sc.pe32(lambda e, pt_=pt_, A_=A_: e.transpose(out=pt_[:, 0:128], in_=A_, identity=ident_f[:]), reads=[A_r, r_const], writes=[pt_r])
                    if K5 != "nope" and K5 != "pe1":
                        sc.pe32(lambda e, pt_=pt_, at_=at_: e.transpose(out=pt_[:, 128:256], in_=at_, identity=ident_f[:]), reads=[at_r, r_const], writes=[pt_r])
                    if K5 == "nodve":
                        continue
                    sc.add("dve", lambda e, pt_=pt_, X_=X_: e.tensor_copy(out=X_, in_=pt_[:, 0:128]), reads=[pt_r], writes=[X_r])
                    if KB == 4 and K4 <= "d":
                        continue
                    sc.add("pool", lambda e, X_=X_, Bo_=Bo_: e.tensor_tensor(out=Bo_, in0=X_.unsqueeze(1).to_broadcast([128, 6, 128]), in1=boff, op=ALU.mult), reads=[X_r, pc_r], writes=[Bo_r])
                    if KB == 4 and K4 <= "e":
                        continue
                    sc.add("dve", lambda e, pt_=pt_, atT_=atT_: e.tensor_copy(out=atT_, in_=pt_[:, 128:256]), reads=[pt_r], writes=[atT_r])
                    if KB <= 4:
                        continue
                    (E_, E_r) = nxt(Em, "E"); (Dk_, Dk_r) = nxt(Dk, "Dk")
                    sc.add("pool", lambda e, E_=E_, Bo_=Bo_: e.tensor_tensor(out=E_, in0=ident_f[:], in1=Bo_[:, 0, :], op=ALU.subtract), reads=[Bo_r, r_const], writes=[E_r])
                    sc.add("pool", lambda e, Dk_=Dk_, A_=A_: e.tensor_tensor(out=Dk_, in0=A_, in1=aoff1, op=ALU.mult), reads=[A_r, pc_r], writes=[Dk_r])
                    sc.add("pool", lambda e, Dk_=Dk_: e.tensor_tensor(out=Dk_, in0=ident_f[:], in1=Dk_, op=ALU.subtract), reads=[Dk_r, r_const], writes=[Dk_r])
                    for lvl in range(1, 6):
                        px_, px_r = pbank()
                        sc.pe32(lambda e, px_=px_, Bo_=Bo_, lvl=lvl, Dk_=Dk_: e.matmul(px_[:, 0:128], lhsT=Bo_[:, lvl, :], rhs=Dk_, start=True, stop=True), reads=[Bo_r, Dk_r], writes=[px_r])
                        sc.add("dve", lambda e, px_=px_, X_=X_: e.tensor_copy(out=X_, in_=px_[:, 0:128]), reads=[px_r], writes=[X_r])
                        py_, py_r = pbank()
                        sc.pe32(lambda e, py_=py_, X_=X_, E_=E_: e.matmul(py_[:, 0:128], lhsT=X_, rhs=E_, start=True, stop=True), reads=[X_r, E_r], writes=[py_r])
                        (E2_, E2_r) = nxt(Em, "E")
                        sc.add("dve", lambda e, py_=py_, E_=E_, E2_=E2_: e.tensor_tensor(out=E2_, in0=E_, in1=py_[:, 0:128], op=ALU.subtract), reads=[py_r, E_r], writes=[E2_r])
                        E_, E_r = E2_, E2_r
                        if lvl < 5:
                            pd_, pd_r = pbank()
                            sc.pe32(lambda e, pd_=pd_, E_=E_: e.transpose(out=pd_[:, 0:128], in_=E_, identity=ident_f[:]), reads=[E_r, r_const], writes=[pd_r])
                            (Dk_, Dk_r) = nxt(Dk, "Dk")
                            sc.add("dve", lambda e, pd_=pd_, Dk_=Dk_: e.tensor_copy(out=Dk_, in_=pd_[:, 0:128]), reads=[pd_r], writes=[Dk_r])
                    if KB <= 5:
                        continue
                    if os.environ.get("HB", "0") == "1":
                        sc.barrier()
                    sc.add("pool", lambda e, R_=R_, Vt=Vt, h=h, beta_h=beta_h: e.tensor_scalar(out=R_[:, 0:128], in0=Vt[:, h, :], scalar1=beta_h, scalar2=None, op0=ALU.mult), reads=[Vt_r, sm_r], writes=[R_r])
                    sc.add("pool", lambda e, R_=R_, Kt=Kt, h=h, bk_h=bk_h: e.tensor_scalar(out=R_[:, 128:256], in0=Kt[:, h, :], scalar1=bk_h, scalar2=None, op0=ALU.mult), reads=[Kt_r, sm_r], writes=[R_r], partial=True)
                    sc.add("pool", lambda e, Kd_=Kd_, Kt=Kt, h=h, egu_h=egu_h: e.tensor_scalar(out=Kd_, in0=Kt[:, h, :], scalar1=egu_h, scalar2=None, op0=ALU.mult), reads=[Kt_r, sm_r], writes=[Kd_r])
                    sc.add("pool", lambda e, Qd_=Qd_, Qt=Qt, h=h, egc_h=egc_h: e.tensor_scalar(out=Qd_, in0=Qt[:, h, :], scalar1=egc_h, scalar2=None, op0=ALU.mult), reads=[Qt_r, sm_r], writes=[Qd_r])
                    pu_, pu_r = pbank()
                    sc.pe32(lambda e, pu_=pu_, E_=E_, R_=R_: e.matmul(pu_[:, 0:256], lhsT=E_, rhs=R_, start=True, stop=True), reads=[E_r, R_r], writes=[pu_r])
                    sc.add("dve", lambda e, pu_=pu_, UW_=UW_: e.tensor_copy(out=UW_[:, 0:128], in_=pu_[:, 0:128]), reads=[pu_r], writes=[UW_r])
                    sc.add("dve", lambda e, pu_=pu_, UW_=UW_: e.tensor_scalar(out=UW_[:, 128:256], in0=pu_[:, 128:256], scalar1=-1.0, scalar2=None, op0=ALU.mult), reads=[pu_r], writes=[UW_r], partial=True)
                    pq_, pq_r = pbank()
                    sc.pe32(lambda e, pq_=pq_, Qd_=Qd_: e.matmul(pq_[:, 0:128], lhsT=Qd_, rhs=ident_f[:], start=True, stop=False), reads=[Qd_r, r_const], writes=[pq_r])
                    sc.pe32(lambda e, pq_=pq_, UW_=UW_, atT_=atT_: e.matmul(pq_[:, 0:128], lhsT=UW_[:, 128:256], rhs=atT_, start=False, stop=True), reads=[UW_r, atT_r], writes=[pq_r])
                    sc.add("dve", lambda e, pq_=pq_, QpT_=QpT_: e.tensor_copy(out=QpT_, in_=pq_[:, 0:128]), reads=[pq_r], writes=[QpT_r])
                    for ci in range(2):
                        pm_, pm_r = pbank()
                        ps_ = slice(ci * 64, ci * 64 + 64)
                        sc.pe32(lambda e, pm_=pm_, UW_=UW_, Kd_=Kd_, ps_=ps_: e.matmul(pm_[:, 0:128], lhsT=UW_[ps_, 128:256], rhs=Kd_[ps_, :], start=True, stop=True), reads=[UW_r, Kd_r], writes=[pm_r])
                        if ci == 0:
                            sc.add("dve", lambda e, pm_=pm_, Mp_=Mp_, ci=ci: e.tensor_copy(out=Mp_[:, ci, :], in_=pm_[:, 0:128]), reads=[pm_r], writes=[Mp_r])
                        else:
                            sc.add("dve", lambda e, pm_=pm_, Mp_=Mp_, ci=ci: e.tensor_copy(out=Mp_[:, ci, :], in_=pm_[:, 0:128]), reads=[pm_r], writes=[Mp_r], partial=True)
                    if os.environ.get("HB", "0") == "1":
                        sc.barrier()
                    for ci in range(2 if KB > 6 else 0):
                        ps_ = slice(ci * 64, ci * 64 + 64)
                        Sp, Sp_r = Sst[cur], S_r[cur][h]
                        Sn, Sn_r = Sst[1 - cur], S_r[1 - cur][h]
                        po_, po_r = pbank()
                        tp = (0, ci * 64)
                        sc.pe32(lambda e, po_=po_, QpT_=QpT_, Sp=Sp, h=h, ps_=ps_, tp=tp: e.matmul(po_[ps_, 0:128], lhsT=QpT_[:, ps_], rhs=Sp[:, h, :], start=True, stop=False, tile_position=tp), reads=[QpT_r, Sp_r], writes=[po_r])
                        sc.pe32(lambda e, po_=po_, atT_=atT_, UW_=UW_, ps_=ps_, tp=tp: e.matmul(po_[ps_, 0:128], lhsT=atT_[:, ps_], rhs=UW_[:, 0:128], start=False, stop=True, tile_position=tp), reads=[atT_r, UW_r], writes=[po_r])
                        sc.add("dve", lambda e, po_=po_, osb_t=osb_t, h=h, ps_=ps_: e.tensor_copy(out=osb_t[ps_, h, :], in_=po_[ps_, 0:128]), reads=[po_r], writes=[osb_r], partial=True)
                        sc.add("dve", lambda e, osb_t=osb_t, h=h, ps_=ps_: e.tensor_tensor(out=junk[0][ps_, :], in0=osb_t[ps_, h, :], in1=osb_t[ps_, h, :], op=ALU.mult), reads=[osb_r], writes=[junk[1]])
                        sc.add("dve", lambda e, oss_t=oss_t, h=h, ps_=ps_: e.tensor_reduce(out=oss_t[ps_, h:h + 1], in_=junk[0][ps_, :], axis=AX.X, op=ALU.add), reads=[junk[1]], writes=[oss_r], partial=True)
                        pS_, pS_r = pbank()
                        sc.pe32(lambda e, pS_=pS_, Mp_=Mp_, ci=ci, Sp=Sp, h=h: e.matmul(pS_[:, 0:128], lhsT=Mp_[:, ci, :], rhs=Sp[:, h, :], start=True, stop=False), reads=[Mp_r, Sp_r], writes=[pS_r])
                        sc.pe32(lambda e, pS_=pS_, Kd_=Kd_, UW_=UW_, ps_=ps_: e.matmul(pS_[:, 0:128], lhsT=Kd_[ps_, :], rhs=UW_[ps_, 0:128], start=False, stop=True), reads=[Kd_r, UW_r], writes=[pS_r])
                        egl = smt[:, 24 + 4 * ci + h:25 + 4 * ci + h]
                        sc.add("dve", lambda e, pS_=pS_, Sp=Sp, Sn=Sn, h=h, egl=egl: e.scalar_tensor_tensor(out=Sn[:, h, :], in0=Sp[:, h, :], scalar=egl, in1=pS_[:, 0:128], op0=ALU.mult, op1=ALU.add), reads=[pS_r, Sp_r, sm_r], writes=[Sn_r])
                        cur = 1 - cur
                sc.safe = False
                if KB <= 7:
                    continue
                (oab_t, oab_r) = nxt(oab, "oab"); (oaT_t, oaT_r) = nxt(oaT, "oaT")
                sc.add("act", lambda e, oss_t=oss_t: e.activation(out=oss_t[:, 4:8], in_=oss_t[:, 0:4], func=AF.Sqrt, scale=1.0 / 128, bias=cbias[:, 0:1]), reads=[oss_r, r_const], writes=[oss_r])
                sc.add("dve", lambda e, oss_t=oss_t: e.reciprocal(out=oss_t[:, 4:8], in_=oss_t[:, 4:8]), reads=[oss_r], writes=[oss_r])
                sc.add("dve", lambda e, osb_t=osb_t, oss_t=oss_t: e.tensor_tensor(out=osb_t, in0=osb_t, in1=oss_t[:, 4:8].unsqueeze(2).to_broadcast([128, 4, 128]), op=ALU.mult), reads=[osb_r, oss_r], writes=[osb_r])
                sc.add("dve", lambda e, osb_t=osb_t, zst=zst, oab_t=oab_t: e.tensor_tensor(out=oab_t, in0=osb_t.rearrange("p h d -> p (h d)"), in1=zst, op=ALU.mult), reads=[osb_r, zs_r], writes=[oab_r])
                ptb = bank[5][:].bitcast(BF16).rearrange("p (k c) -> p k c", k=8)
                for h in range(4):
                    sc.pe16(ptb[:, h, :], lambda e, h=h, oab_t=oab_t: e.transpose(out=ptb[:, h, :], in_=oab_t[:, h * 128:(h + 1) * 128], identity=ident_b[:]), reads=[oab_r, r_const], writes=[bank_r[5]])
                sc.add("dve", lambda e, oaT_t=oaT_t: e.tensor_copy(out=oaT_t, in_=ptb[:, 0:4, :]), reads=[bank_r[5]], writes=[oaT_r])
                sc.add("sp", lambda e, oaT_t=oaT_t, tsl=tsl: e.dma_start(out=oT_d[0:4, :, tsl].rearrange("j p c -> p j c"), in_=oaT_t), reads=[oaT_r], writes=[], dma=True, key="oaT%d" % (ctr["oaT"] % 2))

    def ln_tile(L_, t, ps_lo, ps_lo_r, ps_hi, ps_hi_r, xr, xr_r, g_bc, b_bc, lnp_r, out_d, write_xT):
        tsl = slice(t * 128, (t + 1) * 128)
        (y, y_r) = L_["y"][t % 2]; (st, st_r) = L_["st"][t % 2]; (xb_, xb_r_) = L_["xb"][t % 2]
        for half, (pp, pp_r) in enumerate(((ps_lo, ps_lo_r), (ps_hi, ps_hi_r))):
            hs = slice(half * 512, (half + 1) * 512)
            sc.add("dve", lambda e, pp=pp, hs=hs, y=y, xr=xr: e.scalar_tensor_tensor(out=y[:, hs], in0=xr[:, hs], scalar=ALPHA, in1=pp[:, 0:512], op0=ALU.mult, op1=ALU.add),
                   reads=[pp_r, xr_r], writes=[y_r], partial=(half == 1))
        for half in range(2):
            hs = slice(half * 512, (half + 1) * 512)
            sc.add("dve", lambda e, half=half, hs=hs, y=y, st=st: e.bn_stats(out=st[:, half * 6:(half + 1) * 6], in_=y[:, hs]), reads=[y_r], writes=[st_r], partial=(half == 1))
        sc.add("dve", lambda e, st=st: e.bn_aggr(out=st[:, 12:14], in_=st[:, 0:12]), reads=[st_r], writes=[st_r])
        sc.add("act", lambda e, st=st: e.activation(out=st[:, 14:15], in_=st[:, 13:14], func=AF.Sqrt, bias=cbias[:, 2:3]), reads=[st_r, r_const], writes=[st_r])
        sc.add("dve", lambda e, st=st: e.reciprocal(out=st[:, 14:15], in_=st[:, 14:15]), reads=[st_r], writes=[st_r])
        sc.add("dve", lambda e, y=y, st=st: e.tensor_scalar(out=y, in0=y, scalar1=st[:, 12:13], scalar2=st[:, 14:15], op0=ALU.subtract, op1=ALU.mult), reads=[y_r, st_r], writes=[y_r])
        sc.add("pool", lambda e, y=y: e.tensor_tensor(out=y, in0=y, in1=g_bc, op=ALU.mult), reads=[y_r, lnp_r], writes=[y_r])
        sc.add("dve", lambda e, y=y: e.tensor_tensor(out=y, in0=y, in1=b_bc, op=ALU.add), reads=[y_r, lnp_r], writes=[y_r])
        sc.add("sp", lambda e, y=y, tsl=tsl: e.dma_start(out=out_d[tsl, :], in_=y), reads=[y_r], writes=[], dma=True, key="ysto%d" % (t % 2))
        if write_xT:
            sc.add("act", lambda e, y=y, xb_=xb_: e.copy(out=xb_, in_=y), reads=[y_r], writes=[xb_r_])
            ptb = bank[7][:].bitcast(BF16).rearrange("p (k c) -> p k c", k=8)
            for kc in range(8):
                sc.pe16(ptb[:, kc, :], lambda e, kc=kc, xb_=xb_: e.transpose(out=ptb[:, kc, :], in_=xb_[:, kc * 128:(kc + 1) * 128], identity=ident_b[:]), reads=[xb_r_, r_const], writes=[bank_r[7]])
            sc.add("dve", lambda e, tsl=tsl: e.tensor_copy(out=xT[:, :, tsl], in_=ptb), reads=[bank_r[7]], writes=[xT_r[t]])

    def ln_bufs(g_d, b_d, l):
        L_ = {}
        L_["y"] = [(cv.get([128, 1024]), sc.res("y%d" % i)) for i in range(2)]
        L_["st"] = [(cv.get([128, 16]), sc.res("st%d" % i)) for i in range(2)]
        L_["xb"] = [(cv.get([128, 1024], BF16), sc.res("xbln%d" % i)) for i in range(2)]
        g_bc = cv.get([128, 1024]); b_bc = cv.get([128, 1024]); lnp_r = sc.res("lnp")
        sc.add("sp", lambda e: e.dma_start(out=g_bc, in_=g_d[l, :].partition_broadcast(128)), writes=[lnp_r], dma=True, key="lnp", partial=True)
        sc.add("sp", lambda e: e.dma_start(out=b_bc, in_=b_d[l, :].partition_broadcast(128)), writes=[lnp_r], dma=True, key="lnp", partial=True)
        return L_, g_bc, b_bc, lnp_r

    def phaseC(l, xin_d):
        cv.reset()
        wO = cv.get([128, 8, 1024], BF16); wO_r = [sc.res("wO%d" % k) for k in range(8)]
        for kc in range(8):
            sc.add("pool", lambda e, kc=kc: e.dma_start(out=wO[:, kc, :], in_=w_out_d[l, kc * 128:(kc + 1) * 128, :]), writes=[wO_r[kc]], dma=True, key="wO%d" % kc)
        for kc in range(8):
            sc.add("pool", lambda e, kc=kc: e.dma_start(out=wupbf_d[l].rearrange("t p k f -> p t k f")[:, :, kc, :], in_=w_up_d[l, kc * 128:(kc + 1) * 128, :].rearrange("p (t f) -> p t f", f=128)),
                   writes=[wupbf_r[l]], dma=True, key="wcv%d" % (kc % 4), partial=True)
        L_, g_bc, b_bc, lnp_r = ln_bufs(ln1g_d, ln1b_d, l)
        oTt = [(cv.get([128, 8, 128], BF16), sc.res("oTt%d" % i)) for i in range(3)]
        xrs = [(cv.get([128, 1024]), sc.res("xr%d" % i)) for i in range(3)]
        for t in range(NT):
            tsl = slice(t * 128, (t + 1) * 128)
            (ot, ot_r) = oTt[t % 3]; (xr, xr_r) = xrs[t % 3]
            sc.add("sp", lambda e, ot=ot, tsl=tsl: e.dma_start(out=ot, in_=oT_d[:, :, tsl].rearrange("k p c -> p k c")), writes=[ot_r], dma=True, key="oTt%d" % (t % 3))
            sc.add("sp", lambda e, xr=xr, tsl=tsl: e.dma_start(out=xr, in_=xin_d[tsl, :]), writes=[xr_r], dma=True, key="xr%d" % (t % 3))
            bl, bh = 2 * (t % 2), 2 * (t % 2) + 1
            for half, bi in ((0, bl), (1, bh)):
                for kc in range(8):
                    sc.pe16(bank[bi][:], lambda e, bi=bi, kc=kc, ot=ot, half=half: e.matmul(bank[bi][:], lhsT=ot[:, kc, :], rhs=wO[:, kc, half * 512:(half + 1) * 512], start=(kc == 0), stop=(kc == 7)),
                            reads=[ot_r, wO_r[kc]], writes=[bank_r[bi]])
            ln_tile(L_, t, bank[bl], bank_r[bl], bank[bh], bank_r[bh], xr, xr_r, g_bc, b_bc, lnp_r, x1_d, True)

    def phaseD(l, out_d, write_xT):
        cv.reset()
        NJ = DFF // 128
        NBLK = S // 512
        wD = cv.get([128, NJ, 1024], BF16); wD_r = [sc.res("wD%d" % j) for j in range(NJ)]
        for j in range(NJ):
            sc.add("pool", lambda e, j=j: e.dma_start(out=wD[:, j, :], in_=w_down_d[l, j * 128:(j + 1) * 128, :]), writes=[wD_r[j]], dma=True, key="wD%d" % (j % 4))
        L_, g_bc, b_bc, lnp_r = ln_bufs(ln2g_d, ln2b_d, l)
        fc4 = cv.get([128, 4, 44]); fc_r = sc.res("fc4")
        hT = cv.get([128, NJ, 512], BF16); hT_r = [sc.res("hT%d" % j) for j in range(NJ)]
        wU = [(cv.get([128, 8, 256], BF16), sc.res("wU%d" % i)) for i in range(3)]
        raw = [(cv.get([128, 2, 514]), sc.res("raw%d" % i)) for i in range(2)]
        acc = [(cv.get([128, 2, 512]), sc.res("facc%d" % i)) for i in range(2)]
        halo = cv.get([128, 44, 2]); halo_r = [sc.res("fhalo%d" % j) for j in range(44)]
        xrs = [(cv.get([128, 1024]), sc.res("xrD%d" % i)) for i in range(2)]
        w44 = hT.rearrange("p j t -> p (j t)").bitcast(F32)[0:44, 0:512].rearrange("p (a b) -> p a b", a=4)
        w44_r = sc.res("w44")
        for j3 in range(3):
            sc.add("sp", lambda e, j3=j3: e.dma_start(out=w44[:, j3, :], in_=fconvw_d[l, j3, :].rearrange("(f p) -> f p", p=128)), writes=[w44_r] + hT_r[0:2], dma=True, key="w44", partial=True)
        sc.add("sp", lambda e: e.dma_start(out=w44[:, 3, :], in_=fconvb_d[l, :].rearrange("(f p) -> f p", p=128)), writes=[w44_r], dma=True, key="w44", partial=True)
        for a4 in range(4):
            sc.pe32(lambda e, a4=a4: e.transpose(out=bank[0][:, a4 * 44:(a4 + 1) * 44], in_=w44[:, a4, :], identity=ident_f[0:44, 0:44]), reads=[w44_r, r_const], writes=[bank_r[0]])
        sc.add("dve", lambda e: e.tensor_copy(out=fc4.rearrange("p a f -> p (a f)"), in_=bank[0][:, 0:176]), reads=[bank_r[0]], writes=[fc_r])
        sc.add("pool", lambda e: e.memset(halo, 0.0), reads=[], writes=halo_r)
        sc.barrier()
        for c in range(NBLK):
            csl = slice(c * 512, (c + 1) * 512)
            for j in range(NJ):
                (wu, wu_r) = wU[j % 3]
                sc.add("sp", lambda e, wu=wu, j=j: e.dma_start(out=wu[:, :, 0:128], in_=wupbf_d[l][j, :, :, :]), reads=[wupbf_r[l]], writes=[wu_r], dma=True, key="wUa%d" % (j % 3))
                sc.add("sp", lambda e, wu=wu, j=j: e.dma_start(out=wu[:, :, 128:256], in_=wupbf_d[l][22 + j, :, :, :]), reads=[wupbf_r[l]], writes=[wu_r], dma=True, key="wUb%d" % (j % 3), partial=True)
                (rw, rw_r) = raw[j % 2]; (ac, ac_r) = acc[j % 2]
                for gv in range(2):
                    f = gv * 22 + j
                    bi = 2 * (j % 2) + gv
                    for kc in range(8):
                        sc.pe16(bank[bi][:], lambda e, bi=bi, kc=kc, wu=wu, gv=gv, csl=csl: e.matmul(bank[bi][:], lhsT=wu[:, kc, gv * 128:(gv + 1) * 128], rhs=xT[:, kc, csl], start=(kc == 0), stop=(kc == 7)),
                                reads=[wu_r] + xT_r[4 * c:4 * c + 4], writes=[bank_r[bi]])
                    sc.add("pool", lambda e, rw=rw, gv=gv, f=f: e.tensor_copy(out=rw[:, gv, 0:2], in_=halo[:, f, :]), reads=[halo_r[f]], writes=[rw_r], partial=(gv == 1))
                    sc.add("act", lambda e, rw=rw, gv=gv, bi=bi: e.copy(out=rw[:, gv, 2:514], in_=bank[bi][:]), reads=[bank_r[bi]], writes=[rw_r], partial=True)
                    sc.add("pool", lambda e, rw=rw, gv=gv, f=f: e.tensor_copy(out=halo[:, f, :], in_=rw[:, gv, 512:514]), reads=[rw_r], writes=[halo_r[f]])
                    sc.add("act", lambda e, ac=ac, gv=gv, bi=bi, f=f: e.activation(out=ac[:, gv, :], in_=bank[bi][:], func=AF.Identity, scale=fc4[:, 2, f:f + 1], bias=fc4[:, 3, f:f + 1]), reads=[bank_r[bi], fc_r], writes=[ac_r], partial=(gv == 1))
                    eng = "dve"
                    for tap in (1, 0):
                        sc.add(eng, lambda e, ac=ac, rw=rw, gv=gv, tap=tap, f=f: e.scalar_tensor_tensor(out=ac[:, gv, :], in0=rw[:, gv, tap:tap + 512], scalar=fc4[:, tap, f:f + 1], in1=ac[:, gv, :], op0=ALU.mult, op1=ALU.add),
                               reads=[rw_r, fc_r, ac_r], writes=[ac_r])
                sc.add("act", lambda e, ac=ac: e.activation(out=ac[:, 0, :], in_=ac[:, 0, :], func=AF.Silu, bias=cbias[:, 3:4]), reads=[ac_r, r_const], writes=[ac_r])
                sc.add("dve", lambda e, ac=ac, j=j: e.tensor_tensor(out=hT[:, j, :], in0=ac[:, 0, :], in1=ac[:, 1, :], op=ALU.mult), reads=[ac_r], writes=[hT_r[j]])
            for tt in range(4):
                t = c * 4 + tt
                tsl = slice(t * 128, (t + 1) * 128)
                (xr, xr_r) = xrs[t % 2]
                sc.add("sp", lambda e, xr=xr, tsl=tsl: e.dma_start(out=xr, in_=x1_d[tsl, :]), writes=[xr_r], dma=True, key="xrD%d" % (t % 2))
                bl, bh = 4 + 2 * (t % 2), 5 + 2 * (t % 2)
                if bh == 7 and write_xT:
                    bl, bh = 4, 5
                for half, bi in ((0, bl), (1, bh)):
                    for j in range(NJ):
                        sc.pe16(bank[bi][:], lambda e, bi=bi, j=j, tt=tt, half=half: e.matmul(bank[bi][:], lhsT=hT[:, j, tt * 128:(tt + 1) * 128], rhs=wD[:, j, half * 512:(half + 1) * 512], start=(j == 0), stop=(j == NJ - 1)),
                                reads=[hT_r[j], wD_r[j]], writes=[bank_r[bi]])
                ln_tile(L_, t, bank[bl], bank_r[bl], bank[bh], bank_r[bh], xr, xr_r, g_bc, b_bc, lnp_r, out_d, write_xT)

    phase0(x_d)
    sc.barrier()
    if stop_after == "0":
        sc.add("sp", lambda e: e.dma_start(out=oT_d[:, :, :].rearrange("k p s -> p k s"), in_=xT[:]), reads=xT_r, writes=[], dma=True, key="dbg")
    elif stop_after == "A":
        phaseA(0)
    elif stop_after == "B":
        phaseB(0)
    else:
        for l in range(L):
            xin = x_d if l == 0 else x2_d
            last = (l == L - 1)
            phaseA(l)
            sc.barrier()
            phaseB(l)
            sc.barrier()
            phaseC(l, xin)
            sc.barrier()
            if stop_after == "C" and l == 0:
                break
            phaseD(l, y_d if last else x2_d, not last)
            sc.barrier()
            if stop_after == "D" and l == 0:
                break
    if dbg and os.environ.get("DUMPARENA") and not os.environ.get("SIM"):
        sc.barrier()
        dbg_arena = nc.dram_tensor("dbg_arena", [128, ARENA], F32, kind="ExternalOutput").ap()
        for q in range(4):
            sc.add("sp", lambda e, q=q: e.dma_start(out=dbg_arena[:, q * (ARENA // 4):(q + 1) * (ARENA // 4)], in_=arena[:, q * (ARENA // 4):(q + 1) * (ARENA // 4)]), dma=True, key="dbga")
    sc.emit(nc, es)
    es.close()
    return nc


_CACHE = {}


def kernel(**inputs):
    x = np.asarray(inputs["x"], dtype=np.float32)
    B, S, _ = x.shape
    L = int(np.asarray(inputs["w_in"]).shape[0])
    key = (S, L)
    if key not in _CACHE:
        _CACHE[key] = (build(S=S, L=L), make_consts(S))
    nc, consts = _CACHE[key]
    shared = {k: np.ascontiguousarray(np.asarray(v, dtype=np.float32)) for k, v in inputs.items() if k != "x"}
    for k, v in consts.items():
        shared["c_" + k] = v
    in_maps = []
    for b in range(B):
        m = dict(shared)
        m["x"] = np.ascontiguousarray(x[b])
        in_maps.append(m)
    res = run_bass_kernel_spmd(nc, in_maps, core_ids=list(range(B)))
    return np.stack([np.asarray(r["y"], dtype=np.float32) for r in res.results], axis=0)
```

```python
import os
import numpy as np
import ml_dtypes
from contextlib import ExitStack
import concourse.bass as bass
import concourse.mybir as mybir
from concourse.bass_utils import run_bass_kernel_spmd

F32 = mybir.dt.float32
BF16 = mybir.dt.bfloat16
AF = mybir.ActivationFunctionType
ALU = mybir.AluOpType
AX = mybir.AxisListType

D = 1024
NIN = 3592
DFF = 2816
ALPHA = float((2 * 2) ** 0.25)
NEG = -30000.0


class Res:
    __slots__ = ("name", "writers", "readers")

    def __init__(self, name):
        self.name = name
        self.writers = []
        self.readers = {}


class Op:
    __slots__ = ("eng", "fn", "dma", "key", "value", "deps", "signal", "barrier")

    def __init__(self, eng, fn, dma=False, key=None):
        self.eng = eng
        self.fn = fn
        self.dma = dma
        self.key = key
        self.value = None
        self.deps = []
        self.signal = False
        self.barrier = False


ENGS = ("pe", "act", "dve", "pool", "sp")


class Sched:
    def __init__(self):
        self.ops = {e: [] for e in ENGS}
        self.keycount = {}
        self.keylast = {}
        self.allres = []
        self.last_pe_f32 = False
        self.ident_b = None
        self.safe = False
        self.safecnt = 0
        self.safek = int(os.environ.get("SAFEK", "0"))
        self.safeeng = tuple(x for x in os.environ.get("SAFEENG", "act").split(",") if x)

    def res(self, name):
        r = Res(name)
        self.allres.append(r)
        return r

    def _dep(self, op, prod, raw):
        if prod is op:
            return
        if (not prod.dma) and (not op.dma) and prod.eng == op.eng:
            if op.eng == "pe":
                return
        op.deps.append(prod)
        prod.signal = True

    def add(self, eng, fn, reads=(), writes=(), dma=False, key=None, partial=False, f32=False, out=None):
        if eng == "pe":
            if (not f32) and self.last_pe_f32 and out is not None:
                fn0 = fn
                dmy = out.bitcast(F32) if out.dtype != F32 else out
                idb = self.ident_b

                def fn(e, fn0=fn0, dmy=dmy, idb=idb):
                    e.matmul(dmy[0:64, 0:8], lhsT=idb[:, 0:64], rhs=idb[:, 0:8], start=True, stop=True)
                    return fn0(e)
            self.last_pe_f32 = f32
        excl = self.safe and (not dma) and (eng in self.safeeng)
        if excl:
            self.barrier()
        op = Op(eng, fn, dma, key)
        for r in reads:
            for w in r.writers:
                self._dep(op, w, True)
        for r in writes:
            for w in r.writers:
                if not (partial and w.dma and op.dma):
                    self._dep(op, w, False)
            for rd in r.readers.values():
                if isinstance(rd, list):
                    for x in rd:
                        self._dep(op, x, False)
                else:
                    self._dep(op, rd, False)
        for r in reads:
            if dma:
                r.readers.setdefault("dma", []).append(op)
            else:
                r.readers[eng] = op
        for r in writes:
            if partial:
                r.writers = r.writers + [op]
            else:
                r.writers = [op]
            r.readers = {}
        if dma:
            assert key is not None
            self.keycount[key] = self.keycount.get(key, 0) + 16
            op.value = self.keycount[key]
            self.keylast[key] = op
        self.ops[eng].append(op)
        if excl:
            self.barrier()
        elif self.safe and not dma and self.safek > 0:
            self.safecnt += 1
            if self.safecnt % self.safek == 0:
                self.barrier()
        return op

    def pe32(self, fn, **kw):
        return self.add("pe", fn, f32=True, **kw)

    def pe16(self, out, fn, **kw):
        return self.add("pe", fn, out=out, **kw)

    def barrier(self):
        prods = []
        for e in ENGS:
            for o in reversed(self.ops[e]):
                if not o.dma and not o.barrier:
                    prods.append(o)
                    break
        prods += list(self.keylast.values())
        for e in ENGS:
            b = Op(e, None)
            b.barrier = True
            for p in prods:
                if p.dma or p.eng != e or e != "pe":
                    b.deps.append(p)
                    p.signal = True
            self.ops[e].append(b)
        for r in self.allres:
            r.writers = []
            r.readers = {}

    def emit(self, nc, es):
        esem = {e: es.enter_context(nc.semaphore("s_" + e)) for e in ENGS}
        ksem = {}
        for i, k in enumerate(self.keycount):
            ksem[k] = es.enter_context(nc.semaphore("k%d" % i))
        for e in ENGS:
            c = 0
            for o in self.ops[e]:
                if (not o.dma) and o.signal and not o.barrier:
                    c += 1
                    o.value = c
            if os.environ.get("SEMDBG"): print("SEM", e, "final", c, "nops", len(self.ops[e]))
        block = es.enter_context(nc.Block())
        hooks = {"pe": block.tensor, "act": block.scalar, "dve": block.vector,
                 "pool": block.gpsimd, "sp": block.sync}
        final_keys = dict(self.keycount)

        def mk(ename):
            def body(eng):
                waited = {}
                for o in self.ops[ename]:
                    need = {}
                    for p in o.deps:
                        s = ksem[p.key] if p.dma else esem[p.eng]
                        sid = id(s)
                        v = p.value
                        if waited.get(sid, 0) >= v:
                            continue
                        if sid not in need or need[sid][1] < v:
                            need[sid] = (s, v)
                    for sid, (s, v) in need.items():
                        eng.wait_ge(s, v)
                        waited[sid] = v
                    if o.fn is None:
                        continue
                    ins = o.fn(eng)
                    if o.dma:
                        ins.then_inc(ksem[o.key], 16)
                    elif o.signal:
                        ins.then_inc(esem[ename], 1)
                if ename == "sp":
                    for k, v in final_keys.items():
                        if waited.get(id(ksem[k]), 0) < v:
                            eng.wait_ge(ksem[k], v)
            return body

        for e in ENGS:
            hooks[e](mk(e))


def make_consts(S):
    i = np.arange(128)[:, None]
    j = np.arange(128)[None, :]
    same = (i // 64) == (j // 64)
    c = {}
    c["ident"] = np.eye(128, dtype=np.float32)
    c["caus01"] = (i <= j).astype(np.float32)
    c["mstrict"] = (same & (j < i)).astype(np.float32)
    c["negincl"] = (same & (j <= i)).astype(np.float32)
    LT = (same & (j <= i)).T.astype(np.float32)
    UT = (same & (j > i)).T.astype(np.float32)
    CS0 = np.zeros((128, 128), np.float32); CS0[:64, :] = 1.0
    CS1 = np.zeros((128, 128), np.float32); CS1[64:, :] = 1.0
    c["gl"] = np.concatenate([LT, UT, CS0, CS1], 1)
    offs = []
    for s in (1, 2, 4, 8, 16, 32):
        m = ((i // (2 * s)) == (j // (2 * s))) & ((i // s) != (j // s)) & (i > j)
        offs.append(m.T.astype(np.float32))
    c["boff"] = np.concatenate(offs, 1)
    c["aoff1"] = (((i // 2) == (j // 2)) & (i != j) & (i > j)).astype(np.float32)
    half = 8
    inv = 500000.0 ** (-np.arange(half, dtype=np.float32) / half)
    ang = np.arange(S, dtype=np.float32)[:, None] * inv[None, :]
    cos = np.cos(ang).astype(np.float32)
    sin = np.sin(ang).astype(np.float32)
    NT = S // 128
    cc = np.concatenate([cos, cos], 1).reshape(NT, 128, 16).transpose(1, 0, 2)
    ss = np.concatenate([sin, sin], 1).reshape(NT, 128, 16).transpose(1, 0, 2)
    c["rope"] = np.ascontiguousarray(np.concatenate([cc, ss], 2)).reshape(128, NT * 32)
    return c


def build(S=4096, L=2, dbg=False, stop_after=None):
    NT = S // 128
    NB = S // 256
    nc = bass.Bass("TRN2", target_bir_lowering=False)
    sc = Sched()
    es = ExitStack()

    def din(name, shape, dt=F32):
        return nc.dram_tensor(name, list(shape), dt, kind="ExternalInput").ap()

    def dscr(name, shape, dt=F32, out=False):
        kind = "ExternalOutput" if (out or dbg) else "Internal"
        return nc.dram_tensor(name, list(shape), dt, kind=kind).ap()

    x_d = din("x", [S, D])
    w_in_d = din("w_in", [L, D, NIN])
    gconv_d = din("gdn_conv_w", [L, 4, 1536])
    alog_d = din("gdn_a_log", [L, 4])
    dtb_d = din("gdn_dt_bias", [L, 4])
    gng_d = din("gdn_norm_g", [L, 128])
    w_out_d = din("w_out", [L, D, D])
    ln1g_d = din("ln1_g", [L, D])
    ln1b_d = din("ln1_b", [L, D])
    w_up_d = din("w_up", [L, D, 2 * DFF])
    fconvw_d = din("ffn_conv_w", [L, 3, 2 * DFF])
    fconvb_d = din("ffn_conv_b", [L, 2 * DFF])
    w_down_d = din("w_down", [L, DFF, D])
    ln2g_d = din("ln2_g", [L, D])
    ln2b_d = din("ln2_b", [L, D])
    c_ident_d = din("c_ident", [128, 128])
    c_caus_d = din("c_caus01", [128, 128])
    c_mstrict_d = din("c_mstrict", [128, 128])
    c_negincl_d = din("c_negincl", [128, 128])
    c_gl_d = din("c_gl", [128, 512])
    c_boff_d = din("c_boff", [128, 768])
    c_aoff1_d = din("c_aoff1", [128, 128])
    c_rope_d = din("c_rope", [128, NT * 32])

    y_d = dscr("y", [S, D], out=True)
    x1_d = dscr("x1res", [S, D])
    x2_d = dscr("x2res", [S, D]) if L > 1 else None
    oT_d = dscr("oT", [8, 128, S], BF16)
    wupbf_d = [nc.dram_tensor("wupbf%d" % l_, [44, 128, 8, 128], BF16, kind="Internal").ap() for l_ in range(L)]
    wupbf_r = [sc.res("wupbf%d" % l_) for l_ in range(L)]

    def sb(name, shape, dt=F32):
        return es.enter_context(nc.sbuf_tensor(name, list(shape), dt))

    def ps(name, shape, dt=F32):
        return es.enter_context(nc.psum_tensor(name, list(shape), dt))

    xT = sb("xT", [128, 8, S], BF16)
    xT_r = [sc.res("xT%d" % t) for t in range(NT)]
    ident_f = sb("ident_f", [128, 128]); ident_b = sb("ident_b", [128, 128], BF16)
    caus_b = sb("caus_b", [128, 128], BF16)
    rope = sb("rope", [128, NT, 32])
    r_const = sc.res("consts")
    sc.ident_b = ident_b
    cbias = sb("cbias", [128, 4])
    sc.add("dve", lambda e: e.memset(cbias[:, 0:1], 1e-6), writes=[r_const], partial=True)
    sc.add("dve", lambda e: e.memset(cbias[:, 1:2], 1.0), writes=[r_const], partial=True)
    sc.add("dve", lambda e: e.memset(cbias[:, 2:3], 1e-5), writes=[r_const], partial=True)
    sc.add("dve", lambda e: e.memset(cbias[:, 3:4], 0.0), writes=[r_const], partial=True)

    bank = [ps("bank%d" % i, [128, 512]) for i in range(8)]
    bank_r = [sc.res("bank%d" % i) for i in range(8)]

    ARENA = 136 * 1024 // 4
    arena = sb("arena", [128, ARENA])

    class Carver:
        def __init__(self):
            self.off = 0

        def reset(self):
            self.off = 0

        def get(self, shape, dt=F32):
            n = int(np.prod(shape[1:]))
            nwords = n if dt == F32 else (n + 1) // 2
            a = arena[0:shape[0], self.off:self.off + nwords]
            self.off += (nwords + 15) // 16 * 16
            assert self.off <= ARENA, "arena overflow %d" % self.off
            if dt != F32:
                a = a.bitcast(dt)[:, 0:n]
            if len(shape) > 2:
                names = " ".join("d%d" % k for k in range(len(shape) - 1))
                kw = {"d%d" % k: shape[k + 1] for k in range(len(shape) - 2)}
                a = a.rearrange("p (%s) -> p %s" % (names, names), **kw)
            return a

    cv = Carver()

    sc.add("sp", lambda e: e.dma_start(out=ident_f[:], in_=c_ident_d[:, :]), writes=[r_const], dma=True, key="c0", partial=True)
    sc.add("pool", lambda e: e.dma_start(out=ident_b[:], in_=c_ident_d[:, :]), writes=[r_const], dma=True, key="c1", partial=True)
    sc.add("pool", lambda e: e.dma_start(out=caus_b[:], in_=c_caus_d[:, :]), writes=[r_const], dma=True, key="c1", partial=True)
    sc.add("sp", lambda e: e.dma_start(out=rope[:].rearrange("p t c -> p (t c)"), in_=c_rope_d[:, :]), writes=[r_const], dma=True, key="c0", partial=True)

    def phase0(src_d):
        cv.reset()
        xb = [cv.get([128, 1024], BF16) for _ in range(3)]
        xb_r = [sc.res("xb%d" % i) for i in range(3)]
        pst = [bank[0][:].bitcast(BF16), bank[1][:].bitcast(BF16)]
        for t in range(NT):
            s = t % 3
            sc.add("pool", lambda e, t=t, s=s: e.dma_start(out=xb[s], in_=src_d[t * 128:(t + 1) * 128, :]),
                   writes=[xb_r[s]], dma=True, key="xb%d" % s)
            p = t % 2
            pt = pst[p].rearrange("p (k c) -> p k c", k=8)
            for kc in range(8):
                sc.add("pe", lambda e, kc=kc, s=s, pt=pt: e.transpose(out=pt[:, kc, :], in_=xb[s][:, kc * 128:(kc + 1) * 128], identity=ident_b[:]),
                       reads=[xb_r[s], r_const], writes=[bank_r[p]])
            if t % 2 == 0:
                sc.add("act", lambda e, t=t, pt=pt: e.copy(out=xT[:, :, t * 128:(t + 1) * 128], in_=pt),
                       reads=[bank_r[p]], writes=[xT_r[t]])
            else:
                sc.add("dve", lambda e, t=t, pt=pt: e.tensor_copy(out=xT[:, :, t * 128:(t + 1) * 128], in_=pt),
                       reads=[bank_r[p]], writes=[xT_r[t]])

    def phaseA(l):
        cv.reset()
        wA = cv.get([128, 8, 1536], BF16)
        wA_r = [sc.res("wA%d" % k) for k in range(8)]
        KT = cv.get([128, 4, S], BF16)
        KT_r = [sc.res("KT%d" % t) for t in range(NT)]
        Vp = cv.get([128, NT, 8, 65], BF16)
        Vp_r = [sc.res("Vp%d" % t) for t in range(NT)]
        QT = [cv.get([128, 4, 256], BF16) for _ in range(2)]
        QT_r = [sc.res("QT%d" % i) for i in range(2)]
        kmT = cv.get([128, 4, 16], BF16)
        kmf = cv.get([128, 4])
        kmT_r = sc.res("kmT")
        qb = [cv.get([128, 512], BF16) for _ in range(2)]
        kb = [cv.get([128, 512], BF16) for _ in range(2)]
        qb_r = [sc.res("qb%d" % i) for i in range(2)]
        kb_r = [sc.res("kb%d" % i) for i in range(2)]
        t1 = cv.get([128, 8, 16]); t2 = cv.get([128, 8, 16])
        t1_r = sc.res("t1"); t2_r = sc.res("t2")
        gsb = cv.get([128, 16, 16]); m8 = cv.get([128, 16, 8]); sel = cv.get([128, 16, 16])
        gsb_r = sc.res("gsb"); sel_r = sc.res("sel")
        NPT = 4
        PT = [cv.get([128, 2, 256], BF16) for _ in range(NPT)]
        PT_r = [sc.res("PT%d" % i) for i in range(NPT)]
        acc = cv.get([128, 2, 8, 65])
        acc_r = [[sc.res("acc%d_%d" % (q, h)) for h in range(8)] for q in range(2)]
        rec = cv.get([128, 16])
        ob = cv.get([128, 2, 512], BF16)
        ob_r = sc.res("ob")
        obT = [cv.get([128, 4, 256], BF16) for _ in range(2)]
        obT_r = [sc.res("obT%d" % i) for i in range(2)]

        for kc in range(8):
            sc.add("pool", lambda e, kc=kc: e.dma_start(out=wA[:, kc, :], in_=w_in_d[l, kc * 128:(kc + 1) * 128, 2056:3592]),
                   writes=[wA_r[kc]], dma=True, key="wA%d" % kc)
        sc.add("pool", lambda e: e.memset(Vp[:, :, :, 64:65], 1.0), writes=Vp_r)
        sc.add("pool", lambda e: e.memset(gsb[:], -1e30), writes=[gsb_r])
        sc.add("pool", lambda e: e.memset(kmT[:], 0.0), writes=[kmT_r])

        pq, pk, pv = bank[0], bank[1], bank[2]
        ptr = bank[3][:].bitcast(BF16).rearrange("p (k c) -> p k c", k=8)
        pg0 = bank[4][:, 0:128].rearrange("p (a b) -> p a b", a=8)
        pg1 = bank[6][:, 0:128].rearrange("p (a b) -> p a b", a=8)
        SB = (0, 1, 2, 5)
        OB = (6, 7)
        cnt = {"s": 0, "o": 0, "pt": 0, "ev": 0}


        KSTOP = int(os.environ.get("KSTOP", "99"))
        for t in range(NT):
            b = t // 2
            qt_ = t % 2
            tsl = slice(t * 128, (t + 1) * 128)
            if KSTOP <= 0:
                break
            for g, pp in enumerate((pq, pk, pv)):
                for kc in range(8):
                    sc.add("pe", lambda e, g=g, kc=kc, pp=pp, tsl=tsl: e.matmul(pp[:], lhsT=xT[:, kc, tsl], rhs=wA[:, kc, g * 512:(g + 1) * 512], start=(kc == 0), stop=(kc == 7)),
                           reads=[xT_r[t], wA_r[kc]], writes=[bank_r[g]])
            if KSTOP <= 1:
                continue
            sc.add("act", lambda e, t=t: e.copy(out=Vp[:, t, :, 0:64], in_=pv[:].rearrange("p (h d) -> p h d", h=8)),
                   reads=[bank_r[2]], writes=[Vp_r[t]])
            s2 = t % 2
            for (pp, dst, dst_r, bi) in ((pq, qb[s2], qb_r[s2], 0), (pk, kb[s2], kb_r[s2], 1)):
                p3 = pp[:].rearrange("p (h d) -> p h d", h=8)
                d3 = dst.rearrange("p (h d) -> p h d", h=8)
                sc.add("act", lambda e, p3=p3, d3=d3: e.copy(out=d3[:, :, 16:64], in_=p3[:, :, 16:64]),
                       reads=[bank_r[bi]], writes=[dst_r])
                ccb = rope[:, t, 0:16].unsqueeze(1).to_broadcast([128, 8, 16])
                ssb = rope[:, t, 16:32].unsqueeze(1).to_broadcast([128, 8, 16])
                sc.add("dve", lambda e, p3=p3, ccb=ccb: e.tensor_tensor(out=t1, in0=p3[:, :, 0:16], in1=ccb, op=ALU.mult),
                       reads=[bank_r[bi], r_const], writes=[t1_r])
                sc.add("dve", lambda e, p3=p3, ssb=ssb: e.tensor_tensor(out=t2, in0=p3[:, :, 0:16], in1=ssb, op=ALU.mult),
                       reads=[bank_r[bi], r_const], writes=[t2_r])
                sc.add("dve", lambda e, d3=d3: e.tensor_tensor(out=d3[:, :, 0:8], in0=t1[:, :, 0:8], in1=t2[:, :, 8:16], op=ALU.subtract),
                       reads=[t1_r, t2_r], writes=[dst_r], partial=True)
                sc.add("dve", lambda e, d3=d3: e.tensor_tensor(out=d3[:, :, 8:16], in0=t1[:, :, 8:16], in1=t2[:, :, 0:8], op=ALU.add),
                       reads=[t1_r, t2_r], writes=[dst_r], partial=True)
            if KSTOP <= 2:
                continue
            for j in range(4):
                sc.add("pe", lambda e, j=j, s2=s2: e.transpose(out=ptr[:, j, :], in_=qb[s2][:, j * 128:(j + 1) * 128], identity=ident_b[:]),
                       reads=[qb_r[s2], r_const], writes=[bank_r[3]])
            for j in range(4):
                sc.add("pe", lambda e, j=j, s2=s2: e.transpose(out=ptr[:, 4 + j, :], in_=kb[s2][:, j * 128:(j + 1) * 128], identity=ident_b[:]),
                       reads=[kb_r[s2], r_const], writes=[bank_r[3]])
            qs = b % 2
            if KSTOP == 3 and os.environ.get("KSUB") == "a":
                continue
            sc.add("dve", lambda e, qs=qs, qt_=qt_: e.tensor_copy(out=QT[qs][:, :, qt_ * 128:(qt_ + 1) * 128], in_=ptr[:, 0:4, :]),
                   reads=[bank_r[3]], writes=[QT_r[qs]], partial=(qt_ == 1))
            if KSTOP == 3 and os.environ.get("KSUB") == "b":
                continue
            sc.add("dve", lambda e, tsl=tsl: e.tensor_copy(out=KT[:, :, tsl], in_=ptr[:, 4:8, :]),
                   reads=[bank_r[3]], writes=[KT_r[t]])
            if qt_ == 0 or KSTOP <= 3:
                continue
            if b + 1 < NB:
                sc.add("dve", lambda e, b=b: e.tensor_reduce(out=kmf, in_=KT[:, :, b * 256:(b + 1) * 256], axis=AX.X, op=ALU.add),
                       reads=[KT_r[t - 1], KT_r[t]], writes=[kmT_r])
                sc.add("dve", lambda e, b=b: e.tensor_scalar(out=kmT[:, :, b], in0=kmf, scalar1=1.0 / 256, scalar2=None, op0=ALU.mult),
                       reads=[kmT_r], writes=[kmT_r], partial=True)
            topk = b > 3
            if topk:
                KV_ = os.environ.get("KVAR", "")
                for par in range(2):
                    pgp = (pg0, pg1)[par]
                    for q2 in range(2):
                        for hh in range(4):
                            if KV_ == "q0" and q2 == 1: continue
                            if KV_ == "p0" and par == 1: continue
                            if KV_ == "h0" and hh > 0: continue
                            base = par * 64
                            sc.add("pe", lambda e, pgp=pgp, q2=q2, hh=hh, base=base, qs=qs: e.matmul(pgp[:, q2 * 4 + hh, :], lhsT=QT[qs][base:base + 64, hh, q2 * 128:(q2 + 1) * 128], rhs=kmT[base:base + 64, hh, :], start=True, stop=True),
                                   reads=[QT_r[qs], kmT_r], writes=[bank_r[(4, 6)[par]]])
                KT_ = os.environ.get("KTOPK", "full")
                if KT_ in ("gc", "gcm", "full"):
                    sc.add("dve", lambda e, b=b: e.tensor_copy(out=gsb[:, 0:8, 0:b], in_=pg0[:, :, 0:b]), reads=[bank_r[4]], writes=[gsb_r])
                    sc.add("dve", lambda e, b=b: e.tensor_copy(out=gsb[:, 8:16, 0:b], in_=pg1[:, :, 0:b]), reads=[bank_r[6]], writes=[gsb_r], partial=True)
                if KT_ in ("gcm", "full"):
                    for i16 in range(16):
                        sc.add("dve", lambda e, i16=i16: e.max(out=m8[:, i16, :], in_=gsb[:, i16, :]), reads=[gsb_r], writes=[sel_r], partial=True)
                if KT_ == "full":
                    sc.add("dve", lambda e: e.tensor_tensor(out=sel[:], in0=gsb[:], in1=m8[:, :, 2:3].to_broadcast([128, 16, 16]), op=ALU.is_ge),
                           reads=[gsb_r, sel_r], writes=[sel_r])
                else:
                    sc.add("dve", lambda e: e.memset(sel[:], 1.0), reads=[gsb_r, bank_r[4]], writes=[sel_r])
            units = [(h, n) for h in range(8 if KSTOP > 4 else 0) for n in ([b] + list(range(b)))]
            ust = {}

            def emit_st(u, b=b, qs=qs):
                h, n = u
                j = h // 2; base = (h % 2) * 64
                si = SB[cnt["s"] % 4]; cnt["s"] += 1
                pi = cnt["pt"] % NPT; cnt["pt"] += 1
                pss = bank[si][:].rearrange("p (k q) -> p k q", k=2)
                for kt in range(2):
                    ktile = 2 * n + kt
                    sc.add("pe", lambda e, pss=pss, kt=kt, j=j, base=base, ktile=ktile, qs=qs: e.matmul(pss[:, kt, :], lhsT=KT[base:base + 64, j, ktile * 128:(ktile + 1) * 128], rhs=QT[qs][base:base + 64, j, :], start=True, stop=True),
                           reads=[KT_r[ktile], QT_r[qs]], writes=[bank_r[si]])
                sc.add("act", lambda e, pss=pss, pi=pi: e.activation(out=PT[pi][:], in_=pss, func=AF.Exp, scale=0.125, bias=cbias[:, 3:4]),
                       reads=[bank_r[si], r_const], writes=[PT_r[pi]])
                if n == b:
                    for kt in range(2):
                        sc.add("pool", lambda e, pi=pi, kt=kt: e.tensor_tensor(out=PT[pi][:, kt, kt * 128:(kt + 1) * 128], in0=PT[pi][:, kt, kt * 128:(kt + 1) * 128], in1=caus_b[:], op=ALU.mult),
                               reads=[PT_r[pi], r_const], writes=[PT_r[pi]])
                ust[u] = pi

            def emit_pv(u, b=b, topk=topk):
                h, n = u
                pi = ust.pop(u)
                oi = OB[cnt["o"] % 2]; cnt["o"] += 1
                pso = bank[oi][:, 0:130].rearrange("p (q d) -> p q d", q=2)
                for q2 in range(2):
                    kts = [0] if (n == b and q2 == 0) else [0, 1]
                    for ii, kt in enumerate(kts):
                        sc.add("pe", lambda e, pso=pso, pi=pi, q2=q2, kt=kt, n=n, h=h, ii=ii, last=(ii == len(kts) - 1): e.matmul(pso[:, q2, :], lhsT=PT[pi][:, kt, q2 * 128:(q2 + 1) * 128], rhs=Vp[:, 2 * n + kt, h, :], start=(ii == 0), stop=last),
                               reads=[PT_r[pi], Vp_r[2 * n + kt]], writes=[bank_r[oi]])
                for q2 in range(2):
                    if n == b:
                        sc.add("dve", lambda e, pso=pso, q2=q2, h=h: e.tensor_copy(out=acc[:, q2, h, :], in_=pso[:, q2, :]),
                               reads=[bank_r[oi]], writes=[acc_r[q2][h]])
                    elif topk:
                        sc.add("dve", lambda e, pso=pso, q2=q2, h=h, n=n: e.scalar_tensor_tensor(out=acc[:, q2, h, :], in0=pso[:, q2, :], scalar=sel[:, (h % 2) * 8 + q2 * 4 + h // 2, n:n + 1], in1=acc[:, q2, h, :], op0=ALU.mult, op1=ALU.add),
                               reads=[bank_r[oi], sel_r, acc_r[q2][h]], writes=[acc_r[q2][h]])
                    else:
                        sc.add("dve", lambda e, pso=pso, q2=q2, h=h: e.tensor_tensor(out=acc[:, q2, h, :], in0=pso[:, q2, :], in1=acc[:, q2, h, :], op=ALU.add),
                               reads=[bank_r[oi], acc_r[q2][h]], writes=[acc_r[q2][h]])

            LOOK = 3
            for i in range(min(LOOK, len(units))):
                emit_st(units[i])
            for i, u in enumerate(units):
                if i + LOOK < len(units):
                    emit_st(units[i + LOOK])
                emit_pv(u)
            if KSTOP <= 5:
                continue
            allacc = [acc_r[q][h] for q in range(2) for h in range(8)]
            sc.add("dve", lambda e: e.reciprocal(out=rec, in_=acc[:].rearrange("p q h d -> p (q h) d")[:, :, 64]), reads=allacc, writes=[ob_r])
            sc.add("dve", lambda e: e.tensor_tensor(out=ob[:].rearrange("p q (h d) -> p (q h) d", h=8), in0=acc[:].rearrange("p q h d -> p (q h) d")[:, :, 0:64], in1=rec.unsqueeze(2).to_broadcast([128, 16, 64]), op=ALU.mult),
                   reads=allacc + [ob_r], writes=[ob_r])
            os_ = b % 2
            for q2 in range(2):
                for j in range(4):
                    sc.add("pe", lambda e, q2=q2, j=j: e.transpose(out=ptr[:, q2 * 4 + j, :], in_=ob[:, q2, j * 128:(j + 1) * 128], identity=ident_b[:]),
                           reads=[ob_r, r_const], writes=[bank_r[3]])
            sc.add("dve", lambda e, os_=os_: e.tensor_copy(out=obT[os_][:].rearrange("p j (q c) -> p q j c", q=2), in_=ptr.rearrange("p (q j) c -> p q j c", q=2)),
                   reads=[bank_r[3]], writes=[obT_r[os_]])
            sc.add("sp", lambda e, os_=os_, b=b: e.dma_start(out=oT_d[4:8, :, b * 256:(b + 1) * 256].rearrange("j p c -> p j c"), in_=obT[os_][:]),
                   reads=[obT_r[os_]], writes=[], dma=True, key="obT%d" % os_)

    def phaseB(l):
        cv.reset()
        for _ in range(int(os.environ.get("ACTPAD", "0"))):
            sc.add("act", lambda e: e.copy(out=arena[:, 0:8], in_=ident_f[:, 0:8]))
        NBLK = S // 512
        wB = cv.get([128, 8, 2056], BF16)
        wB_r = [sc.res("wB%d" % k) for k in range(8)]
        gcw = cv.get([128, 12, 4]); dtb = cv.get([128, 4]); nexpA = cv.get([128, 4]); gng = cv.get([128, 128])
        mstrict = cv.get([128, 128]); mincl = cv.get([128, 128]); glc = cv.get([128, 4, 128])
        boff = cv.get([128, 6, 128]); aoff1 = cv.get([128, 128]); ones_f = cv.get([128, 128])
        pc_r = sc.res("pconst")
        rawb = cv.get([128, 2, 515]); rawb_r = [sc.res("rawb%d" % f) for f in range(2)]
        halo = cv.get([128, 12, 3]); halo_r = [sc.res("halo%d" % f) for f in range(12)]
        cacc = [cv.get([128, 512]) for _ in range(2)]; cacc_r = [sc.res("cacc%d" % i) for i in range(2)]
        cT = cv.get([128, 12, 512]); cT_r = [sc.res("cT%d" % f) for f in range(12)]
        Sst = [cv.get([128, 4, 128]) for _ in range(2)]
        S_r = [[sc.res("S%d_%d" % (i, h)) for h in range(4)] for i in range(2)]

        def tmp(name, shape, dt=F32, n=2):
            return [(cv.get(shape, dt), sc.res("%s%d" % (name, i))) for i in range(n)]

        Qtm = tmp("Qtm", [128, 4, 128]); Ktm = tmp("Ktm", [128, 4, 128]); Vtm = tmp("Vtm", [128, 4, 128])
        ssq = tmp("ssq", [128, 8]); rn = tmp("rn", [128, 8])
        sm = tmp("sm", [128, 64])
        zs = tmp("zs", [128, 512])
        junk = tmp("junk", [128, 128], n=1)[0]
        HT = 2
        QTh = tmp("QTh", [128, 128], F32, HT); KTh = tmp("KTh", [128, 128], F32, HT)
        GR = tmp("GR", [128, 128], F32, HT); Dm = tmp("Dm", [128, 128], F32, HT); Ds = tmp("Ds", [128, 128], F32, HT)
        Am = tmp("Am", [128, 128], F32, 2 * HT); attn = tmp("attn", [128, 128], F32, 2 * HT); attnT = tmp("attnT", [128, 128], F32, HT)
        Boall = tmp("Boall", [128, 6, 128], F32, HT); Em = tmp("Em", [128, 128], F32, 2 * HT); Dk = tmp("Dk", [128, 128], F32, 2 * HT)
        Xm = tmp("Xm", [128, 128], F32, HT); Rm = tmp("Rm", [128, 256], F32, HT); UW = tmp("UW", [128, 256], F32, HT)
        Kd = tmp("Kd", [128, 128], F32, HT); Qd = tmp("Qd", [128, 128], F32, HT); QpT = tmp("QpT", [128, 128], F32, HT)
        MpT = tmp("MpT", [128, 2, 128], F32, HT)
        osb = tmp("osb", [128, 4, 128], F32, 2); oss = tmp("oss", [128, 8], F32, 2)
        oab = tmp("oab", [128, 512], BF16, 2); oaT = tmp("oaT", [128, 4, 128], BF16, 2)
        ctr = {}

        def nxt(lst, key):
            i = ctr.get(key, 0); ctr[key] = i + 1
            return lst[i % len(lst)]

        PB = tuple(int(x) for x in os.environ.get("PB", "2,3,4,6,7").split(","))

        def pbank():
            i = ctr.get("pb", 0); ctr["pb"] = i + 1
            bi = PB[i % len(PB)]
            return bank[bi], bank_r[bi]

        for kc in range(8):
            sc.add("pool", lambda e, kc=kc: e.dma_start(out=wB[:, kc, :], in_=w_in_d[l, kc * 128:(kc + 1) * 128, 0:2056]),
                   writes=[wB_r[kc]], dma=True, key="wB%d" % kc)
        w4 = cT.rearrange("p f t -> p (f t)")[0:4, 0:1536]; w4_r = sc.res("w4")
        sc.add("sp", lambda e: e.dma_start(out=w4, in_=gconv_d[l, :, :]), writes=[w4_r, cT_r[0], cT_r[1], cT_r[2]], dma=True, key="w4")
        for f in range(12):
            sc.pe32(lambda e, f=f: e.transpose(out=bank[0][:, f * 4:(f + 1) * 4], in_=w4[0:4, f * 128:(f + 1) * 128], identity=ident_f[0:4, 0:4]), reads=[w4_r, cT_r[0], cT_r[1], cT_r[2], r_const], writes=[bank_r[0]])
        sc.add("dve", lambda e: e.tensor_copy(out=gcw.rearrange("p f j -> p (f j)"), in_=bank[0][:, 0:48]), reads=[bank_r[0]], writes=[pc_r], partial=True)
        sc.add("sp", lambda e: e.dma_start(out=dtb, in_=dtb_d[l, :].partition_broadcast(128)), writes=[pc_r], dma=True, key="pc", partial=True)
        sc.add("sp", lambda e: e.dma_start(out=nexpA, in_=alog_d[l, :].partition_broadcast(128)), writes=[pc_r], dma=True, key="pc", partial=True)
        sc.add("sp", lambda e: e.dma_start(out=gng, in_=gng_d[l, :].partition_broadcast(128)), writes=[pc_r], dma=True, key="pc", partial=True)
        sc.add("sp", lambda e: e.dma_start(out=mstrict, in_=c_mstrict_d[:, :]), writes=[pc_r], dma=True, key="pc", partial=True)
        sc.add("sp", lambda e: e.dma_start(out=glc.rearrange("p a b -> p (a b)"), in_=c_gl_d[:, :]), writes=[pc_r], dma=True, key="pc", partial=True)
        sc.add("sp", lambda e: e.dma_start(out=boff.rearrange("p a b -> p (a b)"), in_=c_boff_d[:, :]), writes=[pc_r], dma=True, key="pc", partial=True)
        sc.add("sp", lambda e: e.dma_start(out=aoff1, in_=c_aoff1_d[:, :]), writes=[pc_r], dma=True, key="pc", partial=True)
        sc.add("sp", lambda e: e.dma_start(out=mincl, in_=c_negincl_d[:, :]), writes=[pc_r], dma=True, key="pc", partial=True)
        sc.add("act", lambda e: e.activation(out=nexpA, in_=nexpA, func=AF.Exp, bias=cbias[:, 3:4]), reads=[pc_r], writes=[pc_r])
        sc.add("dve", lambda e: e.tensor_scalar(out=nexpA, in0=nexpA, scalar1=-1.0, scalar2=None, op0=ALU.mult), reads=[pc_r], writes=[pc_r])
        sc.add("dve", lambda e: e.memset(ones_f, 1.0), writes=[pc_r], reads=[pc_r])
        sc.add("dve", lambda e: e.memset(Sst[0][:], 0.0), writes=S_r[0])
        sc.add("pool", lambda e: e.memset(halo, 0.0), writes=halo_r)
        LTc, UTc, CS0c, CS1c = (glc[:, i, :] for i in range(4))

        cur = 0

        KB = int(os.environ.get("KB", "99"))
        for c in range(NBLK):
            csl = slice(c * 512, (c + 1) * 512)
            for f in range(12):
                pb, pb_r = bank[f % 2], bank_r[f % 2]
                for kc in range(8):
                    sc.pe16(pb[:], lambda e, pb=pb, f=f, kc=kc, csl=csl: e.matmul(pb[:], lhsT=wB[:, kc, f * 128:(f + 1) * 128], rhs=xT[:, kc, csl], start=(kc == 0), stop=(kc == 7)),
                           reads=[wB_r[kc]] + xT_r[4 * c:4 * c + 4], writes=[pb_r])
                rs = f % 2
                sc.add("pool", lambda e, f=f, rs=rs: e.tensor_copy(out=rawb[:, rs, 0:3], in_=halo[:, f, :]), reads=[halo_r[f]], writes=[rawb_r[rs]])
                sc.add("act", lambda e, pb=pb, rs=rs: e.copy(out=rawb[:, rs, 3:515], in_=pb[:]), reads=[pb_r], writes=[rawb_r[rs]], partial=True)
                sc.add("pool", lambda e, f=f, rs=rs: e.tensor_copy(out=halo[:, f, :], in_=rawb[:, rs, 512:515]), reads=[rawb_r[rs]], writes=[halo_r[f]])
                ca, ca_r = cacc[f % 2], cacc_r[f % 2]
                sc.add("act", lambda e, pb=pb, f=f, ca=ca: e.activation(out=ca, in_=pb[:], func=AF.Copy, scale=gcw[:, f, 3:4]), reads=[pb_r, pc_r], writes=[ca_r])
                for j in (2, 1, 0):
                    sc.add("dve", lambda e, f=f, j=j, ca=ca, rs=rs: e.scalar_tensor_tensor(out=ca, in0=rawb[:, rs, j:j + 512], scalar=gcw[:, f, j:j + 1], in1=ca, op0=ALU.mult, op1=ALU.add),
                           reads=[rawb_r[rs], pc_r, ca_r], writes=[ca_r])
                sc.add("act", lambda e, f=f, ca=ca: e.activation(out=cT[:, f, :], in_=ca, func=AF.Silu, bias=cbias[:, 3:4]), reads=[ca_r], writes=[cT_r[f]])
            for tt in range(4 if KB > 1 else 0):
                t = c * 4 + tt
                tsl = slice(t * 128, (t + 1) * 128)
                lsl = slice(tt * 128, (tt + 1) * 128)
                (Qt, Qt_r) = nxt(Qtm, "Qtm"); (Kt, Kt_r) = nxt(Ktm, "Ktm"); (Vt, Vt_r) = nxt(Vtm, "Vtm")
                (sq, sq_r) = nxt(ssq, "ssq"); (rnn, rn_r) = nxt(rn, "rn"); (smt, sm_r) = nxt(sm, "sm"); (zst, zs_r) = nxt(zs, "zs")
                for g in range(3):
                    pb, pb_r = bank[2 + g], bank_r[2 + g]
                    for h in range(4):
                        sc.pe32(lambda e, pb=pb, g=g, h=h, lsl=lsl: e.transpose(out=pb[:, h * 128:(h + 1) * 128], in_=cT[:, g * 4 + h, lsl], identity=ident_f[:]),
                               reads=[cT_r[g * 4 + h], r_const], writes=[pb_r])
                for g in range(2):
                    for h in range(4):
                        sc.add("act", lambda e, g=g, h=h, sq=sq: e.activation(out=junk[0], in_=bank[2 + g][:, h * 128:(h + 1) * 128], func=AF.Square, bias=cbias[:, 3:4], accum_out=sq[:, g * 4 + h:g * 4 + h + 1]),
                               reads=[bank_r[2 + g]], writes=[sq_r, junk[1]], partial=True)
                sc.add("act", lambda e, sq=sq, rnn=rnn: e.activation(out=rnn, in_=sq, func=AF.Sqrt, bias=cbias[:, 0:1]), reads=[sq_r, r_const], writes=[rn_r])
                sc.add("dve", lambda e, rnn=rnn: e.reciprocal(out=rnn, in_=rnn), reads=[rn_r], writes=[rn_r])
                sc.add("dve", lambda e, rnn=rnn: e.tensor_scalar(out=rnn[:, 0:4], in0=rnn[:, 0:4], scalar1=float(128 ** -0.5), scalar2=None, op0=ALU.mult), reads=[rn_r], writes=[rn_r])
                sc.add("dve", lambda e, Qt=Qt, rnn=rnn: e.tensor_tensor(out=Qt, in0=bank[2][:].rearrange("p (h d) -> p h d", h=4), in1=rnn[:, 0:4].unsqueeze(2).to_broadcast([128, 4, 128]), op=ALU.mult),
                       reads=[bank_r[2], rn_r], writes=[Qt_r])
                sc.add("dve", lambda e, Kt=Kt, rnn=rnn: e.tensor_tensor(out=Kt, in0=bank[3][:].rearrange("p (h d) -> p h d", h=4), in1=rnn[:, 4:8].unsqueeze(2).to_broadcast([128, 4, 128]), op=ALU.mult),
                       reads=[bank_r[3], rn_r], writes=[Kt_r])
                sc.add("act", lambda e, Vt=Vt: e.copy(out=Vt, in_=bank[4][:].rearrange("p (h d) -> p h d", h=4)), reads=[bank_r[4]], writes=[Vt_r])
                if KB <= 2:
                    continue
                pab, pab_r = bank[5], bank_r[5]
                for kc in range(8):
                    sc.pe16(bank[5][:, 0:8], lambda e, kc=kc, tsl=tsl: e.matmul(bank[5][:, 0:8], lhsT=xT[:, kc, tsl], rhs=wB[:, kc, 1536:1544], start=(kc == 0), stop=(kc == 7)),
                           reads=[xT_r[t], wB_r[kc]], writes=[pab_r])
                sc.add("dve", lambda e, smt=smt: e.tensor_tensor(out=smt[:, 0:4], in0=bank[5][:, 0:4], in1=dtb, op=ALU.add), reads=[pab_r, pc_r], writes=[sm_r])
                sc.add("dve", lambda e, smt=smt: e.tensor_scalar(out=smt[:, 36:40], in0=smt[:, 0:4], scalar1=-1.0, scalar2=None, op0=ALU.mult), reads=[sm_r], writes=[sm_r])
                sc.add("dve", lambda e, smt=smt: e.tensor_tensor(out=smt[:, 4:8], in0=smt[:, 0:4], in1=smt[:, 36:40], op=ALU.min), reads=[sm_r], writes=[sm_r])
                sc.add("act", lambda e, smt=smt: e.activation(out=smt[:, 4:8], in_=smt[:, 4:8], func=AF.Exp, bias=cbias[:, 3:4]), reads=[sm_r], writes=[sm_r])
                sc.add("act", lambda e, smt=smt: e.activation(out=smt[:, 4:8], in_=smt[:, 4:8], func=AF.Ln, bias=cbias[:, 1:2]), reads=[sm_r, r_const], writes=[sm_r])
                sc.add("dve", lambda e, smt=smt: e.scalar_tensor_tensor(out=smt[:, 8:12], in0=smt[:, 0:4], scalar=0.0, in1=smt[:, 4:8], op0=ALU.max, op1=ALU.add), reads=[sm_r], writes=[sm_r])
                sc.add("dve", lambda e, smt=smt: e.tensor_tensor(out=smt[:, 8:12], in0=smt[:, 8:12], in1=nexpA, op=ALU.mult), reads=[sm_r, pc_r], writes=[sm_r])
                sc.add("act", lambda e, smt=smt: e.activation(out=smt[:, 12:16], in_=bank[5][:, 4:8], func=AF.Exp, scale=-1.0, bias=cbias[:, 3:4]), reads=[pab_r], writes=[sm_r])
                sc.add("dve", lambda e, smt=smt: e.tensor_scalar(out=smt[:, 12:16], in0=smt[:, 12:16], scalar1=1.0, scalar2=None, op0=ALU.add), reads=[sm_r], writes=[sm_r])
                sc.add("dve", lambda e, smt=smt: e.reciprocal(out=smt[:, 12:16], in_=smt[:, 12:16]), reads=[sm_r], writes=[sm_r])
                for kc in range(8):
                    sc.pe16(bank[5][:], lambda e, kc=kc, tsl=tsl: e.matmul(bank[5][:], lhsT=xT[:, kc, tsl], rhs=wB[:, kc, 1544:2056], start=(kc == 0), stop=(kc == 7)),
                           reads=[xT_r[t], wB_r[kc]], writes=[pab_r])
                sc.add("act", lambda e, zst=zst: e.activation(out=zst, in_=bank[5][:], func=AF.Silu, bias=cbias[:, 3:4]), reads=[pab_r], writes=[zs_r])
                sc.add("pool", lambda e, zst=zst: e.tensor_tensor(out=zst.rearrange("p (h d) -> p h d", h=4), in0=zst.rearrange("p (h d) -> p h d", h=4), in1=gng.unsqueeze(1).to_broadcast([128, 4, 128]), op=ALU.mult),
                       reads=[zs_r, pc_r], writes=[zs_r])
                for i4, lt in enumerate((LTc, UTc, CS0c, CS1c)):
                    sc.pe32(lambda e, i4=i4, lt=lt, smt=smt: e.matmul(bank[5][:, 16 + 4 * i4:20 + 4 * i4], lhsT=lt, rhs=smt[:, 8:12], start=True, stop=True),
                           reads=[sm_r, pc_r], writes=[pab_r])
                sc.add("dve", lambda e, smt=smt: e.tensor_copy(out=smt[:, 40:44], in_=bank[5][:, 16:20]), reads=[pab_r], writes=[sm_r])
                sc.add("dve", lambda e, smt=smt: e.tensor_scalar(out=smt[:, 16:32], in0=bank[5][:, 16:32], scalar1=-60.0, scalar2=None, op0=ALU.max), reads=[pab_r], writes=[sm_r])
                sc.add("act", lambda e, smt=smt: e.activation(out=smt[:, 16:32], in_=smt[:, 16:32], func=AF.Exp, bias=cbias[:, 3:4]), reads=[sm_r], writes=[sm_r])
                sc.add("dve", lambda e, smt=smt: e.tensor_tensor(out=smt[:, 32:36], in0=smt[:, 12:16], in1=smt[:, 16:20], op=ALU.mult), reads=[sm_r], writes=[sm_r])
                (osb_t, osb_r) = nxt(osb, "osb"); (oss_t, oss_r) = nxt(oss, "oss")
                if KB <= 3:
                    continue
                sc.safe = os.environ.get("SAFE", "1") == "1"
                for h in range(4):
                    (QT_, QT_r_) = nxt(QTh, "QTh"); (KT_, KT_r_) = nxt(KTh, "KTh"); (GR_, GR_r_) = nxt(GR, "GR")
                    (D_, D_r) = nxt(Dm, "Dm"); (Ds_, Ds_r) = nxt(Ds, "Ds"); (A_, A_r) = nxt(Am, "Am")
                    (at_, at_r) = nxt(attn, "attn"); (atT_, atT_r) = nxt(attnT, "attnT"); (Bo_, Bo_r) = nxt(Boall, "Bo")
                    (X_, X_r) = nxt(Xm, "X"); (R_, R_r) = nxt(Rm, "R"); (UW_, UW_r) = nxt(UW, "UW")
                    (Kd_, Kd_r) = nxt(Kd, "Kd"); (Qd_, Qd_r) = nxt(Qd, "Qd"); (QpT_, QpT_r) = nxt(QpT, "QpT"); (Mp_, Mp_r) = nxt(MpT, "MpT")
                    beta_h = smt[:, 12 + h:13 + h]; egc_h = smt[:, 16 + h:17 + h]; egu_h = smt[:, 20 + h:21 + h]
                    gcum_h = smt[:, 40 + h:41 + h]; bk_h = smt[:, 32 + h:33 + h]
                    pb, pb_r = pbank()
                    sc.pe32(lambda e, pb=pb, Qt=Qt, h=h: e.transpose(out=pb[:, 0:128], in_=Qt[:, h, :], identity=ident_f[:]), reads=[Qt_r, r_const], writes=[pb_r])
                    sc.pe32(lambda e, pb=pb, Kt=Kt, h=h: e.transpose(out=pb[:, 128:256], in_=Kt[:, h, :], identity=ident_f[:]), reads=[Kt_r, r_const], writes=[pb_r])
                    sc.add("dve", lambda e, pb=pb, QT_=QT_: e.tensor_copy(out=QT_, in_=pb[:, 0:128]), reads=[pb_r], writes=[QT_r_])
                    sc.add("dve", lambda e, pb=pb, KT_=KT_: e.tensor_copy(out=KT_, in_=pb[:, 128:256]), reads=[pb_r], writes=[KT_r_])
                    sc.add("dve", lambda e, GR_=GR_, smt=smt, h=h: e.tensor_scalar(out=GR_, in0=ones_f, scalar1=smt[:, 8 + h:9 + h], scalar2=None, op0=ALU.mult), reads=[sm_r, pc_r], writes=[GR_r_])
                    K4 = os.environ.get("K4", "z")
                    if KB == 4 and K4 <= "a":
                        continue
                    pg_, pg_r = pbank()
                    sc.pe32(lambda e, pg_=pg_, GR_=GR_: e.matmul(pg_[:, 0:128], lhsT=GR_, rhs=LTc, start=True, stop=True), reads=[GR_r_, pc_r], writes=[pg_r])
                    sc.add("dve", lambda e, pg_=pg_, D_=D_, gcum_h=gcum_h: e.tensor_scalar(out=D_, in0=pg_[:, 0:128], scalar1=gcum_h, scalar2=0.0, op0=ALU.subtract, op1=ALU.max), reads=[pg_r, sm_r], writes=[D_r])
                    sc.add("dve", lambda e, D_=D_: e.tensor_scalar(out=D_, in0=D_, scalar1=60.0, scalar2=None, op0=ALU.min), reads=[D_r], writes=[D_r])
                    if os.environ.get("K5") == "waitD0":
                        sc.pe32(lambda e, pg_=pg_: e.transpose(out=pg_[:, 256:384], in_=ident_f[:], identity=ident_f[:]), reads=[r_const, D_r], writes=[])
                    sc.add("act", lambda e, D_=D_: e.activation(out=D_, in_=D_, func=AF.Exp, scale=-1.0, bias=cbias[:, 3:4]), reads=[D_r], writes=[D_r])
                    if os.environ.get("K5") == "waitD1":
                        sc.pe32(lambda e, pg_=pg_: e.transpose(out=pg_[:, 256:384], in_=ident_f[:], identity=ident_f[:]), reads=[r_const, D_r], writes=[])
                    PD = os.environ.get("PD", "dve")
                    sc.add(PD, lambda e, D_=D_, Ds_=Ds_: e.tensor_tensor(out=Ds_, in0=D_, in1=mstrict, op=ALU.mult), reads=[D_r, pc_r], writes=[Ds_r])
                    sc.add(PD, lambda e, D_=D_: e.tensor_tensor(out=D_, in0=D_, in1=mincl, op=ALU.mult), reads=[D_r, pc_r], writes=[D_r])
                    if os.environ.get("K5") == "waitD2":
                        sc.pe32(lambda e, pg_=pg_: e.transpose(out=pg_[:, 256:384], in_=ident_f[:], identity=ident_f[:]), reads=[r_const, Ds_r], writes=[])
                    if KB == 4 and K4 <= "b":
                        continue
                    pk_, pk_r = pbank()
                    K8 = os.environ.get("K8", "")
                    if K8 != "noKK" and K8 != "none":
                        sc.pe32(lambda e, pk_=pk_, KT_=KT_: e.matmul(pk_[:, 0:128], lhsT=KT_, rhs=KT_, start=True, stop=True), reads=[KT_r_], writes=[pk_r])
                    if K8 != "noQK" and K8 != "none":
                        sc.pe32(lambda e, pk_=pk_, QT_=QT_, KT_=KT_: e.matmul(pk_[:, 128:256], lhsT=QT_, rhs=KT_, start=True, stop=True), reads=[QT_r_, KT_r_], writes=[pk_r])
                    sc.add("dve", lambda e, pk_=pk_, A_=A_, beta_h=beta_h: e.tensor_scalar(out=A_, in0=pk_[:, 0:128], scalar1=beta_h, scalar2=None, op0=ALU.mult), reads=[pk_r, sm_r], writes=[A_r])
                    sc.add("dve", lambda e, pk_=pk_, at_=at_: e.tensor_copy(out=at_, in_=pk_[:, 128:256]), reads=[pk_r], writes=[at_r])
                    (A0_, A0_r) = nxt(Am, "Am"); (at0_, at0_r) = nxt(attn, "attn")
                    sc.add("dve", lambda e, A_=A_, A0_=A0_, Ds_=Ds_: e.tensor_tensor(out=A0_, in0=A_, in1=Ds_, op=ALU.mult), reads=[A_r, Ds_r], writes=[A0_r])
                    sc.add("dve", lambda e, at_=at_, at0_=at0_, D_=D_: e.tensor_tensor(out=at0_, in0=at_, in1=D_, op=ALU.mult), reads=[at_r, D_r], writes=[at0_r])
                    A_, A_r, at_, at_r = A0_, A0_r, at0_, at0_r
                    if KB == 4 and K4 <= "c":
                        continue
                    if os.environ.get("HB", "0") == "1":
                        sc.barrier()
                    if os.environ.get("K6") == "samebank":
                        pt_, pt_r = pk_[:, 256:512], pk_r
                    else:
                        pt_, pt_r = pbank()
                    K5 = os.environ.get("K5", "")
                    if K5 == "waitonly":
                        sc.pe32(lambda e, pt_=pt_: e.transpose(out=pt_[:, 0:128], in_=ident_f[:], identity=ident_f[:]), reads=[r_const, A_r], writes=[pt_r])
                        continue
                    if K5 == "spin":
                        for _ in range(int(os.environ.get("NSPIN", "300"))):
                            sc.pe32(lambda e, pt_=pt_: e.transpose(out=pt_[:, 256:384], in_=ident_f[:], identity=ident_f[:]), reads=[r_const], writes=[pt_r])
                        sc.pe32(lambda e, pt_=pt_: e.transpose(out=pt_[:, 0:128], in_=ident_f[:], identity=ident_f[:]), reads=[r_const, A_r], writes=[pt_r])
                        continue
                    if K5 == "dummy":
                        sc.pe32(lambda e, pt_=pt_: e.transpose(out=pt_[:, 256:384], in_=ident_f[:], identity=ident_f[:]), reads=[r_const], writes=[pt_r])
                        sc.pe32(lambda e, pt_=pt_: e.transpose(out=pt_[:, 0:128], in_=ident_f[:], identity=ident_f[:]), reads=[r_const, A_r], writes=[pt_r])
                        continue
                    if K5 == "viaact2" and ((t * 4 + h) >= int(os.environ.get("KN", "999")) or (t * 4 + h) < int(os.environ.get("KN0", "0"))):
                        continue
                    if K5 == "viaact2":
                        sc.add("dve", lambda e, X_=X_, A_=A_: e.tensor_copy(out=X_, in_=A_), reads=[A_r], writes=[X_r])
                        continue
                    if K5 == "viaact":
                        sc.add("dve", lambda e, X_=X_, A_=A_: e.tensor_copy(out=X_, in_=A_), reads=[A_r], writes=[X_r])
                        sc.pe32(lambda e, pt_=pt_, X_=X_: e.transpose(out=pt_[:, 0:128], in_=X_, identity=ident_f[:]), reads=[r_const, X_r], writes=[pt_r])
                        continue
                    if K5 == "waitbf":
                        ptb_ = pt_.bitcast(BF16)
                        sc.pe16(ptb_[:, 0:128], lambda e, ptb_=ptb_: e.transpose(out=ptb_[:, 0:128], in_=ident_b[:], identity=ident_b[:]), reads=[r_const, A_r], writes=[pt_r])
                        continue
                    if K5 == "waitat":
                        sc.pe32(lambda e, pt_=pt_: e.transpose(out=pt_[:, 0:128], in_=ident_f[:], identity=ident_f[:]), reads=[r_const, at_r], writes=[pt_r])
                        continue
                    if K5 == "waitD":
                        sc.pe32(lambda e, pt_=pt_: e.transpose(out=pt_[:, 0:128], in_=ident_f[:], identity=ident_f[:]), reads=[r_const, Ds_r], writes=[pt_r])
                        continue
                    if K5 == "useGR":
                        sc.pe32(lambda e, pt_=pt_, GR_=GR_: e.transpose(out=pt_[:, 0:128], in_=GR_, identity=ident_f[:]), reads=[GR_r_, r_const] + ([A_r] if os.environ.get("K7") != "nodep" else []), writes=[pt_r])
                        continue
                    if K5 == "useDs":
                        sc.pe32(lambda e, pt_=pt_, Ds_=Ds_: e.transpose(out=pt_[:, 0:128], in_=Ds_, identity=ident_f[:]), reads=[Ds_r, A_r, r_const], writes=[pt_r])
                        continue
                    if K5 != "nope" and K5 != "pe2":
                        sc.pe32(lambda e, pt_=pt_, A_=A_: e.transpose(out=pt_[:, 0:128], in_=A_, identity=ident_f[:]), reads=[A_r, r_const], writes=[pt_r])
                    if K5 != "nope" and K5 != "pe1":
                        sc.pe32(lambda e, pt_=pt_, at_=at_: e.transpose(out=pt_[:, 128:256], in_=at_, identity=ident_f[:]), reads=[at_r, r_const], writes=[pt_r])
                    if K5 == "nodve":
                        continue
                    sc.add("dve", lambda e, pt_=pt_, X_=X_: e.tensor_copy(out=X_, in_=pt_[:, 0:128]), reads=[pt_r], writes=[X_r])
                    if KB == 4 and K4 <= "d":
                        continue
                    sc.add("pool", lambda e, X_=X_, Bo_=Bo_: e.tensor_tensor(out=Bo_, in0=X_.unsqueeze(1).to_broadcast([128, 6, 128]), in1=boff, op=ALU.mult), reads=[X_r, pc_r], writes=[Bo_r])
                    if KB == 4 and K4 <= "e":
                        continue
                    sc.add("dve", lambda e, pt_=pt_, atT_=atT_: e.tensor_copy(out=atT_, in_=pt_[:, 128:256]), reads=[pt_r], writes=[atT_r])
                    if KB <= 4:
                        continue
                    (E_, E_r) = nxt(Em, "E"); (Dk_, Dk_r) = nxt(Dk, "Dk")
                    sc.add("pool", lambda e, E_=E_, Bo_=Bo_: e.tensor_tensor(out=E_, in0=ident_f[:], in1=Bo_[:, 0, :], op=ALU.subtract), reads=[Bo_r, r_const], writes=[E_r])
                    sc.add("pool", lambda e, Dk_=Dk_, A_=A_: e.tensor_tensor(out=Dk_, in0=A_, in1=aoff1, op=ALU.mult), reads=[A_r, pc_r], writes=[Dk_r])
                    sc.add("pool", lambda e, Dk_=Dk_: e.tensor_tensor(out=Dk_, in0=ident_f[:], in1=Dk_, op=ALU.subtract), reads=[Dk_r, r_const], writes=[Dk_r])
                    for lvl in range(1, 6):
                        px_, px_r = pbank()
                        sc.pe32(lambda e, px_=px_, Bo_=Bo_, lvl=lvl, Dk_=Dk_: e.matmul(px_[:, 0:128], lhsT=Bo_[:, lvl, :], rhs=Dk_, start=True, stop=True), reads=[Bo_r, Dk_r], writes=[px_r])
                        sc.add("dve", lambda e, px_=px_, X_=X_: e.tensor_copy(out=X_, in_=px_[:, 0:128]), reads=[px_r], writes=[X_r])
                        py_, py_r = pbank()
                        sc.pe32(lambda e, py_=py_, X_=X_, E_=E_: e.matmul(py_[:, 0:128], lhsT=X_, rhs=E_, start=True, stop=True), reads=[X_r, E_r], writes=[py_r])
                        (E2_, E2_r) = nxt(Em, "E")
                        sc.add("dve", lambda e, py_=py_, E_=E_, E2_=E2_: e.tensor_tensor(out=E2_, in0=E_, in1=py_[:, 0:128], op=ALU.subtract), reads=[py_r, E_r], writes=[E2_r])
                        E_, E_r = E2_, E2_r
                        if lvl < 5:
                            pd_, pd_r = pbank()
                            sc.pe32(lambda e, pd_=pd_, E_=E_: e.transpose(out=pd_[:, 0:128], in_=E_, identity=ident_f[:]), reads=[E_r, r_const], writes=[pd_r])
                            (Dk_, Dk_r) = nxt(Dk, "Dk")
                            sc.add("dve", lambda e, pd_=pd_, Dk_=Dk_: e.tensor_copy(out=Dk_, in_=pd_[:, 0:128]), reads=[pd_r], writes=[Dk_r])
                    if KB <= 5:
                        continue
                    if os.environ.get("HB", "0") == "1":
                        sc.barrier()
                    sc.add("pool", lambda e, R_=R_, Vt=Vt, h=h, beta_h=beta_h: e.tensor_scalar(out=R_[:, 0:128], in0=Vt[:, h, :], scalar1=beta_h, scalar2=None, op0=ALU.mult), reads=[Vt_r, sm_r], writes=[R_r])
                    sc.add("pool", lambda e, R_=R_, Kt=Kt, h=h, bk_h=bk_h: e.tensor_scalar(out=R_[:, 128:256], in0=Kt[:, h, :], scalar1=bk_h, scalar2=None, op0=ALU.mult), reads=[Kt_r, sm_r], writes=[R_r], partial=True)
                    sc.add("pool", lambda e, Kd_=Kd_, Kt=Kt, h=h, egu_h=egu_h: e.tensor_scalar(out=Kd_, in0=Kt[:, h, :], scalar1=egu_h, scalar2=None, op0=ALU.mult), reads=[Kt_r, sm_r], writes=[Kd_r])
                    sc.add("pool", lambda e, Qd_=Qd_, Qt=Qt, h=h, egc_h=egc_h: e.tensor_scalar(out=Qd_, in0=Qt[:, h, :], scalar1=egc_h, scalar2=None, op0=ALU.mult), reads=[Qt_r, sm_r], writes=[Qd_r])
                    pu_, pu_r = pbank()
                    sc.pe32(lambda e, pu_=pu_, E_=E_, R_=R_: e.matmul(pu_[:, 0:256], lhsT=E_, rhs=R_, start=True, stop=True), reads=[E_r, R_r], writes=[pu_r])
                    sc.add("dve", lambda e, pu_=pu_, UW_=UW_: e.tensor_copy(out=UW_[:, 0:128], in_=pu_[:, 0:128]), reads=[pu_r], writes=[UW_r])
                    sc.add("dve", lambda e, pu_=pu_, UW_=UW_: e.tensor_scalar(out=UW_[:, 128:256], in0=pu_[:, 128:256], scalar1=-1.0, scalar2=None, op0=ALU.mult), reads=[pu_r], writes=[UW_r], partial=True)
                    pq_, pq_r = pbank()
                    sc.pe32(lambda e, pq_=pq_, Qd_=Qd_: e.matmul(pq_[:, 0:128], lhsT=Qd_, rhs=ident_f[:], start=True, stop=False), reads=[Qd_r, r_const], writes=[pq_r])
                    sc.pe32(lambda e, pq_=pq_, UW_=UW_, atT_=atT_: e.matmul(pq_[:, 0:128], lhsT=UW_[:, 128:256], rhs=atT_, start=False, stop=True), reads=[UW_r, atT_r], writes=[pq_r])
                    sc.add("dve", lambda e, pq_=pq_, QpT_=QpT_: e.tensor_copy(out=QpT_, in_=pq_[:, 0:128]), reads=[pq_r], writes=[QpT_r])
                    for ci in range(2):
                        pm_, pm_r = pbank()
                        ps_ = slice(ci * 64, ci * 64 + 64)
                        sc.pe32(lambda e, pm_=pm_, UW_=UW_, Kd_=Kd_, ps_=ps_: e.matmul(pm_[:, 0:128], lhsT=UW_[ps_, 128:256], rhs=Kd_[ps_, :], start=True, stop=True), reads=[UW_r, Kd_r], writes=[pm_r])
                        if ci == 0:
                            sc.add("dve", lambda e, pm_=pm_, Mp_=Mp_, ci=ci: e.tensor_copy(out=Mp_[:, ci, :], in_=pm_[:, 0:128]), reads=[pm_r], writes=[Mp_r])
                        else:
                            sc.add("dve", lambda e, pm_=pm_, Mp_=Mp_, ci=ci: e.tensor_copy(out=Mp_[:, ci, :], in_=pm_[:, 0:128]), reads=[pm_r], writes=[Mp_r], partial=True)
                    if os.environ.get("HB", "0") == "1":
                        sc.barrier()
                    for ci in range(2 if KB > 6 else 0):
                        ps_ = slice(ci * 64, ci * 64 + 64)
                        Sp, Sp_r = Sst[cur], S_r[cur][h]
                        Sn, Sn_r = Sst[1 - cur], S_r[1 - cur][h]
                        po_, po_r = pbank()
                        tp = (0, ci * 64)
                        sc.pe32(lambda e, po_=po_, QpT_=QpT_, Sp=Sp, h=h, ps_=ps_, tp=tp: e.matmul(po_[ps_, 0:128], lhsT=QpT_[:, ps_], rhs=Sp[:, h, :], start=True, stop=False, tile_position=tp), reads=[QpT_r, Sp_r], writes=[po_r])
                        sc.pe32(lambda e, po_=po_, atT_=atT_, UW_=UW_, ps_=ps_, tp=tp: e.matmul(po_[ps_, 0:128], lhsT=atT_[:, ps_], rhs=UW_[:, 0:128], start=False, stop=True, tile_position=tp), reads=[atT_r, UW_r], writes=[po_r])
                        sc.add("dve", lambda e, po_=po_, osb_t=osb_t, h=h, ps_=ps_: e.tensor_copy(out=osb_t[ps_, h, :], in_=po_[ps_, 0:128]), reads=[po_r], writes=[osb_r], partial=True)
                        sc.add("dve", lambda e, osb_t=osb_t, h=h, ps_=ps_: e.tensor_tensor(out=junk[0][ps_, :], in0=osb_t[ps_, h, :], in1=osb_t[ps_, h, :], op=ALU.mult), reads=[osb_r], writes=[junk[1]])
                        sc.add("dve", lambda e, oss_t=oss_t, h=h, ps_=ps_: e.tensor_reduce(out=oss_t[ps_, h:h + 1], in_=junk[0][ps_, :], axis=AX.X, op=ALU.add), reads=[junk[1]], writes=[oss_r], partial=True)
                        pS_, pS_r = pbank()
                        sc.pe32(lambda e, pS_=pS_, Mp_=Mp_, ci=ci, Sp=Sp, h=h: e.matmul(pS_[:, 0:128], lhsT=Mp_[:, ci, :], rhs=Sp[:, h, :], start=True, stop=False), reads=[Mp_r, Sp_r], writes=[pS_r])
                        sc.pe32(lambda e, pS_=pS_, Kd_=Kd_, UW_=UW_, ps_=ps_: e.matmul(pS_[:, 0:128], lhsT=Kd_[ps_, :], rhs=UW_[ps_, 0:128], start=False, stop=True), reads=[Kd_r, UW_r], writes=[pS_r])
                        egl = smt[:, 24 + 4 * ci + h:25 + 4 * ci + h]
                        sc.add("dve", lambda e, pS_=pS_, Sp=Sp, Sn=Sn, h=h, egl=egl: e.scalar_tensor_tensor(out=Sn[:, h, :], in0=Sp[:, h, :], scalar=egl, in1=pS_[:, 0:128], op0=ALU.mult, op1=ALU.add), reads=[pS_r, Sp_r, sm_r], writes=[Sn_r])
                        cur = 1 - cur
                sc.safe = False
                if KB <= 7:
                    continue
                (oab_t, oab_r) = nxt(oab, "oab"); (oaT_t, oaT_r) = nxt(oaT, "oaT")
                sc.add("act", lambda e, oss_t=oss_t: e.activation(out=oss_t[:, 4:8], in_=oss_t[:, 0:4], func=AF.Sqrt, scale=1.0 / 128, bias=cbias[:, 0:1]), reads=[oss_r, r_const], writes=[oss_r])
                sc.add("dve", lambda e, oss_t=oss_t: e.reciprocal(out=oss_t[:, 4:8], in_=oss_t[:, 4:8]), reads=[oss_r], writes=[oss_r])
                sc.add("dve", lambda e, osb_t=osb_t, oss_t=oss_t: e.tensor_tensor(out=osb_t, in0=osb_t, in1=oss_t[:, 4:8].unsqueeze(2).to_broadcast([128, 4, 128]), op=ALU.mult), reads=[osb_r, oss_r], writes=[osb_r])
                sc.add("dve", lambda e, osb_t=osb_t, zst=zst, oab_t=oab_t: e.tensor_tensor(out=oab_t, in0=osb_t.rearrange("p h d -> p (h d)"), in1=zst, op=ALU.mult), reads=[osb_r, zs_r], writes=[oab_r])
                ptb = bank[5][:].bitcast(BF16).rearrange("p (k c) -> p k c", k=8)
                for h in range(4):
                    sc.pe16(ptb[:, h, :], lambda e, h=h, oab_t=oab_t: e.transpose(out=ptb[:, h, :], in_=oab_t[:, h * 128:(h + 1) * 128], identity=ident_b[:]), reads=[oab_r, r_const], writes=[bank_r[5]])
                sc.add("dve", lambda e, oaT_t=oaT_t: e.tensor_copy(out=oaT_t, in_=ptb[:, 0:4, :]), reads=[bank_r[5]], writes=[oaT_r])
                sc.add("sp", lambda e, oaT_t=oaT_t, tsl=tsl: e.dma_start(out=oT_d[0:4, :, tsl].rearrange("j p c -> p j c"), in_=oaT_t), reads=[oaT_r], writes=[], dma=True, key="oaT%d" % (ctr["oaT"] % 2))

    def ln_tile(L_, t, ps_lo, ps_lo_r, ps_hi, ps_hi_r, xr, xr_r, g_bc, b_bc, lnp_r, out_d, write_xT):
        tsl = slice(t * 128, (t + 1) * 128)
        (y, y_r) = L_["y"][t % 2]; (st, st_r) = L_["st"][t % 2]; (xb_, xb_r_) = L_["xb"][t % 2]
        for half, (pp, pp_r) in enumerate(((ps_lo, ps_lo_r), (ps_hi, ps_hi_r))):
            hs = slice(half * 512, (half + 1) * 512)
            sc.add("dve", lambda e, pp=pp, hs=hs, y=y, xr=xr: e.scalar_tensor_tensor(out=y[:, hs], in0=xr[:, hs], scalar=ALPHA, in1=pp[:, 0:512], op0=ALU.mult, op1=ALU.add),
                   reads=[pp_r, xr_r], writes=[y_r], partial=(half == 1))
        for half in range(2):
            hs = slice(half * 512, (half + 1) * 512)
            sc.add("dve", lambda e, half=half, hs=hs, y=y, st=st: e.bn_stats(out=st[:, half * 6:(half + 1) * 6], in_=y[:, hs]), reads=[y_r], writes=[st_r], partial=(half == 1))
        sc.add("dve", lambda e, st=st: e.bn_aggr(out=st[:, 12:14], in_=st[:, 0:12]), reads=[st_r], writes=[st_r])
        sc.add("act", lambda e, st=st: e.activation(out=st[:, 14:15], in_=st[:, 13:14], func=AF.Sqrt, bias=cbias[:, 2:3]), reads=[st_r, r_const], writes=[st_r])
        sc.add("dve", lambda e, st=st: e.reciprocal(out=st[:, 14:15], in_=st[:, 14:15]), reads=[st_r], writes=[st_r])
        sc.add("dve", lambda e, y=y, st=st: e.tensor_scalar(out=y, in0=y, scalar1=st[:, 12:13], scalar2=st[:, 14:15], op0=ALU.subtract, op1=ALU.mult), reads=[y_r, st_r], writes=[y_r])
        sc.add("pool", lambda e, y=y: e.tensor_tensor(out=y, in0=y, in1=g_bc, op=ALU.mult), reads=[y_r, lnp_r], writes=[y_r])
        sc.add("dve", lambda e, y=y: e.tensor_tensor(out=y, in0=y, in1=b_bc, op=ALU.add), reads=[y_r, lnp_r], writes=[y_r])
        sc.add("sp", lambda e, y=y, tsl=tsl: e.dma_start(out=out_d[tsl, :], in_=y), reads=[y_r], writes=[], dma=True, key="ysto%d" % (t % 2))
        if write_xT:
            sc.add("act", lambda e, y=y, xb_=xb_: e.copy(out=xb_, in_=y), reads=[y_r], writes=[xb_r_])
            ptb = bank[7][:].bitcast(BF16).rearrange("p (k c) -> p k c", k=8)
            for kc in range(8):
                sc.pe16(ptb[:, kc, :], lambda e, kc=kc, xb_=xb_: e.transpose(out=ptb[:, kc, :], in_=xb_[:, kc * 128:(kc + 1) * 128], identity=ident_b[:]), reads=[xb_r_, r_const], writes=[bank_r[7]])
            sc.add("dve", lambda e, tsl=tsl: e.tensor_copy(out=xT[:, :, tsl], in_=ptb), reads=[bank_r[7]], writes=[xT_r[t]])

    def ln_bufs(g_d, b_d, l):
        L_ = {}
        L_["y"] = [(cv.get([128, 1024]), sc.res("y%d" % i)) for i in range(2)]
        L_["st"] = [(cv.get([128, 16]), sc.res("st%d" % i)) for i in range(2)]
        L_["xb"] = [(cv.get([128, 1024], BF16), sc.res("xbln%d" % i)) for i in range(2)]
        g_bc = cv.get([128, 1024]); b_bc = cv.get([128, 1024]); lnp_r = sc.res("lnp")
        sc.add("sp", lambda e: e.dma_start(out=g_bc, in_=g_d[l, :].partition_broadcast(128)), writes=[lnp_r], dma=True, key="lnp", partial=True)
        sc.add("sp", lambda e: e.dma_start(out=b_bc, in_=b_d[l, :].partition_broadcast(128)), writes=[lnp_r], dma=True, key="lnp", partial=True)
        return L_, g_bc, b_bc, lnp_r

    def phaseC(l, xin_d):
        cv.reset()
        wO = cv.get([128, 8, 1024], BF16); wO_r = [sc.res("wO%d" % k) for k in range(8)]
        for kc in range(8):
            sc.add("pool", lambda e, kc=kc: e.dma_start(out=wO[:, kc, :], in_=w_out_d[l, kc * 128:(kc + 1) * 128, :]), writes=[wO_r[kc]], dma=True, key="wO%d" % kc)
        for kc in range(8):
            sc.add("pool", lambda e, kc=kc: e.dma_start(out=wupbf_d[l].rearrange("t p k f -> p t k f")[:, :, kc, :], in_=w_up_d[l, kc * 128:(kc + 1) * 128, :].rearrange("p (t f) -> p t f", f=128)),
                   writes=[wupbf_r[l]], dma=True, key="wcv%d" % (kc % 4), partial=True)
        L_, g_bc, b_bc, lnp_r = ln_bufs(ln1g_d, ln1b_d, l)
        oTt = [(cv.get([128, 8, 128], BF16), sc.res("oTt%d" % i)) for i in range(3)]
        xrs = [(cv.get([128, 1024]), sc.res("xr%d" % i)) for i in range(3)]
        for t in range(NT):
            tsl = slice(t * 128, (t + 1) * 128)
            (ot, ot_r) = oTt[t % 3]; (xr, xr_r) = xrs[t % 3]
            sc.add("sp", lambda e, ot=ot, tsl=tsl: e.dma_start(out=ot, in_=oT_d[:, :, tsl].rearrange("k p c -> p k c")), writes=[ot_r], dma=True, key="oTt%d" % (t % 3))
            sc.add("sp", lambda e, xr=xr, tsl=tsl: e.dma_start(out=xr, in_=xin_d[tsl, :]), writes=[xr_r], dma=True, key="xr%d" % (t % 3))
            bl, bh = 2 * (t % 2), 2 * (t % 2) + 1
            for half, bi in ((0, bl), (1, bh)):
                for kc in range(8):
                    sc.pe16(bank[bi][:], lambda e, bi=bi, kc=kc, ot=ot, half=half: e.matmul(bank[bi][:], lhsT=ot[:, kc, :], rhs=wO[:, kc, half * 512:(half + 1) * 512], start=(kc == 0), stop=(kc == 7)),
                            reads=[ot_r, wO_r[kc]], writes=[bank_r[bi]])
            ln_tile(L_, t, bank[bl], bank_r[bl], bank[bh], bank_r[bh], xr, xr_r, g_bc, b_bc, lnp_r, x1_d, True)

    def phaseD(l, out_d, write_xT):
        cv.reset()
        NJ = DFF // 128
        NBLK = S // 512
        wD = cv.get([128, NJ, 1024], BF16); wD_r = [sc.res("wD%d" % j) for j in range(NJ)]
        for j in range(NJ):
            sc.add("pool", lambda e, j=j: e.dma_start(out=wD[:, j, :], in_=w_down_d[l, j * 128:(j + 1) * 128, :]), writes=[wD_r[j]], dma=True, key="wD%d" % (j % 4))
        L_, g_bc, b_bc, lnp_r = ln_bufs(ln2g_d, ln2b_d, l)
        fc4 = cv.get([128, 4, 44]); fc_r = sc.res("fc4")
        hT = cv.get([128, NJ, 512], BF16); hT_r = [sc.res("hT%d" % j) for j in range(NJ)]
        wU = [(cv.get([128, 8, 256], BF16), sc.res("wU%d" % i)) for i in range(3)]
        raw = [(cv.get([128, 2, 514]), sc.res("raw%d" % i)) for i in range(2)]
        acc = [(cv.get([128, 2, 512]), sc.res("facc%d" % i)) for i in range(2)]
        halo = cv.get([128, 44, 2]); halo_r = [sc.res("fhalo%d" % j) for j in range(44)]
        xrs = [(cv.get([128, 1024]), sc.res("xrD%d" % i)) for i in range(2)]
        w44 = hT.rearrange("p j t -> p (j t)").bitcast(F32)[0:44, 0:512].rearrange("p (a b) -> p a b", a=4)
        w44_r = sc.res("w44")
        for j3 in range(3):
            sc.add("sp", lambda e, j3=j3: e.dma_start(out=w44[:, j3, :], in_=fconvw_d[l, j3, :].rearrange("(f p) -> f p", p=128)), writes=[w44_r] + hT_r[0:2], dma=True, key="w44", partial=True)
        sc.add("sp", lambda e: e.dma_start(out=w44[:, 3, :], in_=fconvb_d[l, :].rearrange("(f p) -> f p", p=128)), writes=[w44_r], dma=True, key="w44", partial=True)
        for a4 in range(4):
            sc.pe32(lambda e, a4=a4: e.transpose(out=bank[0][:, a4 * 44:(a4 + 1) * 44], in_=w44[:, a4, :], identity=ident_f[0:44, 0:44]), reads=[w44_r, r_const], writes=[bank_r[0]])
        sc.add("dve", lambda e: e.tensor_copy(out=fc4.rearrange("p a f -> p (a f)"), in_=bank[0][:, 0:176]), reads=[bank_r[0]], writes=[fc_r])
        sc.add("pool", lambda e: e.memset(halo, 0.0), reads=[], writes=halo_r)
        sc.barrier()
        for c in range(NBLK):
            csl = slice(c * 512, (c + 1) * 512)
            for j in range(NJ):
                (wu, wu_r) = wU[j % 3]
                sc.add("sp", lambda e, wu=wu, j=j: e.dma_start(out=wu[:, :, 0:128], in_=wupbf_d[l][j, :, :, :]), reads=[wupbf_r[l]], writes=[wu_r], dma=True, key="wUa%d" % (j % 3))
                sc.add("sp", lambda e, wu=wu, j=j: e.dma_start(out=wu[:, :, 128:256], in_=wupbf_d[l][22 + j, :, :, :]), reads=[wupbf_r[l]], writes=[wu_r], dma=True, key="wUb%d" % (j % 3), partial=True)
                (rw, rw_r) = raw[j % 2]; (ac, ac_r) = acc[j % 2]
                for gv in range(2):
                    f = gv * 22 + j
                    bi = 2 * (j % 2) + gv
                    for kc in range(8):
                        sc.pe16(bank[bi][:], lambda e, bi=bi, kc=kc, wu=wu, gv=gv, csl=csl: e.matmul(bank[bi][:], lhsT=wu[:, kc, gv * 128:(gv + 1) * 128], rhs=xT[:, kc, csl], start=(kc == 0), stop=(kc == 7)),
                                reads=[wu_r] + xT_r[4 * c:4 * c + 4], writes=[bank_r[bi]])
                    sc.add("pool", lambda e, rw=rw, gv=gv, f=f: e.tensor_copy(out=rw[:, gv, 0:2], in_=halo[:, f, :]), reads=[halo_r[f]], writes=[rw_r], partial=(gv == 1))
                    sc.add("act", lambda e, rw=rw, gv=gv, bi=bi: e.copy(out=rw[:, gv, 2:514], in_=bank[bi][:]), reads=[bank_r[bi]], writes=[rw_r], partial=True)
                    sc.add("pool", lambda e, rw=rw, gv=gv, f=f: e.tensor_copy(out=halo[:, f, :], in_=rw[:, gv, 512:514]), reads=[rw_r], writes=[halo_r[f]])
                    sc.add("act", lambda e, ac=ac, gv=gv, bi=bi, f=f: e.activation(out=ac[:, gv, :], in_=bank[bi][:], func=AF.Identity, scale=fc4[:, 2, f:f + 1], bias=fc4[:, 3, f:f + 1]), reads=[bank_r[bi], fc_r], writes=[ac_r], partial=(gv == 1))
                    eng = "dve"
                    for tap in (1, 0):
                        sc.add(eng, lambda e, ac=ac, rw=rw, gv=gv, tap=tap, f=f: e.scalar_tensor_tensor(out=ac[:, gv, :], in0=rw[:, gv, tap:tap + 512], scalar=fc4[:, tap, f:f + 1], in1=ac[:, gv, :], op0=ALU.mult, op1=ALU.add),
                               reads=[rw_r, fc_r, ac_r], writes=[ac_r])
                sc.add("act", lambda e, ac=ac: e.activation(out=ac[:, 0, :], in_=ac[:, 0, :], func=AF.Silu, bias=cbias[:, 3:4]), reads=[ac_r, r_const], writes=[ac_r])
                sc.add("dve", lambda e, ac=ac, j=j: e.tensor_tensor(out=hT[:, j, :], in0=ac[:, 0, :], in1=ac[:, 1, :], op=ALU.mult), reads=[ac_r], writes=[hT_r[j]])
            for tt in range(4):
                t = c * 4 + tt
                tsl = slice(t * 128, (t + 1) * 128)
                (xr, xr_r) = xrs[t % 2]
                sc.add("sp", lambda e, xr=xr, tsl=tsl: e.dma_start(out=xr, in_=x1_d[tsl, :]), writes=[xr_r], dma=True, key="xrD%d" % (t % 2))
                bl, bh = 4 + 2 * (t % 2), 5 + 2 * (t % 2)
                if bh == 7 and write_xT:
                    bl, bh = 4, 5
                for half, bi in ((0, bl), (1, bh)):
                    for j in range(NJ):
                        sc.pe16(bank[bi][:], lambda e, bi=bi, j=j, tt=tt, half=half: e.matmul(bank[bi][:], lhsT=hT[:, j, tt * 128:(tt + 1) * 128], rhs=wD[:, j, half * 512:(half + 1) * 512], start=(j == 0), stop=(j == NJ - 1)),
                                reads=[hT_r[j], wD_r[j]], writes=[bank_r[bi]])
                ln_tile(L_, t, bank[bl], bank_r[bl], bank[bh], bank_r[bh], xr, xr_r, g_bc, b_bc, lnp_r, out_d, write_xT)

    phase0(x_d)
    sc.barrier()
    if stop_after == "0":
        sc.add("sp", lambda e: e.dma_start(out=oT_d[:, :, :].rearrange("k p s -> p k s"), in_=xT[:]), reads=xT_r, writes=[], dma=True, key="dbg")
    elif stop_after == "A":
        phaseA(0)
    elif stop_after == "B":
        phaseB(0)
    else:
        for l in range(L):
            xin = x_d if l == 0 else x2_d
            last = (l == L - 1)
            phaseA(l)
            sc.barrier()
            phaseB(l)
            sc.barrier()
            phaseC(l, xin)
            sc.barrier()
            if stop_after == "C" and l == 0:
                break
            phaseD(l, y_d if last else x2_d, not last)
            sc.barrier()
            if stop_after == "D" and l == 0:
                break
    if dbg and os.environ.get("DUMPARENA") and not os.environ.get("SIM"):
        sc.barrier()
        dbg_arena = nc.dram_tensor("dbg_arena", [128, ARENA], F32, kind="ExternalOutput").ap()
        for q in range(4):
            sc.add("sp", lambda e, q=q: e.dma_start(out=dbg_arena[:, q * (ARENA // 4):(q + 1) * (ARENA // 4)], in_=arena[:, q * (ARENA // 4):(q + 1) * (ARENA // 4)]), dma=True, key="dbga")
    sc.emit(nc, es)
    es.close()
    return nc


_CACHE = {}


def kernel(**inputs):
    x = np.asarray(inputs["x"], dtype=np.float32)
    B, S, _ = x.shape
    L = int(np.asarray(inputs["w_in"]).shape[0])
    key = (S, L)
    if key not in _CACHE:
        _CACHE[key] = (build(S=S, L=L), make_consts(S))
    nc, consts = _CACHE[key]
    shared = {k: np.ascontiguousarray(np.asarray(v, dtype=np.float32)) for k, v in inputs.items() if k != "x"}
    for k, v in consts.items():
        shared["c_" + k] = v
    in_maps = []
    for b in range(B):
        m = dict(shared)
        m["x"] = np.ascontiguousarray(x[b])
        in_maps.append(m)
    res = run_bass_kernel_spmd(nc, in_maps, core_ids=list(range(B)))
    return np.stack([np.asarray(r["y"], dtype=np.float32) for r in res.results], axis=0)
```

```python
import os
import numpy as np
import ml_dtypes
from contextlib import ExitStack
import concourse.bass as bass
import concourse.mybir as mybir
from concourse.bass_utils import run_bass_kernel_spmd

F32 = mybir.dt.float32
BF16 = mybir.dt.bfloat16
AF = mybir.ActivationFunctionType
ALU = mybir.AluOpType
AX = mybir.AxisListType

D = 1024
NIN = 3592
DFF = 2816
ALPHA = float((2 * 2) ** 0.25)
NEG = -30000.0


class Res:
    __slots__ = ("name", "writers", "readers")

    def __init__(self, name):
        self.name = name
        self.writers = []
        self.readers = {}


class Op:
    __slots__ = ("eng", "fn", "dma", "key", "value", "deps", "signal", "barrier")

    def __init__(self, eng, fn, dma=False, key=None):
        self.eng = eng
        self.fn = fn
        self.dma = dma
        self.key = key
        self.value = None
        self.deps = []
        self.signal = False
        self.barrier = False


ENGS = ("pe", "act", "dve", "pool", "sp")


class Sched:
    def __init__(self):
        self.ops = {e: [] for e in ENGS}
        self.keycount = {}
        self.keylast = {}
        self.allres = []
        self.last_pe_f32 = False
        self.ident_b = None
        self.safe = False
        self.safecnt = 0
        self.safek = int(os.environ.get("SAFEK", "0"))
        self.safeeng = tuple(x for x in os.environ.get("SAFEENG", "act").split(",") if x)

    def res(self, name):
        r = Res(name)
        self.allres.append(r)
        return r

    def _dep(self, op, prod, raw):
        if prod is op:
            return
        if (not prod.dma) and (not op.dma) and prod.eng == op.eng:
            if op.eng == "pe":
                return
        op.deps.append(prod)
        prod.signal = True

    def add(self, eng, fn, reads=(), writes=(), dma=False, key=None, partial=False, f32=False, out=None):
        if eng == "pe":
            if (not f32) and self.last_pe_f32 and out is not None:
                fn0 = fn
                dmy = out.bitcast(F32) if out.dtype != F32 else out
                idb = self.ident_b

                def fn(e, fn0=fn0, dmy=dmy, idb=idb):
                    e.matmul(dmy[0:64, 0:8], lhsT=idb[:, 0:64], rhs=idb[:, 0:8], start=True, stop=True)
                    return fn0(e)
            self.last_pe_f32 = f32
        excl = self.safe and (not dma) and (eng in self.safeeng)
        if excl:
            self.barrier()
        op = Op(eng, fn, dma, key)
        for r in reads:
            for w in r.writers:
                self._dep(op, w, True)
        for r in writes:
            for w in r.writers:
                if not (partial and w.dma and op.dma):
                    self._dep(op, w, False)
            for rd in r.readers.values():
                if isinstance(rd, list):
                    for x in rd:
                        self._dep(op, x, False)
                else:
                    self._dep(op, rd, False)
        for r in reads:
            if dma:
                r.readers.setdefault("dma", []).append(op)
            else:
                r.readers[eng] = op
        for r in writes:
            if partial:
                r.writers = r.writers + [op]
            else:
                r.writers = [op]
            r.readers = {}
        if dma:
            assert key is not None
            self.keycount[key] = self.keycount.get(key, 0) + 16
            op.value = self.keycount[key]
            self.keylast[key] = op
        self.ops[eng].append(op)
        if excl:
            self.barrier()
        elif self.safe and not dma and self.safek > 0:
            self.safecnt += 1
            if self.safecnt % self.safek == 0:
                self.barrier()
        return op

    def pe32(self, fn, **kw):
        return self.add("pe", fn, f32=True, **kw)

    def pe16(self, out, fn, **kw):
        return self.add("pe", fn, out=out, **kw)

    def barrier(self):
        prods = []
        for e in ENGS:
            for o in reversed(self.ops[e]):
                if not o.dma and not o.barrier:
                    prods.append(o)
                    break
        prods += list(self.keylast.values())
        for e in ENGS:
            b = Op(e, None)
            b.barrier = True
            for p in prods:
                if p.dma or p.eng != e or e != "pe":
                    b.deps.append(p)
                    p.signal = True
            self.ops[e].append(b)
        for r in self.allres:
            r.writers = []
            r.readers = {}

    def emit(self, nc, es):
        esem = {e: es.enter_context(nc.semaphore("s_" + e)) for e in ENGS}
        ksem = {}
        for i, k in enumerate(self.keycount):
            ksem[k] = es.enter_context(nc.semaphore("k%d" % i))
        for e in ENGS:
            c = 0
            for o in self.ops[e]:
                if (not o.dma) and o.signal and not o.barrier:
                    c += 1
                    o.value = c
            if os.environ.get("SEMDBG"): print("SEM", e, "final", c, "nops", len(self.ops[e]))
        block = es.enter_context(nc.Block())
        hooks = {"pe": block.tensor, "act": block.scalar, "dve": block.vector,
                 "pool": block.gpsimd, "sp": block.sync}
        final_keys = dict(self.keycount)

        def mk(ename):
            def body(eng):
                waited = {}
                for o in self.ops[ename]:
                    need = {}
                    for p in o.deps:
                        s = ksem[p.key] if p.dma else esem[p.eng]
                        sid = id(s)
                        v = p.value
                        if waited.get(sid, 0) >= v:
                            continue
                        if sid not in need or need[sid][1] < v:
                            need[sid] = (s, v)
                    for sid, (s, v) in need.items():
                        eng.wait_ge(s, v)
                        waited[sid] = v
                    if o.fn is None:
                        continue
                    ins = o.fn(eng)
                    if o.dma:
                        ins.then_inc(ksem[o.key], 16)
                    elif o.signal:
                        ins.then_inc(esem[ename], 1)
                if ename == "sp":
                    for k, v in final_keys.items():
                        if waited.get(id(ksem[k]), 0) < v:
                            eng.wait_ge(ksem[k], v)
            return body

        for e in ENGS:
            hooks[e](mk(e))


def make_consts(S):
    i = np.arange(128)[:, None]
    j = np.arange(128)[None, :]
    same = (i // 64) == (j // 64)
    c = {}
    c["ident"] = np.eye(128, dtype=np.float32)
    c["caus01"] = (i <= j).astype(np.float32)
    c["mstrict"] = (same & (j < i)).astype(np.float32)
    c["negincl"] = (same & (j <= i)).astype(np.float32)
    LT = (same & (j <= i)).T.astype(np.float32)
    UT = (same & (j > i)).T.astype(np.float32)
    CS0 = np.zeros((128, 128), np.float32); CS0[:64, :] = 1.0
    CS1 = np.zeros((128, 128), np.float32); CS1[64:, :] = 1.0
    c["gl"] = np.concatenate([LT, UT, CS0, CS1], 1)
    offs = []
    for s in (1, 2, 4, 8, 16, 32):
        m = ((i // (2 * s)) == (j // (2 * s))) & ((i // s) != (j // s)) & (i > j)
        offs.append(m.T.astype(np.float32))
    c["boff"] = np.concatenate(offs, 1)
    c["aoff1"] = (((i // 2) == (j // 2)) & (i != j) & (i > j)).astype(np.float32)
    half = 8
    inv = 500000.0 ** (-np.arange(half, dtype=np.float32) / half)
    ang = np.arange(S, dtype=np.float32)[:, None] * inv[None, :]
    cos = np.cos(ang).astype(np.float32)
    sin = np.sin(ang).astype(np.float32)
    NT = S // 128
    cc = np.concatenate([cos, cos], 1).reshape(NT, 128, 16).transpose(1, 0, 2)
    ss = np.concatenate([sin, sin], 1).reshape(NT, 128, 16).transpose(1, 0, 2)
    c["rope"] = np.ascontiguousarray(np.concatenate([cc, ss], 2)).reshape(128, NT * 32)
    return c


def build(S=4096, L=2, dbg=False, stop_after=None):
    NT = S // 128
    NB = S // 256
    nc = bass.Bass("TRN2", target_bir_lowering=False)
    sc = Sched()
    es = ExitStack()

    def din(name, shape, dt=F32):
        return nc.dram_tensor(name, list(shape), dt, kind="ExternalInput").ap()

    def dscr(name, shape, dt=F32, out=False):
        kind = "ExternalOutput" if (out or dbg) else "Internal"
        return nc.dram_tensor(name, list(shape), dt, kind=kind).ap()

    x_d = din("x", [S, D])
    w_in_d = din("w_in", [L, D, NIN])
    gconv_d = din("gdn_conv_w", [L, 4, 1536])
    alog_d = din("gdn_a_log", [L, 4])
    dtb_d = din("gdn_dt_bias", [L, 4])
    gng_d = din("gdn_norm_g", [L, 128])
    w_out_d = din("w_out", [L, D, D])
    ln1g_d = din("ln1_g", [L, D])
    ln1b_d = din("ln1_b", [L, D])
    w_up_d = din("w_up", [L, D, 2 * DFF])
    fconvw_d = din("ffn_conv_w", [L, 3, 2 * DFF])
    fconvb_d = din("ffn_conv_b", [L, 2 * DFF])
    w_down_d = din("w_down", [L, DFF, D])
    ln2g_d = din("ln2_g", [L, D])
    ln2b_d = din("ln2_b", [L, D])
    c_ident_d = din("c_ident", [128, 128])
    c_caus_d = din("c_caus01", [128, 128])
    c_mstrict_d = din("c_mstrict", [128, 128])
    c_negincl_d = din("c_negincl", [128, 128])
    c_gl_d = din("c_gl", [128, 512])
    c_boff_d = din("c_boff", [128, 768])
    c_aoff1_d = din("c_aoff1", [128, 128])
    c_rope_d = din("c_rope", [128, NT * 32])

    y_d = dscr("y", [S, D], out=True)
    x1_d = dscr("x1res", [S, D])
    x2_d = dscr("x2res", [S, D]) if L > 1 else None
    oT_d = dscr("oT", [8, 128, S], BF16)
    wupbf_d = [nc.dram_tensor("wupbf%d" % l_, [44, 128, 8, 128], BF16, kind="Internal").ap() for l_ in range(L)]
    wupbf_r = [sc.res("wupbf%d" % l_) for l_ in range(L)]

    def sb(name, shape, dt=F32):
        return es.enter_context(nc.sbuf_tensor(name, list(shape), dt))

    def ps(name, shape, dt=F32):
        return es.enter_context(nc.psum_tensor(name, list(shape), dt))

    xT = sb("xT", [128, 8, S], BF16)
    xT_r = [sc.res("xT%d" % t) for t in range(NT)]
    ident_f = sb("ident_f", [128, 128]); ident_b = sb("ident_b", [128, 128], BF16)
    caus_b = sb("caus_b", [128, 128], BF16)
    rope = sb("rope", [128, NT, 32])
    r_const = sc.res("consts")
    sc.ident_b = ident_b
    cbias = sb("cbias", [128, 4])
    sc.add("dve", lambda e: e.memset(cbias[:, 0:1], 1e-6), writes=[r_const], partial=True)
    sc.add("dve", lambda e: e.memset(cbias[:, 1:2], 1.0), writes=[r_const], partial=True)
    sc.add("dve", lambda e: e.memset(cbias[:, 2:3], 1e-5), writes=[r_const], partial=True)
    sc.add("dve", lambda e: e.memset(cbias[:, 3:4], 0.0), writes=[r_const], partial=True)

    bank = [ps("bank%d" % i, [128, 512]) for i in range(8)]
    bank_r = [sc.res("bank%d" % i) for i in range(8)]

    ARENA = 136 * 1024 // 4
    arena = sb("arena", [128, ARENA])

    class Carver:
        def __init__(self):
            self.off = 0

        def reset(self):
            self.off = 0

        def get(self, shape, dt=F32):
            n = int(np.prod(shape[1:]))
            nwords = n if dt == F32 else (n + 1) // 2
            a = arena[0:shape[0], self.off:self.off + nwords]
            self.off += (nwords + 15) // 16 * 16
            assert self.off <= ARENA, "arena overflow %d" % self.off
            if dt != F32:
                a = a.bitcast(dt)[:, 0:n]
            if len(shape) > 2:
                names = " ".join("d%d" % k for k in range(len(shape) - 1))
                kw = {"d%d" % k: shape[k + 1] for k in range(len(shape) - 2)}
                a = a.rearrange("p (%s) -> p %s" % (names, names), **kw)
            return a

    cv = Carver()

    sc.add("sp", lambda e: e.dma_start(out=ident_f[:], in_=c_ident_d[:, :]), writes=[r_const], dma=True, key="c0", partial=True)
    sc.add("pool", lambda e: e.dma_start(out=ident_b[:], in_=c_ident_d[:, :]), writes=[r_const], dma=True, key="c1", partial=True)
    sc.add("pool", lambda e: e.dma_start(out=caus_b[:], in_=c_caus_d[:, :]), writes=[r_const], dma=True, key="c1", partial=True)
    sc.add("sp", lambda e: e.dma_start(out=rope[:].rearrange("p t c -> p (t c)"), in_=c_rope_d[:, :]), writes=[r_const], dma=True, key="c0", partial=True)

    def phase0(src_d):
        cv.reset()
        xb = [cv.get([128, 1024], BF16) for _ in range(3)]
        xb_r = [sc.res("xb%d" % i) for i in range(3)]
        pst = [bank[0][:].bitcast(BF16), bank[1][:].bitcast(BF16)]
        for t in range(NT):
            s = t % 3
            sc.add("pool", lambda e, t=t, s=s: e.dma_start(out=xb[s], in_=src_d[t * 128:(t + 1) * 128, :]),
                   writes=[xb_r[s]], dma=True, key="xb%d" % s)
            p = t % 2
            pt = pst[p].rearrange("p (k c) -> p k c", k=8)
            for kc in range(8):
                sc.add("pe", lambda e, kc=kc, s=s, pt=pt: e.transpose(out=pt[:, kc, :], in_=xb[s][:, kc * 128:(kc + 1) * 128], identity=ident_b[:]),
                       reads=[xb_r[s], r_const], writes=[bank_r[p]])
            if t % 2 == 0:
                sc.add("act", lambda e, t=t, pt=pt: e.copy(out=xT[:, :, t * 128:(t + 1) * 128], in_=pt),
                       reads=[bank_r[p]], writes=[xT_r[t]])
            else:
                sc.add("dve", lambda e, t=t, pt=pt: e.tensor_copy(out=xT[:, :, t * 128:(t + 1) * 128], in_=pt),
                       reads=[bank_r[p]], writes=[xT_r[t]])

    def phaseA(l):
        cv.reset()
        wA = cv.get([128, 8, 1536], BF16)
        wA_r = [sc.res("wA%d" % k) for k in range(8)]
        KT = cv.get([128, 4, S], BF16)
        KT_r = [sc.res("KT%d" % t) for t in range(NT)]
        Vp = cv.get([128, NT, 8, 65], BF16)
        Vp_r = [sc.res("Vp%d" % t) for t in range(NT)]
        QT = [cv.get([128, 4, 256], BF16) for _ in range(2)]
        QT_r = [sc.res("QT%d" % i) for i in range(2)]
        kmT = cv.get([128, 4, 16], BF16)
        kmf = cv.get([128, 4])
        kmT_r = sc.res("kmT")
        qb = [cv.get([128, 512], BF16) for _ in range(2)]
        kb = [cv.get([128, 512], BF16) for _ in range(2)]
        qb_r = [sc.res("qb%d" % i) for i in range(2)]
        kb_r = [sc.res("kb%d" % i) for i in range(2)]
        t1 = cv.get([128, 8, 16]); t2 = cv.get([128, 8, 16])
        t1_r = sc.res("t1"); t2_r = sc.res("t2")
        gsb = cv.get([128, 16, 16]); m8 = cv.get([128, 16, 8]); sel = cv.get([128, 16, 16])
        gsb_r = sc.res("gsb"); sel_r = sc.res("sel")
        NPT = 4
        PT = [cv.get([128, 2, 256], BF16) for _ in range(NPT)]
        PT_r = [sc.res("PT%d" % i) for i in range(NPT)]
        acc = cv.get([128, 2, 8, 65])
        acc_r = [[sc.res("acc%d_%d" % (q, h)) for h in range(8)] for q in range(2)]
        rec = cv.get([128, 16])
        ob = cv.get([128, 2, 512], BF16)
        ob_r = sc.res("ob")
        obT = [cv.get([128, 4, 256], BF16) for _ in range(2)]
        obT_r = [sc.res("obT%d" % i) for i in range(2)]

        for kc in range(8):
            sc.add("pool", lambda e, kc=kc: e.dma_start(out=wA[:, kc, :], in_=w_in_d[l, kc * 128:(kc + 1) * 128, 2056:3592]),
                   writes=[wA_r[kc]], dma=True, key="wA%d" % kc)
        sc.add("pool", lambda e: e.memset(Vp[:, :, :, 64:65], 1.0), writes=Vp_r)
        sc.add("pool", lambda e: e.memset(gsb[:], -1e30), writes=[gsb_r])
        sc.add("pool", lambda e: e.memset(kmT[:], 0.0), writes=[kmT_r])

        pq, pk, pv = bank[0], bank[1], bank[2]
        ptr = bank[3][:].bitcast(BF16).rearrange("p (k c) -> p k c", k=8)
        pg0 = bank[4][:, 0:128].rearrange("p (a b) -> p a b", a=8)
        pg1 = bank[6][:, 0:128].rearrange("p (a b) -> p a b", a=8)
        SB = (0, 1, 2, 5)
        OB = (6, 7, 4, 3)
        cnt = {"s": 0, "o": 0, "pt": 0, "ev": 0}


        KSTOP = int(os.environ.get("KSTOP", "99"))
        for t in range(NT):
            b = t // 2
            qt_ = t % 2
            tsl = slice(t * 128, (t + 1) * 128)
            if KSTOP <= 0:
                break
            for g, pp in enumerate((pq, pk, pv)):
                for kc in range(8):
                    sc.add("pe", lambda e, g=g, kc=kc, pp=pp, tsl=tsl: e.matmul(pp[:], lhsT=xT[:, kc, tsl], rhs=wA[:, kc, g * 512:(g + 1) * 512], start=(kc == 0), stop=(kc == 7)),
                           reads=[xT_r[t], wA_r[kc]], writes=[bank_r[g]])
            if KSTOP <= 1:
                continue
            sc.add("act", lambda e, t=t: e.copy(out=Vp[:, t, :, 0:64], in_=pv[:].rearrange("p (h d) -> p h d", h=8)),
                   reads=[bank_r[2]], writes=[Vp_r[t]])
            s2 = t % 2
            for (pp, dst, dst_r, bi) in ((pq, qb[s2], qb_r[s2], 0), (pk, kb[s2], kb_r[s2], 1)):
                p3 = pp[:].rearrange("p (h d) -> p h d", h=8)
                d3 = dst.rearrange("p (h d) -> p h d", h=8)
                sc.add("act", lambda e, p3=p3, d3=d3: e.copy(out=d3[:, :, 16:64], in_=p3[:, :, 16:64]),
                       reads=[bank_r[bi]], writes=[dst_r])
                ccb = rope[:, t, 0:16].unsqueeze(1).to_broadcast([128, 8, 16])
                ssb = rope[:, t, 16:32].unsqueeze(1).to_broadcast([128, 8, 16])
                sc.add("dve", lambda e, p3=p3, ccb=ccb: e.tensor_tensor(out=t1, in0=p3[:, :, 0:16], in1=ccb, op=ALU.mult),
                       reads=[bank_r[bi], r_const], writes=[t1_r])
                sc.add("dve", lambda e, p3=p3, ssb=ssb: e.tensor_tensor(out=t2, in0=p3[:, :, 0:16], in1=ssb, op=ALU.mult),
                       reads=[bank_r[bi], r_const], writes=[t2_r])
                sc.add("dve", lambda e, d3=d3: e.tensor_tensor(out=d3[:, :, 0:8], in0=t1[:, :, 0:8], in1=t2[:, :, 8:16], op=ALU.subtract),
                       reads=[t1_r, t2_r], writes=[dst_r], partial=True)
                sc.add("dve", lambda e, d3=d3: e.tensor_tensor(out=d3[:, :, 8:16], in0=t1[:, :, 8:16], in1=t2[:, :, 0:8], op=ALU.add),
                       reads=[t1_r, t2_r], writes=[dst_r], partial=True)
            if KSTOP <= 2:
                continue
            for j in range(4):
                sc.add("pe", lambda e, j=j, s2=s2: e.transpose(out=ptr[:, j, :], in_=qb[s2][:, j * 128:(j + 1) * 128], identity=ident_b[:]),
                       reads=[qb_r[s2], r_const], writes=[bank_r[3]])
            for j in range(4):
                sc.add("pe", lambda e, j=j, s2=s2: e.transpose(out=ptr[:, 4 + j, :], in_=kb[s2][:, j * 128:(j + 1) * 128], identity=ident_b[:]),
                       reads=[kb_r[s2], r_const], writes=[bank_r[3]])
            qs = b % 2
            if KSTOP == 3 and os.environ.get("KSUB") == "a":
                continue
            sc.add("dve", lambda e, qs=qs, qt_=qt_: e.tensor_copy(out=QT[qs][:, :, qt_ * 128:(qt_ + 1) * 128], in_=ptr[:, 0:4, :]),
                   reads=[bank_r[3]], writes=[QT_r[qs]], partial=(qt_ == 1))
            if KSTOP == 3 and os.environ.get("KSUB") == "b":
                continue
            sc.add("dve", lambda e, tsl=tsl: e.tensor_copy(out=KT[:, :, tsl], in_=ptr[:, 4:8, :]),
                   reads=[bank_r[3]], writes=[KT_r[t]])
            if qt_ == 0 or KSTOP <= 3:
                continue
            if b + 1 < NB:
                sc.add("dve", lambda e, b=b: e.tensor_reduce(out=kmf, in_=KT[:, :, b * 256:(b + 1) * 256], axis=AX.X, op=ALU.add),
                       reads=[KT_r[t - 1], KT_r[t]], writes=[kmT_r])
                sc.add("dve", lambda e, b=b: e.tensor_scalar(out=kmT[:, :, b], in0=kmf, scalar1=1.0 / 256, scalar2=None, op0=ALU.mult),
                       reads=[kmT_r], writes=[kmT_r], partial=True)
            topk = b > 3
            if topk:
                KV_ = os.environ.get("KVAR", "")
                for par in range(2):
                    pgp = (pg0, pg1)[par]
                    for q2 in range(2):
                        for hh in range(4):
                            if KV_ == "q0" and q2 == 1: continue
                            if KV_ == "p0" and par == 1: continue
                            if KV_ == "h0" and hh > 0: continue
                            base = par * 64
                            sc.add("pe", lambda e, pgp=pgp, q2=q2, hh=hh, base=base, qs=qs: e.matmul(pgp[:, q2 * 4 + hh, :], lhsT=QT[qs][base:base + 64, hh, q2 * 128:(q2 + 1) * 128], rhs=kmT[base:base + 64, hh, :], start=True, stop=True),
                                   reads=[QT_r[qs], kmT_r], writes=[bank_r[(4, 6)[par]]])
                KT_ = os.environ.get("KTOPK", "full")
                if KT_ in ("gc", "gcm", "full"):
                    sc.add("dve", lambda e, b=b: e.tensor_copy(out=gsb[:, 0:8, 0:b], in_=pg0[:, :, 0:b]), reads=[bank_r[4]], writes=[gsb_r])
                    sc.add("dve", lambda e, b=b: e.tensor_copy(out=gsb[:, 8:16, 0:b], in_=pg1[:, :, 0:b]), reads=[bank_r[6]], writes=[gsb_r], partial=True)
                if KT_ in ("gcm", "full"):
                    for i16 in range(16):
                        sc.add("dve", lambda e, i16=i16: e.max(out=m8[:, i16, :], in_=gsb[:, i16, :]), reads=[gsb_r], writes=[sel_r], partial=True)
                if KT_ == "full":
                    sc.add("dve", lambda e: e.tensor_tensor(out=sel[:], in0=gsb[:], in1=m8[:, :, 2:3].to_broadcast([128, 16, 16]), op=ALU.is_ge),
                           reads=[gsb_r, sel_r], writes=[sel_r])
                else:
                    sc.add("dve", lambda e: e.memset(sel[:], 1.0), reads=[gsb_r, bank_r[4]], writes=[sel_r])
            units = [(h, n) for h in range(8 if KSTOP > 4 else 0) for n in ([b] + list(range(b)))]
            ust = {}

            def emit_st(u, b=b, qs=qs):
                h, n = u
                j = h // 2; base = (h % 2) * 64
                si = SB[cnt["s"] % 4]; cnt["s"] += 1
                pi = cnt["pt"] % NPT; cnt["pt"] += 1
                pss = bank[si][:].rearrange("p (k q) -> p k q", k=2)
                for kt in range(2):
                    ktile = 2 * n + kt
                    sc.add("pe", lambda e, pss=pss, kt=kt, j=j, base=base, ktile=ktile, qs=qs: e.matmul(pss[:, kt, :], lhsT=KT[base:base + 64, j, ktile * 128:(ktile + 1) * 128], rhs=QT[qs][base:base + 64, j, :], start=True, stop=True),
                           reads=[KT_r[ktile], QT_r[qs]], writes=[bank_r[si]])
                sc.add("act", lambda e, pss=pss, pi=pi: e.activation(out=PT[pi][:], in_=pss, func=AF.Exp, scale=0.125, bias=cbias[:, 3:4]),
                       reads=[bank_r[si], r_const], writes=[PT_r[pi]])
                if n == b:
                    for kt in range(2):
                        sc.add("pool", lambda e, pi=pi, kt=kt: e.tensor_tensor(out=PT[pi][:, kt, kt * 128:(kt + 1) * 128], in0=PT[pi][:, kt, kt * 128:(kt + 1) * 128], in1=caus_b[:], op=ALU.mult),
                               reads=[PT_r[pi], r_const], writes=[PT_r[pi]])
                ust[u] = pi

            def emit_pv(u, b=b, topk=topk):
                h, n = u
                pi = ust.pop(u)
                oi = OB[cnt["o"] % len(OB)]; cnt["o"] += 1
                pso = bank[oi][:, 0:130].rearrange("p (q d) -> p q d", q=2)
                for q2 in range(2):
                    kts = [0] if (n == b and q2 == 0) else [0, 1]
                    for ii, kt in enumerate(kts):
                        sc.add("pe", lambda e, pso=pso, pi=pi, q2=q2, kt=kt, n=n, h=h, ii=ii, last=(ii == len(kts) - 1): e.matmul(pso[:, q2, :], lhsT=PT[pi][:, kt, q2 * 128:(q2 + 1) * 128], rhs=Vp[:, 2 * n + kt, h, :], start=(ii == 0), stop=last),
                               reads=[PT_r[pi], Vp_r[2 * n + kt]], writes=[bank_r[oi]])
                for q2 in range(2):
                    if n == b:
                        sc.add("dve", lambda e, pso=pso, q2=q2, h=h: e.tensor_copy(out=acc[:, q2, h, :], in_=pso[:, q2, :]),
                               reads=[bank_r[oi]], writes=[acc_r[q2][h]])
                    elif topk:
                        sc.add("dve", lambda e, pso=pso, q2=q2, h=h, n=n: e.scalar_tensor_tensor(out=acc[:, q2, h, :], in0=pso[:, q2, :], scalar=sel[:, (h % 2) * 8 + q2 * 4 + h // 2, n:n + 1], in1=acc[:, q2, h, :], op0=ALU.mult, op1=ALU.add),
                               reads=[bank_r[oi], sel_r, acc_r[q2][h]], writes=[acc_r[q2][h]])
                    else:
                        sc.add("dve", lambda e, pso=pso, q2=q2, h=h: e.tensor_tensor(out=acc[:, q2, h, :], in0=pso[:, q2, :], in1=acc[:, q2, h, :], op=ALU.add),
                               reads=[bank_r[oi], acc_r[q2][h]], writes=[acc_r[q2][h]])

            LOOK = 3
            for i in range(min(LOOK, len(units))):
                emit_st(units[i])
            for i, u in enumerate(units):
                if i + LOOK < len(units):
                    emit_st(units[i + LOOK])
                emit_pv(u)
            if KSTOP <= 5:
                continue
            allacc = [acc_r[q][h] for q in range(2) for h in range(8)]
            sc.add("dve", lambda e: e.reciprocal(out=rec, in_=acc[:].rearrange("p q h d -> p (q h) d")[:, :, 64]), reads=allacc, writes=[ob_r])
            sc.add("dve", lambda e: e.tensor_tensor(out=ob[:].rearrange("p q (h d) -> p (q h) d", h=8), in0=acc[:].rearrange("p q h d -> p (q h) d")[:, :, 0:64], in1=rec.unsqueeze(2).to_broadcast([128, 16, 64]), op=ALU.mult),
                   reads=allacc + [ob_r], writes=[ob_r])
            os_ = b % 2
            for q2 in range(2):
                for j in range(4):
                    sc.add("pe", lambda e, q2=q2, j=j: e.transpose(out=ptr[:, q2 * 4 + j, :], in_=ob[:, q2, j * 128:(j + 1) * 128], identity=ident_b[:]),
                           reads=[ob_r, r_const], writes=[bank_r[3]])
            sc.add("dve", lambda e, os_=os_: e.tensor_copy(out=obT[os_][:].rearrange("p j (q c) -> p q j c", q=2), in_=ptr.rearrange("p (q j) c -> p q j c", q=2)),
                   reads=[bank_r[3]], writes=[obT_r[os_]])
            sc.add("sp", lambda e, os_=os_, b=b: e.dma_start(out=oT_d[4:8, :, b * 256:(b + 1) * 256].rearrange("j p c -> p j c"), in_=obT[os_][:]),
                   reads=[obT_r[os_]], writes=[], dma=True, key="obT%d" % os_)

    def phaseB(l):
        cv.reset()
        for _ in range(int(os.environ.get("ACTPAD", "0"))):
            sc.add("act", lambda e: e.copy(out=arena[:, 0:8], in_=ident_f[:, 0:8]))
        NBLK = S // 512
        wB = cv.get([128, 8, 2056], BF16)
        wB_r = [sc.res("wB%d" % k) for k in range(8)]
        gcw = cv.get([128, 12, 4]); dtb = cv.get([128, 4]); nexpA = cv.get([128, 4]); gng = cv.get([128, 128])
        mstrict = cv.get([128, 128]); mincl = cv.get([128, 128]); glc = cv.get([128, 4, 128])
        boff = cv.get([128, 6, 128]); aoff1 = cv.get([128, 128]); ones_f = cv.get([128, 128])
        pc_r = sc.res("pconst")
        rawb = cv.get([128, 2, 515]); rawb_r = [sc.res("rawb%d" % f) for f in range(2)]
        halo = cv.get([128, 12, 3]); halo_r = [sc.res("halo%d" % f) for f in range(12)]
        cacc = [cv.get([128, 512]) for _ in range(2)]; cacc_r = [sc.res("cacc%d" % i) for i in range(2)]
        cT = cv.get([128, 12, 512]); cT_r = [sc.res("cT%d" % f) for f in range(12)]
        Sst = [cv.get([128, 4, 128]) for _ in range(2)]
        S_r = [[sc.res("S%d_%d" % (i, h)) for h in range(4)] for i in range(2)]

        def tmp(name, shape, dt=F32, n=2):
            return [(cv.get(shape, dt), sc.res("%s%d" % (name, i))) for i in range(n)]

        Qtm = tmp("Qtm", [128, 4, 128]); Ktm = tmp("Ktm", [128, 4, 128]); Vtm = tmp("Vtm", [128, 4, 128])
        ssq = tmp("ssq", [128, 8]); rn = tmp("rn", [128, 8])
        sm = tmp("sm", [128, 64])
        zs = tmp("zs", [128, 512])
        junk = tmp("junk", [128, 128], n=1)[0]
        HT = 2
        QTh = tmp("QTh", [128, 128], F32, HT); KTh = tmp("KTh", [128, 128], F32, HT)
        GR = tmp("GR", [128, 128], F32, HT); Dm = tmp("Dm", [128, 128], F32, HT); Ds = tmp("Ds", [128, 128], F32, HT)
        Am = tmp("Am", [128, 128], F32, 2 * HT); attn = tmp("attn", [128, 128], F32, 2 * HT); attnT = tmp("attnT", [128, 128], F32, HT)
        Boall = tmp("Boall", [128, 6, 128], F32, HT); Em = tmp("Em", [128, 128], F32, 2 * HT); Dk = tmp("Dk", [128, 128], F32, 2 * HT)
        Xm = tmp("Xm", [128, 128], F32, HT); Rm = tmp("Rm", [128, 256], F32, HT); UW = tmp("UW", [128, 256], F32, HT)
        Kd = tmp("Kd", [128, 128], F32, HT); Qd = tmp("Qd", [128, 128], F32, HT); QpT = tmp("QpT", [128, 128], F32, HT)
        MpT = tmp("MpT", [128, 2, 128], F32, HT)
        osb = tmp("osb", [128, 4, 128], F32, 2); oss = tmp("oss", [128, 8], F32, 2)
        oab = tmp("oab", [128, 512], BF16, 2); oaT = tmp("oaT", [128, 4, 128], BF16, 2)
        ctr = {}

        def nxt(lst, key):
            i = ctr.get(key, 0); ctr[key] = i + 1
            return lst[i % len(lst)]

        PB = tuple(int(x) for x in os.environ.get("PB", "2,3,4,6,7").split(","))

        def pbank():
            i = ctr.get("pb", 0); ctr["pb"] = i + 1
            bi = PB[i % len(PB)]
            return bank[bi], bank_r[bi]

        for kc in range(8):
            sc.add("pool", lambda e, kc=kc: e.dma_start(out=wB[:, kc, :], in_=w_in_d[l, kc * 128:(kc + 1) * 128, 0:2056]),
                   writes=[wB_r[kc]], dma=True, key="wB%d" % kc)
        w4 = cT.rearrange("p f t -> p (f t)")[0:4, 0:1536]; w4_r = sc.res("w4")
        sc.add("sp", lambda e: e.dma_start(out=w4, in_=gconv_d[l, :, :]), writes=[w4_r, cT_r[0], cT_r[1], cT_r[2]], dma=True, key="w4")
        for f in range(12):
            sc.pe32(lambda e, f=f: e.transpose(out=bank[0][:, f * 4:(f + 1) * 4], in_=w4[0:4, f * 128:(f + 1) * 128], identity=ident_f[0:4, 0:4]), reads=[w4_r, cT_r[0], cT_r[1], cT_r[2], r_const], writes=[bank_r[0]])
        sc.add("dve", lambda e: e.tensor_copy(out=gcw.rearrange("p f j -> p (f j)"), in_=bank[0][:, 0:48]), reads=[bank_r[0]], writes=[pc_r], partial=True)
        sc.add("sp", lambda e: e.dma_start(out=dtb, in_=dtb_d[l, :].partition_broadcast(128)), writes=[pc_r], dma=True, key="pc", partial=True)
        sc.add("sp", lambda e: e.dma_start(out=nexpA, in_=alog_d[l, :].partition_broadcast(128)), writes=[pc_r], dma=True, key="pc", partial=True)
        sc.add("sp", lambda e: e.dma_start(out=gng, in_=gng_d[l, :].partition_broadcast(128)), writes=[pc_r], dma=True, key="pc", partial=True)
        sc.add("sp", lambda e: e.dma_start(out=mstrict, in_=c_mstrict_d[:, :]), writes=[pc_r], dma=True, key="pc", partial=True)
        sc.add("sp", lambda e: e.dma_start(out=glc.rearrange("p a b -> p (a b)"), in_=c_gl_d[:, :]), writes=[pc_r], dma=True, key="pc", partial=True)
        sc.add("sp", lambda e: e.dma_start(out=boff.rearrange("p a b -> p (a b)"), in_=c_boff_d[:, :]), writes=[pc_r], dma=True, key="pc", partial=True)
        sc.add("sp", lambda e: e.dma_start(out=aoff1, in_=c_aoff1_d[:, :]), writes=[pc_r], dma=True, key="pc", partial=True)
        sc.add("sp", lambda e: e.dma_start(out=mincl, in_=c_negincl_d[:, :]), writes=[pc_r], dma=True, key="pc", partial=True)
        sc.add("act", lambda e: e.activation(out=nexpA, in_=nexpA, func=AF.Exp, bias=cbias[:, 3:4]), reads=[pc_r], writes=[pc_r])
        sc.add("dve", lambda e: e.tensor_scalar(out=nexpA, in0=nexpA, scalar1=-1.0, scalar2=None, op0=ALU.mult), reads=[pc_r], writes=[pc_r])
        sc.add("dve", lambda e: e.memset(ones_f, 1.0), writes=[pc_r], reads=[pc_r])
        sc.add("dve", lambda e: e.memset(Sst[0][:], 0.0), writes=S_r[0])
        sc.add("pool", lambda e: e.memset(halo, 0.0), writes=halo_r)
        LTc, UTc, CS0c, CS1c = (glc[:, i, :] for i in range(4))

        cur = 0

        KB = int(os.environ.get("KB", "99"))
        for c in range(NBLK):
            csl = slice(c * 512, (c + 1) * 512)
            for f in range(12):
                pb, pb_r = bank[f % 2], bank_r[f % 2]
                for kc in range(8):
                    sc.pe16(pb[:], lambda e, pb=pb, f=f, kc=kc, csl=csl: e.matmul(pb[:], lhsT=wB[:, kc, f * 128:(f + 1) * 128], rhs=xT[:, kc, csl], start=(kc == 0), stop=(kc == 7)),
                           reads=[wB_r[kc]] + xT_r[4 * c:4 * c + 4], writes=[pb_r])
                rs = f % 2
                sc.add("pool", lambda e, f=f, rs=rs: e.tensor_copy(out=rawb[:, rs, 0:3], in_=halo[:, f, :]), reads=[halo_r[f]], writes=[rawb_r[rs]])
                sc.add("act", lambda e, pb=pb, rs=rs: e.copy(out=rawb[:, rs, 3:515], in_=pb[:]), reads=[pb_r], writes=[rawb_r[rs]], partial=True)
                sc.add("pool", lambda e, f=f, rs=rs: e.tensor_copy(out=halo[:, f, :], in_=rawb[:, rs, 512:515]), reads=[rawb_r[rs]], writes=[halo_r[f]])
                ca, ca_r = cacc[f % 2], cacc_r[f % 2]
                sc.add("act", lambda e, pb=pb, f=f, ca=ca: e.activation(out=ca, in_=pb[:], func=AF.Copy, scale=gcw[:, f, 3:4]), reads=[pb_r, pc_r], writes=[ca_r])
                for j in (2, 1, 0):
                    sc.add("dve", lambda e, f=f, j=j, ca=ca, rs=rs: e.scalar_tensor_tensor(out=ca, in0=rawb[:, rs, j:j + 512], scalar=gcw[:, f, j:j + 1], in1=ca, op0=ALU.mult, op1=ALU.add),
                           reads=[rawb_r[rs], pc_r, ca_r], writes=[ca_r])
                sc.add("act", lambda e, f=f, ca=ca: e.activation(out=cT[:, f, :], in_=ca, func=AF.Silu, bias=cbias[:, 3:4]), reads=[ca_r], writes=[cT_r[f]])
            for tt in range(4 if KB > 1 else 0):
                t = c * 4 + tt
                tsl = slice(t * 128, (t + 1) * 128)
                lsl = slice(tt * 128, (tt + 1) * 128)
                (Qt, Qt_r) = nxt(Qtm, "Qtm"); (Kt, Kt_r) = nxt(Ktm, "Ktm"); (Vt, Vt_r) = nxt(Vtm, "Vtm")
                (sq, sq_r) = nxt(ssq, "ssq"); (rnn, rn_r) = nxt(rn, "rn"); (smt, sm_r) = nxt(sm, "sm"); (zst, zs_r) = nxt(zs, "zs")
                for g in range(3):
                    pb, pb_r = bank[2 + g], bank_r[2 + g]
                    for h in range(4):
                        sc.pe32(lambda e, pb=pb, g=g, h=h, lsl=lsl: e.transpose(out=pb[:, h * 128:(h + 1) * 128], in_=cT[:, g * 4 + h, lsl], identity=ident_f[:]),
                               reads=[cT_r[g * 4 + h], r_const], writes=[pb_r])
                for g in range(2):
                    for h in range(4):
                        sc.add("act", lambda e, g=g, h=h, sq=sq: e.activation(out=junk[0], in_=bank[2 + g][:, h * 128:(h + 1) * 128], func=AF.Square, bias=cbias[:, 3:4], accum_out=sq[:, g * 4 + h:g * 4 + h + 1]),
                               reads=[bank_r[2 + g]], writes=[sq_r, junk[1]], partial=True)
                sc.add("act", lambda e, sq=sq, rnn=rnn: e.activation(out=rnn, in_=sq, func=AF.Sqrt, bias=cbias[:, 0:1]), reads=[sq_r, r_const], writes=[rn_r])
                sc.add("dve", lambda e, rnn=rnn: e.reciprocal(out=rnn, in_=rnn), reads=[rn_r], writes=[rn_r])
                sc.add("dve", lambda e, rnn=rnn: e.tensor_scalar(out=rnn[:, 0:4], in0=rnn[:, 0:4], scalar1=float(128 ** -0.5), scalar2=None, op0=ALU.mult), reads=[rn_r], writes=[rn_r])
                sc.add("dve", lambda e, Qt=Qt, rnn=rnn: e.tensor_tensor(out=Qt, in0=bank[2][:].rearrange("p (h d) -> p h d", h=4), in1=rnn[:, 0:4].unsqueeze(2).to_broadcast([128, 4, 128]), op=ALU.mult),
                       reads=[bank_r[2], rn_r], writes=[Qt_r])
                sc.add("dve", lambda e, Kt=Kt, rnn=rnn: e.tensor_tensor(out=Kt, in0=bank[3][:].rearrange("p (h d) -> p h d", h=4), in1=rnn[:, 4:8].unsqueeze(2).to_broadcast([128, 4, 128]), op=ALU.mult),
                       reads=[bank_r[3], rn_r], writes=[Kt_r])
                sc.add("act", lambda e, Vt=Vt: e.copy(out=Vt, in_=bank[4][:].rearrange("p (h d) -> p h d", h=4)), reads=[bank_r[4]], writes=[Vt_r])
                if KB <= 2:
                    continue
                pab, pab_r = bank[5], bank_r[5]
                for kc in range(8):
                    sc.pe16(bank[5][:, 0:8], lambda e, kc=kc, tsl=tsl: e.matmul(bank[5][:, 0:8], lhsT=xT[:, kc, tsl], rhs=wB[:, kc, 1536:1544], start=(kc == 0), stop=(kc == 7)),
                           reads=[xT_r[t], wB_r[kc]], writes=[pab_r])
                sc.add("dve", lambda e, smt=smt: e.tensor_tensor(out=smt[:, 0:4], in0=bank[5][:, 0:4], in1=dtb, op=ALU.add), reads=[pab_r, pc_r], writes=[sm_r])
                sc.add("dve", lambda e, smt=smt: e.tensor_scalar(out=smt[:, 36:40], in0=smt[:, 0:4], scalar1=-1.0, scalar2=None, op0=ALU.mult), reads=[sm_r], writes=[sm_r])
                sc.add("dve", lambda e, smt=smt: e.tensor_tensor(out=smt[:, 4:8], in0=smt[:, 0:4], in1=smt[:, 36:40], op=ALU.min), reads=[sm_r], writes=[sm_r])
                sc.add("act", lambda e, smt=smt: e.activation(out=smt[:, 4:8], in_=smt[:, 4:8], func=AF.Exp, bias=cbias[:, 3:4]), reads=[sm_r], writes=[sm_r])
                sc.add("act", lambda e, smt=smt: e.activation(out=smt[:, 4:8], in_=smt[:, 4:8], func=AF.Ln, bias=cbias[:, 1:2]), reads=[sm_r, r_const], writes=[sm_r])
                sc.add("dve", lambda e, smt=smt: e.scalar_tensor_tensor(out=smt[:, 8:12], in0=smt[:, 0:4], scalar=0.0, in1=smt[:, 4:8], op0=ALU.max, op1=ALU.add), reads=[sm_r], writes=[sm_r])
                sc.add("dve", lambda e, smt=smt: e.tensor_tensor(out=smt[:, 8:12], in0=smt[:, 8:12], in1=nexpA, op=ALU.mult), reads=[sm_r, pc_r], writes=[sm_r])
                sc.add("act", lambda e, smt=smt: e.activation(out=smt[:, 12:16], in_=bank[5][:, 4:8], func=AF.Exp, scale=-1.0, bias=cbias[:, 3:4]), reads=[pab_r], writes=[sm_r])
                sc.add("dve", lambda e, smt=smt: e.tensor_scalar(out=smt[:, 12:16], in0=smt[:, 12:16], scalar1=1.0, scalar2=None, op0=ALU.add), reads=[sm_r], writes=[sm_r])
                sc.add("dve", lambda e, smt=smt: e.reciprocal(out=smt[:, 12:16], in_=smt[:, 12:16]), reads=[sm_r], writes=[sm_r])
                for kc in range(8):
                    sc.pe16(bank[5][:], lambda e, kc=kc, tsl=tsl: e.matmul(bank[5][:], lhsT=xT[:, kc, tsl], rhs=wB[:, kc, 1544:2056], start=(kc == 0), stop=(kc == 7)),
                           reads=[xT_r[t], wB_r[kc]], writes=[pab_r])
                sc.add("act", lambda e, zst=zst: e.activation(out=zst, in_=bank[5][:], func=AF.Silu, bias=cbias[:, 3:4]), reads=[pab_r], writes=[zs_r])
                sc.add("pool", lambda e, zst=zst: e.tensor_tensor(out=zst.rearrange("p (h d) -> p h d", h=4), in0=zst.rearrange("p (h d) -> p h d", h=4), in1=gng.unsqueeze(1).to_broadcast([128, 4, 128]), op=ALU.mult),
                       reads=[zs_r, pc_r], writes=[zs_r])
                for i4, lt in enumerate((LTc, UTc, CS0c, CS1c)):
                    sc.pe32(lambda e, i4=i4, lt=lt, smt=smt: e.matmul(bank[5][:, 16 + 4 * i4:20 + 4 * i4], lhsT=lt, rhs=smt[:, 8:12], start=True, stop=True),
                           reads=[sm_r, pc_r], writes=[pab_r])
                sc.add("dve", lambda e, smt=smt: e.tensor_copy(out=smt[:, 40:44], in_=bank[5][:, 16:20]), reads=[pab_r], writes=[sm_r])
                sc.add("dve", lambda e, smt=smt: e.tensor_scalar(out=smt[:, 16:32], in0=bank[5][:, 16:32], scalar1=-60.0, scalar2=None, op0=ALU.max), reads=[pab_r], writes=[sm_r])
                sc.add("act", lambda e, smt=smt: e.activation(out=smt[:, 16:32], in_=smt[:, 16:32], func=AF.Exp, bias=cbias[:, 3:4]), reads=[sm_r], writes=[sm_r])
                sc.add("dve", lambda e, smt=smt: e.tensor_tensor(out=smt[:, 32:36], in0=smt[:, 12:16], in1=smt[:, 16:20], op=ALU.mult), reads=[sm_r], writes=[sm_r])
                (osb_t, osb_r) = nxt(osb, "osb"); (oss_t, oss_r) = nxt(oss, "oss")
                if KB <= 3:
                    continue
                sc.safe = os.environ.get("SAFE", "1") == "1"
                for h in range(4):
                    (QT_, QT_r_) = nxt(QTh, "QTh"); (KT_, KT_r_) = nxt(KTh, "KTh"); (GR_, GR_r_) = nxt(GR, "GR")
                    (D_, D_r) = nxt(Dm, "Dm"); (Ds_, Ds_r) = nxt(Ds, "Ds"); (A_, A_r) = nxt(Am, "Am")
                    (at_, at_r) = nxt(attn, "attn"); (atT_, atT_r) = nxt(attnT, "attnT"); (Bo_, Bo_r) = nxt(Boall, "Bo")
                    (X_, X_r) = nxt(Xm, "X"); (R_, R_r) = nxt(Rm, "R"); (UW_, UW_r) = nxt(UW, "UW")
                    (Kd_, Kd_r) = nxt(Kd, "Kd"); (Qd_, Qd_r) = nxt(Qd, "Qd"); (QpT_, QpT_r) = nxt(QpT, "QpT"); (Mp_, Mp_r) = nxt(MpT, "MpT")
                    beta_h = smt[:, 12 + h:13 + h]; egc_h = smt[:, 16 + h:17 + h]; egu_h = smt[:, 20 + h:21 + h]
                    gcum_h = smt[:, 40 + h:41 + h]; bk_h = smt[:, 32 + h:33 + h]
                    pb, pb_r = pbank()
                    sc.pe32(lambda e, pb=pb, Qt=Qt, h=h: e.transpose(out=pb[:, 0:128], in_=Qt[:, h, :], identity=ident_f[:]), reads=[Qt_r, r_const], writes=[pb_r])
                    sc.pe32(lambda e, pb=pb, Kt=Kt, h=h: e.transpose(out=pb[:, 128:256], in_=Kt[:, h, :], identity=ident_f[:]), reads=[Kt_r, r_const], writes=[pb_r])
                    sc.add("dve", lambda e, pb=pb, QT_=QT_: e.tensor_copy(out=QT_, in_=pb[:, 0:128]), reads=[pb_r], writes=[QT_r_])
                    sc.add("dve", lambda e, pb=pb, KT_=KT_: e.tensor_copy(out=KT_, in_=pb[:, 128:256]), reads=[pb_r], writes=[KT_r_])
                    sc.add("dve", lambda e, GR_=GR_, smt=smt, h=h: e.tensor_scalar(out=GR_, in0=ones_f, scalar1=smt[:, 8 + h:9 + h], scalar2=None, op0=ALU.mult), reads=[sm_r, pc_r], writes=[GR_r_])
                    K4 = os.environ.get("K4", "z")
                    if KB == 4 and K4 <= "a":
                        continue
                    pg_, pg_r = pbank()
                    sc.pe32(lambda e, pg_=pg_, GR_=GR_: e.matmul(pg_[:, 0:128], lhsT=GR_, rhs=LTc, start=True, stop=True), reads=[GR_r_, pc_r], writes=[pg_r])
                    sc.add("dve", lambda e, pg_=pg_, D_=D_, gcum_h=gcum_h: e.tensor_scalar(out=D_, in0=pg_[:, 0:128], scalar1=gcum_h, scalar2=0.0, op0=ALU.subtract, op1=ALU.max), reads=[pg_r, sm_r], writes=[D_r])
                    sc.add("dve", lambda e, D_=D_: e.tensor_scalar(out=D_, in0=D_, scalar1=60.0, scalar2=None, op0=ALU.min), reads=[D_r], writes=[D_r])
                    if os.environ.get("K5") == "waitD0":
                        sc.pe32(lambda e, pg_=pg_: e.transpose(out=pg_[:, 256:384], in_=ident_f[:], identity=ident_f[:]), reads=[r_const, D_r], writes=[])
                    sc.add("act", lambda e, D_=D_: e.activation(out=D_, in_=D_, func=AF.Exp, scale=-1.0, bias=cbias[:, 3:4]), reads=[D_r], writes=[D_r])
                    if os.environ.get("K5") == "waitD1":
                        sc.pe32(lambda e, pg_=pg_: e.transpose(out=pg_[:, 256:384], in_=ident_f[:], identity=ident_f[:]), reads=[r_const, D_r], writes=[])
                    PD = os.environ.get("PD", "dve")
                    sc.add(PD, lambda e, D_=D_, Ds_=Ds_: e.tensor_tensor(out=Ds_, in0=D_, in1=mstrict, op=ALU.mult), reads=[D_r, pc_r], writes=[Ds_r])
                    sc.add(PD, lambda e, D_=D_: e.tensor_tensor(out=D_, in0=D_, in1=mincl, op=ALU.mult), reads=[D_r, pc_r], writes=[D_r])
                    if os.environ.get("K5") == "waitD2":
                        sc.pe32(lambda e, pg_=pg_: e.transpose(out=pg_[:, 256:384], in_=ident_f[:], identity=ident_f[:]), reads=[r_const, Ds_r], writes=[])
                    if KB == 4 and K4 <= "b":
                        continue
                    pk_, pk_r = pbank()
                    K8 = os.environ.get("K8", "")
                    if K8 != "noKK" and K8 != "none":
                        sc.pe32(lambda e, pk_=pk_, KT_=KT_: e.matmul(pk_[:, 0:128], lhsT=KT_, rhs=KT_, start=True, stop=True), reads=[KT_r_], writes=[pk_r])
                    if K8 != "noQK" and K8 != "none":
                        sc.pe32(lambda e, pk_=pk_, QT_=QT_, KT_=KT_: e.matmul(pk_[:, 128:256], lhsT=QT_, rhs=KT_, start=True, stop=True), reads=[QT_r_, KT_r_], writes=[pk_r])
                    sc.add("dve", lambda e, pk_=pk_, A_=A_, beta_h=beta_h: e.tensor_scalar(out=A_, in0=pk_[:, 0:128], scalar1=beta_h, scalar2=None, op0=ALU.mult), reads=[pk_r, sm_r], writes=[A_r])
                    sc.add("dve", lambda e, pk_=pk_, at_=at_: e.tensor_copy(out=at_, in_=pk_[:, 128:256]), reads=[pk_r], writes=[at_r])
                    (A0_, A0_r) = nxt(Am, "Am"); (at0_, at0_r) = nxt(attn, "attn")
                    sc.add("dve", lambda e, A_=A_, A0_=A0_, Ds_=Ds_: e.tensor_tensor(out=A0_, in0=A_, in1=Ds_, op=ALU.mult), reads=[A_r, Ds_r], writes=[A0_r])
                    sc.add("dve", lambda e, at_=at_, at0_=at0_, D_=D_: e.tensor_tensor(out=at0_, in0=at_, in1=D_, op=ALU.mult), reads=[at_r, D_r], writes=[at0_r])
                    A_, A_r, at_, at_r = A0_, A0_r, at0_, at0_r
                    if KB == 4 and K4 <= "c":
                        continue
                    if os.environ.get("HB", "0") == "1":
                        sc.barrier()
                    if os.environ.get("K6") == "samebank":
                        pt_, pt_r = pk_[:, 256:512], pk_r
                    else:
                        pt_, pt_r = pbank()
                    K5 = os.environ.get("K5", "")
                    if K5 == "waitonly":
                        sc.pe32(lambda e, pt_=pt_: e.transpose(out=pt_[:, 0:128], in_=ident_f[:], identity=ident_f[:]), reads=[r_const, A_r], writes=[pt_r])
                        continue
                    if K5 == "spin":
                        for _ in range(int(os.environ.get("NSPIN", "300"))):
                            sc.pe32(lambda e, pt_=pt_: e.transpose(out=pt_[:, 256:384], in_=ident_f[:], identity=ident_f[:]), reads=[r_const], writes=[pt_r])
                        sc.pe32(lambda e, pt_=pt_: e.transpose(out=pt_[:, 0:128], in_=ident_f[:], identity=ident_f[:]), reads=[r_const, A_r], writes=[pt_r])
                        continue
                    if K5 == "dummy":
                        sc.pe32(lambda e, pt_=pt_: e.transpose(out=pt_[:, 256:384], in_=ident_f[:], identity=ident_f[:]), reads=[r_const], writes=[pt_r])
                        sc.pe32(lambda e, pt_=pt_: e.transpose(out=pt_[:, 0:128], in_=ident_f[:], identity=ident_f[:]), reads=[r_const, A_r], writes=[pt_r])
                        continue
                    if K5 == "viaact2" and ((t * 4 + h) >= int(os.environ.get("KN", "999")) or (t * 4 + h) < int(os.environ.get("KN0", "0"))):
                        continue
                    if K5 == "viaact2":
                        sc.add("dve", lambda e, X_=X_, A_=A_: e.tensor_copy(out=X_, in_=A_), reads=[A_r], writes=[X_r])
                        continue
                    if K5 == "viaact":
                        sc.add("dve", lambda e, X_=X_, A_=A_: e.tensor_copy(out=X_, in_=A_), reads=[A_r], writes=[X_r])
                        sc.pe32(lambda e, pt_=pt_, X_=X_: e.transpose(out=pt_[:, 0:128], in_=X_, identity=ident_f[:]), reads=[r_const, X_r], writes=[pt_r])
                        continue
                    if K5 == "waitbf":
                        ptb_ = pt_.bitcast(BF16)
                        sc.pe16(ptb_[:, 0:128], lambda e, ptb_=ptb_: e.transpose(out=ptb_[:, 0:128], in_=ident_b[:], identity=ident_b[:]), reads=[r_const, A_r], writes=[pt_r])
                        continue
                    if K5 == "waitat":
                        sc.pe32(lambda e, pt_=pt_: e.transpose(out=pt_[:, 0:128], in_=ident_f[:], identity=ident_f[:]), reads=[r_const, at_r], writes=[pt_r])
                        continue
                    if K5 == "waitD":
                        sc.pe32(lambda e, pt_=pt_: e.transpose(out=pt_[:, 0:128], in_=ident_f[:], identity=ident_f[:]), reads=[r_const, Ds_r], writes=[pt_r])
                        continue
                    if K5 == "useGR":
                        sc.pe32(lambda e, pt_=pt_, GR_=GR_: e.transpose(out=pt_[:, 0:128], in_=GR_, identity=ident_f[:]), reads=[GR_r_, r_const] + ([A_r] if os.environ.get("K7") != "nodep" else []), writes=[pt_r])
                        continue
                    if K5 == "useDs":
                        sc.pe32(lambda e, pt_=pt_, Ds_=Ds_: e.transpose(out=pt_[:, 0:128], in_=Ds_, identity=ident_f[:]), reads=[Ds_r, A_r, r_const], writes=[pt_r])
                        continue
                    if K5 != "nope" and K5 != "pe2":
                        sc.pe32(lambda e, pt_=pt_, A_=A_: e.transpose(out=pt_[:, 0:128], in_=A_, identity=ident_f[:]), reads=[A_r, r_const], writes=[pt_r])
                    if K5 != "nope" and K5 != "pe1":
                        sc.pe32(lambda e, pt_=pt_, at_=at_: e.transpose(out=pt_[:, 128:256], in_=at_, identity=ident_f[:]), reads=[at_r, r_const], writes=[pt_r])
                    if K5 == "nodve":
                        continue
                    sc.add("dve", lambda e, pt_=pt_, X_=X_: e.tensor_copy(out=X_, in_=pt_[:, 0:128]), reads=[pt_r], writes=[X_r])
                    if KB == 4 and K4 <= "d":
                        continue
                    sc.add("pool", lambda e, X_=X_, Bo_=Bo_: e.tensor_tensor(out=Bo_, in0=X_.unsqueeze(1).to_broadcast([128, 6, 128]), in1=boff, op=ALU.mult), reads=[X_r, pc_r], writes=[Bo_r])
                    if KB == 4 and K4 <= "e":
                        continue
                    sc.add("dve", lambda e, pt_=pt_, atT_=atT_: e.tensor_copy(out=atT_, in_=pt_[:, 128:256]), reads=[pt_r], writes=[atT_r])
                    if KB <= 4:
                        continue
                    (E_, E_r) = nxt(Em, "E"); (Dk_, Dk_r) = nxt(Dk, "Dk")
                    sc.add("pool", lambda e, E_=E_, Bo_=Bo_: e.tensor_tensor(out=E_, in0=ident_f[:], in1=Bo_[:, 0, :], op=ALU.subtract), reads=[Bo_r, r_const], writes=[E_r])
                    sc.add("pool", lambda e, Dk_=Dk_, A_=A_: e.tensor_tensor(out=Dk_, in0=A_, in1=aoff1, op=ALU.mult), reads=[A_r, pc_r], writes=[Dk_r])
                    sc.add("pool", lambda e, Dk_=Dk_: e.tensor_tensor(out=Dk_, in0=ident_f[:], in1=Dk_, op=ALU.subtract), reads=[Dk_r, r_const], writes=[Dk_r])
                    for lvl in range(1, 6):
                        px_, px_r = pbank()
                        sc.pe32(lambda e, px_=px_, Bo_=Bo_, lvl=lvl, Dk_=Dk_: e.matmul(px_[:, 0:128], lhsT=Bo_[:, lvl, :], rhs=Dk_, start=True, stop=True), reads=[Bo_r, Dk_r], writes=[px_r])
                        sc.add("dve", lambda e, px_=px_, X_=X_: e.tensor_copy(out=X_, in_=px_[:, 0:128]), reads=[px_r], writes=[X_r])
                        py_, py_r = pbank()
                        sc.pe32(lambda e, py_=py_, X_=X_, E_=E_: e.matmul(py_[:, 0:128], lhsT=X_, rhs=E_, start=True, stop=True), reads=[X_r, E_r], writes=[py_r])
                        (E2_, E2_r) = nxt(Em, "E")
                        sc.add("dve", lambda e, py_=py_, E_=E_, E2_=E2_: e.tensor_tensor(out=E2_, in0=E_, in1=py_[:, 0:128], op=ALU.subtract), reads=[py_r, E_r], writes=[E2_r])
                        E_, E_r = E2_, E2_r
                        if lvl < 5:
                            pd_, pd_r = pbank()
                            sc.pe32(lambda e, pd_=pd_, E_=E_: e.transpose(out=pd_[:, 0:128], in_=E_, identity=ident_f[:]), reads=[E_r, r_const], writes=[pd_r])
                            (Dk_, Dk_r) = nxt(Dk, "Dk")
                            sc.add("dve", lambda e, pd_=pd_, Dk_=Dk_: e.tensor_copy(out=Dk_, in_=pd_[:, 0:128]), reads=[pd_r], writes=[Dk_r])
                    if KB <= 5:
                        continue
                    if os.environ.get("HB", "0") == "1":
                        sc.barrier()
                    sc.add("pool", lambda e, R_=R_, Vt=Vt, h=h, beta_h=beta_h: e.tensor_scalar(out=R_[:, 0:128], in0=Vt[:, h, :], scalar1=beta_h, scalar2=None, op0=ALU.mult), reads=[Vt_r, sm_r], writes=[R_r])
                    sc.add("pool", lambda e, R_=R_, Kt=Kt, h=h, bk_h=bk_h: e.tensor_scalar(out=R_[:, 128:256], in0=Kt[:, h, :], scalar1=bk_h, scalar2=None, op0=ALU.mult), reads=[Kt_r, sm_r], writes=[R_r], partial=True)
                    sc.add("pool", lambda e, Kd_=Kd_, Kt=Kt, h=h, egu_h=egu_h: e.tensor_scalar(out=Kd_, in0=Kt[:, h, :], scalar1=egu_h, scalar2=None, op0=ALU.mult), reads=[Kt_r, sm_r], writes=[Kd_r])
                    sc.add("pool", lambda e, Qd_=Qd_, Qt=Qt, h=h, egc_h=egc_h: e.tensor_scalar(out=Qd_, in0=Qt[:, h, :], scalar1=egc_h, scalar2=None, op0=ALU.mult), reads=[Qt_r, sm_r], writes=[Qd_r])
                    pu_, pu_r = pbank()
                    sc.pe32(lambda e, pu_=pu_, E_=E_, R_=R_: e.matmul(pu_[:, 0:256], lhsT=E_, rhs=R_, start=True, stop=True), reads=[E_r, R_r], writes=[pu_r])
                    sc.add("dve", lambda e, pu_=pu_, UW_=UW_: e.tensor_copy(out=UW_[:, 0:128], in_=pu_[:, 0:128]), reads=[pu_r], writes=[UW_r])
                    sc.add("dve", lambda e, pu_=pu_, UW_=UW_: e.tensor_scalar(out=UW_[:, 128:256], in0=pu_[:, 128:256], scalar1=-1.0, scalar2=None, op0=ALU.mult), reads=[pu_r], writes=[UW_r], partial=True)
                    pq_, pq_r = pbank()
                    sc.pe32(lambda e, pq_=pq_, Qd_=Qd_: e.matmul(pq_[:, 0:128], lhsT=Qd_, rhs=ident_f[:], start=True, stop=False), reads=[Qd_r, r_const], writes=[pq_r])
                    sc.pe32(lambda e, pq_=pq_, UW_=UW_, atT_=atT_: e.matmul(pq_[:, 0:128], lhsT=UW_[:, 128:256], rhs=atT_, start=False, stop=True), reads=[UW_r, atT_r], writes=[pq_r])
                    sc.add("dve", lambda e, pq_=pq_, QpT_=QpT_: e.tensor_copy(out=QpT_, in_=pq_[:, 0:128]), reads=[pq_r], writes=[QpT_r])
                    for ci in range(2):
                        pm_, pm_r = pbank()
                        ps_ = slice(ci * 64, ci * 64 + 64)
                        sc.pe32(lambda e, pm_=pm_, UW_=UW_, Kd_=Kd_, ps_=ps_: e.matmul(pm_[:, 0:128], lhsT=UW_[ps_, 128:256], rhs=Kd_[ps_, :], start=True, stop=True), reads=[UW_r, Kd_r], writes=[pm_r])
                        if ci == 0:
                            sc.add("dve", lambda e, pm_=pm_, Mp_=Mp_, ci=ci: e.tensor_copy(out=Mp_[:, ci, :], in_=pm_[:, 0:128]), reads=[pm_r], writes=[Mp_r])
                        else:
                            sc.add("dve", lambda e, pm_=pm_, Mp_=Mp_, ci=ci: e.tensor_copy(out=Mp_[:, ci, :], in_=pm_[:, 0:128]), reads=[pm_r], writes=[Mp_r], partial=True)
                    if os.environ.get("HB", "0") == "1":
                        sc.barrier()
                    for ci in range(2 if KB > 6 else 0):
                        ps_ = slice(ci * 64, ci * 64 + 64)
                        Sp, Sp_r = Sst[cur], S_r[cur][h]
                        Sn, Sn_r = Sst[1 - cur], S_r[1 - cur][h]
                        po_, po_r = pbank()
                        tp = (0, ci * 64)
                        sc.pe32(lambda e, po_=po_, QpT_=QpT_, Sp=Sp, h=h, ps_=ps_, tp=tp: e.matmul(po_[ps_, 0:128], lhsT=QpT_[:, ps_], rhs=Sp[:, h, :], start=True, stop=False, tile_position=tp), reads=[QpT_r, Sp_r], writes=[po_r])
                        sc.pe32(lambda e, po_=po_, atT_=atT_, UW_=UW_, ps_=ps_, tp=tp: e.matmul(po_[ps_, 0:128], lhsT=atT_[:, ps_], rhs=UW_[:, 0:128], start=False, stop=True, tile_position=tp), reads=[atT_r, UW_r], writes=[po_r])
                        sc.add("dve", lambda e, po_=po_, osb_t=osb_t, h=h, ps_=ps_: e.tensor_copy(out=osb_t[ps_, h, :], in_=po_[ps_, 0:128]), reads=[po_r], writes=[osb_r], partial=True)
                        sc.add("dve", lambda e, osb_t=osb_t, h=h, ps_=ps_: e.tensor_tensor(out=junk[0][ps_, :], in0=osb_t[ps_, h, :], in1=osb_t[ps_, h, :], op=ALU.mult), reads=[osb_r], writes=[junk[1]])
                        sc.add("dve", lambda e, oss_t=oss_t, h=h, ps_=ps_: e.tensor_reduce(out=oss_t[ps_, h:h + 1], in_=junk[0][ps_, :], axis=AX.X, op=ALU.add), reads=[junk[1]], writes=[oss_r], partial=True)
                        pS_, pS_r = pbank()
                        sc.pe32(lambda e, pS_=pS_, Mp_=Mp_, ci=ci, Sp=Sp, h=h: e.matmul(pS_[:, 0:128], lhsT=Mp_[:, ci, :], rhs=Sp[:, h, :], start=True, stop=False), reads=[Mp_r, Sp_r], writes=[pS_r])
                        sc.pe32(lambda e, pS_=pS_, Kd_=Kd_, UW_=UW_, ps_=ps_: e.matmul(pS_[:, 0:128], lhsT=Kd_[ps_, :], rhs=UW_[ps_, 0:128], start=False, stop=True), reads=[Kd_r, UW_r], writes=[pS_r])
                        egl = smt[:, 24 + 4 * ci + h:25 + 4 * ci + h]
                        sc.add("dve", lambda e, pS_=pS_, Sp=Sp, Sn=Sn, h=h, egl=egl: e.scalar_tensor_tensor(out=Sn[:, h, :], in0=Sp[:, h, :], scalar=egl, in1=pS_[:, 0:128], op0=ALU.mult, op1=ALU.add), reads=[pS_r, Sp_r, sm_r], writes=[Sn_r])
                        cur = 1 - cur
                sc.safe = False
                if KB <= 7:
                    continue
                (oab_t, oab_r) = nxt(oab, "oab"); (oaT_t, oaT_r) = nxt(oaT, "oaT")
                sc.add("act", lambda e, oss_t=oss_t: e.activation(out=oss_t[:, 4:8], in_=oss_t[:, 0:4], func=AF.Sqrt, scale=1.0 / 128, bias=cbias[:, 0:1]), reads=[oss_r, r_const], writes=[oss_r])
                sc.add("dve", lambda e, oss_t=oss_t: e.reciprocal(out=oss_t[:, 4:8], in_=oss_t[:, 4:8]), reads=[oss_r], writes=[oss_r])
                sc.add("dve", lambda e, osb_t=osb_t, oss_t=oss_t: e.tensor_tensor(out=osb_t, in0=osb_t, in1=oss_t[:, 4:8].unsqueeze(2).to_broadcast([128, 4, 128]), op=ALU.mult), reads=[osb_r, oss_r], writes=[osb_r])
                sc.add("dve", lambda e, osb_t=osb_t, zst=zst, oab_t=oab_t: e.tensor_tensor(out=oab_t, in0=osb_t.rearrange("p h d -> p (h d)"), in1=zst, op=ALU.mult), reads=[osb_r, zs_r], writes=[oab_r])
                ptb = bank[5][:].bitcast(BF16).rearrange("p (k c) -> p k c", k=8)
                for h in range(4):
                    sc.pe16(ptb[:, h, :], lambda e, h=h, oab_t=oab_t: e.transpose(out=ptb[:, h, :], in_=oab_t[:, h * 128:(h + 1) * 128], identity=ident_b[:]), reads=[oab_r, r_const], writes=[bank_r[5]])
                sc.add("dve", lambda e, oaT_t=oaT_t: e.tensor_copy(out=oaT_t, in_=ptb[:, 0:4, :]), reads=[bank_r[5]], writes=[oaT_r])
                sc.add("sp", lambda e, oaT_t=oaT_t, tsl=tsl: e.dma_start(out=oT_d[0:4, :, tsl].rearrange("j p c -> p j c"), in_=oaT_t), reads=[oaT_r], writes=[], dma=True, key="oaT%d" % (ctr["oaT"] % 2))

    def ln_tile(L_, t, ps_lo, ps_lo_r, ps_hi, ps_hi_r, xr, xr_r, g_bc, b_bc, lnp_r, out_d, write_xT):
        tsl = slice(t * 128, (t + 1) * 128)
        (y, y_r) = L_["y"][t % 2]; (st, st_r) = L_["st"][t % 2]; (xb_, xb_r_) = L_["xb"][t % 2]
        for half, (pp, pp_r) in enumerate(((ps_lo, ps_lo_r), (ps_hi, ps_hi_r))):
            hs = slice(half * 512, (half + 1) * 512)
            sc.add("dve", lambda e, pp=pp, hs=hs, y=y, xr=xr: e.scalar_tensor_tensor(out=y[:, hs], in0=xr[:, hs], scalar=ALPHA, in1=pp[:, 0:512], op0=ALU.mult, op1=ALU.add),
                   reads=[pp_r, xr_r], writes=[y_r], partial=(half == 1))
        for half in range(2):
            hs = slice(half * 512, (half + 1) * 512)
            sc.add("dve", lambda e, half=half, hs=hs, y=y, st=st: e.bn_stats(out=st[:, half * 6:(half + 1) * 6], in_=y[:, hs]), reads=[y_r], writes=[st_r], partial=(half == 1))
        sc.add("dve", lambda e, st=st: e.bn_aggr(out=st[:, 12:14], in_=st[:, 0:12]), reads=[st_r], writes=[st_r])
        sc.add("act", lambda e, st=st: e.activation(out=st[:, 14:15], in_=st[:, 13:14], func=AF.Sqrt, bias=cbias[:, 2:3]), reads=[st_r, r_const], writes=[st_r])
        sc.add("dve", lambda e, st=st: e.reciprocal(out=st[:, 14:15], in_=st[:, 14:15]), reads=[st_r], writes=[st_r])
        sc.add("dve", lambda e, y=y, st=st: e.tensor_scalar(out=y, in0=y, scalar1=st[:, 12:13], scalar2=st[:, 14:15], op0=ALU.subtract, op1=ALU.mult), reads=[y_r, st_r], writes=[y_r])
        sc.add("pool", lambda e, y=y: e.tensor_tensor(out=y, in0=y, in1=g_bc, op=ALU.mult), reads=[y_r, lnp_r], writes=[y_r])
        sc.add("dve", lambda e, y=y: e.tensor_tensor(out=y, in0=y, in1=b_bc, op=ALU.add), reads=[y_r, lnp_r], writes=[y_r])
        sc.add("sp", lambda e, y=y, tsl=tsl: e.dma_start(out=out_d[tsl, :], in_=y), reads=[y_r], writes=[], dma=True, key="ysto%d" % (t % 2))
        if write_xT:
            sc.add("act", lambda e, y=y, xb_=xb_: e.copy(out=xb_, in_=y), reads=[y_r], writes=[xb_r_])
            ptb = bank[7][:].bitcast(BF16).rearrange("p (k c) -> p k c", k=8)
            for kc in range(8):
                sc.pe16(ptb[:, kc, :], lambda e, kc=kc, xb_=xb_: e.transpose(out=ptb[:, kc, :], in_=xb_[:, kc * 128:(kc + 1) * 128], identity=ident_b[:]), reads=[xb_r_, r_const], writes=[bank_r[7]])
            sc.add("dve", lambda e, tsl=tsl: e.tensor_copy(out=xT[:, :, tsl], in_=ptb), reads=[bank_r[7]], writes=[xT_r[t]])

    def ln_bufs(g_d, b_d, l):
        L_ = {}
        L_["y"] = [(cv.get([128, 1024]), sc.res("y%d" % i)) for i in range(2)]
        L_["st"] = [(cv.get([128, 16]), sc.res("st%d" % i)) for i in range(2)]
        L_["xb"] = [(cv.get([128, 1024], BF16), sc.res("xbln%d" % i)) for i in range(2)]
        g_bc = cv.get([128, 1024]); b_bc = cv.get([128, 1024]); lnp_r = sc.res("lnp")
        sc.add("sp", lambda e: e.dma_start(out=g_bc, in_=g_d[l, :].partition_broadcast(128)), writes=[lnp_r], dma=True, key="lnp", partial=True)
        sc.add("sp", lambda e: e.dma_start(out=b_bc, in_=b_d[l, :].partition_broadcast(128)), writes=[lnp_r], dma=True, key="lnp", partial=True)
        return L_, g_bc, b_bc, lnp_r

    def phaseC(l, xin_d):
        cv.reset()
        wO = cv.get([128, 8, 1024], BF16); wO_r = [sc.res("wO%d" % k) for k in range(8)]
        for kc in range(8):
            sc.add("pool", lambda e, kc=kc: e.dma_start(out=wO[:, kc, :], in_=w_out_d[l, kc * 128:(kc + 1) * 128, :]), writes=[wO_r[kc]], dma=True, key="wO%d" % kc)
        for kc in range(8):
            sc.add("pool", lambda e, kc=kc: e.dma_start(out=wupbf_d[l].rearrange("t p k f -> p t k f")[:, :, kc, :], in_=w_up_d[l, kc * 128:(kc + 1) * 128, :].rearrange("p (t f) -> p t f", f=128)),
                   writes=[wupbf_r[l]], dma=True, key="wcv%d" % (kc % 4), partial=True)
        L_, g_bc, b_bc, lnp_r = ln_bufs(ln1g_d, ln1b_d, l)
        oTt = [(cv.get([128, 8, 128], BF16), sc.res("oTt%d" % i)) for i in range(3)]
        xrs = [(cv.get([128, 1024]), sc.res("xr%d" % i)) for i in range(3)]
        for t in range(NT):
            tsl = slice(t * 128, (t + 1) * 128)
            (ot, ot_r) = oTt[t % 3]; (xr, xr_r) = xrs[t % 3]
            sc.add("sp", lambda e, ot=ot, tsl=tsl: e.dma_start(out=ot, in_=oT_d[:, :, tsl].rearrange("k p c -> p k c")), writes=[ot_r], dma=True, key="oTt%d" % (t % 3))
            sc.add("sp", lambda e, xr=xr, tsl=tsl: e.dma_start(out=xr, in_=xin_d[tsl, :]), writes=[xr_r], dma=True, key="xr%d" % (t % 3))
            bl, bh = 2 * (t % 2), 2 * (t % 2) + 1
            for half, bi in ((0, bl), (1, bh)):
                for kc in range(8):
                    sc.pe16(bank[bi][:], lambda e, bi=bi, kc=kc, ot=ot, half=half: e.matmul(bank[bi][:], lhsT=ot[:, kc, :], rhs=wO[:, kc, half * 512:(half + 1) * 512], start=(kc == 0), stop=(kc == 7)),
                            reads=[ot_r, wO_r[kc]], writes=[bank_r[bi]])
            ln_tile(L_, t, bank[bl], bank_r[bl], bank[bh], bank_r[bh], xr, xr_r, g_bc, b_bc, lnp_r, x1_d, True)

    def phaseD(l, out_d, write_xT):
        cv.reset()
        NJ = DFF // 128
        NBLK = S // 512
        wD = cv.get([128, NJ, 1024], BF16); wD_r = [sc.res("wD%d" % j) for j in range(NJ)]
        for j in range(NJ):
            sc.add("pool", lambda e, j=j: e.dma_start(out=wD[:, j, :], in_=w_down_d[l, j * 128:(j + 1) * 128, :]), writes=[wD_r[j]], dma=True, key="wD%d" % (j % 4))
        L_, g_bc, b_bc, lnp_r = ln_bufs(ln2g_d, ln2b_d, l)
        fc4 = cv.get([128, 4, 44]); fc_r = sc.res("fc4")
        hT = cv.get([128, NJ, 512], BF16); hT_r = [sc.res("hT%d" % j) for j in range(NJ)]
        wU = [(cv.get([128, 8, 256], BF16), sc.res("wU%d" % i)) for i in range(3)]
        raw = [(cv.get([128, 2, 514]), sc.res("raw%d" % i)) for i in range(2)]
        acc = [(cv.get([128, 2, 512]), sc.res("facc%d" % i)) for i in range(2)]
        halo = cv.get([128, 44, 2]); halo_r = [sc.res("fhalo%d" % j) for j in range(44)]
        xrs = [(cv.get([128, 1024]), sc.res("xrD%d" % i)) for i in range(2)]
        w44 = hT.rearrange("p j t -> p (j t)").bitcast(F32)[0:44, 0:512].rearrange("p (a b) -> p a b", a=4)
        w44_r = sc.res("w44")
        for j3 in range(3):
            sc.add("sp", lambda e, j3=j3: e.dma_start(out=w44[:, j3, :], in_=fconvw_d[l, j3, :].rearrange("(f p) -> f p", p=128)), writes=[w44_r] + hT_r[0:2], dma=True, key="w44", partial=True)
        sc.add("sp", lambda e: e.dma_start(out=w44[:, 3, :], in_=fconvb_d[l, :].rearrange("(f p) -> f p", p=128)), writes=[w44_r], dma=True, key="w44", partial=True)
        for a4 in range(4):
            sc.pe32(lambda e, a4=a4: e.transpose(out=bank[0][:, a4 * 44:(a4 + 1) * 44], in_=w44[:, a4, :], identity=ident_f[0:44, 0:44]), reads=[w44_r, r_const], writes=[bank_r[0]])
        sc.add("dve", lambda e: e.tensor_copy(out=fc4.rearrange("p a f -> p (a f)"), in_=bank[0][:, 0:176]), reads=[bank_r[0]], writes=[fc_r])
        sc.add("pool", lambda e: e.memset(halo, 0.0), reads=[], writes=halo_r)
        sc.barrier()
        for c in range(NBLK):
            csl = slice(c * 512, (c + 1) * 512)
            for j in range(NJ):
                (wu, wu_r) = wU[j % 3]
                sc.add("sp", lambda e, wu=wu, j=j: e.dma_start(out=wu[:, :, 0:128], in_=wupbf_d[l][j, :, :, :]), reads=[wupbf_r[l]], writes=[wu_r], dma=True, key="wUa%d" % (j % 3))
                sc.add("sp", lambda e, wu=wu, j=j: e.dma_start(out=wu[:, :, 128:256], in_=wupbf_d[l][22 + j, :, :, :]), reads=[wupbf_r[l]], writes=[wu_r], dma=True, key="wUb%d" % (j % 3), partial=True)
                (rw, rw_r) = raw[j % 2]; (ac, ac_r) = acc[j % 2]
                for gv in range(2):
                    f = gv * 22 + j
                    bi = 2 * (j % 2) + gv
                    for kc in range(8):
                        sc.pe16(bank[bi][:], lambda e, bi=bi, kc=kc, wu=wu, gv=gv, csl=csl: e.matmul(bank[bi][:], lhsT=wu[:, kc, gv * 128:(gv + 1) * 128], rhs=xT[:, kc, csl], start=(kc == 0), stop=(kc == 7)),
                                reads=[wu_r] + xT_r[4 * c:4 * c + 4], writes=[bank_r[bi]])
                    sc.add("pool", lambda e, rw=rw, gv=gv, f=f: e.tensor_copy(out=rw[:, gv, 0:2], in_=halo[:, f, :]), reads=[halo_r[f]], writes=[rw_r], partial=(gv == 1))
                    sc.add("act", lambda e, rw=rw, gv=gv, bi=bi: e.copy(out=rw[:, gv, 2:514], in_=bank[bi][:]), reads=[bank_r[bi]], writes=[rw_r], partial=True)
                    sc.add("pool", lambda e, rw=rw, gv=gv, f=f: e.tensor_copy(out=halo[:, f, :], in_=rw[:, gv, 512:514]), reads=[rw_r], writes=[halo_r[f]])
                    sc.add("act", lambda e, ac=ac, gv=gv, bi=bi, f=f: e.activation(out=ac[:, gv, :], in_=bank[bi][:], func=AF.Identity, scale=fc4[:, 2, f:f + 1], bias=fc4[:, 3, f:f + 1]), reads=[bank_r[bi], fc_r], writes=[ac_r], partial=(gv == 1))
                    eng = "dve"
                    for tap in (1, 0):
                        sc.add(eng, lambda e, ac=ac, rw=rw, gv=gv, tap=tap, f=f: e.scalar_tensor_tensor(out=ac[:, gv, :], in0=rw[:, gv, tap:tap + 512], scalar=fc4[:, tap, f:f + 1], in1=ac[:, gv, :], op0=ALU.mult, op1=ALU.add),
                               reads=[rw_r, fc_r, ac_r], writes=[ac_r])
                sc.add("act", lambda e, ac=ac: e.activation(out=ac[:, 0, :], in_=ac[:, 0, :], func=AF.Silu, bias=cbias[:, 3:4]), reads=[ac_r, r_const], writes=[ac_r])
                sc.add("dve", lambda e, ac=ac, j=j: e.tensor_tensor(out=hT[:, j, :], in0=ac[:, 0, :], in1=ac[:, 1, :], op=ALU.mult), reads=[ac_r], writes=[hT_r[j]])
            for tt in range(4):
                t = c * 4 + tt
                tsl = slice(t * 128, (t + 1) * 128)
                (xr, xr_r) = xrs[t % 2]
                sc.add("sp", lambda e, xr=xr, tsl=tsl: e.dma_start(out=xr, in_=x1_d[tsl, :]), writes=[xr_r], dma=True, key="xrD%d" % (t % 2))
                bl, bh = 4 + 2 * (t % 2), 5 + 2 * (t % 2)
                if bh == 7 and write_xT:
                    bl, bh = 4, 5
                for half, bi in ((0, bl), (1, bh)):
                    for j in range(NJ):
                        sc.pe16(bank[bi][:], lambda e, bi=bi, j=j, tt=tt, half=half: e.matmul(bank[bi][:], lhsT=hT[:, j, tt * 128:(tt + 1) * 128], rhs=wD[:, j, half * 512:(half + 1) * 512], start=(j == 0), stop=(j == NJ - 1)),
                                reads=[hT_r[j], wD_r[j]], writes=[bank_r[bi]])
                ln_tile(L_, t, bank[bl], bank_r[bl], bank[bh], bank_r[bh], xr, xr_r, g_bc, b_bc, lnp_r, out_d, write_xT)

    phase0(x_d)
    sc.barrier()
    if stop_after == "0":
        sc.add("sp", lambda e: e.dma_start(out=oT_d[:, :, :].rearrange("k p s -> p k s"), in_=xT[:]), reads=xT_r, writes=[], dma=True, key="dbg")
    elif stop_after == "A":
        phaseA(0)
    elif stop_after == "B":
        phaseB(0)
    else:
        for l in range(L):
            xin = x_d if l == 0 else x2_d
            last = (l == L - 1)
            phaseA(l)
            sc.barrier()
            phaseB(l)
            sc.barrier()
            phaseC(l, xin)
            sc.barrier()
            if stop_after == "C" and l == 0:
                break
            phaseD(l, y_d if last else x2_d, not last)
            sc.barrier()
            if stop_after == "D" and l == 0:
                break
    if dbg and os.environ.get("DUMPARENA") and not os.environ.get("SIM"):
        sc.barrier()
        dbg_arena = nc.dram_tensor("dbg_arena", [128, ARENA], F32, kind="ExternalOutput").ap()
        for q in range(4):
            sc.add("sp", lambda e, q=q: e.dma_start(out=dbg_arena[:, q * (ARENA // 4):(q + 1) * (ARENA // 4)], in_=arena[:, q * (ARENA // 4):(q + 1) * (ARENA // 4)]), dma=True, key="dbga")
    sc.emit(nc, es)
    es.close()
    return nc


_CACHE = {}


def kernel(**inputs):
    x = np.asarray(inputs["x"], dtype=np.float32)
    B, S, _ = x.shape
    L = int(np.asarray(inputs["w_in"]).shape[0])
    key = (S, L)
    if key not in _CACHE:
        _CACHE[key] = (build(S=S, L=L), make_consts(S))
    nc, consts = _CACHE[key]
    shared = {k: np.ascontiguousarray(np.asarray(v, dtype=np.float32)) for k, v in inputs.items() if k != "x"}
    for k, v in consts.items():
        shared["c_" + k] = v
    in_maps = []
    for b in range(B):
        m = dict(shared)
        m["x"] = np.ascontiguousarray(x[b])
        in_maps.append(m)
    res = run_bass_kernel_spmd(nc, in_maps, core_ids=list(range(B)))
    return np.stack([np.asarray(r["y"], dtype=np.float32) for r in res.results], axis=0)
```
